# Optimizing a Trainium2 kernel written in Bass

```python
import jax
import jax.numpy as jnp
from jax import lax
import numpy as np

D_MODEL = 1024
BATCH = 4
SEQ = 8192
DEPTH = 2

CHUNK = 64
EPS = 1e-6
CONV_CH = D_MODEL // 2
CONV_WIDTH = 31
GLA_HEADS = 4
GLA_DK = D_MODEL // 16
GLA_DV = D_MODEL // 8
GLA_GATE_RANK = 16
GLA_GATE_TAU = 16.0
FOX_HEADS = 16
FOX_HD = D_MODEL // FOX_HEADS
Q_BLOCK = 128
PEER_HEADS = 8
PEER_NKEYS = 128
PEER_N = PEER_NKEYS * PEER_NKEYS
PEER_QDIM = 256
PEER_TOPK = 16
PEER_TOKEN_BLOCK = 128

EVEN_IN = 2 * CONV_CH + 2 * GLA_HEADS * GLA_DK + 2 * GLA_HEADS * GLA_DV + GLA_GATE_RANK
ODD_IN = 3 * FOX_HEADS * FOX_HD + FOX_HEADS

kernel_name = 'hybrid_conv_gla_fox_peer_trunk'


def rms_norm(x, g):
    xf = x.astype(jnp.float32)
    y = xf * lax.rsqrt(jnp.mean(xf * xf, axis=-1, keepdims=True) + EPS)
    return (y * g.astype(jnp.float32)).astype(x.dtype)


def layer_norm(x, g, b):
    xf = x.astype(jnp.float32)
    mu = jnp.mean(xf, axis=-1, keepdims=True)
    xc = xf - mu
    y = xc * lax.rsqrt(jnp.mean(xc * xc, axis=-1, keepdims=True) + EPS)
    return (y * g.astype(jnp.float32) + b.astype(jnp.float32)).astype(x.dtype)


def conformer_conv(val, gate, conv_w, conv_b, ln_g, ln_b):
    u = val * jax.nn.sigmoid(gate)
    w = conv_w.astype(u.dtype)[:, None, :]
    y = lax.conv_general_dilated(u, w, window_strides=(1,), padding=[(CONV_WIDTH - 1, 0)],
                                 dimension_numbers=('NWC', 'WIO', 'NWC'),
                                 feature_group_count=u.shape[-1])
    y = y + conv_b.astype(y.dtype)
    return jax.nn.silu(layer_norm(y, ln_g, ln_b))


def gla_chunked(q, k, v, log_a):
    B, S, H, dk = q.shape
    dv = v.shape[-1]
    n = S // CHUNK

    def to_chunks(t):
        return t.reshape(B, n, CHUNK, H, t.shape[-1]).transpose(1, 0, 3, 2, 4)

    qc, kc, vc, gc = to_chunks(q * (dk ** -0.5)), to_chunks(k), to_chunks(v), to_chunks(log_a)
    causal = jnp.tril(jnp.ones((CHUNK, CHUNK), dtype=bool))

    def step(state, inp):
        qi, ki, vi, gi = inp
        b = jnp.cumsum(gi, axis=2)
        b_last = b[:, :, -1:, :]
        o_inter = jnp.einsum('bhtk,bhkv->bhtv', qi * jnp.exp(b), state)
        diff = b[:, :, :, None, :] - b[:, :, None, :, :]
        decay = jnp.exp(jnp.where(causal[:, :, None], diff, -jnp.inf))
        attn = jnp.einsum('bhtk,bhsk,bhtsk->bhts', qi, ki, decay)
        o = o_inter + jnp.einsum('bhts,bhsv->bhtv', attn, vi)
        k_dec = ki * jnp.exp(b_last - b)
        new_state = state * jnp.exp(b_last[:, :, 0, :, None]) + jnp.einsum('bhsk,bhsv->bhkv', k_dec, vi)
        return new_state, o

    s0 = jnp.zeros((B, H, dk, dv), jnp.float32)
    _, o = lax.scan(step, s0, (qc, kc, vc, gc))
    return o.transpose(1, 0, 3, 2, 4).reshape(B, S, H, dv)


def even_mix(h, w_in, conv_w, conv_b, ln_g, ln_b, gate_w2, gate_b, gla_norm_g, w_out):
    B, S, _ = h.shape
    hk = GLA_HEADS * GLA_DK
    hv = GLA_HEADS * GLA_DV
    p = h @ w_in
    cuts = [int(c) for c in np.cumsum([CONV_CH, CONV_CH, hk, hk, hv, hv])]
    a_val, a_gate, q, k, v, r, glr = jnp.split(p, cuts, axis=-1)
    y_a = conformer_conv(a_val, a_gate, conv_w, conv_b, ln_g, ln_b)
    f32 = jnp.float32
    log_a = jax.nn.log_sigmoid((glr @ gate_w2 + gate_b).astype(f32)) / GLA_GATE_TAU
    o = gla_chunked(q.reshape(B, S, GLA_HEADS, GLA_DK).astype(f32),
                    k.reshape(B, S, GLA_HEADS, GLA_DK).astype(f32),
                    v.reshape(B, S, GLA_HEADS, GLA_DV).astype(f32),
                    log_a.reshape(B, S, GLA_HEADS, GLA_DK))
    o = rms_norm(o, gla_norm_g).reshape(B, S, hv).astype(h.dtype)
    y_b = o * jax.nn.silu(r)
    return jnp.concatenate([y_a, y_b], axis=-1) @ w_out


def fox_mix(h, w_in, fgate_b, q_g, k_g, w_out):
    B, S, _ = h.shape
    hd = FOX_HEADS * FOX_HD
    p = h @ w_in
    q, k, v, fz = jnp.split(p, [hd, 2 * hd, 3 * hd], axis=-1)
    q = rms_norm(q.reshape(B, S, FOX_HEADS, FOX_HD), q_g).transpose(0, 2, 1, 3)
    k = rms_norm(k.reshape(B, S, FOX_HEADS, FOX_HD), k_g).transpose(0, 2, 1, 3)
    v = v.reshape(B, S, FOX_HEADS, FOX_HD).transpose(0, 2, 1, 3)
    f32 = jnp.float32
    log_f = jax.nn.log_sigmoid(fz.astype(f32) + fgate_b.astype(f32))
    c = jnp.cumsum(log_f, axis=1).transpose(0, 2, 1)
    scale = FOX_HD ** -0.5
    pos = jnp.arange(S)
    outs = []
    for i in range(S // Q_BLOCK):
        lo, hi = i * Q_BLOCK, (i + 1) * Q_BLOCK
        logits = jnp.einsum('bhqd,bhkd->bhqk', q[:, :, lo:hi], k[:, :, :hi]).astype(f32) * scale
        logits = logits + c[:, :, lo:hi, None] - c[:, :, None, :hi]
        mask = pos[lo:hi, None] >= pos[None, :hi]
        probs = jax.nn.softmax(jnp.where(mask, logits, -jnp.inf), axis=-1)
        outs.append(jnp.einsum('bhqk,bhkd->bhqd', probs.astype(v.dtype), v[:, :, :hi]))
    o = jnp.concatenate(outs, axis=2).transpose(0, 2, 1, 3).reshape(B, S, hd)
    return o @ w_out


def peer_ffn(h, wq, keys, u, v):
    B, S, D = h.shape
    f32 = jnp.float32
    half = PEER_QDIM // 2
    q = (h @ wq).reshape(B, S, PEER_HEADS, 2, half).astype(f32)
    s1 = jnp.einsum('bshd,hnd->bshn', q[..., 0, :], keys[:, 0].astype(f32))
    s2 = jnp.einsum('bshd,hnd->bshn', q[..., 1, :], keys[:, 1].astype(f32))
    v1, i1 = lax.top_k(s1, PEER_TOPK)
    v2, i2 = lax.top_k(s2, PEER_TOPK)
    cand = (v1[..., :, None] + v2[..., None, :]).reshape(B, S, PEER_HEADS, PEER_TOPK * PEER_TOPK)
    sc, j = lax.top_k(cand, PEER_TOPK)
    e1 = jnp.take_along_axis(i1, j // PEER_TOPK, axis=-1)
    e2 = jnp.take_along_axis(i2, j % PEER_TOPK, axis=-1)
    experts = e1 * PEER_NKEYS + e2
    gates = jax.nn.softmax(sc, axis=-1)
    nb = (B * S) // PEER_TOKEN_BLOCK
    ne = PEER_HEADS * PEER_TOPK
    hb = h.reshape(nb, PEER_TOKEN_BLOCK, D)
    eb = experts.reshape(nb, PEER_TOKEN_BLOCK, ne)
    gb = gates.reshape(nb, PEER_TOKEN_BLOCK, ne).astype(h.dtype)

    def block(args):
        ht, et, gt = args
        act = jax.nn.gelu(jnp.einsum('td,ted->te', ht, u[et]), approximate=False)
        return jnp.einsum('te,ted->td', gt * act, v[et])

    out = lax.map(block, (hb, eb, gb))
    return out.reshape(B, S, D)


def setup_inputs(seed: int = 0) -> dict:
    key = jax.random.key(seed)
    ks = iter(jax.random.split(key, 32))
    n_even = (DEPTH + 1) // 2
    n_odd = DEPTH // 2
    f32 = jnp.float32
    D = D_MODEL

    def nrm(shape, scale):
        return jax.random.normal(next(ks), shape, f32) * scale

    def gain(shape):
        return 1.0 + 0.05 * jax.random.normal(next(ks), shape, f32)

    return {
        'x': nrm((BATCH, SEQ, D), 1.0),
        'ev_norm_mix': gain((n_even, D)),
        'ev_w_in': nrm((n_even, D, EVEN_IN), D ** -0.5),
        'ev_conv_w': nrm((n_even, CONV_WIDTH, CONV_CH), CONV_WIDTH ** -0.5),
        'ev_conv_b': nrm((n_even, CONV_CH), 0.02),
        'ev_conv_ln_g': gain((n_even, CONV_CH)),
        'ev_conv_ln_b': nrm((n_even, CONV_CH), 0.02),
        'ev_gate_w2': nrm((n_even, GLA_GATE_RANK, GLA_HEADS * GLA_DK), GLA_GATE_RANK ** -0.5),
        'ev_gate_b': nrm((n_even, GLA_HEADS * GLA_DK), 0.1),
        'ev_gla_norm_g': gain((n_even, GLA_DV)),
        'ev_w_out': nrm((n_even, CONV_CH + GLA_HEADS * GLA_DV, D), (CONV_CH + GLA_HEADS * GLA_DV) ** -0.5),
        'od_norm_mix': gain((n_odd, D)),
        'od_w_in': nrm((n_odd, D, ODD_IN), D ** -0.5),
        'od_fgate_b': jax.random.uniform(next(ks), (n_odd, FOX_HEADS), f32, minval=1.0, maxval=4.0),
        'od_q_norm_g': gain((n_odd, FOX_HD)),
        'od_k_norm_g': gain((n_odd, FOX_HD)),
        'od_w_out': nrm((n_odd, FOX_HEADS * FOX_HD, D), (FOX_HEADS * FOX_HD) ** -0.5),
        'ffn_norm': gain((DEPTH, D)),
        'peer_wq': nrm((DEPTH, D, PEER_HEADS * PEER_QDIM), D ** -0.5),
        'peer_keys': nrm((DEPTH, PEER_HEADS, 2, PEER_NKEYS, PEER_QDIM // 2), (PEER_QDIM // 2) ** -0.5),
        'peer_u': nrm((DEPTH, PEER_N, D), D ** -0.5),
        'peer_v': nrm((DEPTH, PEER_N, D), (PEER_HEADS * PEER_TOPK) ** -0.5),
    }


def reference(x, ev_norm_mix, ev_w_in, ev_conv_w, ev_conv_b, ev_conv_ln_g, ev_conv_ln_b,
              ev_gate_w2, ev_gate_b, ev_gla_norm_g, ev_w_out,
              od_norm_mix, od_w_in, od_fgate_b, od_q_norm_g, od_k_norm_g, od_w_out,
              ffn_norm, peer_wq, peer_keys, peer_u, peer_v):
    for layer in range(DEPTH):
        j = layer // 2
        if layer % 2 == 0:
            h = rms_norm(x, ev_norm_mix[j])
            x = x + even_mix(h, ev_w_in[j], ev_conv_w[j], ev_conv_b[j], ev_conv_ln_g[j], ev_conv_ln_b[j],
                             ev_gate_w2[j], ev_gate_b[j], ev_gla_norm_g[j], ev_w_out[j])
        else:
            h = rms_norm(x, od_norm_mix[j])
            x = x + fox_mix(h, od_w_in[j], od_fgate_b[j], od_q_norm_g[j], od_k_norm_g[j], od_w_out[j])
        h = rms_norm(x, ffn_norm[layer])
        x = x + peer_ffn(h, peer_wq[layer], peer_keys[layer], peer_u[layer], peer_v[layer])
    return x
```

```python
from contextlib import ExitStack

import numpy as np
import concourse.bass as bass
import concourse.mybir as mybir
from concourse.bass_utils import run_bass_kernel_spmd

F32 = mybir.dt.float32
BF16 = mybir.dt.bfloat16
U32 = mybir.dt.uint32
I32 = mybir.dt.int32
ALU = mybir.AluOpType
AF = mybir.ActivationFunctionType
AX = mybir.AxisListType

NCORES = 8
EPOCH = 30000
ENGS = ("pe", "dve", "act", "pool", "sp")


class Buf:
    __slots__ = ("w", "r", "name")

    def __init__(self, name=""):
        self.w = None
        self.r = []
        self.name = name


class TT:
    def __init__(self, t, name):
        self.t = t
        self.b = Buf(name)

    def __getitem__(self, k):
        return self.t[k]


class Sched:
    def __init__(self, nc, stack):
        self.nc = nc
        self.stack = stack
        self.q = {e: [] for e in ENGS}
        self.csem = {e: None for e in ENGS}
        self.ccnt = {e: 0 for e in ENGS}
        self.dsem = {e: [] for e in ENGS}
        self.dcnt = {e: [] for e in ENGS}
        self.drr = {e: 0 for e in ENGS}
        self.seen = {e: {} for e in ENGS}
        self.nsem = 0
        self.ninst = 0

    def _newsem(self, nm):
        self.nsem += 1
        return self.stack.enter_context(self.nc.semaphore(f"{nm}{self.nsem}"))

    def _ticket(self, e, dma):
        if not dma:
            if self.csem[e] is None or self.ccnt[e] >= EPOCH:
                self.csem[e] = self._newsem("c" + e)
                self.ccnt[e] = 0
            self.ccnt[e] += 1
            return (self.csem[e], self.ccnt[e], e, 1)
        if not self.dsem[e]:
            self.dsem[e] = [self._newsem("d" + e) for _ in range(8)]
            self.dcnt[e] = [0] * 8
        i = self.drr[e] % 8
        self.drr[e] += 1
        if self.dcnt[e][i] + 16 >= EPOCH:
            self.dsem[e][i] = self._newsem("d" + e)
            self.dcnt[e][i] = 0
        self.dcnt[e][i] += 16
        return (self.dsem[e][i], self.dcnt[e][i], e + "_dma", 16)

    def op(self, e, fn, reads=(), writes=(), dma=False):
        deps = {}

        def add(t):
            if t is None:
                return
            sem, val, src, _ = t
            if src == "pe" and e == "pe" and not dma:
                return
            k = id(sem)
            if self.seen[e].get(k, 0) >= val:
                return
            if k not in deps or deps[k][1] < val:
                deps[k] = (sem, val)

        for b in reads:
            b = b.b if isinstance(b, TT) else b
            add(b.w)
        for b in writes:
            b = b.b if isinstance(b, TT) else b
            add(b.w)
            for t in b.r:
                add(t)
        waits = list(deps.values())
        for sem, val in waits:
            self.seen[e][id(sem)] = val
        t = self._ticket(e, dma)
        self.q[e].append((waits, fn, t[0], t[3]))
        self.ninst += 1 + len(waits)
        for b in reads:
            b = b.b if isinstance(b, TT) else b
            b.r = [x for x in b.r if x[0] is not t[0]] + [t]
        for b in writes:
            b = b.b if isinstance(b, TT) else b
            b.w = t
            b.r = []
        return t

    def barrier(self):
        waits = []
        for e in ENGS:
            if self.csem[e] is not None and self.ccnt[e] > 0:
                waits.append((self.csem[e], self.ccnt[e]))
            for sem, c in zip(self.dsem[e], self.dcnt[e]):
                if c > 0:
                    waits.append((sem, c))
        for e in ENGS:
            self.q[e].append((list(waits), None, None, 0))
            for sem, val in waits:
                self.seen[e][id(sem)] = max(self.seen[e].get(id(sem), 0), val)

    def final_wait(self, e, bufs):
        waits = []
        for b in bufs:
            b = b.b if isinstance(b, TT) else b
            for t in [b.w] + list(b.r):
                if t is not None:
                    waits.append((t[0], t[1]))
        self.q[e].append((waits, None, None, 0))

    def emit(self):
        nc = self.nc
        q = self.q

        def run(e, eng):
            for waits, fn, sem, inc in q[e]:
                for s, v in waits:
                    eng.wait_ge(s, v)
                if fn is not None:
                    ins = fn(eng)
                    ins.then_inc(sem, inc)

        with nc.Block() as block:

            @block.tensor
            def _(eng):
                run("pe", eng)

            @block.vector
            def _(eng):
                run("dve", eng)

            @block.scalar
            def _(eng):
                run("act", eng)

            @block.gpsimd
            def _(eng):
                run("pool", eng)

            @block.sync
            def _(eng):
                run("sp", eng)


class KB:
    def __init__(self, nc, stack, sched=None):
        self.nc = nc
        self.stack = stack
        self.s = sched if sched is not None else Sched(nc, stack)

    def scope(self, stack):
        return KB(self.nc, stack, self.s)

    def sb(self, name, shape, dt):
        t = self.stack.enter_context(self.nc.sbuf_tensor(name, list(shape), dt))
        return TT(t, name)

    def dram(self, name, shape, dt, kind):
        t = self.nc.dram_tensor(name, list(shape), dt, kind=kind)
        return TT(t.ap(), name)

    def dma(self, out, in_, reads, writes, e="sp", **kw):
        return self.s.op(e, lambda g: g.dma_start(out=out, in_=in_, **kw), reads, writes, dma=True)

    def mm(self, out, lhsT, rhs, start, stop, reads, writes):
        return self.s.op("pe", lambda g: g.matmul(out, lhsT, rhs, start=start, stop=stop), reads, writes)

    def tr(self, out, in_, ident, reads, writes):
        return self.s.op("pe", lambda g: g.transpose(out, in_, ident), reads, writes)

    def act(self, out, in_, func, reads, writes, bias=None, scale=None, accum_out=None):
        kw = {}
        if bias is not None:
            kw["bias"] = bias
        if scale is not None:
            kw["scale"] = scale
        if accum_out is not None:
            kw["accum_out"] = accum_out
        return self.s.op("act", lambda g: g.activation(out, in_, func, **kw), reads, writes)

    def tt(self, out, in0, in1, op, reads, writes, e="dve"):
        return self.s.op(e, lambda g: g.tensor_tensor(out, in0, in1, op), reads, writes)

    def ts(self, out, in0, s1, s2, op0, op1, reads, writes, e="dve", accum_out=None):
        if op1 is None:
            return self.s.op(e, lambda g: g.tensor_scalar(out, in0, s1, None, op0), reads, writes)
        if accum_out is not None:
            return self.s.op(e, lambda g: g.tensor_scalar(out, in0, s1, s2, op0, op1, accum_out), reads, writes)
        return self.s.op(e, lambda g: g.tensor_scalar(out, in0, s1, s2, op0, op1), reads, writes)

    def stt(self, out, in0, scalar, in1, op0, op1, reads, writes, e="dve"):
        return self.s.op(e, lambda g: g.scalar_tensor_tensor(out, in0, scalar, in1, op0, op1), reads, writes)

    def copy(self, out, in_, reads, writes, e="dve"):
        if e == "act":
            return self.s.op(e, lambda g: g.copy(out, in_), reads, writes)
        return self.s.op(e, lambda g: g.tensor_copy(out, in_), reads, writes)

    def memset(self, ap, val, writes, e="dve"):
        return self.s.op(e, lambda g: g.memset(ap, val), (), writes)

    def reduce(self, out, in_, op, reads, writes, axis=AX.X, e="dve"):
        return self.s.op(e, lambda g: g.tensor_reduce(out, in_, axis, op), reads, writes)


def to_bf16_dram(kb, src, dst, R, C, tag):
    with ExitStack() as st:
        k = kb.scope(st)
        W = 2048
        stg = [k.sb(f"cv_in{tag}{i}", [128, W], F32) for i in range(2)]
        outb = [k.sb(f"cv_out{tag}{i}", [128, W], BF16) for i in range(2)]
        n = 0
        for r in range(R // 128):
            for c0 in range(0, C, W):
                w = min(W, C - c0)
                i = n % 2
                k.dma(stg[i][:, 0:w], src[r * 128:(r + 1) * 128, c0:c0 + w], [src], [stg[i]])
                k.copy(outb[i][:, 0:w], stg[i][:, 0:w], [stg[i]], [outb[i]], e=("dve" if n % 2 == 0 else "act"))
                k.dma(dst[r * 128:(r + 1) * 128, c0:c0 + w], outb[i][:, 0:w], [outb[i]], [], e="pool")
                n += 1
        k.s.barrier()


def load_bf16_resident(kb, dst_tt, dst_ap_fn, src, nrows_blocks, C, tag):
    with ExitStack() as st:
        k = kb.scope(st)
        W = 2048
        stg = [k.sb(f"ld_in{tag}{i}", [128, W], F32) for i in range(2)]
        n = 0
        for kc in range(nrows_blocks):
            for c0 in range(0, C, W):
                w = min(W, C - c0)
                i = n % 2
                k.dma(stg[i][:, 0:w], src[kc * 128:(kc + 1) * 128, c0:c0 + w], [src], [stg[i]])
                k.copy(dst_ap_fn(kc, c0, w), stg[i][:, 0:w], [stg[i]], [dst_tt], e=("dve" if n % 2 == 0 else "act"))
                n += 1
        k.s.barrier()


def peer_tables(kb0, uT, vtab, tag):
    us = kb0.dram(f"us{tag}", [1024, 16384], BF16, "Internal")
    vs = kb0.dram(f"vs{tag}", [16384, 1024], BF16, "Internal")
    to_bf16_dram(kb0, uT, us, 1024, 16384, tag + "u")
    to_bf16_dram(kb0, vtab, vs, 16384, 1024, tag + "v")
    return us, vs


def peer_block(kb0, x_in, x_out, gvec, wq, keysT, uT, vtab, ident_d, iota_d, T, tag, G=256, CH=4, OHT=32, tables=None, jobs=None):
    nc = kb0.nc
    if jobs is None:
        jobs = [(x_in, x_out, T)]
    with ExitStack() as st:
        kb = kb0.scope(st)
        TPG = G // 128
        if tables is None:
            tables = peer_tables(kb, uT, vtab, tag)
        us, vs = tables
        wqb = kb.sb("wqb" + tag, [128, 8, 2048], BF16)
        load_bf16_resident(kb, wqb, lambda kc, c0, w: wqb[:, kc, c0:c0 + w], wq, 8, 2048, tag + "wq")
        kTb = kb.sb("kTb" + tag, [128, 16 * 128], BF16)
        load_bf16_resident(kb, kTb, lambda kc, c0, w: kTb[:, c0:c0 + w], keysT, 1, 2048, tag + "kt")
        gB = kb.sb("gB" + tag, [128, 1024], F32)
        kb.dma(gB[:], gvec.t.partition_broadcast(128)[:, 0, :], [gvec], [gB])
        ident = kb.sb("ident" + tag, [128, 128], F32)
        kb.dma(ident[:], ident_d[:], [ident_d], [ident])
        iota = kb.sb("iota" + tag, [128, 128], F32)
        kb.dma(iota[:], iota_d[:], [iota_d], [iota])

        xs = [kb.sb(f"xs{tag}{j}", [128, 1024], F32) for j in range(TPG)]
        hf = kb.sb("hf" + tag, [128, 1024], F32)
        sq = kb.sb("sq" + tag, [128, 1024], BF16)
        ss = kb.sb("ss" + tag, [128, 4], F32)
        hT = kb.sb("hT" + tag, [128, 8, G], BF16)
        qT = kb.sb("qT" + tag, [128, 16, 128], BF16)
        sc = kb.sb("sc" + tag, [128, 16, 128], F32)
        tmp = kb.sb("tmp" + tag, [128, 256], F32)
        v16 = kb.sb("v16" + tag, [128, 16, 16], F32)
        i16 = kb.sb("i16" + tag, [128, 16, 16], U32)
        i16f = kb.sb("i16f" + tag, [128, 16, 16], F32)
        cand = kb.sb("cand" + tag, [128, 8, 16, 16], F32)
        s16 = kb.sb("s16" + tag, [128, 8, 16], F32)
        j16 = kb.sb("j16" + tag, [128, 8, 16], U32)
        jaf = kb.sb("jaf" + tag, [128, 8, 16], F32)
        ja = kb.sb("ja" + tag, [128, 8, 16], U32)
        jb = kb.sb("jb" + tag, [128, 8, 16], U32)
        jbf = kb.sb("jbf" + tag, [128, 8, 16], F32)
        eq = cand
        trio = kb.sb("trio" + tag, [128, 3, 128], F32)
        zz = kb.sb("zz" + tag, [128, 8], F32)
        trioT = kb.sb("trioT" + tag, [128, 3, 128], F32)
        oh1 = kb.sb("oh1" + tag, [128, OHT, 128], BF16)
        oh2 = kb.sb("oh2" + tag, [128, OHT, 128], BF16)
        WT = kb.sb("WT" + tag, [128, 128, G], BF16)
        ub = [kb.sb(f"ub{tag}{i}", [128, 8, CH * 128], BF16) for i in range(2)]
        vb = [kb.sb(f"vb{tag}{i}", [128, CH, 1024], BF16) for i in range(2)]
        actT = [kb.sb(f"actT{tag}{i}", [128, G], BF16) for i in range(2)]
        ct = [kb.sb(f"ct{tag}{i}", [128, G], BF16) for i in range(2)]
        xo = [kb.sb(f"xo{tag}{i}", [128, 1024], F32) for i in range(1)]
        ps = st.enter_context(nc.psum_tensor("ps" + tag, [128, 4096], F32))
        pb = [Buf(f"pb{i}") for i in range(8)]

        def bank(i, w=512):
            return ps[:, i * 512:i * 512 + w]

        us_v = us.t.rearrange("(kc p) e -> p kc e", p=128)
        vs_v = vs.t.rearrange("(e1 p) d -> p e1 d", p=128)
        wcount = 0
        xocount = 0
        for x_in, x_out, g in [(a_, b_, g_) for (a_, b_, t_) in jobs for g_ in range(t_ // G)]:
            for j in range(TPG):
                tok0 = g * G + j * 128
                kb.dma(xs[j][:], x_in[tok0:tok0 + 128, :], [x_in], [xs[j]])
                kb.act(sq[:], xs[j][:], AF.Square, [xs[j]], [sq])
                kb.reduce(ss[:, 0:1], sq[:], ALU.add, [sq], [ss])
                kb.ts(ss[:, 1:2], ss[:, 0:1], 1.0 / 1024, 1e-6, ALU.mult, ALU.add, [ss], [ss])
                kb.act(ss[:, 3:4], ss[:, 1:2], AF.Sqrt, [ss], [ss])
                kb.s.op("dve", lambda g_: g_.reciprocal(ss[:, 2:3], ss[:, 3:4]), [ss], [ss])
                kb.stt(hf[:], xs[j][:], ss[:, 2:3], gB[:], ALU.mult, ALU.mult, [xs[j], ss, gB], [hf])
                for kc in range(8):
                    kb.tr(ps[:, kc * 128:(kc + 1) * 128], hf[:, kc * 128:(kc + 1) * 128], ident[:], [hf, ident], [pb[kc // 4]])
                for hb in range(2):
                    kb.copy(hT[:, hb * 4:(hb + 1) * 4, j * 128:(j + 1) * 128],
                            bank(hb).rearrange("p (k t) -> p k t", t=128), [pb[hb]], [hT], e=("act" if hb == 0 else "dve"))
                for c in range(16):
                    for kc in range(8):
                        kb.mm(ps[:, 2048 + c * 128:2048 + (c + 1) * 128], wqb[:, kc, c * 128:(c + 1) * 128],
                              hT[:, kc, j * 128:(j + 1) * 128], kc == 0, kc == 7, [wqb, hT], [pb[4 + c // 4]])
                for b4 in range(4):
                    kb.copy(qT[:, b4 * 4:(b4 + 1) * 4, :], bank(4 + b4).rearrange("p (k t) -> p k t", t=128),
                            [pb[4 + b4]], [qT], e=("act" if b4 % 2 == 0 else "dve"))
                for c in range(16):
                    kb.mm(ps[:, c * 128:(c + 1) * 128], qT[:, c, :], kTb[:, c * 128:(c + 1) * 128], True, True,
                          [qT, kTb], [pb[c // 4]])
                for b4 in range(4):
                    kb.copy(sc[:, b4 * 4:(b4 + 1) * 4, :], bank(b4).rearrange("p (k t) -> p k t", t=128),
                            [pb[b4]], [sc], e=("act" if b4 % 2 == 0 else "dve"))
                for c in range(16):
                    kb.s.op("dve", lambda g_, c=c: g_.max(out=v16[:, c, 0:8], in_=sc[:, c, :]), [sc], [v16])
                    kb.s.op("dve", lambda g_, c=c: g_.match_replace(out=tmp[:, 0:128], in_to_replace=v16[:, c, 0:8],
                                                                    in_values=sc[:, c, :], imm_value=-1e30), [sc, v16], [tmp])
                    kb.s.op("dve", lambda g_, c=c: g_.max(out=v16[:, c, 8:16], in_=tmp[:, 0:128]), [tmp], [v16])
                    kb.s.op("dve", lambda g_, c=c: g_.max_index(out=i16[:, c, 0:8], in_max=v16[:, c, 0:8], in_values=sc[:, c, :]),
                            [sc, v16], [i16])
                    kb.s.op("dve", lambda g_, c=c: g_.max_index(out=i16[:, c, 8:16], in_max=v16[:, c, 8:16], in_values=sc[:, c, :]),
                            [sc, v16], [i16])
                kb.copy(i16f[:], i16[:], [i16], [i16f], e="pool")
                v16v = v16[:].rearrange("p (h two) k -> p h two k", two=2)
                kb.tt(cand[:], v16v[:, :, 0, :].unsqueeze(3).to_broadcast([128, 8, 16, 16]),
                      v16v[:, :, 1, :].unsqueeze(2).to_broadcast([128, 8, 16, 16]), ALU.add, [v16], [cand])
                for h in range(8):
                    ch = cand[:, h, :, :].rearrange("p a b -> p (a b)")
                    kb.s.op("dve", lambda g_, h=h, ch=ch: g_.max(out=s16[:, h, 0:8], in_=ch), [cand], [s16])
                    kb.s.op("dve", lambda g_, h=h, ch=ch: g_.match_replace(out=tmp[:], in_to_replace=s16[:, h, 0:8],
                                                                           in_values=ch, imm_value=-1e30), [cand, s16], [tmp])
                    kb.s.op("dve", lambda g_, h=h: g_.max(out=s16[:, h, 8:16], in_=tmp[:]), [tmp], [s16])
                    kb.s.op("dve", lambda g_, h=h, ch=ch: g_.max_index(out=j16[:, h, 0:8], in_max=s16[:, h, 0:8], in_values=ch),
                            [cand, s16], [j16])
                    kb.s.op("dve", lambda g_, h=h, ch=ch: g_.max_index(out=j16[:, h, 8:16], in_max=s16[:, h, 8:16], in_values=ch),
                            [cand, s16], [j16])
                gt = trio[:, 2, :].rearrange("p (h k) -> p h k", k=16)
                kb.tt(gt, s16[:], s16[:, :, 0:1].to_broadcast([128, 8, 16]), ALU.subtract, [s16], [trio], e="pool")
                kb.act(gt, gt, AF.Exp, [trio], [trio])
                kb.reduce(zz[:], gt, ALU.add, [trio], [zz])
                kb.s.op("dve", lambda g_: g_.reciprocal(zz[:], zz[:]), [zz], [zz])
                kb.tt(gt, gt, zz[:].unsqueeze(2).to_broadcast([128, 8, 16]), ALU.mult, [trio, zz], [trio])
                kb.s.op("dve", lambda g_: g_.tensor_single_scalar(ja[:], j16[:], 4, ALU.logical_shift_right), [j16], [ja])
                kb.s.op("dve", lambda g_: g_.tensor_single_scalar(jb[:], j16[:], 15, ALU.bitwise_and), [j16], [jb])
                kb.copy(jaf[:], ja[:], [ja], [jaf])
                kb.copy(jbf[:], jb[:], [jb], [jbf])
                i16v = i16f[:].rearrange("p (h two) k -> p h two k", two=2)
                iota16 = iota[:, 0:16].unsqueeze(1).unsqueeze(1).to_broadcast([128, 8, 16, 16])
                for which, jf in ((0, jaf), (1, jbf)):
                    kb.tt(eq[:], iota16, jf[:].unsqueeze(3).to_broadcast([128, 8, 16, 16]), ALU.is_equal, [iota, jf], [eq])
                    kb.tt(eq[:], eq[:], i16v[:, :, which, :].unsqueeze(2).to_broadcast([128, 8, 16, 16]), ALU.mult,
                          [eq, i16f], [eq])
                    kb.reduce(trio[:, which, :], eq[:].rearrange("p h k a -> p (h k) a"), ALU.add, [eq], [trio])
                for w3 in range(3):
                    kb.tr(ps[:, 3584 + w3 * 128:3584 + (w3 + 1) * 128], trio[:, w3, :], ident[:], [trio, ident], [pb[7]])
                kb.copy(trioT[:].rearrange("p a t -> p (a t)"), ps[:, 3584:3584 + 384], [pb[7]], [trioT], e="act")
                for t0 in range(0, 128, OHT):
                    iob = iota[:].unsqueeze(1).to_broadcast([128, OHT, 128])
                    kb.tt(oh2[:], iob, trioT[:, 1, t0:t0 + OHT].unsqueeze(2).to_broadcast([128, OHT, 128]), ALU.is_equal,
                          [iota, trioT], [oh2])
                    kb.tt(oh1[:], iob, trioT[:, 0, t0:t0 + OHT].unsqueeze(2).to_broadcast([128, OHT, 128]), ALU.is_equal,
                          [iota, trioT], [oh1])
                    kb.tt(oh1[:], oh1[:], trioT[:, 2, t0:t0 + OHT].unsqueeze(2).to_broadcast([128, OHT, 128]), ALU.mult,
                          [oh1, trioT], [oh1])
                    for t16 in range(0, OHT, 16):
                        half = wcount % 2
                        wcount += 1
                        for tt_ in range(16):
                            tl = t16 + tt_
                            col = half * 2048 + tt_ * 128
                            kb.mm(ps[:, col:col + 128], oh2[:, tl, :], oh1[:, tl, :], True, True, [oh1, oh2],
                                  [pb[half * 4 + tt_ // 4]])
                        tcol = j * 128 + t0 + t16
                        kb.copy(WT[:, :, tcol:tcol + 16],
                                ps[:, half * 2048:(half + 1) * 2048].rearrange("p (t e) -> p e t", e=128),
                                [pb[half * 4 + q_] for q_ in range(4)], [WT], e=("act" if half == 0 else "dve"))
            ncg = 128 // CH
            for cg in range(ncg):
                sl = cg % 2
                kb.dma(ub[sl][:], us_v[:, :, cg * CH * 128:(cg + 1) * CH * 128], [us], [ub[sl]])
                kb.dma(vb[sl][:], vs_v[:, cg * CH:(cg + 1) * CH, :], [vs], [vb[sl]])
                for el in range(CH):
                    e1 = cg * CH + el
                    a = e1 % 2
                    pu = ps[:, 2048 + a * 512:2048 + a * 512 + G]
                    for kc in range(8):
                        kb.mm(pu, ub[sl][:, kc, el * 128:(el + 1) * 128], hT[:, kc, :], kc == 0, kc == 7,
                              [ub[sl], hT], [pb[4 + a]])
                    kb.act(actT[a][:], pu, AF.Gelu, [pb[4 + a]], [actT[a]])
                    kb.tt(ct[a][:], actT[a][:], WT[:, e1, :], ALU.mult, [actT[a], WT], [ct[a]])
                    for j in range(TPG):
                        for hh in range(2):
                            kb.mm(bank(2 * j + hh), ct[a][:, j * 128:(j + 1) * 128], vb[sl][:, el, hh * 512:(hh + 1) * 512],
                                  e1 == 0, e1 == 127, [ct[a], vb[sl]], [pb[2 * j + hh]])
            for j in range(TPG):
                tok0 = g * G + j * 128
                o = xo[0]
                xocount += 1
                kb.tt(o[:], ps[:, j * 1024:(j + 1) * 1024], xs[j][:], ALU.add, [pb[2 * j], pb[2 * j + 1], xs[j]], [o])
                kb.dma(x_out[tok0:tok0 + 128, :], o[:], [o], [], e="pool")
        kb.s.barrier()


def rms_to_hT(kb, xs_t, gB, ident, hf, sq, ss, ps, pbA, pbB, hT, col0, eps=1e-6):
    kb.act(sq[:], xs_t[:], AF.Square, [xs_t], [sq])
    kb.reduce(ss[:, 0:1], sq[:], ALU.add, [sq], [ss])
    kb.ts(ss[:, 1:2], ss[:, 0:1], 1.0 / 1024, eps, ALU.mult, ALU.add, [ss], [ss])
    kb.act(ss[:, 3:4], ss[:, 1:2], AF.Sqrt, [ss], [ss])
    kb.s.op("dve", lambda g_: g_.reciprocal(ss[:, 2:3], ss[:, 3:4]), [ss], [ss])
    kb.stt(hf[:], xs_t[:], ss[:, 2:3], gB[:], ALU.mult, ALU.mult, [xs_t, ss, gB], [hf])
    for kc in range(8):
        kb.tr(ps[:, kc * 128:(kc + 1) * 128], hf[:, kc * 128:(kc + 1) * 128], ident[:], [hf, ident], [pbA if kc < 4 else pbB])
    kb.copy(hT[:, 0:4, col0:col0 + 128], ps[:, 0:512].rearrange("p (k t) -> p k t", t=128), [pbA], [hT], e="act")
    kb.copy(hT[:, 4:8, col0:col0 + 128], ps[:, 512:1024].rearrange("p (k t) -> p k t", t=128), [pbB], [hT], e="dve")


def mixer0_block(kb0, x_own, x_pre, x_out, gvec, w_in, w_out, convwT, convb, lng, lnb, w2, gateb, glag,
                 ident_d, tri_d, T, TP, tag="m0", x_out_pre=None):
    nc = kb0.nc
    NTO = T // 128
    NTP = TP // 128
    with ExitStack() as st:
        kb = kb0.scope(st)
        winb = kb.sb("winb", [128, 8, 2576], BF16)
        load_bf16_resident(kb, winb, lambda kc, c0, w: winb[:, kc, c0:c0 + w], w_in, 8, 2576, "win")
        woutb = kb.sb("woutb", [128, 8, 1024], BF16)
        load_bf16_resident(kb, woutb, lambda kc, c0, w: woutb[:, kc, c0:c0 + w], w_out, 8, 1024, "wout")
        gB = kb.sb("gBm", [128, 1024], F32)
        kb.dma(gB[:], gvec.t.partition_broadcast(128)[:, 0, :], [gvec], [gB])
        gbB = kb.sb("gbB", [128, 256], F32)
        kb.dma(gbB[:], gateb.t.partition_broadcast(128)[:, 0, :], [gateb], [gbB])
        ident = kb.sb("identm", [128, 128], F32)
        kb.dma(ident[:], ident_d[:], [ident_d], [ident])
        tri = kb.sb("trim", [128, 128], F32)
        kb.dma(tri[:], tri_d[:], [tri_d], [tri])
        ones = kb.sb("onesm", [128, 128], F32)
        kb.memset(ones[:], 1.0, [ones])
        cw = kb.sb("cw", [128, 4, 31], F32)
        kb.dma(cw[:], convwT[:], [convwT], [cw])
        cols = kb.sb("colsm", [128, 16], F32)
        kb.dma(cols[:, 0:4], convb[:], [convb], [cols])
        kb.dma(cols[:, 4:8], lng[:], [lng], [cols])
        kb.dma(cols[:, 8:12], lnb[:], [lnb], [cols])
        kb.dma(cols[:, 12:13], glag[:], [glag], [cols])
        w2f = kb.sb("w2f", [16, 256], F32)
        kb.dma(w2f[:], w2[:], [w2], [w2f])
        w2b = kb.sb("w2b", [16, 256], BF16)
        kb.copy(w2b[:], w2f[:], [w2f], [w2b])
        diag = kb.sb("diag", [128, 4 * 31, 128], BF16)
        for c in range(4):
            for j in range(31):
                kb.ts(diag[:, c * 31 + j, :], ident[:], cw[:, c, j:j + 1], None, ALU.mult, None, [ident, cw], [diag],
                      e=("dve" if (c * 31 + j) % 2 == 0 else "pool"))
        full = x_out_pre is not None
        UW = 158 if full else 30 + T
        uT = kb.sb("uT", [128, 4, UW], BF16)
        kb.memset(uT[:, :, 0:30], 0.0, [uT])
        Sf = [kb.sb(f"Sf{h}", [64, 128], F32) for h in range(4)]
        Sb = [kb.sb(f"Sb{h}", [64, 128], BF16) for h in range(4)]
        for h in range(4):
            kb.memset(Sf[h][:], 0.0, [Sf[h]])
            kb.memset(Sb[h][:], 0.0, [Sb[h]])
        xs = kb.sb("xsm", [128, 1024], F32)
        hf = kb.sb("hfm", [128, 1024], F32)
        sq = kb.sb("sqm", [128, 1024], BF16)
        ss = kb.sb("ssm", [128, 4], F32)
        hT = kb.sb("hTm", [128, 8, 128], BF16)
        sg = kb.sb("sg", [128, 512], F32)
        glrT = kb.sb("glrT", [16, 128], BF16)
        zb = kb.sb("zb", [128, 256], F32)
        la = kb.sb("la", [128, 256], F32)
        enb_tm = kb.sb("enb_tm", [128, 256], F32)
        ebl_tm = kb.sb("ebl_tm", [128, 256], F32)
        kdec = kb.sb("kdec", [128, 256], BF16)
        vbf = kb.sb("vbf", [128, 512], BF16)
        eblc = kb.sb("eblc", [64, 8], F32)
        eb = kb.sb("eb", [64, 512], F32)
        enb = kb.sb("enb", [64, 512], F32)
        qt = kb.sb("qt", [64, 4, 128], BF16)
        kt = kb.sb("kt", [64, 4, 128], BF16)
        Am = kb.sb("Am", [128, 4, 128], BF16)
        osq = kb.sb("osq", [128, 512], F32)
        rs = kb.sb("rs", [128, 512], F32)
        sr = kb.sb("sr", [128, 512], F32)
        t1 = kb.sb("t1", [128, 512], F32)
        ybT = kb.sb("ybT", [128, 4, 128], BF16)
        ycs = kb.sb("ycs", [128, 4, 128], F32)
        ysq = kb.sb("ysq", [128, 4, 128], F32)
        mean = kb.sb("mean", [128, 128], F32)
        var = kb.sb("var", [128, 128], F32)
        yaT = kb.sb("yaT", [128, 4, 128], BF16)
        xo = kb.sb("xom", [128, 1024], F32)
        ps = st.enter_context(nc.psum_tensor("psm", [128, 4096], F32))
        pb = [Buf(f"pbm{i}") for i in range(8)]

        def B(i, lo=0, hi=512):
            return ps[:, i * 512 + lo:i * 512 + hi]

        def fm_proj(colbase, ncols, bank, slot):
            for kc in range(8):
                kb.mm(ps[0:ncols, bank * 512 + slot * 128:bank * 512 + (slot + 1) * 128], winb[:, kc, colbase:colbase + ncols],
                      hT[:, kc, :], kc == 0, kc == 7, [winb, hT], [pb[bank]])

        def state_part(u_needed, own):
            for kc in range(8):
                kb.mm(B(5), hT[:, kc, :], winb[:, kc, 1536:2048], kc == 0, kc == 7, [hT, winb], [pb[5]])
            for kc in range(8):
                kb.mm(B(6, 0, 256), hT[:, kc, :], winb[:, kc, 1280:1536], kc == 0, kc == 7, [hT, winb], [pb[6]])
            fm_proj(2560, 16, 4, 0)
            kb.copy(glrT[:], ps[0:16, 4 * 512:4 * 512 + 128], [pb[4]], [glrT], e="act")
            kb.mm(B(6, 256, 512), glrT[:], w2b[:], True, True, [glrT, w2b], [pb[6]])
            kb.tt(zb[:], B(6, 256, 512), gbB[:], ALU.add, [pb[6], gbB], [zb])
            kb.act(zb[:], zb[:], AF.Exp, [zb], [zb], scale=-1.0)
            kb.act(zb[:], zb[:], AF.Ln, [zb], [zb], bias=1.0)
            kb.ts(la[:], zb[:], -1.0 / 16.0, None, ALU.mult, None, [zb], [la])
            kb.mm(B(7, 0, 256), tri[:], la[:], True, True, [tri, la], [pb[7]])
            kb.mm(B(7, 256, 512), ones[:], la[:], True, True, [ones, la], [pb[7]])
            for h in range(4):
                kb.mm(ps[0:64, 4 * 512 + 384 + 2 * h:4 * 512 + 386 + 2 * h], la[:, h * 64:(h + 1) * 64], ones[:, 0:2], True, True,
                      [la, ones], [pb[4]])
            kb.act(eblc[:], ps[0:64, 4 * 512 + 384:4 * 512 + 392], AF.Exp, [pb[4]], [eblc])
            kb.act(enb_tm[:], B(7, 0, 256), AF.Exp, [pb[7]], [enb_tm], scale=-1.0)
            kb.act(ebl_tm[:], B(7, 256, 512), AF.Exp, [pb[7]], [ebl_tm])
            kb.tt(enb_tm[:], enb_tm[:], ebl_tm[:], ALU.mult, [enb_tm, ebl_tm], [enb_tm])
            kb.tt(kdec[:], B(6, 0, 256), enb_tm[:], ALU.mult, [pb[6], enb_tm], [kdec])
            kb.copy(vbf[:], B(5), [pb[5]], [vbf], e="act")

        def state_update():
            for h in range(4):
                kb.mm(ps[0:64, 2 * 512 + h * 128:2 * 512 + (h + 1) * 128], kdec[:, h * 64:(h + 1) * 64], vbf[:, h * 128:(h + 1) * 128],
                      True, True, [kdec, vbf], [pb[2]])
            for h in range(4):
                kb.stt(Sf[h][:], Sf[h][:], eblc[:, 2 * h:2 * h + 1], ps[0:64, 2 * 512 + h * 128:2 * 512 + (h + 1) * 128],
                       ALU.mult, ALU.add, [Sf[h], eblc, pb[2]], [Sf[h]])
                kb.copy(Sb[h][:], Sf[h][:], [Sf[h]], [Sb[h]], e="act")

        def conv_u(tokcol):
            for c in range(4):
                fm_proj(c * 128, 128, 0, c)
                fm_proj(512 + c * 128, 128, 1, c)
            kb.act(sg[:], B(1), AF.Sigmoid, [pb[1]], [sg])
            kb.tt(uT[:, :, 30 + tokcol:30 + tokcol + 128], B(0).rearrange("p (c t) -> p c t", t=128),
                  sg[:].rearrange("p (c t) -> p c t", t=128), ALU.mult, [pb[0], sg], [uT])

        for i in range(0 if full else NTP):
            kb.dma(xs[:], x_pre[i * 128:(i + 1) * 128, :], [x_pre], [xs])
            rms_to_hT(kb, xs, gB, ident, hf, sq, ss, ps, pb[0], pb[1], hT, 0)
            state_part(False, False)
            state_update()
            if i == NTP - 1:
                for c in range(4):
                    fm_proj(c * 128, 128, 0, c)
                    fm_proj(512 + c * 128, 128, 1, c)
                kb.act(sg[:], B(1), AF.Sigmoid, [pb[1]], [sg])
                kb.tt(uT[:, :, 0:30], B(0).rearrange("p (c t) -> p c t", t=128)[:, :, 98:128],
                      sg[:].rearrange("p (c t) -> p c t", t=128)[:, :, 98:128], ALU.mult, [pb[0], sg], [uT])
        for ii in range((NTP + NTO) if full else NTO):
            if full:
                isown = ii >= NTP
                i = 0
                srcx = x_own[(ii - NTP) * 128:(ii - NTP + 1) * 128, :] if isown else x_pre[ii * 128:(ii + 1) * 128, :]
                dsty = x_out[(ii - NTP) * 128:(ii - NTP + 1) * 128, :] if isown else x_out_pre[ii * 128:(ii + 1) * 128, :]
            else:
                i = ii
                srcx = x_own[i * 128:(i + 1) * 128, :]
                dsty = x_out[i * 128:(i + 1) * 128, :]
            kb.dma(xs[:], srcx, [x_own, x_pre], [xs])
            rms_to_hT(kb, xs, gB, ident, hf, sq, ss, ps, pb[0], pb[1], hT, 0)
            state_part(True, True)
            conv_u(i * 128)
            for h in range(4):
                fm_proj(1024 + h * 64, 64, 2, h)
                fm_proj(1280 + h * 64, 64, 3, h)
            for sl in range(4):
                fm_proj(2048 + sl * 128, 128, 4, sl)
            for h in range(4):
                kb.mm(ps[0:64, 5 * 512 + h * 128:5 * 512 + (h + 1) * 128], la[:, h * 64:(h + 1) * 64], tri[:], True, True,
                      [la, tri], [pb[5]])
            kb.act(eb[:], ps[0:64, 5 * 512:6 * 512], AF.Exp, [pb[5]], [eb])
            kb.act(enb[:], ps[0:64, 5 * 512:6 * 512], AF.Exp, [pb[5]], [enb], scale=-1.0)
            kb.stt(qt[:].rearrange("p h t -> p (h t)"), ps[0:64, 2 * 512:3 * 512], 0.125, eb[:], ALU.mult, ALU.mult, [pb[2], eb], [qt])
            kb.tt(kt[:].rearrange("p h t -> p (h t)"), ps[0:64, 3 * 512:4 * 512], enb[:], ALU.mult, [pb[3], enb], [kt])
            kb.act(sr[:], B(4), AF.Silu, [pb[4]], [sr])
            for h in range(4):
                kb.mm(B(0, h * 128, (h + 1) * 128), kt[:, h, :], qt[:, h, :], True, True, [kt, qt], [pb[0]])
            kb.tt(Am[:], B(0).rearrange("p (h t) -> p h t", t=128), tri[:].unsqueeze(1).to_broadcast([128, 4, 128]), ALU.mult,
                  [pb[0], tri], [Am])
            for h in range(4):
                kb.mm(B(1, h * 128, (h + 1) * 128), vbf[:, h * 128:(h + 1) * 128], Am[:, h, :], True, False, [vbf, Am], [pb[1]])
                kb.mm(B(1, h * 128, (h + 1) * 128), Sb[h][:], qt[:, h, :], False, True, [Sb[h], qt], [pb[1]])
            state_update()
            kb.act(osq[:], B(1), AF.Square, [pb[1]], [osq])
            kb.mm(B(3), ones[:], osq[:], True, True, [ones, osq], [pb[3]])
            kb.ts(rs[:], B(3), 1.0 / 128, 1e-6, ALU.mult, ALU.add, [pb[3]], [rs])
            kb.act(rs[:], rs[:], AF.Sqrt, [rs], [rs])
            kb.s.op("dve", lambda g_: g_.reciprocal(rs[:], rs[:]), [rs], [rs])
            kb.tt(t1[:], B(1), rs[:], ALU.mult, [pb[1], rs], [t1])
            kb.stt(ybT[:].rearrange("p h t -> p (h t)"), t1[:], cols[:, 12:13], sr[:], ALU.mult, ALU.mult, [t1, cols, sr], [ybT])
            for c in range(4):
                for j in range(31):
                    kb.mm(B(4, c * 128, (c + 1) * 128), diag[:, c * 31 + j, :], uT[:, c, i * 128 + j:i * 128 + j + 128],
                          j == 0, j == 30, [diag, uT], [pb[4]])
            for c in range(4):
                kb.ts(ycs[:, c, :], B(4, c * 128, (c + 1) * 128), cols[:, c:c + 1], None, ALU.add, None, [pb[4], cols], [ycs])
            kb.act(ysq[:], ycs[:], AF.Square, [ycs], [ysq])
            for c in range(4):
                kb.mm(B(5, 0, 128), ones[:], ycs[:, c, :], c == 0, c == 3, [ones, ycs], [pb[5]])
            for c in range(4):
                kb.mm(B(5, 128, 256), ones[:], ysq[:, c, :], c == 0, c == 3, [ones, ysq], [pb[5]])
            kb.ts(mean[:], B(5, 0, 128), 1.0 / 512, None, ALU.mult, None, [pb[5]], [mean])
            kb.tt(var[:], mean[:], mean[:], ALU.mult, [mean], [var])
            kb.stt(var[:], B(5, 128, 256), 1.0 / 512, var[:], ALU.mult, ALU.subtract, [pb[5], var], [var])
            kb.ts(var[:], var[:], 1e-6, None, ALU.add, None, [var], [var])
            kb.act(var[:], var[:], AF.Sqrt, [var], [var])
            kb.s.op("dve", lambda g_: g_.reciprocal(var[:], var[:]), [var], [var])
            kb.tt(ycs[:], ycs[:], mean[:].unsqueeze(1).to_broadcast([128, 4, 128]), ALU.subtract, [ycs, mean], [ycs])
            kb.tt(ycs[:], ycs[:], var[:].unsqueeze(1).to_broadcast([128, 4, 128]), ALU.mult, [ycs, var], [ycs])
            for c in range(4):
                kb.ts(ycs[:, c, :], ycs[:, c, :], cols[:, 4 + c:5 + c], cols[:, 8 + c:9 + c], ALU.mult, ALU.add, [ycs, cols], [ycs])
            kb.act(yaT[:], ycs[:], AF.Silu, [ycs], [yaT])
            for hh in range(2):
                for kc in range(8):
                    lhsT = yaT[:, kc, :] if kc < 4 else ybT[:, kc - 4, :]
                    kb.mm(B(6 + hh), lhsT, woutb[:, kc, hh * 512:(hh + 1) * 512], kc == 0, kc == 7, [yaT, ybT, woutb], [pb[6 + hh]])
            kb.tt(xo[:], ps[:, 6 * 512:8 * 512], xs[:], ALU.add, [pb[6], pb[7], xs], [xo])
            kb.dma(dsty, xo[:], [xo], [], e="pool")
            if full:
                kb.copy(uT[:, :, 0:30], uT[:, :, 128:158], [uT], [uT], e="pool")
        kb.s.barrier()


def fox_block(kb0, x_own, x_pre, x_out, gvec, w_in, w_out, fb, qg, kg, pflag, ident_d, tri_d, masks_d, T, TP, tag="fx"):
    nc = kb0.nc
    NTO = T // 128
    NTP = TP // 128
    NTA = NTO + NTP
    NSB = T // 512
    QT = kb0.dram("fxQT", [16, 65, T], BF16, "Internal")
    KT = kb0.dram("fxKT", [16, 65, TP + T], BF16, "Internal")
    VA = kb0.dram("fxVA", [NTA, 128, 16 * 65], BF16, "Internal")
    OT = kb0.dram("fxOT", [16, 64, T], BF16, "Internal")
    with ExitStack() as st0:
        kbp = kb0.scope(st0)
        negc = kbp.sb("negc", [128, NTA, 16], F32)
        ident = kbp.sb("identf", [128, 128], F32)
        kbp.dma(ident[:], ident_d[:], [ident_d], [ident])
        ones = kbp.sb("onesf", [128, 128], F32)
        kbp.memset(ones[:], 1.0, [ones])
        woutb = kbp.sb("woutbf", [128, 8, 1024], BF16)
        load_bf16_resident(kbp, woutb, lambda kc, c0, w: woutb[:, kc, c0:c0 + w], w_out, 8, 1024, "fwout")
        with ExitStack() as st:
            kb = kbp.scope(st)
            winb = kb.sb("winbf", [128, 8, 3088], BF16)
            load_bf16_resident(kb, winb, lambda kc, c0, w: winb[:, kc, c0:c0 + w], w_in, 8, 3088, "fwin")
            gB = kb.sb("gBf", [128, 1024], F32)
            kb.dma(gB[:], gvec.t.partition_broadcast(128)[:, 0, :], [gvec], [gB])
            fbB = kb.sb("fbB", [128, 16], F32)
            kb.dma(fbB[:], fb.t.partition_broadcast(128)[:, 0, :], [fb], [fbB])
            qgB = kb.sb("qgB", [128, 64], F32)
            kb.dma(qgB[:], qg.t.partition_broadcast(128)[:, 0, :], [qg], [qgB])
            kgB = kb.sb("kgB", [128, 64], F32)
            kb.dma(kgB[:], kg.t.partition_broadcast(128)[:, 0, :], [kg], [kgB])
            pfl = kb.sb("pfl", [128, 1], F32)
            kb.dma(pfl[:], pflag[:], [pflag], [pfl])
            tri = kb.sb("trif", [128, 128], F32)
            kb.dma(tri[:], tri_d[:], [tri_d], [tri])
            xs = kb.sb("xsf", [128, 1024], F32)
            hf = kb.sb("hff", [128, 1024], F32)
            sq = kb.sb("sqf", [128, 1024], BF16)
            ss = kb.sb("ssf", [128, 4], F32)
            hT = kb.sb("hTf", [128, 8, 128], BF16)
            nsq = kb.sb("nsq", [128, 1024], F32)
            nss = kb.sb("nss", [128, 16], F32)
            qa = kb.sb("qa", [128, 16, 65], F32)
            ka = kb.sb("ka", [128, 16, 65], F32)
            kb.memset(ka[:, :, 64:65], 1.0, [ka])
            va = kb.sb("va", [128, 16, 65], BF16)
            kb.memset(va[:, :, 0:1], 1.0, [va])
            qTs = kb.sb("qTs", [65, 16, 128], BF16)
            kTs = kb.sb("kTs", [65, 16, 128], BF16)
            lf = kb.sb("lf", [128, 16], F32)
            Lsum = kb.sb("Lsum", [128, 16], F32)
            kb.memset(Lsum[:], 0.0, [Lsum])
            ctile = kb.sb("ctile", [128, 16], F32)
            ps = st.enter_context(nc.psum_tensor("psf1", [128, 4096], F32))
            pb = [Buf(f"pbf{i}") for i in range(8)]

            def normed(dst, bank0, gt):
                src = ps[:, bank0 * 512:(bank0 + 2) * 512]
                kb.act(nsq[:], src, AF.Square, [pb[bank0], pb[bank0 + 1]], [nsq])
                kb.reduce(nss[:], nsq[:].rearrange("p (h d) -> p h d", d=64), ALU.add, [nsq], [nss])
                kb.ts(nss[:], nss[:], 1.0 / 64, 1e-6, ALU.mult, ALU.add, [nss], [nss])
                kb.act(nss[:], nss[:], AF.Sqrt, [nss], [nss])
                kb.s.op("dve", lambda g_: g_.reciprocal(nss[:], nss[:]), [nss], [nss])
                kb.tt(dst[:, :, 0:64], src.rearrange("p (h d) -> p h d", d=64), nss[:].unsqueeze(2).to_broadcast([128, 16, 64]),
                      ALU.mult, [pb[bank0], pb[bank0 + 1], nss], [dst])
                kb.tt(dst[:, :, 0:64], dst[:, :, 0:64], gt[:].unsqueeze(1).to_broadcast([128, 16, 64]), ALU.mult, [dst, gt], [dst])

            def transposed_store(src, dstT, dram, tokcol, b0):
                for h in range(16):
                    kb.tr(ps[0:65, b0 * 512 + h * 128:b0 * 512 + (h + 1) * 128], src[:, h, :], ident[:], [src, ident], [pb[b0 + h // 4]])
                for q4 in range(4):
                    kb.copy(dstT[:, q4 * 4:(q4 + 1) * 4, :], ps[0:65, (b0 + q4) * 512:(b0 + q4 + 1) * 512].rearrange("p (h t) -> p h t", t=128),
                            [pb[b0 + q4]], [dstT], e=("act" if q4 % 2 == 0 else "dve"))
                kb.dma(dram.t.rearrange("h r t -> r h t")[:, :, tokcol:tokcol + 128], dstT[:], [dstT], [], e="pool")

            for i in range(NTA):
                own = i >= NTP
                src = x_own[(i - NTP) * 128:(i - NTP + 1) * 128, :] if own else x_pre[i * 128:(i + 1) * 128, :]
                kb.dma(xs[:], src, [x_own, x_pre], [xs])
                rms_to_hT(kb, xs, gB, ident, hf, sq, ss, ps, pb[0], pb[1], hT, 0)
                for kc in range(8):
                    kb.mm(ps[:, 6 * 512:6 * 512 + 16], hT[:, kc, :], winb[:, kc, 3072:3088], kc == 0, kc == 7, [hT, winb], [pb[6]])
                kb.tt(lf[:], ps[:, 6 * 512:6 * 512 + 16], fbB[:], ALU.add, [pb[6], fbB], [lf])
                kb.act(lf[:], lf[:], AF.Exp, [lf], [lf], scale=-1.0)
                kb.act(lf[:], lf[:], AF.Ln, [lf], [lf], bias=1.0)
                kb.ts(lf[:], lf[:], -1.0, None, ALU.mult, None, [lf], [lf])
                kb.mm(ps[:, 6 * 512 + 16:6 * 512 + 32], tri[:], lf[:], True, False, [tri, lf], [pb[6]])
                kb.mm(ps[:, 6 * 512 + 16:6 * 512 + 32], ones[:], Lsum[:], False, True, [ones, Lsum], [pb[6]])
                kb.tt(Lsum[:], Lsum[:], lf[:], ALU.add, [Lsum, lf], [Lsum])
                kb.copy(ctile[:], ps[:, 6 * 512 + 16:6 * 512 + 32], [pb[6]], [ctile], e="act")
                if own:
                    kb.ts(negc[:, i, :], ctile[:], -1.0, None, ALU.mult, None, [ctile], [negc])
                else:
                    kb.ts(negc[:, i, :], ctile[:], -1.0, pfl[:, 0:1], ALU.mult, ALU.add, [ctile, pfl], [negc])
                for hh in range(2):
                    for kc in range(8):
                        kb.mm(ps[:, (2 + hh) * 512:(3 + hh) * 512], hT[:, kc, :], winb[:, kc, 1024 + hh * 512:1536 + hh * 512],
                              kc == 0, kc == 7, [hT, winb], [pb[2 + hh]])
                normed(ka, 2, kgB)
                for hh in range(2):
                    for kc in range(8):
                        kb.mm(ps[:, (4 + hh) * 512:(5 + hh) * 512], hT[:, kc, :], winb[:, kc, 2048 + hh * 512:2560 + hh * 512],
                              kc == 0, kc == 7, [hT, winb], [pb[4 + hh]])
                kb.copy(va[:, :, 1:65], ps[:, 4 * 512:6 * 512].rearrange("p (h d) -> p h d", d=64), [pb[4], pb[5]], [va], e="act")
                kb.dma(VA[i], va[:].rearrange("p h d -> p (h d)"), [va], [], e="pool")
                if own:
                    for hh in range(2):
                        for kc in range(8):
                            kb.mm(ps[:, hh * 512:(hh + 1) * 512], hT[:, kc, :], winb[:, kc, hh * 512:(hh + 1) * 512],
                                  kc == 0, kc == 7, [hT, winb], [pb[hh]])
                    normed(qa, 0, qgB)
                    kb.ts(qa[:, :, 64:65], ctile[:].unsqueeze(2), 8.0, None, ALU.mult, None, [ctile], [qa])
                transposed_store(ka, kTs, KT, i * 128, 2)
                if own:
                    transposed_store(qa, qTs, QT, (i - NTP) * 128, 2)
            kb.s.barrier()
        with ExitStack() as st:
            kb = kbp.scope(st)
            kts = kb.sb("kts", [65, TP + T], BF16)
            vas = kb.sb("vas", [128, NTA, 65], BF16)
            qts = kb.sb("qts", [65, T], BF16)
            masks = kb.sb("masksb", [128, 4, 512], F32)
            kb.dma(masks[:], masks_d[:], [masks_d], [masks])
            stmp = kb.sb("stmp", [128, 512], F32)
            pT = [kb.sb(f"pT{i}", [128, 512], BF16) for i in range(2)]
            osb = kb.sb("osb", [65, 512], F32)
            rden = kb.sb("rden", [1, 512], F32)
            oTs = kb.sb("oTs", [65, 512], BF16)
            ps = st.enter_context(nc.psum_tensor("psf2", [128, 4096], F32))
            pb = [Buf(f"pbg{i}") for i in range(8)]
            VAv = VA.t.rearrange("n p (h d) -> p n h d", d=65)
            cnt = 0
            for h in range(16):
                kb.dma(kts[:], KT[h], [], [kts])
                kb.dma(qts[:], QT[h], [], [qts])
                kb.dma(vas[:], VAv[:, :, h, :], [], [vas])
                for j in range(NSB):
                    nkb = NTP + 4 * j + 4
                    ob = 4 + (j % 2)
                    for kb_ in range(nkb):
                        a = cnt % 2
                        cnt += 1
                        kb.mm(ps[:, a * 512:(a + 1) * 512], kts[:, kb_ * 128:(kb_ + 1) * 128], qts[:, j * 512:(j + 1) * 512], True, True,
                              [kts, qts], [pb[a]])
                        m = kb_ - NTP - 4 * j
                        if m >= 0:
                            kb.tt(stmp[:], ps[:, a * 512:(a + 1) * 512], masks[:, m, :], ALU.add, [pb[a], masks], [stmp])
                            kb.act(pT[a][:], stmp[:], AF.Exp, [stmp, negc], [pT[a]], bias=negc[:, kb_, h:h + 1], scale=0.125)
                        else:
                            kb.act(pT[a][:], ps[:, a * 512:(a + 1) * 512], AF.Exp, [pb[a], negc], [pT[a]], bias=negc[:, kb_, h:h + 1], scale=0.125)
                        kb.mm(ps[0:65, ob * 512:(ob + 1) * 512], vas[:, kb_, :], pT[a][:], kb_ == 0, kb_ == nkb - 1, [vas, pT[a]], [pb[ob]])
                    kb.copy(osb[:], ps[0:65, ob * 512:(ob + 1) * 512], [pb[ob]], [osb], e="act")
                    kb.s.op("dve", lambda g_: g_.reciprocal(rden[:], osb[0:1, :]), [osb], [rden])
                    kb.mm(ps[0:65, 6 * 512:7 * 512], ones[0:1, 0:65], rden[:], True, True, [ones, rden], [pb[6]])
                    kb.tt(oTs[:], osb[:], ps[0:65, 6 * 512:7 * 512], ALU.mult, [osb, pb[6]], [oTs])
                    kb.dma(OT[h, :, j * 512:(j + 1) * 512], oTs[1:65, :], [oTs], [], e="pool")
            kb.s.barrier()
        with ExitStack() as st:
            kb = kbp.scope(st)
            oTt = [kb.sb(f"oTt{i}", [128, 8, 128], BF16) for i in range(2)]
            xs2 = [kb.sb(f"xs2{i}", [128, 1024], F32) for i in range(2)]
            xo = [kb.sb(f"xof{i}", [128, 1024], F32) for i in range(2)]
            ps = st.enter_context(nc.psum_tensor("psf3", [128, 4096], F32))
            pb = [Buf(f"pbh{i}") for i in range(8)]
            OTv = OT.t.rearrange("(p two) d t -> (two d) p t", two=2)
            for i in range(NTO):
                a = i % 2
                kb.dma(oTt[a][:], OTv[:, :, i * 128:(i + 1) * 128], [], [oTt[a]])
                kb.dma(xs2[a][:], x_own[i * 128:(i + 1) * 128, :], [x_own], [xs2[a]])
                for hh in range(2):
                    for p in range(8):
                        kb.mm(ps[:, (2 * a + hh) * 512:(2 * a + hh + 1) * 512], oTt[a][:, p, :], woutb[:, p, hh * 512:(hh + 1) * 512],
                              p == 0, p == 7, [oTt[a], woutb], [pb[2 * a + hh]])
                kb.tt(xo[a][:], ps[:, 2 * a * 512:(2 * a + 2) * 512], xs2[a][:], ALU.add, [pb[2 * a], pb[2 * a + 1], xs2[a]], [xo[a]])
                kb.dma(x_out[i * 128:(i + 1) * 128, :], xo[a][:], [xo[a]], [], e="pool")
            kb.s.barrier()


TOK = 4096


def _consts():
    k = np.arange(128)[:, None]
    q = np.arange(512)[None, :]
    fm = np.zeros((128, 4, 512), np.float32)
    for mm in range(4):
        fm[:, mm, :] = np.where((mm * 128 + k) <= q, 0.0, -240000.0)
    return {
        "ident": np.eye(128, dtype=np.float32),
        "iota": np.tile(np.arange(128, dtype=np.float32), (128, 1)),
        "tri": np.triu(np.ones((128, 128), np.float32)),
        "masks": fm,
    }


def _col4(v):
    return np.ascontiguousarray(v.reshape(4, 128).T)


_NC_CACHE = {}


def _build_fused(T):
    key = ("fused", T)
    if key in _NC_CACHE:
        return _NC_CACHE[key]
    nc = bass.Bass("TRN2", target_bir_lowering=False)
    with ExitStack() as st:
        kb = KB(nc, st)
        D = lambda n, s: kb.dram(n, s, F32, "ExternalInput")
        I = lambda n: kb.dram(n, [T, 1024], F32, "Internal")
        x, xp = D("x", [T, 1024]), D("xp", [T, 1024])
        y = kb.dram("y", [T, 1024], F32, "ExternalOutput")
        ident, iota, tri, masks = D("ident", [128, 128]), D("iota", [128, 128]), D("tri", [128, 128]), D("masks", [128, 4, 512])
        x1o, x1p, x2o, x2p, x3o = I("x1o"), I("x1p"), I("x2o"), I("x2p"), I("x3o")
        mixer0_block(kb, x, xp, x1o, D("m_g", [1, 1024]), D("m_w_in", [1024, 2576]), D("m_w_out", [1024, 1024]),
                     D("m_convwT", [128, 4, 31]), D("m_convb", [128, 4]), D("m_lng", [128, 4]), D("m_lnb", [128, 4]),
                     D("m_w2", [16, 256]), D("m_gateb", [1, 256]), D("m_glag", [128, 1]), ident, tri, T, T, x_out_pre=x1p)
        tabs0 = peer_tables(kb, D("p0_uT", [1024, 16384]), D("p0_v", [16384, 1024]), "L0")
        peer_block(kb, None, None, D("p0_g", [1, 1024]), D("p0_wq", [1024, 2048]), D("p0_keysT", [128, 2048]), None, None,
                   ident, iota, T, "L0", tables=tabs0, jobs=[(x1p, x2p, T), (x1o, x2o, T)])
        fox_block(kb, x2o, x2p, x3o, D("f_g", [1, 1024]), D("f_w_in", [1024, 3088]), D("f_w_out", [1024, 1024]), D("f_fb", [1, 16]),
                  D("f_qg", [1, 64]), D("f_kg", [1, 64]), D("pflag", [128, 1]), ident, tri, masks, T, T)
        peer_block(kb, x3o, y, D("p1_g", [1, 1024]), D("p1_wq", [1024, 2048]), D("p1_keysT", [128, 2048]),
                   D("p1_uT", [1024, 16384]), D("p1_v", [16384, 1024]), ident, iota, T, "L1")
        kb.s.emit()
        print("fused program: ninst", kb.s.ninst, "nsem", kb.s.nsem, flush=True)
    _NC_CACHE[key] = nc
    return nc


def _fused_common(inp, cst):
    c = {"m_g": np.ascontiguousarray(inp["ev_norm_mix"][0][None]), "m_w_in": np.ascontiguousarray(inp["ev_w_in"][0]),
         "m_w_out": np.ascontiguousarray(inp["ev_w_out"][0]),
         "m_convwT": np.ascontiguousarray(inp["ev_conv_w"][0].T.reshape(4, 128, 31).transpose(1, 0, 2)),
         "m_convb": _col4(inp["ev_conv_b"][0]), "m_lng": _col4(inp["ev_conv_ln_g"][0]), "m_lnb": _col4(inp["ev_conv_ln_b"][0]),
         "m_w2": np.ascontiguousarray(inp["ev_gate_w2"][0]), "m_gateb": np.ascontiguousarray(inp["ev_gate_b"][0][None]),
         "m_glag": np.ascontiguousarray(inp["ev_gla_norm_g"][0][:, None]),
         "f_g": np.ascontiguousarray(inp["od_norm_mix"][0][None]), "f_w_in": np.ascontiguousarray(inp["od_w_in"][0]),
         "f_w_out": np.ascontiguousarray(inp["od_w_out"][0]), "f_fb": np.ascontiguousarray(inp["od_fgate_b"][0][None]),
         "f_qg": np.ascontiguousarray(inp["od_q_norm_g"][0][None]), "f_kg": np.ascontiguousarray(inp["od_k_norm_g"][0][None]),
         "ident": cst["ident"], "iota": cst["iota"], "tri": cst["tri"], "masks": cst["masks"]}
    for L in range(2):
        p = _peer_inputs(inp, L, cst)
        for k in ("g", "wq", "keysT", "uT", "v"):
            c[f"p{L}_{k}"] = p[k]
    return c


def _build(kind):
    if kind in _NC_CACHE:
        return _NC_CACHE[kind]
    nc = bass.Bass("TRN2", target_bir_lowering=False)
    with ExitStack() as st:
        kb = KB(nc, st)
        D = lambda n, s: kb.dram(n, s, F32, "ExternalInput")
        T = TOK
        if kind == "m0":
            x, xp = D("x", [T, 1024]), D("xp", [T, 1024])
            y = kb.dram("y", [T, 1024], F32, "ExternalOutput")
            mixer0_block(kb, x, xp, y, D("g", [1, 1024]), D("w_in", [1024, 2576]), D("w_out", [1024, 1024]), D("convwT", [128, 4, 31]),
                         D("convb", [128, 4]), D("lng", [128, 4]), D("lnb", [128, 4]), D("w2", [16, 256]), D("gateb", [1, 256]),
                         D("glag", [128, 1]), D("ident", [128, 128]), D("tri", [128, 128]), T, T)
        elif kind == "peer":
            x = D("x", [T, 1024])
            y = kb.dram("y", [T, 1024], F32, "ExternalOutput")
            peer_block(kb, x, y, D("g", [1, 1024]), D("wq", [1024, 2048]), D("keysT", [128, 2048]), D("uT", [1024, 16384]),
                       D("v", [16384, 1024]), D("ident", [128, 128]), D("iota", [128, 128]), T, "p0")
        elif kind == "fox":
            x, xp = D("x", [T, 1024]), D("xp", [T, 1024])
            y = kb.dram("y", [T, 1024], F32, "ExternalOutput")
            fox_block(kb, x, xp, y, D("g", [1, 1024]), D("w_in", [1024, 3088]), D("w_out", [1024, 1024]), D("fb", [1, 16]),
                      D("qg", [1, 64]), D("kg", [1, 64]), D("pflag", [128, 1]), D("ident", [128, 128]), D("tri", [128, 128]),
                      D("masks", [128, 4, 512]), T, T)
        kb.s.emit()
    _NC_CACHE[kind] = nc
    return nc


def _shards(xfull):
    own, pre = [], []
    for c in range(NCORES):
        b, half = c // 2, c % 2
        own.append(np.ascontiguousarray(xfull[b, half * TOK:(half + 1) * TOK]))
        pre.append(np.ascontiguousarray(xfull[b, 0:TOK]) if half == 1 else np.zeros((TOK, 1024), np.float32))
    return own, pre


def _gather(res):
    out = np.empty((4, 8192, 1024), np.float32)
    for c in range(NCORES):
        b, half = c // 2, c % 2
        out[b, half * TOK:(half + 1) * TOK] = res.results[c]["y"]
    return out


def _run(kind, per_core):
    nc = _build(kind)
    return run_bass_kernel_spmd(nc, per_core, core_ids=list(range(NCORES)))


def _peer_inputs(inp, layer, cst):
    keys = inp["peer_keys"][layer]
    return {"g": np.ascontiguousarray(inp["ffn_norm"][layer][None]), "wq": np.ascontiguousarray(inp["peer_wq"][layer]),
            "keysT": np.ascontiguousarray(keys.transpose(3, 0, 1, 2).reshape(128, 2048)),
            "uT": np.ascontiguousarray(inp["peer_u"][layer].T), "v": np.ascontiguousarray(inp["peer_v"][layer]),
            "ident": cst["ident"], "iota": cst["iota"]}


def kernel(**inp):
    inp = {k: np.asarray(v, dtype=np.float32) for k, v in inp.items()}
    cst = _consts()
    own, pre = _shards(inp["x"])
    common = _fused_common(inp, cst)
    pfl = [np.full((128, 1), 0.0 if c % 2 == 1 else -30000.0, np.float32) for c in range(NCORES)]
    nc = _build_fused(TOK)
    res = run_bass_kernel_spmd(nc, [dict(common, x=own[c], xp=pre[c], pflag=pfl[c]) for c in range(NCORES)],
                               core_ids=list(range(NCORES)))
    return _gather(res)


def kernel_unfused(**inp):
    inp = {k: np.asarray(v, dtype=np.float32) for k, v in inp.items()}
    cst = _consts()
    x = inp["x"]
    own, pre = _shards(x)
    common = {"g": np.ascontiguousarray(inp["ev_norm_mix"][0][None]), "w_in": np.ascontiguousarray(inp["ev_w_in"][0]),
              "w_out": np.ascontiguousarray(inp["ev_w_out"][0]),
              "convwT": np.ascontiguousarray(inp["ev_conv_w"][0].T.reshape(4, 128, 31).transpose(1, 0, 2)),
              "convb": _col4(inp["ev_conv_b"][0]), "lng": _col4(inp["ev_conv_ln_g"][0]), "lnb": _col4(inp["ev_conv_ln_b"][0]),
              "w2": np.ascontiguousarray(inp["ev_gate_w2"][0]), "gateb": np.ascontiguousarray(inp["ev_gate_b"][0][None]),
              "glag": np.ascontiguousarray(inp["ev_gla_norm_g"][0][:, None]), "ident": cst["ident"], "tri": cst["tri"]}
    x1 = _gather(_run("m0", [dict(common, x=own[c], xp=pre[c]) for c in range(NCORES)]))
    own, _ = _shards(x1)
    common = _peer_inputs(inp, 0, cst)
    x2 = _gather(_run("peer", [dict(common, x=own[c]) for c in range(NCORES)]))
    own, pre = _shards(x2)
    common = {"g": np.ascontiguousarray(inp["od_norm_mix"][0][None]), "w_in": np.ascontiguousarray(inp["od_w_in"][0]),
              "w_out": np.ascontiguousarray(inp["od_w_out"][0]), "fb": np.ascontiguousarray(inp["od_fgate_b"][0][None]),
              "qg": np.ascontiguousarray(inp["od_q_norm_g"][0][None]), "kg": np.ascontiguousarray(inp["od_k_norm_g"][0][None]),
              "ident": cst["ident"], "tri": cst["tri"], "masks": cst["masks"]}
    pfl = [np.full((128, 1), 0.0 if c % 2 == 1 else -30000.0, np.float32) for c in range(NCORES)]
    x3 = _gather(_run("fox", [dict(common, x=own[c], xp=pre[c], pflag=pfl[c]) for c in range(NCORES)]))
    own, _ = _shards(x3)
    common = _peer_inputs(inp, 1, cst)
    x4 = _gather(_run("peer", [dict(common, x=own[c]) for c in range(NCORES)]))
    return x4
```

```python
from contextlib import ExitStack

import numpy as np
import concourse.bass as bass
import concourse.mybir as mybir
from concourse.bass_utils import run_bass_kernel_spmd

F32 = mybir.dt.float32
BF16 = mybir.dt.bfloat16
U32 = mybir.dt.uint32
I32 = mybir.dt.int32
ALU = mybir.AluOpType
AF = mybir.ActivationFunctionType
AX = mybir.AxisListType

NCORES = 8
EPOCH = 30000
ENGS = ("pe", "dve", "act", "pool", "sp")


class Buf:
    __slots__ = ("w", "r", "name")

    def __init__(self, name=""):
        self.w = None
        self.r = []
        self.name = name


class TT:
    def __init__(self, t, name):
        self.t = t
        self.b = Buf(name)

    def __getitem__(self, k):
        return self.t[k]


class Sched:
    def __init__(self, nc, stack):
        self.nc = nc
        self.stack = stack
        self.q = {e: [] for e in ENGS}
        self.csem = {e: None for e in ENGS}
        self.ccnt = {e: 0 for e in ENGS}
        self.dsem = {e: [] for e in ENGS}
        self.dcnt = {e: [] for e in ENGS}
        self.drr = {e: 0 for e in ENGS}
        self.seen = {e: {} for e in ENGS}
        self.nsem = 0
        self.ninst = 0

    def _newsem(self, nm):
        self.nsem += 1
        return self.stack.enter_context(self.nc.semaphore(f"{nm}{self.nsem}"))

    def _ticket(self, e, dma):
        if not dma:
            if self.csem[e] is None or self.ccnt[e] >= EPOCH:
                self.csem[e] = self._newsem("c" + e)
                self.ccnt[e] = 0
            self.ccnt[e] += 1
            return (self.csem[e], self.ccnt[e], e, 1)
        if not self.dsem[e]:
            self.dsem[e] = [self._newsem("d" + e) for _ in range(8)]
            self.dcnt[e] = [0] * 8
        i = self.drr[e] % 8
        self.drr[e] += 1
        if self.dcnt[e][i] + 16 >= EPOCH:
            self.dsem[e][i] = self._newsem("d" + e)
            self.dcnt[e][i] = 0
        self.dcnt[e][i] += 16
        return (self.dsem[e][i], self.dcnt[e][i], e + "_dma", 16)

    def op(self, e, fn, reads=(), writes=(), dma=False):
        deps = {}

        def add(t):
            if t is None:
                return
            sem, val, src, _ = t
            if src == "pe" and e == "pe" and not dma:
                return
            k = id(sem)
            if self.seen[e].get(k, 0) >= val:
                return
            if k not in deps or deps[k][1] < val:
                deps[k] = (sem, val)

        for b in reads:
            b = b.b if isinstance(b, TT) else b
            add(b.w)
        for b in writes:
            b = b.b if isinstance(b, TT) else b
            add(b.w)
            for t in b.r:
                add(t)
        waits = list(deps.values())
        for sem, val in waits:
            self.seen[e][id(sem)] = val
        t = self._ticket(e, dma)
        self.q[e].append((waits, fn, t[0], t[3]))
        self.ninst += 1 + len(waits)
        for b in reads:
            b = b.b if isinstance(b, TT) else b
            b.r = [x for x in b.r if x[0] is not t[0]] + [t]
        for b in writes:
            b = b.b if isinstance(b, TT) else b
            b.w = t
            b.r = []
        return t

    def barrier(self):
        waits = []
        for e in ENGS:
            if self.csem[e] is not None and self.ccnt[e] > 0:
                waits.append((self.csem[e], self.ccnt[e]))
            for sem, c in zip(self.dsem[e], self.dcnt[e]):
                if c > 0:
                    waits.append((sem, c))
        for e in ENGS:
            self.q[e].append((list(waits), None, None, 0))
            for sem, val in waits:
                self.seen[e][id(sem)] = max(self.seen[e].get(id(sem), 0), val)

    def final_wait(self, e, bufs):
        waits = []
        for b in bufs:
            b = b.b if isinstance(b, TT) else b
            for t in [b.w] + list(b.r):
                if t is not None:
                    waits.append((t[0], t[1]))
        self.q[e].append((waits, None, None, 0))

    def emit(self):
        nc = self.nc
        q = self.q

        def run(e, eng):
            for waits, fn, sem, inc in q[e]:
                for s, v in waits:
                    eng.wait_ge(s, v)
                if fn is not None:
                    ins = fn(eng)
                    ins.then_inc(sem, inc)

        with nc.Block() as block:

            @block.tensor
            def _(eng):
                run("pe", eng)

            @block.vector
            def _(eng):
                run("dve", eng)

            @block.scalar
            def _(eng):
                run("act", eng)

            @block.gpsimd
            def _(eng):
                run("pool", eng)

            @block.sync
            def _(eng):
                run("sp", eng)


class KB:
    def __init__(self, nc, stack, sched=None):
        self.nc = nc
        self.stack = stack
        self.s = sched if sched is not None else Sched(nc, stack)

    def scope(self, stack):
        return KB(self.nc, stack, self.s)

    def sb(self, name, shape, dt):
        t = self.stack.enter_context(self.nc.sbuf_tensor(name, list(shape), dt))
        return TT(t, name)

    def dram(self, name, shape, dt, kind):
        t = self.nc.dram_tensor(name, list(shape), dt, kind=kind)
        return TT(t.ap(), name)

    def dma(self, out, in_, reads, writes, e="sp", **kw):
        return self.s.op(e, lambda g: g.dma_start(out=out, in_=in_, **kw), reads, writes, dma=True)

    def mm(self, out, lhsT, rhs, start, stop, reads, writes):
        return self.s.op("pe", lambda g: g.matmul(out, lhsT, rhs, start=start, stop=stop), reads, writes)

    def tr(self, out, in_, ident, reads, writes):
        return self.s.op("pe", lambda g: g.transpose(out, in_, ident), reads, writes)

    def act(self, out, in_, func, reads, writes, bias=None, scale=None, accum_out=None):
        kw = {}
        if bias is not None:
            kw["bias"] = bias
        if scale is not None:
            kw["scale"] = scale
        if accum_out is not None:
            kw["accum_out"] = accum_out
        return self.s.op("act", lambda g: g.activation(out, in_, func, **kw), reads, writes)

    def tt(self, out, in0, in1, op, reads, writes, e="dve"):
        return self.s.op(e, lambda g: g.tensor_tensor(out, in0, in1, op), reads, writes)

    def ts(self, out, in0, s1, s2, op0, op1, reads, writes, e="dve", accum_out=None):
        if op1 is None:
            return self.s.op(e, lambda g: g.tensor_scalar(out, in0, s1, None, op0), reads, writes)
        if accum_out is not None:
            return self.s.op(e, lambda g: g.tensor_scalar(out, in0, s1, s2, op0, op1, accum_out), reads, writes)
        return self.s.op(e, lambda g: g.tensor_scalar(out, in0, s1, s2, op0, op1), reads, writes)

    def stt(self, out, in0, scalar, in1, op0, op1, reads, writes, e="dve"):
        return self.s.op(e, lambda g: g.scalar_tensor_tensor(out, in0, scalar, in1, op0, op1), reads, writes)

    def copy(self, out, in_, reads, writes, e="dve"):
        if e == "act":
            return self.s.op(e, lambda g: g.copy(out, in_), reads, writes)
        return self.s.op(e, lambda g: g.tensor_copy(out, in_), reads, writes)

    def memset(self, ap, val, writes, e="dve"):
        return self.s.op(e, lambda g: g.memset(ap, val), (), writes)

    def reduce(self, out, in_, op, reads, writes, axis=AX.X, e="dve"):
        return self.s.op(e, lambda g: g.tensor_reduce(out, in_, axis, op), reads, writes)


def to_bf16_dram(kb, src, dst, R, C, tag):
    with ExitStack() as st:
        k = kb.scope(st)
        W = 2048
        stg = [k.sb(f"cv_in{tag}{i}", [128, W], F32) for i in range(2)]
        outb = [k.sb(f"cv_out{tag}{i}", [128, W], BF16) for i in range(2)]
        n = 0
        for r in range(R // 128):
            for c0 in range(0, C, W):
                w = min(W, C - c0)
                i = n % 2
                k.dma(stg[i][:, 0:w], src[r * 128:(r + 1) * 128, c0:c0 + w], [src], [stg[i]])
                k.copy(outb[i][:, 0:w], stg[i][:, 0:w], [stg[i]], [outb[i]], e=("dve" if n % 2 == 0 else "act"))
                k.dma(dst[r * 128:(r + 1) * 128, c0:c0 + w], outb[i][:, 0:w], [outb[i]], [], e="pool")
                n += 1
        k.s.barrier()


def load_bf16_resident(kb, dst_tt, dst_ap_fn, src, nrows_blocks, C, tag):
    with ExitStack() as st:
        k = kb.scope(st)
        W = 2048
        stg = [k.sb(f"ld_in{tag}{i}", [128, W], F32) for i in range(2)]
        n = 0
        for kc in range(nrows_blocks):
            for c0 in range(0, C, W):
                w = min(W, C - c0)
                i = n % 2
                k.dma(stg[i][:, 0:w], src[kc * 128:(kc + 1) * 128, c0:c0 + w], [src], [stg[i]])
                k.copy(dst_ap_fn(kc, c0, w), stg[i][:, 0:w], [stg[i]], [dst_tt], e=("dve" if n % 2 == 0 else "act"))
                n += 1
        k.s.barrier()


def peer_tables(kb0, uT, vtab, tag):
    us = kb0.dram(f"us{tag}", [1024, 16384], BF16, "Internal")
    vs = kb0.dram(f"vs{tag}", [16384, 1024], BF16, "Internal")
    to_bf16_dram(kb0, uT, us, 1024, 16384, tag + "u")
    to_bf16_dram(kb0, vtab, vs, 16384, 1024, tag + "v")
    return us, vs


def peer_block(kb0, x_in, x_out, gvec, wq, keysT, uT, vtab, ident_d, iota_d, T, tag, G=256, CH=2, OHT=32, tables=None, jobs=None, NSL=4):
    nc = kb0.nc
    if jobs is None:
        jobs = [(x_in, x_out, T)]
    with ExitStack() as st:
        kb = kb0.scope(st)
        TPG = G // 128
        if tables is None:
            tables = peer_tables(kb, uT, vtab, tag)
        us, vs = tables
        wqb = kb.sb("wqb" + tag, [128, 8, 2048], BF16)
        load_bf16_resident(kb, wqb, lambda kc, c0, w: wqb[:, kc, c0:c0 + w], wq, 8, 2048, tag + "wq")
        kTb = kb.sb("kTb" + tag, [128, 16 * 128], BF16)
        load_bf16_resident(kb, kTb, lambda kc, c0, w: kTb[:, c0:c0 + w], keysT, 1, 2048, tag + "kt")
        gB = kb.sb("gB" + tag, [128, 1024], F32)
        kb.dma(gB[:], gvec.t.partition_broadcast(128)[:, 0, :], [gvec], [gB])
        ident = kb.sb("ident" + tag, [128, 128], F32)
        kb.dma(ident[:], ident_d[:], [ident_d], [ident])
        iota = kb.sb("iota" + tag, [128, 128], F32)
        kb.dma(iota[:], iota_d[:], [iota_d], [iota])

        xs = [kb.sb(f"xs{tag}{j}", [128, 1024], F32) for j in range(TPG)]
        hf = kb.sb("hf" + tag, [128, 1024], F32)
        sq = kb.sb("sq" + tag, [128, 1024], BF16)
        ss = kb.sb("ss" + tag, [128, 4], F32)
        hT = kb.sb("hT" + tag, [128, 8, G], BF16)
        qT = kb.sb("qT" + tag, [128, 16, 128], BF16)
        sc = kb.sb("sc" + tag, [128, 16, 128], F32)
        tmp = kb.sb("tmp" + tag, [128, 256], F32)
        v16 = kb.sb("v16" + tag, [128, 16, 16], F32)
        i16 = kb.sb("i16" + tag, [128, 16, 16], U32)
        i16f = kb.sb("i16f" + tag, [128, 16, 16], F32)
        cand = kb.sb("cand" + tag, [128, 8, 16, 16], F32)
        s16 = kb.sb("s16" + tag, [128, 8, 16], F32)
        j16 = kb.sb("j16" + tag, [128, 8, 16], U32)
        jaf = kb.sb("jaf" + tag, [128, 8, 16], F32)
        ja = kb.sb("ja" + tag, [128, 8, 16], U32)
        jb = kb.sb("jb" + tag, [128, 8, 16], U32)
        jbf = kb.sb("jbf" + tag, [128, 8, 16], F32)
        eq = cand
        trio = kb.sb("trio" + tag, [128, 3, 128], F32)
        zz = kb.sb("zz" + tag, [128, 8], F32)
        trioT = kb.sb("trioT" + tag, [128, 3, 128], F32)
        oh1 = kb.sb("oh1" + tag, [128, OHT, 128], BF16)
        oh2 = kb.sb("oh2" + tag, [128, OHT, 128], BF16)
        WT = kb.sb("WT" + tag, [128, 128, G], BF16)
        ub = [kb.sb(f"ub{tag}{i}", [128, 8, CH * 128], BF16) for i in range(NSL)]
        vb = [kb.sb(f"vb{tag}{i}", [128, CH, 1024], BF16) for i in range(NSL)]
        actT = [kb.sb(f"actT{tag}{i}", [128, G], BF16) for i in range(2)]
        ct = [kb.sb(f"ct{tag}{i}", [128, G], BF16) for i in range(2)]
        xo = [kb.sb(f"xo{tag}{i}", [128, 1024], F32) for i in range(1)]
        ps = st.enter_context(nc.psum_tensor("ps" + tag, [128, 4096], F32))
        pb = [Buf(f"pb{i}") for i in range(8)]

        def bank(i, w=512):
            return ps[:, i * 512:i * 512 + w]

        us_v = us.t.rearrange("(kc p) e -> p kc e", p=128)
        vs_v = vs.t.rearrange("(e1 p) d -> p e1 d", p=128)
        wcount = 0
        xocount = 0
        for x_in, x_out, g in [(a_, b_, g_) for (a_, b_, t_) in jobs for g_ in range(t_ // G)]:
            for j in range(TPG):
                tok0 = g * G + j * 128
                kb.dma(xs[j][:], x_in[tok0:tok0 + 128, :], [x_in], [xs[j]])
                kb.act(sq[:], xs[j][:], AF.Square, [xs[j]], [sq])
                kb.reduce(ss[:, 0:1], sq[:], ALU.add, [sq], [ss])
                kb.ts(ss[:, 1:2], ss[:, 0:1], 1.0 / 1024, 1e-6, ALU.mult, ALU.add, [ss], [ss])
                kb.act(ss[:, 3:4], ss[:, 1:2], AF.Sqrt, [ss], [ss])
                kb.s.op("dve", lambda g_: g_.reciprocal(ss[:, 2:3], ss[:, 3:4]), [ss], [ss])
                kb.stt(hf[:], xs[j][:], ss[:, 2:3], gB[:], ALU.mult, ALU.mult, [xs[j], ss, gB], [hf])
                for kc in range(8):
                    kb.tr(ps[:, kc * 128:(kc + 1) * 128], hf[:, kc * 128:(kc + 1) * 128], ident[:], [hf, ident], [pb[kc // 4]])
                for hb in range(2):
                    kb.copy(hT[:, hb * 4:(hb + 1) * 4, j * 128:(j + 1) * 128],
                            bank(hb).rearrange("p (k t) -> p k t", t=128), [pb[hb]], [hT], e=("act" if hb == 0 else "dve"))
                for c in range(16):
                    for kc in range(8):
                        kb.mm(ps[:, 2048 + c * 128:2048 + (c + 1) * 128], wqb[:, kc, c * 128:(c + 1) * 128],
                              hT[:, kc, j * 128:(j + 1) * 128], kc == 0, kc == 7, [wqb, hT], [pb[4 + c // 4]])
                for b4 in range(4):
                    kb.copy(qT[:, b4 * 4:(b4 + 1) * 4, :], bank(4 + b4).rearrange("p (k t) -> p k t", t=128),
                            [pb[4 + b4]], [qT], e=("act" if b4 % 2 == 0 else "dve"))
                for c in range(16):
                    kb.mm(ps[:, c * 128:(c + 1) * 128], qT[:, c, :], kTb[:, c * 128:(c + 1) * 128], True, True,
                          [qT, kTb], [pb[c // 4]])
                for b4 in range(4):
                    kb.copy(sc[:, b4 * 4:(b4 + 1) * 4, :], bank(b4).rearrange("p (k t) -> p k t", t=128),
                            [pb[b4]], [sc], e=("act" if b4 % 2 == 0 else "dve"))
                for c in range(16):
                    kb.s.op("dve", lambda g_, c=c: g_.max(out=v16[:, c, 0:8], in_=sc[:, c, :]), [sc], [v16])
                    kb.s.op("dve", lambda g_, c=c: g_.match_replace(out=tmp[:, 0:128], in_to_replace=v16[:, c, 0:8],
                                                                    in_values=sc[:, c, :], imm_value=-1e30), [sc, v16], [tmp])
                    kb.s.op("dve", lambda g_, c=c: g_.max(out=v16[:, c, 8:16], in_=tmp[:, 0:128]), [tmp], [v16])
                    kb.s.op("dve", lambda g_, c=c: g_.max_index(out=i16[:, c, 0:8], in_max=v16[:, c, 0:8], in_values=sc[:, c, :]),
                            [sc, v16], [i16])
                    kb.s.op("dve", lambda g_, c=c: g_.max_index(out=i16[:, c, 8:16], in_max=v16[:, c, 8:16], in_values=sc[:, c, :]),
                            [sc, v16], [i16])
                kb.copy(i16f[:], i16[:], [i16], [i16f], e="pool")
                v16v = v16[:].rearrange("p (h two) k -> p h two k", two=2)
                kb.tt(cand[:], v16v[:, :, 0, :].unsqueeze(3).to_broadcast([128, 8, 16, 16]),
                      v16v[:, :, 1, :].unsqueeze(2).to_broadcast([128, 8, 16, 16]), ALU.add, [v16], [cand])
                for h in range(8):
                    ch = cand[:, h, :, :].rearrange("p a b -> p (a b)")
                    kb.s.op("dve", lambda g_, h=h, ch=ch: g_.max(out=s16[:, h, 0:8], in_=ch), [cand], [s16])
                    kb.s.op("dve", lambda g_, h=h, ch=ch: g_.match_replace(out=tmp[:], in_to_replace=s16[:, h, 0:8],
                                                                           in_values=ch, imm_value=-1e30), [cand, s16], [tmp])
                    kb.s.op("dve", lambda g_, h=h: g_.max(out=s16[:, h, 8:16], in_=tmp[:]), [tmp], [s16])
                    kb.s.op("dve", lambda g_, h=h, ch=ch: g_.max_index(out=j16[:, h, 0:8], in_max=s16[:, h, 0:8], in_values=ch),
                            [cand, s16], [j16])
                    kb.s.op("dve", lambda g_, h=h, ch=ch: g_.max_index(out=j16[:, h, 8:16], in_max=s16[:, h, 8:16], in_values=ch),
                            [cand, s16], [j16])
                gt = trio[:, 2, :].rearrange("p (h k) -> p h k", k=16)
                kb.tt(gt, s16[:], s16[:, :, 0:1].to_broadcast([128, 8, 16]), ALU.subtract, [s16], [trio], e="pool")
                kb.act(gt, gt, AF.Exp, [trio], [trio])
                kb.reduce(zz[:], gt, ALU.add, [trio], [zz])
                kb.s.op("dve", lambda g_: g_.reciprocal(zz[:], zz[:]), [zz], [zz])
                kb.tt(gt, gt, zz[:].unsqueeze(2).to_broadcast([128, 8, 16]), ALU.mult, [trio, zz], [trio])
                kb.s.op("dve", lambda g_: g_.tensor_single_scalar(ja[:], j16[:], 4, ALU.logical_shift_right), [j16], [ja])
                kb.s.op("dve", lambda g_: g_.tensor_single_scalar(jb[:], j16[:], 15, ALU.bitwise_and), [j16], [jb])
                kb.copy(jaf[:], ja[:], [ja], [jaf])
                kb.copy(jbf[:], jb[:], [jb], [jbf])
                i16v = i16f[:].rearrange("p (h two) k -> p h two k", two=2)
                iota16 = iota[:, 0:16].unsqueeze(1).unsqueeze(1).to_broadcast([128, 8, 16, 16])
                for which, jf in ((0, jaf), (1, jbf)):
                    kb.tt(eq[:], iota16, jf[:].unsqueeze(3).to_broadcast([128, 8, 16, 16]), ALU.is_equal, [iota, jf], [eq])
                    kb.tt(eq[:], eq[:], i16v[:, :, which, :].unsqueeze(2).to_broadcast([128, 8, 16, 16]), ALU.mult,
                          [eq, i16f], [eq])
                    kb.reduce(trio[:, which, :], eq[:].rearrange("p h k a -> p (h k) a"), ALU.add, [eq], [trio])
                for w3 in range(3):
                    kb.tr(ps[:, 3584 + w3 * 128:3584 + (w3 + 1) * 128], trio[:, w3, :], ident[:], [trio, ident], [pb[7]])
                kb.copy(trioT[:].rearrange("p a t -> p (a t)"), ps[:, 3584:3584 + 384], [pb[7]], [trioT], e="act")
                for t0 in range(0, 128, OHT):
                    iob = iota[:].unsqueeze(1).to_broadcast([128, OHT, 128])
                    kb.tt(oh2[:], iob, trioT[:, 1, t0:t0 + OHT].unsqueeze(2).to_broadcast([128, OHT, 128]), ALU.is_equal,
                          [iota, trioT], [oh2])
                    kb.tt(oh1[:], iob, trioT[:, 0, t0:t0 + OHT].unsqueeze(2).to_broadcast([128, OHT, 128]), ALU.is_equal,
                          [iota, trioT], [oh1])
                    kb.tt(oh1[:], oh1[:], trioT[:, 2, t0:t0 + OHT].unsqueeze(2).to_broadcast([128, OHT, 128]), ALU.mult,
                          [oh1, trioT], [oh1])
                    for t16 in range(0, OHT, 16):
                        half = wcount % 2
                        wcount += 1
                        for tt_ in range(16):
                            tl = t16 + tt_
                            col = half * 2048 + tt_ * 128
                            kb.mm(ps[:, col:col + 128], oh2[:, tl, :], oh1[:, tl, :], True, True, [oh1, oh2],
                                  [pb[half * 4 + tt_ // 4]])
                        tcol = j * 128 + t0 + t16
                        kb.copy(WT[:, :, tcol:tcol + 16],
                                ps[:, half * 2048:(half + 1) * 2048].rearrange("p (t e) -> p e t", e=128),
                                [pb[half * 4 + q_] for q_ in range(4)], [WT], e=("act" if half == 0 else "dve"))
            ncg = 128 // CH

            def u_stage(e1):
                cg, el = e1 // CH, e1 % CH
                sl = cg % NSL
                if el == 0:
                    kb.dma(ub[sl][:], us_v[:, :, cg * CH * 128:(cg + 1) * CH * 128], [us], [ub[sl]])
                    kb.dma(vb[sl][:], vs_v[:, cg * CH:(cg + 1) * CH, :], [vs], [vb[sl]], e="pool")
                a = e1 % 2
                pu = ps[:, 2048 + a * 512:2048 + a * 512 + G]
                for kc in range(8):
                    kb.mm(pu, ub[sl][:, kc, el * 128:(el + 1) * 128], hT[:, kc, :], kc == 0, kc == 7,
                          [ub[sl], hT], [pb[4 + a]])
                kb.act(actT[a][:], pu, AF.Gelu, [pb[4 + a]], [actT[a]])
                kb.tt(ct[a][:], actT[a][:], WT[:, e1, :], ALU.mult, [actT[a], WT], [ct[a]])

            def v_stage(e1):
                cg, el = e1 // CH, e1 % CH
                sl = cg % NSL
                a = e1 % 2
                for j in range(TPG):
                    for hh in range(2):
                        kb.mm(bank(2 * j + hh), ct[a][:, j * 128:(j + 1) * 128], vb[sl][:, el, hh * 512:(hh + 1) * 512],
                              e1 == 0, e1 == 127, [ct[a], vb[sl]], [pb[2 * j + hh]])

            u_stage(0)
            for e1 in range(128):
                if e1 + 1 < 128:
                    u_stage(e1 + 1)
                v_stage(e1)
            for j in range(TPG):
                tok0 = g * G + j * 128
                o = xo[0]
                xocount += 1
                kb.tt(o[:], ps[:, j * 1024:(j + 1) * 1024], xs[j][:], ALU.add, [pb[2 * j], pb[2 * j + 1], xs[j]], [o])
                kb.dma(x_out[tok0:tok0 + 128, :], o[:], [o], [], e="pool")
        kb.s.barrier()


def rms_to_hT(kb, xs_t, gB, ident, hf, sq, ss, ps, pbA, pbB, hT, col0, eps=1e-6):
    kb.act(sq[:], xs_t[:], AF.Square, [xs_t], [sq])
    kb.reduce(ss[:, 0:1], sq[:], ALU.add, [sq], [ss])
    kb.ts(ss[:, 1:2], ss[:, 0:1], 1.0 / 1024, eps, ALU.mult, ALU.add, [ss], [ss])
    kb.act(ss[:, 3:4], ss[:, 1:2], AF.Sqrt, [ss], [ss])
    kb.s.op("dve", lambda g_: g_.reciprocal(ss[:, 2:3], ss[:, 3:4]), [ss], [ss])
    kb.stt(hf[:], xs_t[:], ss[:, 2:3], gB[:], ALU.mult, ALU.mult, [xs_t, ss, gB], [hf])
    for kc in range(8):
        kb.tr(ps[:, kc * 128:(kc + 1) * 128], hf[:, kc * 128:(kc + 1) * 128], ident[:], [hf, ident], [pbA if kc < 4 else pbB])
    kb.copy(hT[:, 0:4, col0:col0 + 128], ps[:, 0:512].rearrange("p (k t) -> p k t", t=128), [pbA], [hT], e="act")
    kb.copy(hT[:, 4:8, col0:col0 + 128], ps[:, 512:1024].rearrange("p (k t) -> p k t", t=128), [pbB], [hT], e="dve")


def mixer0_block(kb0, x_own, x_pre, x_out, gvec, w_in, w_out, convwT, convb, lng, lnb, w2, gateb, glag,
                 ident_d, tri_d, T, TP, tag="m0", x_out_pre=None):
    nc = kb0.nc
    NTO = T // 128
    NTP = TP // 128
    with ExitStack() as st:
        kb = kb0.scope(st)
        winb = kb.sb("winb", [128, 8, 2576], BF16)
        load_bf16_resident(kb, winb, lambda kc, c0, w: winb[:, kc, c0:c0 + w], w_in, 8, 2576, "win")
        woutb = kb.sb("woutb", [128, 8, 1024], BF16)
        load_bf16_resident(kb, woutb, lambda kc, c0, w: woutb[:, kc, c0:c0 + w], w_out, 8, 1024, "wout")
        gB = kb.sb("gBm", [128, 1024], F32)
        kb.dma(gB[:], gvec.t.partition_broadcast(128)[:, 0, :], [gvec], [gB])
        gbB = kb.sb("gbB", [128, 256], F32)
        kb.dma(gbB[:], gateb.t.partition_broadcast(128)[:, 0, :], [gateb], [gbB])
        ident = kb.sb("identm", [128, 128], F32)
        kb.dma(ident[:], ident_d[:], [ident_d], [ident])
        tri = kb.sb("trim", [128, 128], F32)
        kb.dma(tri[:], tri_d[:], [tri_d], [tri])
        ones = kb.sb("onesm", [128, 128], F32)
        kb.memset(ones[:], 1.0, [ones])
        cw = kb.sb("cw", [128, 4, 31], F32)
        kb.dma(cw[:], convwT[:], [convwT], [cw])
        cols = kb.sb("colsm", [128, 16], F32)
        kb.dma(cols[:, 0:4], convb[:], [convb], [cols])
        kb.dma(cols[:, 4:8], lng[:], [lng], [cols])
        kb.dma(cols[:, 8:12], lnb[:], [lnb], [cols])
        kb.dma(cols[:, 12:13], glag[:], [glag], [cols])
        w2f = kb.sb("w2f", [16, 256], F32)
        kb.dma(w2f[:], w2[:], [w2], [w2f])
        w2b = kb.sb("w2b", [16, 256], BF16)
        kb.copy(w2b[:], w2f[:], [w2f], [w2b])
        diag = kb.sb("diag", [128, 4 * 31, 128], BF16)
        for c in range(4):
            for j in range(31):
                kb.ts(diag[:, c * 31 + j, :], ident[:], cw[:, c, j:j + 1], None, ALU.mult, None, [ident, cw], [diag],
                      e=("dve" if (c * 31 + j) % 2 == 0 else "pool"))
        full = x_out_pre is not None
        UW = 158 if full else 30 + T
        uT = kb.sb("uT", [128, 4, UW], BF16)
        kb.memset(uT[:, :, 0:30], 0.0, [uT])
        Sf = [kb.sb(f"Sf{h}", [64, 128], F32) for h in range(4)]
        Sb = [kb.sb(f"Sb{h}", [64, 128], BF16) for h in range(4)]
        for h in range(4):
            kb.memset(Sf[h][:], 0.0, [Sf[h]])
            kb.memset(Sb[h][:], 0.0, [Sb[h]])
        xs = kb.sb("xsm", [128, 1024], F32)
        hf = kb.sb("hfm", [128, 1024], F32)
        sq = kb.sb("sqm", [128, 1024], BF16)
        ss = kb.sb("ssm", [128, 4], F32)
        hT = kb.sb("hTm", [128, 8, 128], BF16)
        sg = kb.sb("sg", [128, 512], F32)
        glrT = kb.sb("glrT", [16, 128], BF16)
        zb = kb.sb("zb", [128, 256], F32)
        la = kb.sb("la", [128, 256], F32)
        enb_tm = kb.sb("enb_tm", [128, 256], F32)
        ebl_tm = kb.sb("ebl_tm", [128, 256], F32)
        kdec = kb.sb("kdec", [128, 256], BF16)
        vbf = kb.sb("vbf", [128, 512], BF16)
        eblc = kb.sb("eblc", [64, 8], F32)
        eb = kb.sb("eb", [64, 512], F32)
        enb = kb.sb("enb", [64, 512], F32)
        qt = kb.sb("qt", [64, 4, 128], BF16)
        kt = kb.sb("kt", [64, 4, 128], BF16)
        Am = kb.sb("Am", [128, 4, 128], BF16)
        osq = kb.sb("osq", [128, 512], F32)
        rs = kb.sb("rs", [128, 512], F32)
        sr = kb.sb("sr", [128, 512], F32)
        t1 = kb.sb("t1", [128, 512], F32)
        ybT = kb.sb("ybT", [128, 4, 128], BF16)
        ycs = kb.sb("ycs", [128, 4, 128], F32)
        ysq = kb.sb("ysq", [128, 4, 128], F32)
        mean = kb.sb("mean", [128, 128], F32)
        var = kb.sb("var", [128, 128], F32)
        yaT = kb.sb("yaT", [128, 4, 128], BF16)
        xo = kb.sb("xom", [128, 1024], F32)
        ps = st.enter_context(nc.psum_tensor("psm", [128, 4096], F32))
        pb = [Buf(f"pbm{i}") for i in range(8)]

        def B(i, lo=0, hi=512):
            return ps[:, i * 512 + lo:i * 512 + hi]

        def fm_proj(colbase, ncols, bank, slot):
            for kc in range(8):
                kb.mm(ps[0:ncols, bank * 512 + slot * 128:bank * 512 + (slot + 1) * 128], winb[:, kc, colbase:colbase + ncols],
                      hT[:, kc, :], kc == 0, kc == 7, [winb, hT], [pb[bank]])

        def state_part(u_needed, own):
            for kc in range(8):
                kb.mm(B(5), hT[:, kc, :], winb[:, kc, 1536:2048], kc == 0, kc == 7, [hT, winb], [pb[5]])
            for kc in range(8):
                kb.mm(B(6, 0, 256), hT[:, kc, :], winb[:, kc, 1280:1536], kc == 0, kc == 7, [hT, winb], [pb[6]])
            fm_proj(2560, 16, 4, 0)
            kb.copy(glrT[:], ps[0:16, 4 * 512:4 * 512 + 128], [pb[4]], [glrT], e="act")
            kb.mm(B(6, 256, 512), glrT[:], w2b[:], True, True, [glrT, w2b], [pb[6]])
            kb.tt(zb[:], B(6, 256, 512), gbB[:], ALU.add, [pb[6], gbB], [zb])
            kb.act(zb[:], zb[:], AF.Exp, [zb], [zb], scale=-1.0)
            kb.act(zb[:], zb[:], AF.Ln, [zb], [zb], bias=1.0)
            kb.ts(la[:], zb[:], -1.0 / 16.0, None, ALU.mult, None, [zb], [la])
            kb.mm(B(7, 0, 256), tri[:], la[:], True, True, [tri, la], [pb[7]])
            kb.mm(B(7, 256, 512), ones[:], la[:], True, True, [ones, la], [pb[7]])
            for h in range(4):
                kb.mm(ps[0:64, 4 * 512 + 384 + 2 * h:4 * 512 + 386 + 2 * h], la[:, h * 64:(h + 1) * 64], ones[:, 0:2], True, True,
                      [la, ones], [pb[4]])
            kb.act(eblc[:], ps[0:64, 4 * 512 + 384:4 * 512 + 392], AF.Exp, [pb[4]], [eblc])
            kb.act(enb_tm[:], B(7, 0, 256), AF.Exp, [pb[7]], [enb_tm], scale=-1.0)
            kb.act(ebl_tm[:], B(7, 256, 512), AF.Exp, [pb[7]], [ebl_tm])
            kb.tt(enb_tm[:], enb_tm[:], ebl_tm[:], ALU.mult, [enb_tm, ebl_tm], [enb_tm])
            kb.tt(kdec[:], B(6, 0, 256), enb_tm[:], ALU.mult, [pb[6], enb_tm], [kdec])
            kb.copy(vbf[:], B(5), [pb[5]], [vbf], e="act")

        def state_update():
            for h in range(4):
                kb.mm(ps[0:64, 2 * 512 + h * 128:2 * 512 + (h + 1) * 128], kdec[:, h * 64:(h + 1) * 64], vbf[:, h * 128:(h + 1) * 128],
                      True, True, [kdec, vbf], [pb[2]])
            for h in range(4):
                kb.stt(Sf[h][:], Sf[h][:], eblc[:, 2 * h:2 * h + 1], ps[0:64, 2 * 512 + h * 128:2 * 512 + (h + 1) * 128],
                       ALU.mult, ALU.add, [Sf[h], eblc, pb[2]], [Sf[h]])
                kb.copy(Sb[h][:], Sf[h][:], [Sf[h]], [Sb[h]], e="act")

        def conv_u(tokcol):
            for c in range(4):
                fm_proj(c * 128, 128, 0, c)
                fm_proj(512 + c * 128, 128, 1, c)
            kb.act(sg[:], B(1), AF.Sigmoid, [pb[1]], [sg])
            kb.tt(uT[:, :, 30 + tokcol:30 + tokcol + 128], B(0).rearrange("p (c t) -> p c t", t=128),
                  sg[:].rearrange("p (c t) -> p c t", t=128), ALU.mult, [pb[0], sg], [uT])

        for i in range(0 if full else NTP):
            kb.dma(xs[:], x_pre[i * 128:(i + 1) * 128, :], [x_pre], [xs])
            rms_to_hT(kb, xs, gB, ident, hf, sq, ss, ps, pb[0], pb[1], hT, 0)
            state_part(False, False)
            state_update()
            if i == NTP - 1:
                for c in range(4):
                    fm_proj(c * 128, 128, 0, c)
                    fm_proj(512 + c * 128, 128, 1, c)
                kb.act(sg[:], B(1), AF.Sigmoid, [pb[1]], [sg])
                kb.tt(uT[:, :, 0:30], B(0).rearrange("p (c t) -> p c t", t=128)[:, :, 98:128],
                      sg[:].rearrange("p (c t) -> p c t", t=128)[:, :, 98:128], ALU.mult, [pb[0], sg], [uT])
        for ii in range((NTP + NTO) if full else NTO):
            if full:
                isown = ii >= NTP
                i = 0
                srcx = x_own[(ii - NTP) * 128:(ii - NTP + 1) * 128, :] if isown else x_pre[ii * 128:(ii + 1) * 128, :]
                dsty = x_out[(ii - NTP) * 128:(ii - NTP + 1) * 128, :] if isown else x_out_pre[ii * 128:(ii + 1) * 128, :]
            else:
                i = ii
                srcx = x_own[i * 128:(i + 1) * 128, :]
                dsty = x_out[i * 128:(i + 1) * 128, :]
            kb.dma(xs[:], srcx, [x_own, x_pre], [xs])
            rms_to_hT(kb, xs, gB, ident, hf, sq, ss, ps, pb[0], pb[1], hT, 0)
            state_part(True, True)
            conv_u(i * 128)
            for h in range(4):
                fm_proj(1024 + h * 64, 64, 2, h)
                fm_proj(1280 + h * 64, 64, 3, h)
            for sl in range(4):
                fm_proj(2048 + sl * 128, 128, 4, sl)
            for h in range(4):
                kb.mm(ps[0:64, 5 * 512 + h * 128:5 * 512 + (h + 1) * 128], la[:, h * 64:(h + 1) * 64], tri[:], True, True,
                      [la, tri], [pb[5]])
            kb.act(eb[:], ps[0:64, 5 * 512:6 * 512], AF.Exp, [pb[5]], [eb])
            kb.act(enb[:], ps[0:64, 5 * 512:6 * 512], AF.Exp, [pb[5]], [enb], scale=-1.0)
            kb.stt(qt[:].rearrange("p h t -> p (h t)"), ps[0:64, 2 * 512:3 * 512], 0.125, eb[:], ALU.mult, ALU.mult, [pb[2], eb], [qt])
            kb.tt(kt[:].rearrange("p h t -> p (h t)"), ps[0:64, 3 * 512:4 * 512], enb[:], ALU.mult, [pb[3], enb], [kt])
            kb.act(sr[:], B(4), AF.Silu, [pb[4]], [sr])
            for h in range(4):
                kb.mm(B(0, h * 128, (h + 1) * 128), kt[:, h, :], qt[:, h, :], True, True, [kt, qt], [pb[0]])
            kb.tt(Am[:], B(0).rearrange("p (h t) -> p h t", t=128), tri[:].unsqueeze(1).to_broadcast([128, 4, 128]), ALU.mult,
                  [pb[0], tri], [Am])
            for h in range(4):
                kb.mm(B(1, h * 128, (h + 1) * 128), vbf[:, h * 128:(h + 1) * 128], Am[:, h, :], True, False, [vbf, Am], [pb[1]])
                kb.mm(B(1, h * 128, (h + 1) * 128), Sb[h][:], qt[:, h, :], False, True, [Sb[h], qt], [pb[1]])
            state_update()
            kb.act(osq[:], B(1), AF.Square, [pb[1]], [osq])
            kb.mm(B(3), ones[:], osq[:], True, True, [ones, osq], [pb[3]])
            kb.ts(rs[:], B(3), 1.0 / 128, 1e-6, ALU.mult, ALU.add, [pb[3]], [rs])
            kb.act(rs[:], rs[:], AF.Sqrt, [rs], [rs])
            kb.s.op("dve", lambda g_: g_.reciprocal(rs[:], rs[:]), [rs], [rs])
            kb.tt(t1[:], B(1), rs[:], ALU.mult, [pb[1], rs], [t1])
            kb.stt(ybT[:].rearrange("p h t -> p (h t)"), t1[:], cols[:, 12:13], sr[:], ALU.mult, ALU.mult, [t1, cols, sr], [ybT])
            for c in range(4):
                for j in range(31):
                    kb.mm(B(4, c * 128, (c + 1) * 128), diag[:, c * 31 + j, :], uT[:, c, i * 128 + j:i * 128 + j + 128],
                          j == 0, j == 30, [diag, uT], [pb[4]])
            for c in range(4):
                kb.ts(ycs[:, c, :], B(4, c * 128, (c + 1) * 128), cols[:, c:c + 1], None, ALU.add, None, [pb[4], cols], [ycs])
            kb.act(ysq[:], ycs[:], AF.Square, [ycs], [ysq])
            for c in range(4):
                kb.mm(B(5, 0, 128), ones[:], ycs[:, c, :], c == 0, c == 3, [ones, ycs], [pb[5]])
            for c in range(4):
                kb.mm(B(5, 128, 256), ones[:], ysq[:, c, :], c == 0, c == 3, [ones, ysq], [pb[5]])
            kb.ts(mean[:], B(5, 0, 128), 1.0 / 512, None, ALU.mult, None, [pb[5]], [mean])
            kb.tt(var[:], mean[:], mean[:], ALU.mult, [mean], [var])
            kb.stt(var[:], B(5, 128, 256), 1.0 / 512, var[:], ALU.mult, ALU.subtract, [pb[5], var], [var])
            kb.ts(var[:], var[:], 1e-6, None, ALU.add, None, [var], [var])
            kb.act(var[:], var[:], AF.Sqrt, [var], [var])
            kb.s.op("dve", lambda g_: g_.reciprocal(var[:], var[:]), [var], [var])
            kb.tt(ycs[:], ycs[:], mean[:].unsqueeze(1).to_broadcast([128, 4, 128]), ALU.subtract, [ycs, mean], [ycs])
            kb.tt(ycs[:], ycs[:], var[:].unsqueeze(1).to_broadcast([128, 4, 128]), ALU.mult, [ycs, var], [ycs])
            for c in range(4):
                kb.ts(ycs[:, c, :], ycs[:, c, :], cols[:, 4 + c:5 + c], cols[:, 8 + c:9 + c], ALU.mult, ALU.add, [ycs, cols], [ycs])
            kb.act(yaT[:], ycs[:], AF.Silu, [ycs], [yaT])
            for hh in range(2):
                for kc in range(8):
                    lhsT = yaT[:, kc, :] if kc < 4 else ybT[:, kc - 4, :]
                    kb.mm(B(6 + hh), lhsT, woutb[:, kc, hh * 512:(hh + 1) * 512], kc == 0, kc == 7, [yaT, ybT, woutb], [pb[6 + hh]])
            kb.tt(xo[:], ps[:, 6 * 512:8 * 512], xs[:], ALU.add, [pb[6], pb[7], xs], [xo])
            kb.dma(dsty, xo[:], [xo], [], e="pool")
            if full:
                kb.copy(uT[:, :, 0:30], uT[:, :, 128:158], [uT], [uT], e="pool")
        kb.s.barrier()


def fox_block(kb0, x_own, x_pre, x_out, gvec, w_in, w_out, fb, qg, kg, pflag, ident_d, tri_d, masks_d, T, TP, tag="fx"):
    nc = kb0.nc
    NTO = T // 128
    NTP = TP // 128
    NTA = NTO + NTP
    NSB = T // 512
    QT = kb0.dram("fxQT", [16, 65, T], BF16, "Internal")
    KT = kb0.dram("fxKT", [16, 65, TP + T], BF16, "Internal")
    VA = kb0.dram("fxVA", [NTA, 128, 16 * 65], BF16, "Internal")
    OT = kb0.dram("fxOT", [16, 64, T], BF16, "Internal")
    with ExitStack() as st0:
        kbp = kb0.scope(st0)
        negc = kbp.sb("negc", [128, NTA, 16], F32)
        ident = kbp.sb("identf", [128, 128], F32)
        kbp.dma(ident[:], ident_d[:], [ident_d], [ident])
        ones = kbp.sb("onesf", [128, 128], F32)
        kbp.memset(ones[:], 1.0, [ones])
        woutb = kbp.sb("woutbf", [128, 8, 1024], BF16)
        load_bf16_resident(kbp, woutb, lambda kc, c0, w: woutb[:, kc, c0:c0 + w], w_out, 8, 1024, "fwout")
        with ExitStack() as st:
            kb = kbp.scope(st)
            winb = kb.sb("winbf", [128, 8, 3088], BF16)
            load_bf16_resident(kb, winb, lambda kc, c0, w: winb[:, kc, c0:c0 + w], w_in, 8, 3088, "fwin")
            gB = kb.sb("gBf", [128, 1024], F32)
            kb.dma(gB[:], gvec.t.partition_broadcast(128)[:, 0, :], [gvec], [gB])
            fbB = kb.sb("fbB", [128, 16], F32)
            kb.dma(fbB[:], fb.t.partition_broadcast(128)[:, 0, :], [fb], [fbB])
            qgB = kb.sb("qgB", [128, 64], F32)
            kb.dma(qgB[:], qg.t.partition_broadcast(128)[:, 0, :], [qg], [qgB])
            kgB = kb.sb("kgB", [128, 64], F32)
            kb.dma(kgB[:], kg.t.partition_broadcast(128)[:, 0, :], [kg], [kgB])
            pfl = kb.sb("pfl", [128, 1], F32)
            kb.dma(pfl[:], pflag[:], [pflag], [pfl])
            tri = kb.sb("trif", [128, 128], F32)
            kb.dma(tri[:], tri_d[:], [tri_d], [tri])
            xs = kb.sb("xsf", [128, 1024], F32)
            hf = kb.sb("hff", [128, 1024], F32)
            sq = kb.sb("sqf", [128, 1024], BF16)
            ss = kb.sb("ssf", [128, 4], F32)
            hT = kb.sb("hTf", [128, 8, 128], BF16)
            nsq = kb.sb("nsq", [128, 1024], F32)
            nss = kb.sb("nss", [128, 16], F32)
            qa = kb.sb("qa", [128, 16, 65], F32)
            ka = kb.sb("ka", [128, 16, 65], F32)
            kb.memset(ka[:, :, 64:65], 1.0, [ka])
            va = kb.sb("va", [128, 16, 65], BF16)
            kb.memset(va[:, :, 0:1], 1.0, [va])
            qTs = kb.sb("qTs", [65, 16, 128], BF16)
            kTs = kb.sb("kTs", [65, 16, 128], BF16)
            lf = kb.sb("lf", [128, 16], F32)
            Lsum = kb.sb("Lsum", [128, 16], F32)
            kb.memset(Lsum[:], 0.0, [Lsum])
            ctile = kb.sb("ctile", [128, 16], F32)
            ps = st.enter_context(nc.psum_tensor("psf1", [128, 4096], F32))
            pb = [Buf(f"pbf{i}") for i in range(8)]

            def normed(dst, bank0, gt):
                src = ps[:, bank0 * 512:(bank0 + 2) * 512]
                kb.act(nsq[:], src, AF.Square, [pb[bank0], pb[bank0 + 1]], [nsq])
                kb.reduce(nss[:], nsq[:].rearrange("p (h d) -> p h d", d=64), ALU.add, [nsq], [nss])
                kb.ts(nss[:], nss[:], 1.0 / 64, 1e-6, ALU.mult, ALU.add, [nss], [nss])
                kb.act(nss[:], nss[:], AF.Sqrt, [nss], [nss])
                kb.s.op("dve", lambda g_: g_.reciprocal(nss[:], nss[:]), [nss], [nss])
                kb.tt(dst[:, :, 0:64], src.rearrange("p (h d) -> p h d", d=64), nss[:].unsqueeze(2).to_broadcast([128, 16, 64]),
                      ALU.mult, [pb[bank0], pb[bank0 + 1], nss], [dst])
                kb.tt(dst[:, :, 0:64], dst[:, :, 0:64], gt[:].unsqueeze(1).to_broadcast([128, 16, 64]), ALU.mult, [dst, gt], [dst])

            def transposed_store(src, dstT, dram, tokcol, b0):
                for h in range(16):
                    kb.tr(ps[0:65, b0 * 512 + h * 128:b0 * 512 + (h + 1) * 128], src[:, h, :], ident[:], [src, ident], [pb[b0 + h // 4]])
                for q4 in range(4):
                    kb.copy(dstT[:, q4 * 4:(q4 + 1) * 4, :], ps[0:65, (b0 + q4) * 512:(b0 + q4 + 1) * 512].rearrange("p (h t) -> p h t", t=128),
                            [pb[b0 + q4]], [dstT], e=("act" if q4 % 2 == 0 else "dve"))
                kb.dma(dram.t.rearrange("h r t -> r h t")[:, :, tokcol:tokcol + 128], dstT[:], [dstT], [], e="pool")

            for i in range(NTA):
                own = i >= NTP
                src = x_own[(i - NTP) * 128:(i - NTP + 1) * 128, :] if own else x_pre[i * 128:(i + 1) * 128, :]
                kb.dma(xs[:], src, [x_own, x_pre], [xs])
                rms_to_hT(kb, xs, gB, ident, hf, sq, ss, ps, pb[0], pb[1], hT, 0)
                for kc in range(8):
                    kb.mm(ps[:, 6 * 512:6 * 512 + 16], hT[:, kc, :], winb[:, kc, 3072:3088], kc == 0, kc == 7, [hT, winb], [pb[6]])
                kb.tt(lf[:], ps[:, 6 * 512:6 * 512 + 16], fbB[:], ALU.add, [pb[6], fbB], [lf])
                kb.act(lf[:], lf[:], AF.Exp, [lf], [lf], scale=-1.0)
                kb.act(lf[:], lf[:], AF.Ln, [lf], [lf], bias=1.0)
                kb.ts(lf[:], lf[:], -1.0, None, ALU.mult, None, [lf], [lf])
                kb.mm(ps[:, 6 * 512 + 16:6 * 512 + 32], tri[:], lf[:], True, False, [tri, lf], [pb[6]])
                kb.mm(ps[:, 6 * 512 + 16:6 * 512 + 32], ones[:], Lsum[:], False, True, [ones, Lsum], [pb[6]])
                kb.tt(Lsum[:], Lsum[:], lf[:], ALU.add, [Lsum, lf], [Lsum])
                kb.copy(ctile[:], ps[:, 6 * 512 + 16:6 * 512 + 32], [pb[6]], [ctile], e="act")
                if own:
                    kb.ts(negc[:, i, :], ctile[:], -1.0, None, ALU.mult, None, [ctile], [negc])
                else:
                    kb.ts(negc[:, i, :], ctile[:], -1.0, pfl[:, 0:1], ALU.mult, ALU.add, [ctile, pfl], [negc])
                for hh in range(2):
                    for kc in range(8):
                        kb.mm(ps[:, (2 + hh) * 512:(3 + hh) * 512], hT[:, kc, :], winb[:, kc, 1024 + hh * 512:1536 + hh * 512],
                              kc == 0, kc == 7, [hT, winb], [pb[2 + hh]])
                normed(ka, 2, kgB)
                for hh in range(2):
                    for kc in range(8):
                        kb.mm(ps[:, (4 + hh) * 512:(5 + hh) * 512], hT[:, kc, :], winb[:, kc, 2048 + hh * 512:2560 + hh * 512],
                              kc == 0, kc == 7, [hT, winb], [pb[4 + hh]])
                kb.copy(va[:, :, 1:65], ps[:, 4 * 512:6 * 512].rearrange("p (h d) -> p h d", d=64), [pb[4], pb[5]], [va], e="act")
                kb.dma(VA[i], va[:].rearrange("p h d -> p (h d)"), [va], [], e="pool")
                if own:
                    for hh in range(2):
                        for kc in range(8):
                            kb.mm(ps[:, hh * 512:(hh + 1) * 512], hT[:, kc, :], winb[:, kc, hh * 512:(hh + 1) * 512],
                                  kc == 0, kc == 7, [hT, winb], [pb[hh]])
                    normed(qa, 0, qgB)
                    kb.ts(qa[:, :, 64:65], ctile[:].unsqueeze(2), 8.0, None, ALU.mult, None, [ctile], [qa])
                transposed_store(ka, kTs, KT, i * 128, 2)
                if own:
                    transposed_store(qa, qTs, QT, (i - NTP) * 128, 2)
            kb.s.barrier()
        with ExitStack() as st:
            kb = kbp.scope(st)
            kts = kb.sb("kts", [65, TP + T], BF16)
            vas = kb.sb("vas", [128, NTA, 65], BF16)
            qts = kb.sb("qts", [65, T], BF16)
            masks = kb.sb("masksb", [128, 4, 512], F32)
            kb.dma(masks[:], masks_d[:], [masks_d], [masks])
            stmp = kb.sb("stmp", [128, 512], F32)
            pT = [kb.sb(f"pT{i}", [128, 512], BF16) for i in range(3)]
            osb = kb.sb("osb", [65, 512], F32)
            rden = kb.sb("rden", [1, 512], F32)
            oTs = kb.sb("oTs", [65, 512], BF16)
            ps = st.enter_context(nc.psum_tensor("psf2", [128, 4096], F32))
            pb = [Buf(f"pbg{i}") for i in range(8)]
            VAv = VA.t.rearrange("n p (h d) -> p n h d", d=65)
            cnt = 0
            for h in range(16):
                kb.dma(kts[:], KT[h], [], [kts])
                kb.dma(qts[:], QT[h], [], [qts])
                kb.dma(vas[:], VAv[:, :, h, :], [], [vas])
                for j in range(NSB):
                    nkb = NTP + 4 * j + 4
                    ob = 4 + (j % 2)
                    def s_stage(kb_, a):
                        kb.mm(ps[:, a * 512:(a + 1) * 512], kts[:, kb_ * 128:(kb_ + 1) * 128], qts[:, j * 512:(j + 1) * 512], True, True,
                              [kts, qts], [pb[a]])
                        m = kb_ - NTP - 4 * j
                        if m >= 0:
                            kb.tt(stmp[:], ps[:, a * 512:(a + 1) * 512], masks[:, m, :], ALU.add, [pb[a], masks], [stmp])
                            kb.act(pT[a][:], stmp[:], AF.Exp, [stmp, negc], [pT[a]], bias=negc[:, kb_, h:h + 1], scale=0.125)
                        else:
                            kb.act(pT[a][:], ps[:, a * 512:(a + 1) * 512], AF.Exp, [pb[a], negc], [pT[a]], bias=negc[:, kb_, h:h + 1], scale=0.125)

                    def pv_stage(kb_, a):
                        kb.mm(ps[0:65, ob * 512:(ob + 1) * 512], vas[:, kb_, :], pT[a][:], kb_ == 0, kb_ == nkb - 1, [vas, pT[a]], [pb[ob]])

                    NBUF = 3
                    for kb_ in range(min(NBUF - 1, nkb)):
                        s_stage(kb_, kb_ % NBUF)
                    for kb_ in range(nkb):
                        if kb_ + NBUF - 1 < nkb:
                            s_stage(kb_ + NBUF - 1, (kb_ + NBUF - 1) % NBUF)
                        pv_stage(kb_, kb_ % NBUF)
                    kb.copy(osb[:], ps[0:65, ob * 512:(ob + 1) * 512], [pb[ob]], [osb], e="act")
                    kb.s.op("dve", lambda g_: g_.reciprocal(rden[:], osb[0:1, :]), [osb], [rden])
                    kb.mm(ps[0:65, 6 * 512:7 * 512], ones[0:1, 0:65], rden[:], True, True, [ones, rden], [pb[6]])
                    kb.tt(oTs[:], osb[:], ps[0:65, 6 * 512:7 * 512], ALU.mult, [osb, pb[6]], [oTs])
                    kb.dma(OT[h, :, j * 512:(j + 1) * 512], oTs[1:65, :], [oTs], [], e="pool")
            kb.s.barrier()
        with ExitStack() as st:
            kb = kbp.scope(st)
            oTt = [kb.sb(f"oTt{i}", [128, 8, 128], BF16) for i in range(2)]
            xs2 = [kb.sb(f"xs2{i}", [128, 1024], F32) for i in range(2)]
            xo = [kb.sb(f"xof{i}", [128, 1024], F32) for i in range(2)]
            ps = st.enter_context(nc.psum_tensor("psf3", [128, 4096], F32))
            pb = [Buf(f"pbh{i}") for i in range(8)]
            OTv = OT.t.rearrange("(p two) d t -> (two d) p t", two=2)
            for i in range(NTO):
                a = i % 2
                kb.dma(oTt[a][:], OTv[:, :, i * 128:(i + 1) * 128], [], [oTt[a]])
                kb.dma(xs2[a][:], x_own[i * 128:(i + 1) * 128, :], [x_own], [xs2[a]])
                for hh in range(2):
                    for p in range(8):
                        kb.mm(ps[:, (2 * a + hh) * 512:(2 * a + hh + 1) * 512], oTt[a][:, p, :], woutb[:, p, hh * 512:(hh + 1) * 512],
                              p == 0, p == 7, [oTt[a], woutb], [pb[2 * a + hh]])
                kb.tt(xo[a][:], ps[:, 2 * a * 512:(2 * a + 2) * 512], xs2[a][:], ALU.add, [pb[2 * a], pb[2 * a + 1], xs2[a]], [xo[a]])
                kb.dma(x_out[i * 128:(i + 1) * 128, :], xo[a][:], [xo[a]], [], e="pool")
            kb.s.barrier()


TOK = 4096


def _consts():
    k = np.arange(128)[:, None]
    q = np.arange(512)[None, :]
    fm = np.zeros((128, 4, 512), np.float32)
    for mm in range(4):
        fm[:, mm, :] = np.where((mm * 128 + k) <= q, 0.0, -240000.0)
    return {
        "ident": np.eye(128, dtype=np.float32),
        "iota": np.tile(np.arange(128, dtype=np.float32), (128, 1)),
        "tri": np.triu(np.ones((128, 128), np.float32)),
        "masks": fm,
    }


def _col4(v):
    return np.ascontiguousarray(v.reshape(4, 128).T)


_NC_CACHE = {}


def _build_fused(T):
    key = ("fused", T)
    if key in _NC_CACHE:
        return _NC_CACHE[key]
    nc = bass.Bass("TRN2", target_bir_lowering=False)
    with ExitStack() as st:
        kb = KB(nc, st)
        D = lambda n, s: kb.dram(n, s, F32, "ExternalInput")
        I = lambda n: kb.dram(n, [T, 1024], F32, "Internal")
        x, xp = D("x", [T, 1024]), D("xp", [T, 1024])
        y = kb.dram("y", [T, 1024], F32, "ExternalOutput")
        ident, iota, tri, masks = D("ident", [128, 128]), D("iota", [128, 128]), D("tri", [128, 128]), D("masks", [128, 4, 512])
        x1o, x1p, x2o, x2p, x3o = I("x1o"), I("x1p"), I("x2o"), I("x2p"), I("x3o")
        mixer0_block(kb, x, xp, x1o, D("m_g", [1, 1024]), D("m_w_in", [1024, 2576]), D("m_w_out", [1024, 1024]),
                     D("m_convwT", [128, 4, 31]), D("m_convb", [128, 4]), D("m_lng", [128, 4]), D("m_lnb", [128, 4]),
                     D("m_w2", [16, 256]), D("m_gateb", [1, 256]), D("m_glag", [128, 1]), ident, tri, T, T, x_out_pre=x1p)
        tabs0 = peer_tables(kb, D("p0_uT", [1024, 16384]), D("p0_v", [16384, 1024]), "L0")
        peer_block(kb, None, None, D("p0_g", [1, 1024]), D("p0_wq", [1024, 2048]), D("p0_keysT", [128, 2048]), None, None,
                   ident, iota, T, "L0", tables=tabs0, jobs=[(x1p, x2p, T), (x1o, x2o, T)])
        fox_block(kb, x2o, x2p, x3o, D("f_g", [1, 1024]), D("f_w_in", [1024, 3088]), D("f_w_out", [1024, 1024]), D("f_fb", [1, 16]),
                  D("f_qg", [1, 64]), D("f_kg", [1, 64]), D("pflag", [128, 1]), ident, tri, masks, T, T)
        peer_block(kb, x3o, y, D("p1_g", [1, 1024]), D("p1_wq", [1024, 2048]), D("p1_keysT", [128, 2048]),
                   D("p1_uT", [1024, 16384]), D("p1_v", [16384, 1024]), ident, iota, T, "L1")
        kb.s.emit()
        print("fused program: ninst", kb.s.ninst, "nsem", kb.s.nsem, flush=True)
    _NC_CACHE[key] = nc
    return nc


def _fused_common(inp, cst):
    c = {"m_g": np.ascontiguousarray(inp["ev_norm_mix"][0][None]), "m_w_in": np.ascontiguousarray(inp["ev_w_in"][0]),
         "m_w_out": np.ascontiguousarray(inp["ev_w_out"][0]),
         "m_convwT": np.ascontiguousarray(inp["ev_conv_w"][0].T.reshape(4, 128, 31).transpose(1, 0, 2)),
         "m_convb": _col4(inp["ev_conv_b"][0]), "m_lng": _col4(inp["ev_conv_ln_g"][0]), "m_lnb": _col4(inp["ev_conv_ln_b"][0]),
         "m_w2": np.ascontiguousarray(inp["ev_gate_w2"][0]), "m_gateb": np.ascontiguousarray(inp["ev_gate_b"][0][None]),
         "m_glag": np.ascontiguousarray(inp["ev_gla_norm_g"][0][:, None]),
         "f_g": np.ascontiguousarray(inp["od_norm_mix"][0][None]), "f_w_in": np.ascontiguousarray(inp["od_w_in"][0]),
         "f_w_out": np.ascontiguousarray(inp["od_w_out"][0]), "f_fb": np.ascontiguousarray(inp["od_fgate_b"][0][None]),
         "f_qg": np.ascontiguousarray(inp["od_q_norm_g"][0][None]), "f_kg": np.ascontiguousarray(inp["od_k_norm_g"][0][None]),
         "ident": cst["ident"], "iota": cst["iota"], "tri": cst["tri"], "masks": cst["masks"]}
    for L in range(2):
        p = _peer_inputs(inp, L, cst)
        for k in ("g", "wq", "keysT", "uT", "v"):
            c[f"p{L}_{k}"] = p[k]
    return c


def _build(kind):
    if kind in _NC_CACHE:
        return _NC_CACHE[kind]
    nc = bass.Bass("TRN2", target_bir_lowering=False)
    with ExitStack() as st:
        kb = KB(nc, st)
        D = lambda n, s: kb.dram(n, s, F32, "ExternalInput")
        T = TOK
        if kind == "m0":
            x, xp = D("x", [T, 1024]), D("xp", [T, 1024])
            y = kb.dram("y", [T, 1024], F32, "ExternalOutput")
            mixer0_block(kb, x, xp, y, D("g", [1, 1024]), D("w_in", [1024, 2576]), D("w_out", [1024, 1024]), D("convwT", [128, 4, 31]),
                         D("convb", [128, 4]), D("lng", [128, 4]), D("lnb", [128, 4]), D("w2", [16, 256]), D("gateb", [1, 256]),
                         D("glag", [128, 1]), D("ident", [128, 128]), D("tri", [128, 128]), T, T)
        elif kind == "peer":
            x = D("x", [T, 1024])
            y = kb.dram("y", [T, 1024], F32, "ExternalOutput")
            peer_block(kb, x, y, D("g", [1, 1024]), D("wq", [1024, 2048]), D("keysT", [128, 2048]), D("uT", [1024, 16384]),
                       D("v", [16384, 1024]), D("ident", [128, 128]), D("iota", [128, 128]), T, "p0")
        elif kind == "fox":
            x, xp = D("x", [T, 1024]), D("xp", [T, 1024])
            y = kb.dram("y", [T, 1024], F32, "ExternalOutput")
            fox_block(kb, x, xp, y, D("g", [1, 1024]), D("w_in", [1024, 3088]), D("w_out", [1024, 1024]), D("fb", [1, 16]),
                      D("qg", [1, 64]), D("kg", [1, 64]), D("pflag", [128, 1]), D("ident", [128, 128]), D("tri", [128, 128]),
                      D("masks", [128, 4, 512]), T, T)
        kb.s.emit()
    _NC_CACHE[kind] = nc
    return nc


def _shards(xfull):
    own, pre = [], []
    for c in range(NCORES):
        b, half = c // 2, c % 2
        own.append(np.ascontiguousarray(xfull[b, half * TOK:(half + 1) * TOK]))
        pre.append(np.ascontiguousarray(xfull[b, 0:TOK]) if half == 1 else np.zeros((TOK, 1024), np.float32))
    return own, pre


def _gather(res):
    out = np.empty((4, 8192, 1024), np.float32)
    for c in range(NCORES):
        b, half = c // 2, c % 2
        out[b, half * TOK:(half + 1) * TOK] = res.results[c]["y"]
    return out


def _run(kind, per_core):
    nc = _build(kind)
    return run_bass_kernel_spmd(nc, per_core, core_ids=list(range(NCORES)))


def _peer_inputs(inp, layer, cst):
    keys = inp["peer_keys"][layer]
    return {"g": np.ascontiguousarray(inp["ffn_norm"][layer][None]), "wq": np.ascontiguousarray(inp["peer_wq"][layer]),
            "keysT": np.ascontiguousarray(keys.transpose(3, 0, 1, 2).reshape(128, 2048)),
            "uT": np.ascontiguousarray(inp["peer_u"][layer].T), "v": np.ascontiguousarray(inp["peer_v"][layer]),
            "ident": cst["ident"], "iota": cst["iota"]}


def kernel(**inp):
    inp = {k: np.asarray(v, dtype=np.float32) for k, v in inp.items()}
    cst = _consts()
    own, pre = _shards(inp["x"])
    common = _fused_common(inp, cst)
    pfl = [np.full((128, 1), 0.0 if c % 2 == 1 else -30000.0, np.float32) for c in range(NCORES)]
    nc = _build_fused(TOK)
    res = run_bass_kernel_spmd(nc, [dict(common, x=own[c], xp=pre[c], pflag=pfl[c]) for c in range(NCORES)],
                               core_ids=list(range(NCORES)))
    return _gather(res)


def kernel_unfused(**inp):
    inp = {k: np.asarray(v, dtype=np.float32) for k, v in inp.items()}
    cst = _consts()
    x = inp["x"]
    own, pre = _shards(x)
    common = {"g": np.ascontiguousarray(inp["ev_norm_mix"][0][None]), "w_in": np.ascontiguousarray(inp["ev_w_in"][0]),
              "w_out": np.ascontiguousarray(inp["ev_w_out"][0]),
              "convwT": np.ascontiguousarray(inp["ev_conv_w"][0].T.reshape(4, 128, 31).transpose(1, 0, 2)),
              "convb": _col4(inp["ev_conv_b"][0]), "lng": _col4(inp["ev_conv_ln_g"][0]), "lnb": _col4(inp["ev_conv_ln_b"][0]),
              "w2": np.ascontiguousarray(inp["ev_gate_w2"][0]), "gateb": np.ascontiguousarray(inp["ev_gate_b"][0][None]),
              "glag": np.ascontiguousarray(inp["ev_gla_norm_g"][0][:, None]), "ident": cst["ident"], "tri": cst["tri"]}
    x1 = _gather(_run("m0", [dict(common, x=own[c], xp=pre[c]) for c in range(NCORES)]))
    own, _ = _shards(x1)
    common = _peer_inputs(inp, 0, cst)
    x2 = _gather(_run("peer", [dict(common, x=own[c]) for c in range(NCORES)]))
    own, pre = _shards(x2)
    common = {"g": np.ascontiguousarray(inp["od_norm_mix"][0][None]), "w_in": np.ascontiguousarray(inp["od_w_in"][0]),
              "w_out": np.ascontiguousarray(inp["od_w_out"][0]), "fb": np.ascontiguousarray(inp["od_fgate_b"][0][None]),
              "qg": np.ascontiguousarray(inp["od_q_norm_g"][0][None]), "kg": np.ascontiguousarray(inp["od_k_norm_g"][0][None]),
              "ident": cst["ident"], "tri": cst["tri"], "masks": cst["masks"]}
    pfl = [np.full((128, 1), 0.0 if c % 2 == 1 else -30000.0, np.float32) for c in range(NCORES)]
    x3 = _gather(_run("fox", [dict(common, x=own[c], xp=pre[c], pflag=pfl[c]) for c in range(NCORES)]))
    own, _ = _shards(x3)
    common = _peer_inputs(inp, 1, cst)
    x4 = _gather(_run("peer", [dict(common, x=own[c]) for c in range(NCORES)]))
    return x4
```

```python
from contextlib import ExitStack

import numpy as np
import concourse.bass as bass
import concourse.mybir as mybir
from concourse.bass_utils import run_bass_kernel_spmd

F32 = mybir.dt.float32
BF16 = mybir.dt.bfloat16
U32 = mybir.dt.uint32
I32 = mybir.dt.int32
ALU = mybir.AluOpType
AF = mybir.ActivationFunctionType
AX = mybir.AxisListType

NCORES = 8
EPOCH = 30000
ENGS = ("pe", "dve", "act", "pool", "sp")


class Buf:
    __slots__ = ("w", "r", "name")

    def __init__(self, name=""):
        self.w = None
        self.r = []
        self.name = name


class TT:
    def __init__(self, t, name):
        self.t = t
        self.b = Buf(name)

    def __getitem__(self, k):
        return self.t[k]


class Sched:
    def __init__(self, nc, stack):
        self.nc = nc
        self.stack = stack
        self.q = {e: [] for e in ENGS}
        self.csem = {e: None for e in ENGS}
        self.ccnt = {e: 0 for e in ENGS}
        self.dsem = {e: [] for e in ENGS}
        self.dcnt = {e: [] for e in ENGS}
        self.drr = {e: 0 for e in ENGS}
        self.seen = {e: {} for e in ENGS}
        self.nsem = 0
        self.ninst = 0

    def _newsem(self, nm):
        self.nsem += 1
        return self.stack.enter_context(self.nc.semaphore(f"{nm}{self.nsem}"))

    def _ticket(self, e, dma):
        if not dma:
            if self.csem[e] is None or self.ccnt[e] >= EPOCH:
                self.csem[e] = self._newsem("c" + e)
                self.ccnt[e] = 0
            self.ccnt[e] += 1
            return (self.csem[e], self.ccnt[e], e, 1)
        if not self.dsem[e]:
            self.dsem[e] = [self._newsem("d" + e) for _ in range(8)]
            self.dcnt[e] = [0] * 8
        i = self.drr[e] % 8
        self.drr[e] += 1
        if self.dcnt[e][i] + 16 >= EPOCH:
            self.dsem[e][i] = self._newsem("d" + e)
            self.dcnt[e][i] = 0
        self.dcnt[e][i] += 16
        return (self.dsem[e][i], self.dcnt[e][i], e + "_dma", 16)

    def op(self, e, fn, reads=(), writes=(), dma=False):
        deps = {}

        def add(t):
            if t is None:
                return
            sem, val, src, _ = t
            if src == "pe" and e == "pe" and not dma:
                return
            k = id(sem)
            if self.seen[e].get(k, 0) >= val:
                return
            if k not in deps or deps[k][1] < val:
                deps[k] = (sem, val)

        for b in reads:
            b = b.b if isinstance(b, TT) else b
            add(b.w)
        for b in writes:
            b = b.b if isinstance(b, TT) else b
            add(b.w)
            for t in b.r:
                add(t)
        waits = list(deps.values())
        for sem, val in waits:
            self.seen[e][id(sem)] = val
        t = self._ticket(e, dma)
        self.q[e].append((waits, fn, t[0], t[3]))
        self.ninst += 1 + len(waits)
        for b in reads:
            b = b.b if isinstance(b, TT) else b
            b.r = [x for x in b.r if x[0] is not t[0]] + [t]
        for b in writes:
            b = b.b if isinstance(b, TT) else b
            b.w = t
            b.r = []
        return t

    def barrier(self):
        waits = []
        for e in ENGS:
            if self.csem[e] is not None and self.ccnt[e] > 0:
                waits.append((self.csem[e], self.ccnt[e]))
            for sem, c in zip(self.dsem[e], self.dcnt[e]):
                if c > 0:
                    waits.append((sem, c))
        for e in ENGS:
            self.q[e].append((list(waits), None, None, 0))
            for sem, val in waits:
                self.seen[e][id(sem)] = max(self.seen[e].get(id(sem), 0), val)

    def final_wait(self, e, bufs):
        waits = []
        for b in bufs:
            b = b.b if isinstance(b, TT) else b
            for t in [b.w] + list(b.r):
                if t is not None:
                    waits.append((t[0], t[1]))
        self.q[e].append((waits, None, None, 0))

    def emit(self):
        nc = self.nc
        q = self.q

        def run(e, eng):
            for waits, fn, sem, inc in q[e]:
                for s, v in waits:
                    eng.wait_ge(s, v)
                if fn is not None:
                    ins = fn(eng)
                    ins.then_inc(sem, inc)

        with nc.Block() as block:

            @block.tensor
            def _(eng):
                run("pe", eng)

            @block.vector
            def _(eng):
                run("dve", eng)

            @block.scalar
            def _(eng):
                run("act", eng)

            @block.gpsimd
            def _(eng):
                run("pool", eng)

            @block.sync
            def _(eng):
                run("sp", eng)


class KB:
    def __init__(self, nc, stack, sched=None):
        self.nc = nc
        self.stack = stack
        self.s = sched if sched is not None else Sched(nc, stack)

    def scope(self, stack):
        return KB(self.nc, stack, self.s)

    def sb(self, name, shape, dt):
        t = self.stack.enter_context(self.nc.sbuf_tensor(name, list(shape), dt))
        return TT(t, name)

    def dram(self, name, shape, dt, kind):
        t = self.nc.dram_tensor(name, list(shape), dt, kind=kind)
        return TT(t.ap(), name)

    def dma(self, out, in_, reads, writes, e="sp", **kw):
        return self.s.op(e, lambda g: g.dma_start(out=out, in_=in_, **kw), reads, writes, dma=True)

    def mm(self, out, lhsT, rhs, start, stop, reads, writes):
        return self.s.op("pe", lambda g: g.matmul(out, lhsT, rhs, start=start, stop=stop), reads, writes)

    def tr(self, out, in_, ident, reads, writes):
        return self.s.op("pe", lambda g: g.transpose(out, in_, ident), reads, writes)

    def act(self, out, in_, func, reads, writes, bias=None, scale=None, accum_out=None):
        kw = {}
        if bias is not None:
            kw["bias"] = bias
        if scale is not None:
            kw["scale"] = scale
        if accum_out is not None:
            kw["accum_out"] = accum_out
        return self.s.op("act", lambda g: g.activation(out, in_, func, **kw), reads, writes)

    def tt(self, out, in0, in1, op, reads, writes, e="dve"):
        return self.s.op(e, lambda g: g.tensor_tensor(out, in0, in1, op), reads, writes)

    def ts(self, out, in0, s1, s2, op0, op1, reads, writes, e="dve", accum_out=None):
        if op1 is None:
            return self.s.op(e, lambda g: g.tensor_scalar(out, in0, s1, None, op0), reads, writes)
        if accum_out is not None:
            return self.s.op(e, lambda g: g.tensor_scalar(out, in0, s1, s2, op0, op1, accum_out), reads, writes)
        return self.s.op(e, lambda g: g.tensor_scalar(out, in0, s1, s2, op0, op1), reads, writes)

    def stt(self, out, in0, scalar, in1, op0, op1, reads, writes, e="dve"):
        return self.s.op(e, lambda g: g.scalar_tensor_tensor(out, in0, scalar, in1, op0, op1), reads, writes)

    def copy(self, out, in_, reads, writes, e="dve"):
        if e == "act":
            return self.s.op(e, lambda g: g.copy(out, in_), reads, writes)
        return self.s.op(e, lambda g: g.tensor_copy(out, in_), reads, writes)

    def memset(self, ap, val, writes, e="dve"):
        return self.s.op(e, lambda g: g.memset(ap, val), (), writes)

    def reduce(self, out, in_, op, reads, writes, axis=AX.X, e="dve"):
        return self.s.op(e, lambda g: g.tensor_reduce(out, in_, axis, op), reads, writes)


def to_bf16_dram(kb, src, dst, R, C, tag):
    with ExitStack() as st:
        k = kb.scope(st)
        W = 2048
        stg = [k.sb(f"cv_in{tag}{i}", [128, W], F32) for i in range(2)]
        outb = [k.sb(f"cv_out{tag}{i}", [128, W], BF16) for i in range(2)]
        n = 0
        for r in range(R // 128):
            for c0 in range(0, C, W):
                w = min(W, C - c0)
                i = n % 2
                k.dma(stg[i][:, 0:w], src[r * 128:(r + 1) * 128, c0:c0 + w], [src], [stg[i]])
                k.copy(outb[i][:, 0:w], stg[i][:, 0:w], [stg[i]], [outb[i]], e=("dve" if n % 2 == 0 else "act"))
                k.dma(dst[r * 128:(r + 1) * 128, c0:c0 + w], outb[i][:, 0:w], [outb[i]], [], e="pool")
                n += 1
        k.s.barrier()


def load_bf16_resident(kb, dst_tt, dst_ap_fn, src, nrows_blocks, C, tag):
    with ExitStack() as st:
        k = kb.scope(st)
        W = 2048
        stg = [k.sb(f"ld_in{tag}{i}", [128, W], F32) for i in range(2)]
        n = 0
        for kc in range(nrows_blocks):
            for c0 in range(0, C, W):
                w = min(W, C - c0)
                i = n % 2
                k.dma(stg[i][:, 0:w], src[kc * 128:(kc + 1) * 128, c0:c0 + w], [src], [stg[i]])
                k.copy(dst_ap_fn(kc, c0, w), stg[i][:, 0:w], [stg[i]], [dst_tt], e=("dve" if n % 2 == 0 else "act"))
                n += 1
        k.s.barrier()


def peer_tables(kb0, uT, vtab, tag):
    us = kb0.dram(f"us{tag}", [1024, 16384], BF16, "Internal")
    vs = kb0.dram(f"vs{tag}", [16384, 1024], BF16, "Internal")
    to_bf16_dram(kb0, uT, us, 1024, 16384, tag + "u")
    to_bf16_dram(kb0, vtab, vs, 16384, 1024, tag + "v")
    return us, vs


def peer_block(kb0, x_in, x_out, gvec, wq, keysT, uT, vtab, ident_d, iota_d, T, tag, G=256, CH=2, OHT=16, tables=None, jobs=None, NSL=4):
    nc = kb0.nc
    if jobs is None:
        jobs = [(x_in, x_out, T)]
    with ExitStack() as st:
        kb = kb0.scope(st)
        TPG = G // 128
        if tables is None:
            tables = peer_tables(kb, uT, vtab, tag)
        us, vs = tables
        wqb = kb.sb("wqb" + tag, [128, 8, 2048], BF16)
        load_bf16_resident(kb, wqb, lambda kc, c0, w: wqb[:, kc, c0:c0 + w], wq, 8, 2048, tag + "wq")
        kTb = kb.sb("kTb" + tag, [128, 16 * 128], BF16)
        load_bf16_resident(kb, kTb, lambda kc, c0, w: kTb[:, c0:c0 + w], keysT, 1, 2048, tag + "kt")
        gB = kb.sb("gB" + tag, [128, 1024], F32)
        kb.dma(gB[:], gvec.t.partition_broadcast(128)[:, 0, :], [gvec], [gB])
        ident = kb.sb("ident" + tag, [128, 128], F32)
        kb.dma(ident[:], ident_d[:], [ident_d], [ident])
        iota = kb.sb("iota" + tag, [128, 128], F32)
        kb.dma(iota[:], iota_d[:], [iota_d], [iota])

        xs = [kb.sb(f"xs{tag}{j}", [128, 1024], F32) for j in range(TPG)]
        hf = kb.sb("hf" + tag, [128, 1024], F32)
        sq = kb.sb("sq" + tag, [128, 1024], BF16)
        ss = kb.sb("ss" + tag, [128, 4], F32)
        hT = kb.sb("hT" + tag, [128, 8, G], BF16)
        qT = kb.sb("qT" + tag, [128, 16, 128], BF16)
        sc = kb.sb("sc" + tag, [128, 16, 128], F32)
        tmp4 = kb.sb("tmp4" + tag, [128, 4, 256], F32)
        tmpb = [Buf() for _ in range(4)]
        scb = [Buf() for _ in range(4)]
        iotab = kb.sb("iotab" + tag, [128, 128], BF16)
        kb.copy(iotab[:], iota[:], [iota], [iotab])
        v16 = kb.sb("v16" + tag, [128, 16, 16], F32)
        i16 = kb.sb("i16" + tag, [128, 16, 16], U32)
        i16f = kb.sb("i16f" + tag, [128, 16, 16], F32)
        cand = kb.sb("cand" + tag, [128, 8, 16, 16], F32)
        s16 = kb.sb("s16" + tag, [128, 8, 16], F32)
        j16 = kb.sb("j16" + tag, [128, 8, 16], U32)
        jaf = kb.sb("jaf" + tag, [128, 8, 16], F32)
        ja = kb.sb("ja" + tag, [128, 8, 16], U32)
        jb = kb.sb("jb" + tag, [128, 8, 16], U32)
        jbf = kb.sb("jbf" + tag, [128, 8, 16], F32)
        eq = cand
        trio = kb.sb("trio" + tag, [128, 3, 128], F32)
        zz = kb.sb("zz" + tag, [128, 8], F32)
        trioT = kb.sb("trioT" + tag, [128, 3, 128], F32)
        oh1 = kb.sb("oh1" + tag, [128, OHT, 128], BF16)
        oh2 = kb.sb("oh2" + tag, [128, OHT, 128], BF16)
        WT = kb.sb("WT" + tag, [128, 128, G], BF16)
        WTb = [Buf() for _ in range(G // 16)]
        ohb = [Buf() for _ in range(OHT)]
        ohb2 = [Buf() for _ in range(OHT)]
        ub = [kb.sb(f"ub{tag}{i}", [128, 8, CH * 128], BF16) for i in range(NSL)]
        vb = [kb.sb(f"vb{tag}{i}", [128, CH, 1024], BF16) for i in range(NSL)]
        actT = [kb.sb(f"actT{tag}{i}", [128, G], BF16) for i in range(2)]
        ct = [kb.sb(f"ct{tag}{i}", [128, G], BF16) for i in range(2)]
        xo = [kb.sb(f"xo{tag}{i}", [128, 1024], F32) for i in range(1)]
        ps = st.enter_context(nc.psum_tensor("ps" + tag, [128, 4096], F32))
        pb = [Buf(f"pb{i}") for i in range(8)]

        def bank(i, w=512):
            return ps[:, i * 512:i * 512 + w]

        us_v = us.t.rearrange("(kc p) e -> p kc e", p=128)
        vs_v = vs.t.rearrange("(e1 p) d -> p e1 d", p=128)
        wcount = 0
        xocount = 0
        for x_in, x_out, g in [(a_, b_, g_) for (a_, b_, t_) in jobs for g_ in range(t_ // G)]:
            for j in range(TPG):
                tok0 = g * G + j * 128
                kb.dma(xs[j][:], x_in[tok0:tok0 + 128, :], [x_in], [xs[j]])
                kb.act(sq[:], xs[j][:], AF.Square, [xs[j]], [sq])
                kb.reduce(ss[:, 0:1], sq[:], ALU.add, [sq], [ss])
                kb.ts(ss[:, 1:2], ss[:, 0:1], 1.0 / 1024, 1e-6, ALU.mult, ALU.add, [ss], [ss])
                kb.act(ss[:, 3:4], ss[:, 1:2], AF.Sqrt, [ss], [ss])
                kb.s.op("dve", lambda g_: g_.reciprocal(ss[:, 2:3], ss[:, 3:4]), [ss], [ss])
                kb.stt(hf[:], xs[j][:], ss[:, 2:3], gB[:], ALU.mult, ALU.mult, [xs[j], ss, gB], [hf])
                for kc in range(8):
                    kb.tr(ps[:, kc * 128:(kc + 1) * 128], hf[:, kc * 128:(kc + 1) * 128], ident[:], [hf, ident], [pb[kc // 4]])
                for hb in range(2):
                    kb.copy(hT[:, hb * 4:(hb + 1) * 4, j * 128:(j + 1) * 128],
                            bank(hb).rearrange("p (k t) -> p k t", t=128), [pb[hb]], [hT], e="act")
                for c in range(16):
                    for kc in range(8):
                        kb.mm(ps[:, 2048 + c * 128:2048 + (c + 1) * 128], wqb[:, kc, c * 128:(c + 1) * 128],
                              hT[:, kc, j * 128:(j + 1) * 128], kc == 0, kc == 7, [wqb, hT], [pb[4 + c // 4]])
                for b4 in range(4):
                    kb.copy(qT[:, b4 * 4:(b4 + 1) * 4, :], bank(4 + b4).rearrange("p (k t) -> p k t", t=128),
                            [pb[4 + b4]], [qT], e="act")
                for c in range(16):
                    kb.mm(ps[:, c * 128:(c + 1) * 128], qT[:, c, :], kTb[:, c * 128:(c + 1) * 128], True, True,
                          [qT, kTb], [pb[c // 4]])
                for b4 in range(4):
                    kb.copy(sc[:, b4 * 4:(b4 + 1) * 4, :], bank(b4).rearrange("p (k t) -> p k t", t=128),
                            [pb[b4]], [scb[b4]], e="act")
                def topk_chain(vals, vdst, idst, tm, vb_, ib_, tb_, srcb):
                    yield kb.s.op("dve", lambda g_: g_.max(out=vdst[:, 0:8], in_=vals), [srcb], [vb_])
                    yield kb.s.op("dve", lambda g_: g_.match_replace(out=tm, in_to_replace=vdst[:, 0:8], in_values=vals,
                                                                      imm_value=-1e30), [srcb, vb_], [tb_])
                    yield kb.s.op("dve", lambda g_: g_.max(out=vdst[:, 8:16], in_=tm), [tb_], [vb_])
                    yield kb.s.op("dve", lambda g_: g_.max_index(out=idst[:, 0:8], in_max=vdst[:, 0:8], in_values=vals), [srcb, vb_], [ib_])
                    yield kb.s.op("dve", lambda g_: g_.max_index(out=idst[:, 8:16], in_max=vdst[:, 8:16], in_values=vals), [srcb, vb_], [ib_])

                def run_interleaved(chains, width=4):
                    live = []
                    chains = list(chains)
                    while chains or live:
                        while chains and len(live) < width:
                            live.append(chains.pop(0))
                        nxt = []
                        for ch_ in live:
                            try:
                                next(ch_)
                                nxt.append(ch_)
                            except StopIteration:
                                pass
                        live = nxt

                v16b = [Buf() for _ in range(16)]
                i16b = [Buf() for _ in range(16)]
                run_interleaved([topk_chain(sc[:, c, :], v16[:, c, :], i16[:, c, :], tmp4[:, c % 4, 0:128], v16b[c], i16b[c], tmpb[c % 4], scb[c // 4])
                                 for c in range(16)])
                kb.copy(i16f[:], i16[:], i16b, [i16f], e="pool")
                v16v = v16[:].rearrange("p (h two) k -> p h two k", two=2)
                kb.tt(cand[:], v16v[:, :, 0, :].unsqueeze(3).to_broadcast([128, 8, 16, 16]),
                      v16v[:, :, 1, :].unsqueeze(2).to_broadcast([128, 8, 16, 16]), ALU.add, v16b, [cand])
                s16b = [Buf() for _ in range(8)]
                j16b = [Buf() for _ in range(8)]
                run_interleaved([topk_chain(cand[:, h, :, :].rearrange("p a b -> p (a b)"), s16[:, h, :], j16[:, h, :], tmp4[:, h % 4, :],
                                            s16b[h], j16b[h], tmpb[h % 4], cand.b) for h in range(8)])
                gt = trio[:, 2, :].rearrange("p (h k) -> p h k", k=16)
                kb.tt(gt, s16[:], s16[:, :, 0:1].to_broadcast([128, 8, 16]), ALU.subtract, s16b, [trio], e="pool")
                kb.act(gt, gt, AF.Exp, [trio], [trio])
                kb.reduce(zz[:], gt, ALU.add, [trio], [zz])
                kb.s.op("dve", lambda g_: g_.reciprocal(zz[:], zz[:]), [zz], [zz])
                kb.tt(gt, gt, zz[:].unsqueeze(2).to_broadcast([128, 8, 16]), ALU.mult, [trio, zz], [trio])
                kb.s.op("dve", lambda g_: g_.tensor_single_scalar(ja[:], j16[:], 4, ALU.logical_shift_right), j16b, [ja])
                kb.s.op("dve", lambda g_: g_.tensor_single_scalar(jb[:], j16[:], 15, ALU.bitwise_and), j16b, [jb])
                kb.copy(jaf[:], ja[:], [ja], [jaf])
                kb.copy(jbf[:], jb[:], [jb], [jbf])
                i16v = i16f[:].rearrange("p (h two) k -> p h two k", two=2)
                iota16 = iota[:, 0:16].unsqueeze(1).unsqueeze(1).to_broadcast([128, 8, 16, 16])
                for which, jf in ((0, jaf), (1, jbf)):
                    kb.tt(eq[:], iota16, jf[:].unsqueeze(3).to_broadcast([128, 8, 16, 16]), ALU.is_equal, [iota, jf], [eq])
                    kb.tt(eq[:], eq[:], i16v[:, :, which, :].unsqueeze(2).to_broadcast([128, 8, 16, 16]), ALU.mult,
                          [eq, i16f], [eq])
                    kb.reduce(trio[:, which, :], eq[:].rearrange("p h k a -> p (h k) a"), ALU.add, [eq], [trio])
                for w3 in range(3):
                    kb.tr(ps[:, 3584 + w3 * 128:3584 + (w3 + 1) * 128], trio[:, w3, :], ident[:], [trio, ident], [pb[7]])
                kb.copy(trioT[:].rearrange("p a t -> p (a t)"), ps[:, 3584:3584 + 384], [pb[7]], [trioT], e="act")
                for t0 in range(0, 128, OHT):
                    for t16 in range(0, OHT, 16):
                        half = wcount % 2
                        wcount += 1
                        for tt_ in range(16):
                            tl = t16 + tt_
                            tg = t0 + tl
                            kb.ts(oh1[:, tl, :], iotab[:], trioT[:, 0, tg:tg + 1], trioT[:, 2, tg:tg + 1], ALU.is_equal, ALU.mult,
                                  [iotab, trioT], [ohb[tl]])
                            kb.ts(oh2[:, tl, :], iotab[:], trioT[:, 1, tg:tg + 1], None, ALU.is_equal, None, [iotab, trioT], [ohb2[tl]])
                            col = half * 2048 + tt_ * 128
                            kb.mm(ps[:, col:col + 128], oh2[:, tl, :], oh1[:, tl, :], True, True, [ohb[tl], ohb2[tl]],
                                  [pb[half * 4 + tt_ // 4]])
                        tcol = j * 128 + t0 + t16
                        wb_ = WTb[(j * 128 + t0 + t16) // 16]
                        kb.copy(WT[:, :, tcol:tcol + 16],
                                ps[:, half * 2048:(half + 1) * 2048].rearrange("p (t e) -> p e t", e=128),
                                [pb[half * 4 + q_] for q_ in range(4)], [wb_], e="act")
            ncg = 128 // CH

            def u_stage(e1):
                cg, el = e1 // CH, e1 % CH
                sl = cg % NSL
                if el == 0:
                    kb.dma(ub[sl][:], us_v[:, :, cg * CH * 128:(cg + 1) * CH * 128], [us], [ub[sl]])
                    kb.dma(vb[sl][:], vs_v[:, cg * CH:(cg + 1) * CH, :], [vs], [vb[sl]], e="pool")
                a = e1 % 2
                pu = ps[:, 2048 + a * 512:2048 + a * 512 + G]
                for kc in range(8):
                    kb.mm(pu, ub[sl][:, kc, el * 128:(el + 1) * 128], hT[:, kc, :], kc == 0, kc == 7,
                          [ub[sl], hT], [pb[4 + a]])
                kb.act(actT[a][:], pu, AF.Gelu, [pb[4 + a]], [actT[a]])
                kb.tt(ct[a][:], actT[a][:], WT[:, e1, :], ALU.mult, [actT[a]] + WTb, [ct[a]])

            def v_stage(e1):
                cg, el = e1 // CH, e1 % CH
                sl = cg % NSL
                a = e1 % 2
                for j in range(TPG):
                    for hh in range(2):
                        kb.mm(bank(2 * j + hh), ct[a][:, j * 128:(j + 1) * 128], vb[sl][:, el, hh * 512:(hh + 1) * 512],
                              e1 == 0, e1 == 127, [ct[a], vb[sl]], [pb[2 * j + hh]])

            u_stage(0)
            for e1 in range(128):
                if e1 + 1 < 128:
                    u_stage(e1 + 1)
                v_stage(e1)
            for j in range(TPG):
                tok0 = g * G + j * 128
                o = xo[0]
                xocount += 1
                kb.tt(o[:], ps[:, j * 1024:(j + 1) * 1024], xs[j][:], ALU.add, [pb[2 * j], pb[2 * j + 1], xs[j]], [o])
                kb.dma(x_out[tok0:tok0 + 128, :], o[:], [o], [], e="pool")
        kb.s.barrier()


def rms_to_hT(kb, xs_t, gB, ident, hf, sq, ss, ps, pbA, pbB, hT, col0, eps=1e-6):
    kb.act(sq[:], xs_t[:], AF.Square, [xs_t], [sq])
    kb.reduce(ss[:, 0:1], sq[:], ALU.add, [sq], [ss])
    kb.ts(ss[:, 1:2], ss[:, 0:1], 1.0 / 1024, eps, ALU.mult, ALU.add, [ss], [ss])
    kb.act(ss[:, 3:4], ss[:, 1:2], AF.Sqrt, [ss], [ss])
    kb.s.op("dve", lambda g_: g_.reciprocal(ss[:, 2:3], ss[:, 3:4]), [ss], [ss])
    kb.stt(hf[:], xs_t[:], ss[:, 2:3], gB[:], ALU.mult, ALU.mult, [xs_t, ss, gB], [hf])
    for kc in range(8):
        kb.tr(ps[:, kc * 128:(kc + 1) * 128], hf[:, kc * 128:(kc + 1) * 128], ident[:], [hf, ident], [pbA if kc < 4 else pbB])
    kb.copy(hT[:, 0:4, col0:col0 + 128], ps[:, 0:512].rearrange("p (k t) -> p k t", t=128), [pbA], [hT], e="act")
    kb.copy(hT[:, 4:8, col0:col0 + 128], ps[:, 512:1024].rearrange("p (k t) -> p k t", t=128), [pbB], [hT], e="dve")


def mixer0_block(kb0, x_own, x_pre, x_out, gvec, w_in, w_out, convwT, convb, lng, lnb, w2, gateb, glag,
                 ident_d, tri_d, T, TP, tag="m0", x_out_pre=None):
    nc = kb0.nc
    NTO = T // 128
    NTP = TP // 128
    with ExitStack() as st:
        kb = kb0.scope(st)
        winb = kb.sb("winb", [128, 8, 2576], BF16)
        load_bf16_resident(kb, winb, lambda kc, c0, w: winb[:, kc, c0:c0 + w], w_in, 8, 2576, "win")
        woutb = kb.sb("woutb", [128, 8, 1024], BF16)
        load_bf16_resident(kb, woutb, lambda kc, c0, w: woutb[:, kc, c0:c0 + w], w_out, 8, 1024, "wout")
        gB = kb.sb("gBm", [128, 1024], F32)
        kb.dma(gB[:], gvec.t.partition_broadcast(128)[:, 0, :], [gvec], [gB])
        gbB = kb.sb("gbB", [128, 256], F32)
        kb.dma(gbB[:], gateb.t.partition_broadcast(128)[:, 0, :], [gateb], [gbB])
        ident = kb.sb("identm", [128, 128], F32)
        kb.dma(ident[:], ident_d[:], [ident_d], [ident])
        tri = kb.sb("trim", [128, 128], F32)
        kb.dma(tri[:], tri_d[:], [tri_d], [tri])
        ones = kb.sb("onesm", [128, 128], F32)
        kb.memset(ones[:], 1.0, [ones])
        cw = kb.sb("cw", [128, 4, 31], F32)
        kb.dma(cw[:], convwT[:], [convwT], [cw])
        cols = kb.sb("colsm", [128, 16], F32)
        kb.dma(cols[:, 0:4], convb[:], [convb], [cols])
        kb.dma(cols[:, 4:8], lng[:], [lng], [cols])
        kb.dma(cols[:, 8:12], lnb[:], [lnb], [cols])
        kb.dma(cols[:, 12:13], glag[:], [glag], [cols])
        w2f = kb.sb("w2f", [16, 256], F32)
        kb.dma(w2f[:], w2[:], [w2], [w2f])
        w2b = kb.sb("w2b", [16, 256], BF16)
        kb.copy(w2b[:], w2f[:], [w2f], [w2b])
        diag = kb.sb("diag", [128, 4 * 31, 128], BF16)
        for c in range(4):
            for j in range(31):
                kb.ts(diag[:, c * 31 + j, :], ident[:], cw[:, c, j:j + 1], None, ALU.mult, None, [ident, cw], [diag],
                      e=("dve" if (c * 31 + j) % 2 == 0 else "pool"))
        full = x_out_pre is not None
        UW = 158 if full else 30 + T
        uT = kb.sb("uT", [128, 4, UW], BF16)
        kb.memset(uT[:, :, 0:30], 0.0, [uT])
        Sf = [kb.sb(f"Sf{h}", [64, 128], F32) for h in range(4)]
        Sb = [kb.sb(f"Sb{h}", [64, 128], BF16) for h in range(4)]
        for h in range(4):
            kb.memset(Sf[h][:], 0.0, [Sf[h]])
            kb.memset(Sb[h][:], 0.0, [Sb[h]])
        xs = kb.sb("xsm", [128, 1024], F32)
        hf = kb.sb("hfm", [128, 1024], F32)
        sq = kb.sb("sqm", [128, 1024], BF16)
        ss = kb.sb("ssm", [128, 4], F32)
        hT = kb.sb("hTm", [128, 8, 128], BF16)
        sg = kb.sb("sg", [128, 512], F32)
        glrT = kb.sb("glrT", [16, 128], BF16)
        zb = kb.sb("zb", [128, 256], F32)
        la = kb.sb("la", [128, 256], F32)
        enb_tm = kb.sb("enb_tm", [128, 256], F32)
        ebl_tm = kb.sb("ebl_tm", [128, 256], F32)
        kdec = kb.sb("kdec", [128, 256], BF16)
        vbf = kb.sb("vbf", [128, 512], BF16)
        eblc = kb.sb("eblc", [64, 8], F32)
        eb = kb.sb("eb", [64, 512], F32)
        enb = kb.sb("enb", [64, 512], F32)
        qt = kb.sb("qt", [64, 4, 128], BF16)
        kt = kb.sb("kt", [64, 4, 128], BF16)
        Am = kb.sb("Am", [128, 4, 128], BF16)
        osq = kb.sb("osq", [128, 512], F32)
        rs = kb.sb("rs", [128, 512], F32)
        sr = kb.sb("sr", [128, 512], F32)
        t1 = kb.sb("t1", [128, 512], F32)
        ybT = kb.sb("ybT", [128, 4, 128], BF16)
        ycs = kb.sb("ycs", [128, 4, 128], F32)
        ysq = kb.sb("ysq", [128, 4, 128], F32)
        mean = kb.sb("mean", [128, 128], F32)
        var = kb.sb("var", [128, 128], F32)
        yaT = kb.sb("yaT", [128, 4, 128], BF16)
        xo = kb.sb("xom", [128, 1024], F32)
        ps = st.enter_context(nc.psum_tensor("psm", [128, 4096], F32))
        pb = [Buf(f"pbm{i}") for i in range(8)]

        def B(i, lo=0, hi=512):
            return ps[:, i * 512 + lo:i * 512 + hi]

        def fm_proj(colbase, ncols, bank, slot):
            for kc in range(8):
                kb.mm(ps[0:ncols, bank * 512 + slot * 128:bank * 512 + (slot + 1) * 128], winb[:, kc, colbase:colbase + ncols],
                      hT[:, kc, :], kc == 0, kc == 7, [winb, hT], [pb[bank]])

        def state_part(u_needed, own):
            for kc in range(8):
                kb.mm(B(5), hT[:, kc, :], winb[:, kc, 1536:2048], kc == 0, kc == 7, [hT, winb], [pb[5]])
            for kc in range(8):
                kb.mm(B(6, 0, 256), hT[:, kc, :], winb[:, kc, 1280:1536], kc == 0, kc == 7, [hT, winb], [pb[6]])
            fm_proj(2560, 16, 4, 0)
            kb.copy(glrT[:], ps[0:16, 4 * 512:4 * 512 + 128], [pb[4]], [glrT], e="act")
            kb.mm(B(6, 256, 512), glrT[:], w2b[:], True, True, [glrT, w2b], [pb[6]])
            kb.tt(zb[:], B(6, 256, 512), gbB[:], ALU.add, [pb[6], gbB], [zb])
            kb.act(zb[:], zb[:], AF.Exp, [zb], [zb], scale=-1.0)
            kb.act(zb[:], zb[:], AF.Ln, [zb], [zb], bias=1.0)
            kb.ts(la[:], zb[:], -1.0 / 16.0, None, ALU.mult, None, [zb], [la])
            kb.mm(B(7, 0, 256), tri[:], la[:], True, True, [tri, la], [pb[7]])
            kb.mm(B(7, 256, 512), ones[:], la[:], True, True, [ones, la], [pb[7]])
            for h in range(4):
                kb.mm(ps[0:64, 4 * 512 + 384 + 2 * h:4 * 512 + 386 + 2 * h], la[:, h * 64:(h + 1) * 64], ones[:, 0:2], True, True,
                      [la, ones], [pb[4]])
            kb.act(eblc[:], ps[0:64, 4 * 512 + 384:4 * 512 + 392], AF.Exp, [pb[4]], [eblc])
            kb.act(enb_tm[:], B(7, 0, 256), AF.Exp, [pb[7]], [enb_tm], scale=-1.0)
            kb.act(ebl_tm[:], B(7, 256, 512), AF.Exp, [pb[7]], [ebl_tm])
            kb.tt(enb_tm[:], enb_tm[:], ebl_tm[:], ALU.mult, [enb_tm, ebl_tm], [enb_tm])
            kb.tt(kdec[:], B(6, 0, 256), enb_tm[:], ALU.mult, [pb[6], enb_tm], [kdec])
            kb.copy(vbf[:], B(5), [pb[5]], [vbf], e="act")

        def state_update():
            for h in range(4):
                kb.mm(ps[0:64, 2 * 512 + h * 128:2 * 512 + (h + 1) * 128], kdec[:, h * 64:(h + 1) * 64], vbf[:, h * 128:(h + 1) * 128],
                      True, True, [kdec, vbf], [pb[2]])
            for h in range(4):
                kb.stt(Sf[h][:], Sf[h][:], eblc[:, 2 * h:2 * h + 1], ps[0:64, 2 * 512 + h * 128:2 * 512 + (h + 1) * 128],
                       ALU.mult, ALU.add, [Sf[h], eblc, pb[2]], [Sf[h]])
                kb.copy(Sb[h][:], Sf[h][:], [Sf[h]], [Sb[h]], e="act")

        def conv_u(tokcol):
            for c in range(4):
                fm_proj(c * 128, 128, 0, c)
                fm_proj(512 + c * 128, 128, 1, c)
            kb.act(sg[:], B(1), AF.Sigmoid, [pb[1]], [sg])
            kb.tt(uT[:, :, 30 + tokcol:30 + tokcol + 128], B(0).rearrange("p (c t) -> p c t", t=128),
                  sg[:].rearrange("p (c t) -> p c t", t=128), ALU.mult, [pb[0], sg], [uT])

        for i in range(0 if full else NTP):
            kb.dma(xs[:], x_pre[i * 128:(i + 1) * 128, :], [x_pre], [xs])
            rms_to_hT(kb, xs, gB, ident, hf, sq, ss, ps, pb[0], pb[1], hT, 0)
            state_part(False, False)
            state_update()
            if i == NTP - 1:
                for c in range(4):
                    fm_proj(c * 128, 128, 0, c)
                    fm_proj(512 + c * 128, 128, 1, c)
                kb.act(sg[:], B(1), AF.Sigmoid, [pb[1]], [sg])
                kb.tt(uT[:, :, 0:30], B(0).rearrange("p (c t) -> p c t", t=128)[:, :, 98:128],
                      sg[:].rearrange("p (c t) -> p c t", t=128)[:, :, 98:128], ALU.mult, [pb[0], sg], [uT])
        for ii in range((NTP + NTO) if full else NTO):
            if full:
                isown = ii >= NTP
                i = 0
                srcx = x_own[(ii - NTP) * 128:(ii - NTP + 1) * 128, :] if isown else x_pre[ii * 128:(ii + 1) * 128, :]
                dsty = x_out[(ii - NTP) * 128:(ii - NTP + 1) * 128, :] if isown else x_out_pre[ii * 128:(ii + 1) * 128, :]
            else:
                i = ii
                srcx = x_own[i * 128:(i + 1) * 128, :]
                dsty = x_out[i * 128:(i + 1) * 128, :]
            kb.dma(xs[:], srcx, [x_own, x_pre], [xs])
            rms_to_hT(kb, xs, gB, ident, hf, sq, ss, ps, pb[0], pb[1], hT, 0)
            state_part(True, True)
            conv_u(i * 128)
            for h in range(4):
                fm_proj(1024 + h * 64, 64, 2, h)
                fm_proj(1280 + h * 64, 64, 3, h)
            for sl in range(4):
                fm_proj(2048 + sl * 128, 128, 4, sl)
            for h in range(4):
                kb.mm(ps[0:64, 5 * 512 + h * 128:5 * 512 + (h + 1) * 128], la[:, h * 64:(h + 1) * 64], tri[:], True, True,
                      [la, tri], [pb[5]])
            kb.act(eb[:], ps[0:64, 5 * 512:6 * 512], AF.Exp, [pb[5]], [eb])
            kb.act(enb[:], ps[0:64, 5 * 512:6 * 512], AF.Exp, [pb[5]], [enb], scale=-1.0)
            kb.stt(qt[:].rearrange("p h t -> p (h t)"), ps[0:64, 2 * 512:3 * 512], 0.125, eb[:], ALU.mult, ALU.mult, [pb[2], eb], [qt])
            kb.tt(kt[:].rearrange("p h t -> p (h t)"), ps[0:64, 3 * 512:4 * 512], enb[:], ALU.mult, [pb[3], enb], [kt])
            kb.act(sr[:], B(4), AF.Silu, [pb[4]], [sr])
            for h in range(4):
                kb.mm(B(0, h * 128, (h + 1) * 128), kt[:, h, :], qt[:, h, :], True, True, [kt, qt], [pb[0]])
            kb.tt(Am[:], B(0).rearrange("p (h t) -> p h t", t=128), tri[:].unsqueeze(1).to_broadcast([128, 4, 128]), ALU.mult,
                  [pb[0], tri], [Am])
            for h in range(4):
                kb.mm(B(1, h * 128, (h + 1) * 128), vbf[:, h * 128:(h + 1) * 128], Am[:, h, :], True, False, [vbf, Am], [pb[1]])
                kb.mm(B(1, h * 128, (h + 1) * 128), Sb[h][:], qt[:, h, :], False, True, [Sb[h], qt], [pb[1]])
            state_update()
            kb.act(osq[:], B(1), AF.Square, [pb[1]], [osq])
            kb.mm(B(3), ones[:], osq[:], True, True, [ones, osq], [pb[3]])
            kb.ts(rs[:], B(3), 1.0 / 128, 1e-6, ALU.mult, ALU.add, [pb[3]], [rs])
            kb.act(rs[:], rs[:], AF.Sqrt, [rs], [rs])
            kb.s.op("dve", lambda g_: g_.reciprocal(rs[:], rs[:]), [rs], [rs])
            kb.tt(t1[:], B(1), rs[:], ALU.mult, [pb[1], rs], [t1])
            kb.stt(ybT[:].rearrange("p h t -> p (h t)"), t1[:], cols[:, 12:13], sr[:], ALU.mult, ALU.mult, [t1, cols, sr], [ybT])
            for c in range(4):
                for j in range(31):
                    kb.mm(B(4, c * 128, (c + 1) * 128), diag[:, c * 31 + j, :], uT[:, c, i * 128 + j:i * 128 + j + 128],
                          j == 0, j == 30, [diag, uT], [pb[4]])
            for c in range(4):
                kb.ts(ycs[:, c, :], B(4, c * 128, (c + 1) * 128), cols[:, c:c + 1], None, ALU.add, None, [pb[4], cols], [ycs])
            kb.act(ysq[:], ycs[:], AF.Square, [ycs], [ysq])
            for c in range(4):
                kb.mm(B(5, 0, 128), ones[:], ycs[:, c, :], c == 0, c == 3, [ones, ycs], [pb[5]])
            for c in range(4):
                kb.mm(B(5, 128, 256), ones[:], ysq[:, c, :], c == 0, c == 3, [ones, ysq], [pb[5]])
            kb.ts(mean[:], B(5, 0, 128), 1.0 / 512, None, ALU.mult, None, [pb[5]], [mean])
            kb.tt(var[:], mean[:], mean[:], ALU.mult, [mean], [var])
            kb.stt(var[:], B(5, 128, 256), 1.0 / 512, var[:], ALU.mult, ALU.subtract, [pb[5], var], [var])
            kb.ts(var[:], var[:], 1e-6, None, ALU.add, None, [var], [var])
            kb.act(var[:], var[:], AF.Sqrt, [var], [var])
            kb.s.op("dve", lambda g_: g_.reciprocal(var[:], var[:]), [var], [var])
            kb.tt(ycs[:], ycs[:], mean[:].unsqueeze(1).to_broadcast([128, 4, 128]), ALU.subtract, [ycs, mean], [ycs])
            kb.tt(ycs[:], ycs[:], var[:].unsqueeze(1).to_broadcast([128, 4, 128]), ALU.mult, [ycs, var], [ycs])
            for c in range(4):
                kb.ts(ycs[:, c, :], ycs[:, c, :], cols[:, 4 + c:5 + c], cols[:, 8 + c:9 + c], ALU.mult, ALU.add, [ycs, cols], [ycs])
            kb.act(yaT[:], ycs[:], AF.Silu, [ycs], [yaT])
            for hh in range(2):
                for kc in range(8):
                    lhsT = yaT[:, kc, :] if kc < 4 else ybT[:, kc - 4, :]
                    kb.mm(B(6 + hh), lhsT, woutb[:, kc, hh * 512:(hh + 1) * 512], kc == 0, kc == 7, [yaT, ybT, woutb], [pb[6 + hh]])
            kb.tt(xo[:], ps[:, 6 * 512:8 * 512], xs[:], ALU.add, [pb[6], pb[7], xs], [xo])
            kb.dma(dsty, xo[:], [xo], [], e="pool")
            if full:
                kb.copy(uT[:, :, 0:30], uT[:, :, 128:158], [uT], [uT], e="pool")
        kb.s.barrier()


def fox_block(kb0, x_own, x_pre, x_out, gvec, w_in, w_out, fb, qg, kg, pflag, ident_d, tri_d, masks_d, T, TP, tag="fx"):
    nc = kb0.nc
    NTO = T // 128
    NTP = TP // 128
    NTA = NTO + NTP
    NSB = T // 512
    QT = kb0.dram("fxQT", [16, 65, T], BF16, "Internal")
    KT = kb0.dram("fxKT", [16, 65, TP + T], BF16, "Internal")
    VA = kb0.dram("fxVA", [NTA, 128, 16 * 65], BF16, "Internal")
    OT = kb0.dram("fxOT", [16, 64, T], BF16, "Internal")
    with ExitStack() as st0:
        kbp = kb0.scope(st0)
        negc = kbp.sb("negc", [128, NTA, 16], F32)
        ident = kbp.sb("identf", [128, 128], F32)
        kbp.dma(ident[:], ident_d[:], [ident_d], [ident])
        ones = kbp.sb("onesf", [128, 128], F32)
        kbp.memset(ones[:], 1.0, [ones])
        woutb = kbp.sb("woutbf", [128, 8, 1024], BF16)
        load_bf16_resident(kbp, woutb, lambda kc, c0, w: woutb[:, kc, c0:c0 + w], w_out, 8, 1024, "fwout")
        with ExitStack() as st:
            kb = kbp.scope(st)
            winb = kb.sb("winbf", [128, 8, 3088], BF16)
            load_bf16_resident(kb, winb, lambda kc, c0, w: winb[:, kc, c0:c0 + w], w_in, 8, 3088, "fwin")
            gB = kb.sb("gBf", [128, 1024], F32)
            kb.dma(gB[:], gvec.t.partition_broadcast(128)[:, 0, :], [gvec], [gB])
            fbB = kb.sb("fbB", [128, 16], F32)
            kb.dma(fbB[:], fb.t.partition_broadcast(128)[:, 0, :], [fb], [fbB])
            qgB = kb.sb("qgB", [128, 64], F32)
            kb.dma(qgB[:], qg.t.partition_broadcast(128)[:, 0, :], [qg], [qgB])
            kgB = kb.sb("kgB", [128, 64], F32)
            kb.dma(kgB[:], kg.t.partition_broadcast(128)[:, 0, :], [kg], [kgB])
            pfl = kb.sb("pfl", [128, 1], F32)
            kb.dma(pfl[:], pflag[:], [pflag], [pfl])
            tri = kb.sb("trif", [128, 128], F32)
            kb.dma(tri[:], tri_d[:], [tri_d], [tri])
            xs = kb.sb("xsf", [128, 1024], F32)
            hf = kb.sb("hff", [128, 1024], F32)
            sq = kb.sb("sqf", [128, 1024], BF16)
            ss = kb.sb("ssf", [128, 4], F32)
            hT = kb.sb("hTf", [128, 8, 128], BF16)
            nsq = kb.sb("nsq", [128, 1024], F32)
            nss = kb.sb("nss", [128, 16], F32)
            qa = kb.sb("qa", [128, 16, 65], F32)
            ka = kb.sb("ka", [128, 16, 65], F32)
            kb.memset(ka[:, :, 64:65], 1.0, [ka])
            va = kb.sb("va", [128, 16, 65], BF16)
            kb.memset(va[:, :, 0:1], 1.0, [va])
            qTs = kb.sb("qTs", [65, 16, 128], BF16)
            kTs = kb.sb("kTs", [65, 16, 128], BF16)
            lf = kb.sb("lf", [128, 16], F32)
            Lsum = kb.sb("Lsum", [128, 16], F32)
            kb.memset(Lsum[:], 0.0, [Lsum])
            ctile = kb.sb("ctile", [128, 16], F32)
            ps = st.enter_context(nc.psum_tensor("psf1", [128, 4096], F32))
            pb = [Buf(f"pbf{i}") for i in range(8)]

            def normed(dst, bank0, gt):
                src = ps[:, bank0 * 512:(bank0 + 2) * 512]
                kb.act(nsq[:], src, AF.Square, [pb[bank0], pb[bank0 + 1]], [nsq])
                kb.reduce(nss[:], nsq[:].rearrange("p (h d) -> p h d", d=64), ALU.add, [nsq], [nss])
                kb.ts(nss[:], nss[:], 1.0 / 64, 1e-6, ALU.mult, ALU.add, [nss], [nss])
                kb.act(nss[:], nss[:], AF.Sqrt, [nss], [nss])
                kb.s.op("dve", lambda g_: g_.reciprocal(nss[:], nss[:]), [nss], [nss])
                kb.tt(dst[:, :, 0:64], src.rearrange("p (h d) -> p h d", d=64), nss[:].unsqueeze(2).to_broadcast([128, 16, 64]),
                      ALU.mult, [pb[bank0], pb[bank0 + 1], nss], [dst])
                kb.tt(dst[:, :, 0:64], dst[:, :, 0:64], gt[:].unsqueeze(1).to_broadcast([128, 16, 64]), ALU.mult, [dst, gt], [dst])

            def transposed_store(src, dstT, dram, tokcol, b0):
                for h in range(16):
                    kb.tr(ps[0:65, b0 * 512 + h * 128:b0 * 512 + (h + 1) * 128], src[:, h, :], ident[:], [src, ident], [pb[b0 + h // 4]])
                for q4 in range(4):
                    kb.copy(dstT[:, q4 * 4:(q4 + 1) * 4, :], ps[0:65, (b0 + q4) * 512:(b0 + q4 + 1) * 512].rearrange("p (h t) -> p h t", t=128),
                            [pb[b0 + q4]], [dstT], e=("act" if q4 % 2 == 0 else "dve"))
                kb.dma(dram.t.rearrange("h r t -> r h t")[:, :, tokcol:tokcol + 128], dstT[:], [dstT], [], e="pool")

            for i in range(NTA):
                own = i >= NTP
                src = x_own[(i - NTP) * 128:(i - NTP + 1) * 128, :] if own else x_pre[i * 128:(i + 1) * 128, :]
                kb.dma(xs[:], src, [x_own, x_pre], [xs])
                rms_to_hT(kb, xs, gB, ident, hf, sq, ss, ps, pb[0], pb[1], hT, 0)
                for kc in range(8):
                    kb.mm(ps[:, 6 * 512:6 * 512 + 16], hT[:, kc, :], winb[:, kc, 3072:3088], kc == 0, kc == 7, [hT, winb], [pb[6]])
                kb.tt(lf[:], ps[:, 6 * 512:6 * 512 + 16], fbB[:], ALU.add, [pb[6], fbB], [lf])
                kb.act(lf[:], lf[:], AF.Exp, [lf], [lf], scale=-1.0)
                kb.act(lf[:], lf[:], AF.Ln, [lf], [lf], bias=1.0)
                kb.ts(lf[:], lf[:], -1.0, None, ALU.mult, None, [lf], [lf])
                kb.mm(ps[:, 6 * 512 + 16:6 * 512 + 32], tri[:], lf[:], True, False, [tri, lf], [pb[6]])
                kb.mm(ps[:, 6 * 512 + 16:6 * 512 + 32], ones[:], Lsum[:], False, True, [ones, Lsum], [pb[6]])
                kb.tt(Lsum[:], Lsum[:], lf[:], ALU.add, [Lsum, lf], [Lsum])
                kb.copy(ctile[:], ps[:, 6 * 512 + 16:6 * 512 + 32], [pb[6]], [ctile], e="act")
                if own:
                    kb.ts(negc[:, i, :], ctile[:], -1.0, None, ALU.mult, None, [ctile], [negc])
                else:
                    kb.ts(negc[:, i, :], ctile[:], -1.0, pfl[:, 0:1], ALU.mult, ALU.add, [ctile, pfl], [negc])
                for hh in range(2):
                    for kc in range(8):
                        kb.mm(ps[:, (2 + hh) * 512:(3 + hh) * 512], hT[:, kc, :], winb[:, kc, 1024 + hh * 512:1536 + hh * 512],
                              kc == 0, kc == 7, [hT, winb], [pb[2 + hh]])
                normed(ka, 2, kgB)
                for hh in range(2):
                    for kc in range(8):
                        kb.mm(ps[:, (4 + hh) * 512:(5 + hh) * 512], hT[:, kc, :], winb[:, kc, 2048 + hh * 512:2560 + hh * 512],
                              kc == 0, kc == 7, [hT, winb], [pb[4 + hh]])
                kb.copy(va[:, :, 1:65], ps[:, 4 * 512:6 * 512].rearrange("p (h d) -> p h d", d=64), [pb[4], pb[5]], [va], e="act")
                kb.dma(VA[i], va[:].rearrange("p h d -> p (h d)"), [va], [], e="pool")
                if own:
                    for hh in range(2):
                        for kc in range(8):
                            kb.mm(ps[:, hh * 512:(hh + 1) * 512], hT[:, kc, :], winb[:, kc, hh * 512:(hh + 1) * 512],
                                  kc == 0, kc == 7, [hT, winb], [pb[hh]])
                    normed(qa, 0, qgB)
                    kb.ts(qa[:, :, 64:65], ctile[:].unsqueeze(2), 8.0, None, ALU.mult, None, [ctile], [qa])
                transposed_store(ka, kTs, KT, i * 128, 2)
                if own:
                    transposed_store(qa, qTs, QT, (i - NTP) * 128, 2)
            kb.s.barrier()
        with ExitStack() as st:
            kb = kbp.scope(st)
            kts = kb.sb("kts", [65, TP + T], BF16)
            vas = kb.sb("vas", [128, NTA, 65], BF16)
            qts = kb.sb("qts", [65, T], BF16)
            masks = kb.sb("masksb", [128, 4, 512], F32)
            kb.dma(masks[:], masks_d[:], [masks_d], [masks])
            stmp = kb.sb("stmp", [128, 512], F32)
            pT = [kb.sb(f"pT{i}", [128, 512], BF16) for i in range(3)]
            osb = kb.sb("osb", [65, 512], F32)
            rden = kb.sb("rden", [1, 512], F32)
            oTs = kb.sb("oTs", [65, 512], BF16)
            ps = st.enter_context(nc.psum_tensor("psf2", [128, 4096], F32))
            pb = [Buf(f"pbg{i}") for i in range(8)]
            VAv = VA.t.rearrange("n p (h d) -> p n h d", d=65)
            cnt = 0
            for h in range(16):
                kb.dma(kts[:], KT[h], [], [kts])
                kb.dma(qts[:], QT[h], [], [qts])
                kb.dma(vas[:], VAv[:, :, h, :], [], [vas])
                for j in range(NSB):
                    nkb = NTP + 4 * j + 4
                    ob = 4 + (j % 2)
                    def s_stage(kb_, a):
                        kb.mm(ps[:, a * 512:(a + 1) * 512], kts[:, kb_ * 128:(kb_ + 1) * 128], qts[:, j * 512:(j + 1) * 512], True, True,
                              [kts, qts], [pb[a]])
                        m = kb_ - NTP - 4 * j
                        if m >= 0:
                            kb.tt(stmp[:], ps[:, a * 512:(a + 1) * 512], masks[:, m, :], ALU.add, [pb[a], masks], [stmp])
                            kb.act(pT[a][:], stmp[:], AF.Exp, [stmp, negc], [pT[a]], bias=negc[:, kb_, h:h + 1], scale=0.125)
                        else:
                            kb.act(pT[a][:], ps[:, a * 512:(a + 1) * 512], AF.Exp, [pb[a], negc], [pT[a]], bias=negc[:, kb_, h:h + 1], scale=0.125)

                    def pv_stage(kb_, a):
                        kb.mm(ps[0:65, ob * 512:(ob + 1) * 512], vas[:, kb_, :], pT[a][:], kb_ == 0, kb_ == nkb - 1, [vas, pT[a]], [pb[ob]])

                    NBUF = 3
                    for kb_ in range(min(NBUF - 1, nkb)):
                        s_stage(kb_, kb_ % NBUF)
                    for kb_ in range(nkb):
                        if kb_ + NBUF - 1 < nkb:
                            s_stage(kb_ + NBUF - 1, (kb_ + NBUF - 1) % NBUF)
                        pv_stage(kb_, kb_ % NBUF)
                    kb.copy(osb[:], ps[0:65, ob * 512:(ob + 1) * 512], [pb[ob]], [osb], e="act")
                    kb.s.op("dve", lambda g_: g_.reciprocal(rden[:], osb[0:1, :]), [osb], [rden])
                    kb.mm(ps[0:65, 6 * 512:7 * 512], ones[0:1, 0:65], rden[:], True, True, [ones, rden], [pb[6]])
                    kb.tt(oTs[:], osb[:], ps[0:65, 6 * 512:7 * 512], ALU.mult, [osb, pb[6]], [oTs])
                    kb.dma(OT[h, :, j * 512:(j + 1) * 512], oTs[1:65, :], [oTs], [], e="pool")
            kb.s.barrier()
        with ExitStack() as st:
            kb = kbp.scope(st)
            oTt = [kb.sb(f"oTt{i}", [128, 8, 128], BF16) for i in range(2)]
            xs2 = [kb.sb(f"xs2{i}", [128, 1024], F32) for i in range(2)]
            xo = [kb.sb(f"xof{i}", [128, 1024], F32) for i in range(2)]
            ps = st.enter_context(nc.psum_tensor("psf3", [128, 4096], F32))
            pb = [Buf(f"pbh{i}") for i in range(8)]
            OTv = OT.t.rearrange("(p two) d t -> (two d) p t", two=2)
            for i in range(NTO):
                a = i % 2
                kb.dma(oTt[a][:], OTv[:, :, i * 128:(i + 1) * 128], [], [oTt[a]])
                kb.dma(xs2[a][:], x_own[i * 128:(i + 1) * 128, :], [x_own], [xs2[a]])
                for hh in range(2):
                    for p in range(8):
                        kb.mm(ps[:, (2 * a + hh) * 512:(2 * a + hh + 1) * 512], oTt[a][:, p, :], woutb[:, p, hh * 512:(hh + 1) * 512],
                              p == 0, p == 7, [oTt[a], woutb], [pb[2 * a + hh]])
                kb.tt(xo[a][:], ps[:, 2 * a * 512:(2 * a + 2) * 512], xs2[a][:], ALU.add, [pb[2 * a], pb[2 * a + 1], xs2[a]], [xo[a]])
                kb.dma(x_out[i * 128:(i + 1) * 128, :], xo[a][:], [xo[a]], [], e="pool")
            kb.s.barrier()


TOK = 4096


def _consts():
    k = np.arange(128)[:, None]
    q = np.arange(512)[None, :]
    fm = np.zeros((128, 4, 512), np.float32)
    for mm in range(4):
        fm[:, mm, :] = np.where((mm * 128 + k) <= q, 0.0, -240000.0)
    return {
        "ident": np.eye(128, dtype=np.float32),
        "iota": np.tile(np.arange(128, dtype=np.float32), (128, 1)),
        "tri": np.triu(np.ones((128, 128), np.float32)),
        "masks": fm,
    }


def _col4(v):
    return np.ascontiguousarray(v.reshape(4, 128).T)


_NC_CACHE = {}


def _build_fused(T):
    key = ("fused", T)
    if key in _NC_CACHE:
        return _NC_CACHE[key]
    nc = bass.Bass("TRN2", target_bir_lowering=False)
    with ExitStack() as st:
        kb = KB(nc, st)
        D = lambda n, s: kb.dram(n, s, F32, "ExternalInput")
        I = lambda n: kb.dram(n, [T, 1024], F32, "Internal")
        x, xp = D("x", [T, 1024]), D("xp", [T, 1024])
        y = kb.dram("y", [T, 1024], F32, "ExternalOutput")
        ident, iota, tri, masks = D("ident", [128, 128]), D("iota", [128, 128]), D("tri", [128, 128]), D("masks", [128, 4, 512])
        x1o, x1p, x2o, x2p, x3o = I("x1o"), I("x1p"), I("x2o"), I("x2p"), I("x3o")
        mixer0_block(kb, x, xp, x1o, D("m_g", [1, 1024]), D("m_w_in", [1024, 2576]), D("m_w_out", [1024, 1024]),
                     D("m_convwT", [128, 4, 31]), D("m_convb", [128, 4]), D("m_lng", [128, 4]), D("m_lnb", [128, 4]),
                     D("m_w2", [16, 256]), D("m_gateb", [1, 256]), D("m_glag", [128, 1]), ident, tri, T, T, x_out_pre=x1p)
        tabs0 = peer_tables(kb, D("p0_uT", [1024, 16384]), D("p0_v", [16384, 1024]), "L0")
        peer_block(kb, None, None, D("p0_g", [1, 1024]), D("p0_wq", [1024, 2048]), D("p0_keysT", [128, 2048]), None, None,
                   ident, iota, T, "L0", tables=tabs0, jobs=[(x1p, x2p, T), (x1o, x2o, T)])
        fox_block(kb, x2o, x2p, x3o, D("f_g", [1, 1024]), D("f_w_in", [1024, 3088]), D("f_w_out", [1024, 1024]), D("f_fb", [1, 16]),
                  D("f_qg", [1, 64]), D("f_kg", [1, 64]), D("pflag", [128, 1]), ident, tri, masks, T, T)
        peer_block(kb, x3o, y, D("p1_g", [1, 1024]), D("p1_wq", [1024, 2048]), D("p1_keysT", [128, 2048]),
                   D("p1_uT", [1024, 16384]), D("p1_v", [16384, 1024]), ident, iota, T, "L1")
        kb.s.emit()
        print("fused program: ninst", kb.s.ninst, "nsem", kb.s.nsem, flush=True)
    _NC_CACHE[key] = nc
    return nc


def _fused_common(inp, cst):
    c = {"m_g": np.ascontiguousarray(inp["ev_norm_mix"][0][None]), "m_w_in": np.ascontiguousarray(inp["ev_w_in"][0]),
         "m_w_out": np.ascontiguousarray(inp["ev_w_out"][0]),
         "m_convwT": np.ascontiguousarray(inp["ev_conv_w"][0].T.reshape(4, 128, 31).transpose(1, 0, 2)),
         "m_convb": _col4(inp["ev_conv_b"][0]), "m_lng": _col4(inp["ev_conv_ln_g"][0]), "m_lnb": _col4(inp["ev_conv_ln_b"][0]),
         "m_w2": np.ascontiguousarray(inp["ev_gate_w2"][0]), "m_gateb": np.ascontiguousarray(inp["ev_gate_b"][0][None]),
         "m_glag": np.ascontiguousarray(inp["ev_gla_norm_g"][0][:, None]),
         "f_g": np.ascontiguousarray(inp["od_norm_mix"][0][None]), "f_w_in": np.ascontiguousarray(inp["od_w_in"][0]),
         "f_w_out": np.ascontiguousarray(inp["od_w_out"][0]), "f_fb": np.ascontiguousarray(inp["od_fgate_b"][0][None]),
         "f_qg": np.ascontiguousarray(inp["od_q_norm_g"][0][None]), "f_kg": np.ascontiguousarray(inp["od_k_norm_g"][0][None]),
         "ident": cst["ident"], "iota": cst["iota"], "tri": cst["tri"], "masks": cst["masks"]}
    for L in range(2):
        p = _peer_inputs(inp, L, cst)
        for k in ("g", "wq", "keysT", "uT", "v"):
            c[f"p{L}_{k}"] = p[k]
    return c


def _build(kind):
    if kind in _NC_CACHE:
        return _NC_CACHE[kind]
    nc = bass.Bass("TRN2", target_bir_lowering=False)
    with ExitStack() as st:
        kb = KB(nc, st)
        D = lambda n, s: kb.dram(n, s, F32, "ExternalInput")
        T = TOK
        if kind == "m0":
            x, xp = D("x", [T, 1024]), D("xp", [T, 1024])
            y = kb.dram("y", [T, 1024], F32, "ExternalOutput")
            mixer0_block(kb, x, xp, y, D("g", [1, 1024]), D("w_in", [1024, 2576]), D("w_out", [1024, 1024]), D("convwT", [128, 4, 31]),
                         D("convb", [128, 4]), D("lng", [128, 4]), D("lnb", [128, 4]), D("w2", [16, 256]), D("gateb", [1, 256]),
                         D("glag", [128, 1]), D("ident", [128, 128]), D("tri", [128, 128]), T, T)
        elif kind == "peer":
            x = D("x", [T, 1024])
            y = kb.dram("y", [T, 1024], F32, "ExternalOutput")
            peer_block(kb, x, y, D("g", [1, 1024]), D("wq", [1024, 2048]), D("keysT", [128, 2048]), D("uT", [1024, 16384]),
                       D("v", [16384, 1024]), D("ident", [128, 128]), D("iota", [128, 128]), T, "p0")
        elif kind == "fox":
            x, xp = D("x", [T, 1024]), D("xp", [T, 1024])
            y = kb.dram("y", [T, 1024], F32, "ExternalOutput")
            fox_block(kb, x, xp, y, D("g", [1, 1024]), D("w_in", [1024, 3088]), D("w_out", [1024, 1024]), D("fb", [1, 16]),
                      D("qg", [1, 64]), D("kg", [1, 64]), D("pflag", [128, 1]), D("ident", [128, 128]), D("tri", [128, 128]),
                      D("masks", [128, 4, 512]), T, T)
        kb.s.emit()
    _NC_CACHE[kind] = nc
    return nc


def _shards(xfull):
    own, pre = [], []
    for c in range(NCORES):
        b, half = c // 2, c % 2
        own.append(np.ascontiguousarray(xfull[b, half * TOK:(half + 1) * TOK]))
        pre.append(np.ascontiguousarray(xfull[b, 0:TOK]) if half == 1 else np.zeros((TOK, 1024), np.float32))
    return own, pre


def _gather(res):
    out = np.empty((4, 8192, 1024), np.float32)
    for c in range(NCORES):
        b, half = c // 2, c % 2
        out[b, half * TOK:(half + 1) * TOK] = res.results[c]["y"]
    return out


def _run(kind, per_core):
    nc = _build(kind)
    return run_bass_kernel_spmd(nc, per_core, core_ids=list(range(NCORES)))


def _peer_inputs(inp, layer, cst):
    keys = inp["peer_keys"][layer]
    return {"g": np.ascontiguousarray(inp["ffn_norm"][layer][None]), "wq": np.ascontiguousarray(inp["peer_wq"][layer]),
            "keysT": np.ascontiguousarray(keys.transpose(3, 0, 1, 2).reshape(128, 2048)),
            "uT": np.ascontiguousarray(inp["peer_u"][layer].T), "v": np.ascontiguousarray(inp["peer_v"][layer]),
            "ident": cst["ident"], "iota": cst["iota"]}


def kernel(**inp):
    inp = {k: np.asarray(v, dtype=np.float32) for k, v in inp.items()}
    cst = _consts()
    own, pre = _shards(inp["x"])
    common = _fused_common(inp, cst)
    pfl = [np.full((128, 1), 0.0 if c % 2 == 1 else -30000.0, np.float32) for c in range(NCORES)]
    nc = _build_fused(TOK)
    res = run_bass_kernel_spmd(nc, [dict(common, x=own[c], xp=pre[c], pflag=pfl[c]) for c in range(NCORES)],
                               core_ids=list(range(NCORES)))
    return _gather(res)


def kernel_unfused(**inp):
    inp = {k: np.asarray(v, dtype=np.float32) for k, v in inp.items()}
    cst = _consts()
    x = inp["x"]
    own, pre = _shards(x)
    common = {"g": np.ascontiguousarray(inp["ev_norm_mix"][0][None]), "w_in": np.ascontiguousarray(inp["ev_w_in"][0]),
              "w_out": np.ascontiguousarray(inp["ev_w_out"][0]),
              "convwT": np.ascontiguousarray(inp["ev_conv_w"][0].T.reshape(4, 128, 31).transpose(1, 0, 2)),
              "convb": _col4(inp["ev_conv_b"][0]), "lng": _col4(inp["ev_conv_ln_g"][0]), "lnb": _col4(inp["ev_conv_ln_b"][0]),
              "w2": np.ascontiguousarray(inp["ev_gate_w2"][0]), "gateb": np.ascontiguousarray(inp["ev_gate_b"][0][None]),
              "glag": np.ascontiguousarray(inp["ev_gla_norm_g"][0][:, None]), "ident": cst["ident"], "tri": cst["tri"]}
    x1 = _gather(_run("m0", [dict(common, x=own[c], xp=pre[c]) for c in range(NCORES)]))
    own, _ = _shards(x1)
    common = _peer_inputs(inp, 0, cst)
    x2 = _gather(_run("peer", [dict(common, x=own[c]) for c in range(NCORES)]))
    own, pre = _shards(x2)
    common = {"g": np.ascontiguousarray(inp["od_norm_mix"][0][None]), "w_in": np.ascontiguousarray(inp["od_w_in"][0]),
              "w_out": np.ascontiguousarray(inp["od_w_out"][0]), "fb": np.ascontiguousarray(inp["od_fgate_b"][0][None]),
              "qg": np.ascontiguousarray(inp["od_q_norm_g"][0][None]), "kg": np.ascontiguousarray(inp["od_k_norm_g"][0][None]),
              "ident": cst["ident"], "tri": cst["tri"], "masks": cst["masks"]}
    pfl = [np.full((128, 1), 0.0 if c % 2 == 1 else -30000.0, np.float32) for c in range(NCORES)]
    x3 = _gather(_run("fox", [dict(common, x=own[c], xp=pre[c], pflag=pfl[c]) for c in range(NCORES)]))
    own, _ = _shards(x3)
    common = _peer_inputs(inp, 1, cst)
    x4 = _gather(_run("peer", [dict(common, x=own[c]) for c in range(NCORES)]))
    return x4
```

```python
from contextlib import ExitStack

import numpy as np
import concourse.bass as bass
import concourse.mybir as mybir
from concourse.bass_utils import run_bass_kernel_spmd

F32 = mybir.dt.float32
BF16 = mybir.dt.bfloat16
U32 = mybir.dt.uint32
I32 = mybir.dt.int32
ALU = mybir.AluOpType
AF = mybir.ActivationFunctionType
AX = mybir.AxisListType

NCORES = 8
EPOCH = 30000
ENGS = ("pe", "dve", "act", "pool", "sp")


class Buf:
    __slots__ = ("w", "r", "name")

    def __init__(self, name=""):
        self.w = None
        self.r = []
        self.name = name


class TT:
    def __init__(self, t, name):
        self.t = t
        self.b = Buf(name)

    def __getitem__(self, k):
        return self.t[k]


class Sched:
    def __init__(self, nc, stack):
        self.nc = nc
        self.stack = stack
        self.q = {e: [] for e in ENGS}
        self.csem = {e: None for e in ENGS}
        self.ccnt = {e: 0 for e in ENGS}
        self.dsem = {e: [] for e in ENGS}
        self.dcnt = {e: [] for e in ENGS}
        self.drr = {e: 0 for e in ENGS}
        self.seen = {e: {} for e in ENGS}
        self.nsem = 0
        self.ninst = 0
        self.defer = None

    def _newsem(self, nm):
        self.nsem += 1
        return self.stack.enter_context(self.nc.semaphore(f"{nm}{self.nsem}"))

    def _ticket(self, e, dma):
        if not dma:
            if self.csem[e] is None or self.ccnt[e] >= EPOCH:
                self.csem[e] = self._newsem("c" + e)
                self.ccnt[e] = 0
            self.ccnt[e] += 1
            return (self.csem[e], self.ccnt[e], e, 1)
        if not self.dsem[e]:
            self.dsem[e] = [self._newsem("d" + e) for _ in range(8)]
            self.dcnt[e] = [0] * 8
        i = self.drr[e] % 8
        self.drr[e] += 1
        if self.dcnt[e][i] + 16 >= EPOCH:
            self.dsem[e][i] = self._newsem("d" + e)
            self.dcnt[e][i] = 0
        self.dcnt[e][i] += 16
        return (self.dsem[e][i], self.dcnt[e][i], e + "_dma", 16)

    def op(self, e, fn, reads=(), writes=(), dma=False):
        if self.defer is not None:
            self.defer.append((e, fn, list(reads), list(writes), dma))
            return None
        deps = {}

        def add(t):
            if t is None:
                return
            sem, val, src, _ = t
            if src == "pe" and e == "pe" and not dma:
                return
            k = id(sem)
            if self.seen[e].get(k, 0) >= val:
                return
            if k not in deps or deps[k][1] < val:
                deps[k] = (sem, val)

        for b in reads:
            b = b.b if isinstance(b, TT) else b
            add(b.w)
        for b in writes:
            b = b.b if isinstance(b, TT) else b
            add(b.w)
            for t in b.r:
                add(t)
        waits = list(deps.values())
        for sem, val in waits:
            self.seen[e][id(sem)] = val
        t = self._ticket(e, dma)
        self.q[e].append((waits, fn, t[0], t[3]))
        self.ninst += 1 + len(waits)
        for b in reads:
            b = b.b if isinstance(b, TT) else b
            b.r = [x for x in b.r if x[0] is not t[0]] + [t]
        for b in writes:
            b = b.b if isinstance(b, TT) else b
            b.w = t
            b.r = []
        return t

    def replay(self, lst, n):
        k = min(n, len(lst))
        for e, fn, reads, writes, dma in lst[:k]:
            self.op(e, fn, reads, writes, dma)
        del lst[:k]

    def barrier(self):
        waits = []
        for e in ENGS:
            if self.csem[e] is not None and self.ccnt[e] > 0:
                waits.append((self.csem[e], self.ccnt[e]))
            for sem, c in zip(self.dsem[e], self.dcnt[e]):
                if c > 0:
                    waits.append((sem, c))
        for e in ENGS:
            self.q[e].append((list(waits), None, None, 0))
            for sem, val in waits:
                self.seen[e][id(sem)] = max(self.seen[e].get(id(sem), 0), val)

    def final_wait(self, e, bufs):
        waits = []
        for b in bufs:
            b = b.b if isinstance(b, TT) else b
            for t in [b.w] + list(b.r):
                if t is not None:
                    waits.append((t[0], t[1]))
        self.q[e].append((waits, None, None, 0))

    def emit(self):
        nc = self.nc
        q = self.q

        def run(e, eng):
            for waits, fn, sem, inc in q[e]:
                for s, v in waits:
                    eng.wait_ge(s, v)
                if fn is not None:
                    ins = fn(eng)
                    ins.then_inc(sem, inc)

        with nc.Block() as block:

            @block.tensor
            def _(eng):
                run("pe", eng)

            @block.vector
            def _(eng):
                run("dve", eng)

            @block.scalar
            def _(eng):
                run("act", eng)

            @block.gpsimd
            def _(eng):
                run("pool", eng)

            @block.sync
            def _(eng):
                run("sp", eng)


class KB:
    def __init__(self, nc, stack, sched=None):
        self.nc = nc
        self.stack = stack
        self.s = sched if sched is not None else Sched(nc, stack)

    def scope(self, stack):
        return KB(self.nc, stack, self.s)

    def sb(self, name, shape, dt):
        t = self.stack.enter_context(self.nc.sbuf_tensor(name, list(shape), dt))
        return TT(t, name)

    def dram(self, name, shape, dt, kind):
        t = self.nc.dram_tensor(name, list(shape), dt, kind=kind)
        return TT(t.ap(), name)

    def dma(self, out, in_, reads, writes, e="sp", **kw):
        return self.s.op(e, lambda g: g.dma_start(out=out, in_=in_, **kw), reads, writes, dma=True)

    def mm(self, out, lhsT, rhs, start, stop, reads, writes):
        return self.s.op("pe", lambda g: g.matmul(out, lhsT, rhs, start=start, stop=stop), reads, writes)

    def tr(self, out, in_, ident, reads, writes):
        return self.s.op("pe", lambda g: g.transpose(out, in_, ident), reads, writes)

    def act(self, out, in_, func, reads, writes, bias=None, scale=None, accum_out=None):
        kw = {}
        if bias is not None:
            kw["bias"] = bias
        if scale is not None:
            kw["scale"] = scale
        if accum_out is not None:
            kw["accum_out"] = accum_out
        return self.s.op("act", lambda g: g.activation(out, in_, func, **kw), reads, writes)

    def tt(self, out, in0, in1, op, reads, writes, e="dve"):
        return self.s.op(e, lambda g: g.tensor_tensor(out, in0, in1, op), reads, writes)

    def ts(self, out, in0, s1, s2, op0, op1, reads, writes, e="dve", accum_out=None):
        if op1 is None:
            return self.s.op(e, lambda g: g.tensor_scalar(out, in0, s1, None, op0), reads, writes)
        if accum_out is not None:
            return self.s.op(e, lambda g: g.tensor_scalar(out, in0, s1, s2, op0, op1, accum_out), reads, writes)
        return self.s.op(e, lambda g: g.tensor_scalar(out, in0, s1, s2, op0, op1), reads, writes)

    def stt(self, out, in0, scalar, in1, op0, op1, reads, writes, e="dve"):
        return self.s.op(e, lambda g: g.scalar_tensor_tensor(out, in0, scalar, in1, op0, op1), reads, writes)

    def copy(self, out, in_, reads, writes, e="dve"):
        if e == "act":
            return self.s.op(e, lambda g: g.copy(out, in_), reads, writes)
        return self.s.op(e, lambda g: g.tensor_copy(out, in_), reads, writes)

    def memset(self, ap, val, writes, e="dve"):
        return self.s.op(e, lambda g: g.memset(ap, val), (), writes)

    def reduce(self, out, in_, op, reads, writes, axis=AX.X, e="dve"):
        return self.s.op(e, lambda g: g.tensor_reduce(out, in_, axis, op), reads, writes)


def to_bf16_dram(kb, src, dst, R, C, tag):
    with ExitStack() as st:
        k = kb.scope(st)
        W = 2048
        stg = [k.sb(f"cv_in{tag}{i}", [128, W], F32) for i in range(2)]
        outb = [k.sb(f"cv_out{tag}{i}", [128, W], BF16) for i in range(2)]
        n = 0
        for r in range(R // 128):
            for c0 in range(0, C, W):
                w = min(W, C - c0)
                i = n % 2
                k.dma(stg[i][:, 0:w], src[r * 128:(r + 1) * 128, c0:c0 + w], [src], [stg[i]])
                k.copy(outb[i][:, 0:w], stg[i][:, 0:w], [stg[i]], [outb[i]], e=("dve" if n % 2 == 0 else "act"))
                k.dma(dst[r * 128:(r + 1) * 128, c0:c0 + w], outb[i][:, 0:w], [outb[i]], [], e="pool")
                n += 1
        k.s.barrier()


def load_bf16_resident(kb, dst_tt, dst_ap_fn, src, nrows_blocks, C, tag):
    with ExitStack() as st:
        k = kb.scope(st)
        W = 2048
        stg = [k.sb(f"ld_in{tag}{i}", [128, W], F32) for i in range(2)]
        n = 0
        for kc in range(nrows_blocks):
            for c0 in range(0, C, W):
                w = min(W, C - c0)
                i = n % 2
                k.dma(stg[i][:, 0:w], src[kc * 128:(kc + 1) * 128, c0:c0 + w], [src], [stg[i]])
                k.copy(dst_ap_fn(kc, c0, w), stg[i][:, 0:w], [stg[i]], [dst_tt], e=("dve" if n % 2 == 0 else "act"))
                n += 1
        k.s.barrier()


def peer_tables(kb0, uT, vtab, tag):
    us = kb0.dram(f"us{tag}", [1024, 16384], BF16, "Internal")
    vs = kb0.dram(f"vs{tag}", [16384, 1024], BF16, "Internal")
    to_bf16_dram(kb0, uT, us, 1024, 16384, tag + "u")
    to_bf16_dram(kb0, vtab, vs, 16384, 1024, tag + "v")
    return us, vs


def peer_block(kb0, x_in, x_out, gvec, wq, keysT, uT, vtab, ident_d, iota_d, T, tag, G=256, CH=2, OHT=16, tables=None, jobs=None, NSL=3):
    nc = kb0.nc
    if jobs is None:
        jobs = [(x_in, x_out, T)]
    with ExitStack() as st:
        kb = kb0.scope(st)
        TPG = G // 128
        if tables is None:
            tables = peer_tables(kb, uT, vtab, tag)
        us, vs = tables
        wqb = kb.sb("wqb" + tag, [128, 8, 2048], BF16)
        load_bf16_resident(kb, wqb, lambda kc, c0, w: wqb[:, kc, c0:c0 + w], wq, 8, 2048, tag + "wq")
        kTb = kb.sb("kTb" + tag, [128, 16 * 128], BF16)
        load_bf16_resident(kb, kTb, lambda kc, c0, w: kTb[:, c0:c0 + w], keysT, 1, 2048, tag + "kt")
        gB = kb.sb("gB" + tag, [128, 1024], F32)
        kb.dma(gB[:], gvec.t.partition_broadcast(128)[:, 0, :], [gvec], [gB])
        ident = kb.sb("ident" + tag, [128, 128], F32)
        kb.dma(ident[:], ident_d[:], [ident_d], [ident])
        iota = kb.sb("iota" + tag, [128, 128], F32)
        kb.dma(iota[:], iota_d[:], [iota_d], [iota])
        iotab = kb.sb("iotab" + tag, [128, 128], BF16)
        kb.copy(iotab[:], iota[:], [iota], [iotab])

        xs = [[kb.sb(f"xs{tag}{p}{j}", [128, 1024], F32) for j in range(TPG)] for p in range(2)]
        hT = [kb.sb(f"hT{tag}{p}", [128, 8, G], BF16) for p in range(2)]
        trioT = [[kb.sb(f"trioT{tag}{p}{j}", [128, 3, 128], F32) for j in range(TPG)] for p in range(2)]
        hf = kb.sb("hf" + tag, [128, 1024], F32)
        ss = kb.sb("ss" + tag, [128, 4], F32)
        qT = kb.sb("qT" + tag, [128, 16, 128], BF16)
        sc = kb.sb("sc" + tag, [128, 16, 128], F32)
        tmp4 = kb.sb("tmp4" + tag, [128, 4, 256], F32)
        tmpb = [Buf() for _ in range(4)]
        scb = [Buf() for _ in range(4)]
        v16 = kb.sb("v16" + tag, [128, 16, 16], F32)
        i16 = kb.sb("i16" + tag, [128, 16, 16], U32)
        i16f = kb.sb("i16f" + tag, [128, 16, 16], F32)
        cand = kb.sb("cand" + tag, [128, 8, 16, 16], F32)
        s16 = kb.sb("s16" + tag, [128, 8, 16], F32)
        j16 = kb.sb("j16" + tag, [128, 8, 16], U32)
        jaf = kb.sb("jaf" + tag, [128, 8, 16], F32)
        ja = kb.sb("ja" + tag, [128, 8, 16], U32)
        jb = kb.sb("jb" + tag, [128, 8, 16], U32)
        jbf = kb.sb("jbf" + tag, [128, 8, 16], F32)
        eq = cand
        trio = kb.sb("trio" + tag, [128, 3, 128], F32)
        zz = kb.sb("zz" + tag, [128, 8], F32)
        oh1 = kb.sb("oh1" + tag, [128, OHT, 128], BF16)
        oh2 = kb.sb("oh2" + tag, [128, OHT, 128], BF16)
        ohb = [Buf() for _ in range(OHT)]
        ohb2 = [Buf() for _ in range(OHT)]
        WT = kb.sb("WT" + tag, [128, 128, G], BF16)
        WTb = [Buf() for _ in range(G // 16)]
        ub = [kb.sb(f"ub{tag}{i}", [128, 8, CH * 128], BF16) for i in range(NSL)]
        vb = [kb.sb(f"vb{tag}{i}", [128, CH, 1024], BF16) for i in range(NSL)]
        actT = [kb.sb(f"actT{tag}{i}", [128, G], BF16) for i in range(2)]
        ct = [kb.sb(f"ct{tag}{i}", [128, G], BF16) for i in range(2)]
        ps = st.enter_context(nc.psum_tensor("ps" + tag, [128, 4096], F32))
        pb = [Buf(f"pb{i}") for i in range(8)]

        def bank(i, w=512):
            return ps[:, i * 512:i * 512 + w]

        us_v = us.t.rearrange("(kc p) e -> p kc e", p=128)
        vs_v = vs.t.rearrange("(e1 p) d -> p e1 d", p=128)

        def topk_chain(vals, vdst, idst, tm, vb_, ib_, tb_, srcb):
            yield kb.s.op("dve", lambda g_: g_.max(out=vdst[:, 0:8], in_=vals), [srcb], [vb_])
            yield kb.s.op("dve", lambda g_: g_.match_replace(out=tm, in_to_replace=vdst[:, 0:8], in_values=vals,
                                                              imm_value=-1e30), [srcb, vb_], [tb_])
            yield kb.s.op("dve", lambda g_: g_.max(out=vdst[:, 8:16], in_=tm), [tb_], [vb_])
            yield kb.s.op("dve", lambda g_: g_.max_index(out=idst[:, 0:8], in_max=vdst[:, 0:8], in_values=vals), [srcb, vb_], [ib_])
            yield kb.s.op("dve", lambda g_: g_.max_index(out=idst[:, 8:16], in_max=vdst[:, 8:16], in_values=vals), [srcb, vb_], [ib_])

        def run_interleaved(chains, width=4):
            live = []
            chains = list(chains)
            while chains or live:
                while chains and len(live) < width:
                    live.append(chains.pop(0))
                nxt = []
                for ch_ in live:
                    try:
                        next(ch_)
                        nxt.append(ch_)
                    except StopIteration:
                        pass
                live = nxt

        def p1a(p, x_in, g, j):
            tok0 = g * G + j * 128
            xt = xs[p][j]
            kb.dma(xt[:], x_in[tok0:tok0 + 128, :], [x_in], [xt])
            kb.act(hf[:], xt[:], AF.Square, [xt], [hf])
            kb.reduce(ss[:, 0:1], hf[:], ALU.add, [hf], [ss])
            kb.ts(ss[:, 1:2], ss[:, 0:1], 1.0 / 1024, 1e-6, ALU.mult, ALU.add, [ss], [ss])
            kb.act(ss[:, 3:4], ss[:, 1:2], AF.Sqrt, [ss], [ss])
            kb.s.op("dve", lambda g_: g_.reciprocal(ss[:, 2:3], ss[:, 3:4]), [ss], [ss])
            kb.stt(hf[:], xt[:], ss[:, 2:3], gB[:], ALU.mult, ALU.mult, [xt, ss, gB], [hf])
            for hb in range(2):
                for k4 in range(4):
                    kc = hb * 4 + k4
                    kb.tr(ps[:, (6 + hb) * 512 + k4 * 128:(6 + hb) * 512 + (k4 + 1) * 128], hf[:, kc * 128:(kc + 1) * 128], ident[:],
                          [hf, ident], [pb[6 + hb]])
                kb.copy(hT[p][:, hb * 4:(hb + 1) * 4, j * 128:(j + 1) * 128],
                        bank(6 + hb).rearrange("p (k t) -> p k t", t=128), [pb[6 + hb]], [hT[p]], e="act")
            for q4 in range(4):
                bk = 6 + q4 % 2
                for c4 in range(4):
                    c = q4 * 4 + c4
                    for kc in range(8):
                        kb.mm(ps[:, bk * 512 + c4 * 128:bk * 512 + (c4 + 1) * 128], wqb[:, kc, c * 128:(c + 1) * 128],
                              hT[p][:, kc, j * 128:(j + 1) * 128], kc == 0, kc == 7, [wqb, hT[p]], [pb[bk]])
                kb.copy(qT[:, q4 * 4:(q4 + 1) * 4, :], bank(bk).rearrange("p (k t) -> p k t", t=128), [pb[bk]], [qT], e="act")
            for q4 in range(4):
                bk = 6 + q4 % 2
                for c4 in range(4):
                    c = q4 * 4 + c4
                    kb.mm(ps[:, bk * 512 + c4 * 128:bk * 512 + (c4 + 1) * 128], qT[:, c, :], kTb[:, c * 128:(c + 1) * 128], True, True,
                          [qT, kTb], [pb[bk]])
                kb.copy(sc[:, q4 * 4:(q4 + 1) * 4, :], bank(bk).rearrange("p (k t) -> p k t", t=128), [pb[bk]], [scb[q4]], e="act")
            v16b = [Buf() for _ in range(16)]
            i16b = [Buf() for _ in range(16)]
            run_interleaved([topk_chain(sc[:, c, :], v16[:, c, :], i16[:, c, :], tmp4[:, c % 4, 0:128], v16b[c], i16b[c], tmpb[c % 4], scb[c // 4])
                             for c in range(16)])
            kb.copy(i16f[:], i16[:], i16b, [i16f], e="pool")
            v16v = v16[:].rearrange("p (h two) k -> p h two k", two=2)
            kb.tt(cand[:], v16v[:, :, 0, :].unsqueeze(3).to_broadcast([128, 8, 16, 16]),
                  v16v[:, :, 1, :].unsqueeze(2).to_broadcast([128, 8, 16, 16]), ALU.add, v16b, [cand])
            s16b = [Buf() for _ in range(8)]
            j16b = [Buf() for _ in range(8)]
            run_interleaved([topk_chain(cand[:, h, :, :].rearrange("p a b -> p (a b)"), s16[:, h, :], j16[:, h, :], tmp4[:, h % 4, :],
                                        s16b[h], j16b[h], tmpb[h % 4], cand.b) for h in range(8)])
            gt = trio[:, 2, :].rearrange("p (h k) -> p h k", k=16)
            kb.tt(gt, s16[:], s16[:, :, 0:1].to_broadcast([128, 8, 16]), ALU.subtract, s16b, [trio], e="pool")
            kb.act(gt, gt, AF.Exp, [trio], [trio])
            kb.reduce(zz[:], gt, ALU.add, [trio], [zz])
            kb.s.op("dve", lambda g_: g_.reciprocal(zz[:], zz[:]), [zz], [zz])
            kb.tt(gt, gt, zz[:].unsqueeze(2).to_broadcast([128, 8, 16]), ALU.mult, [trio, zz], [trio])
            kb.s.op("dve", lambda g_: g_.tensor_single_scalar(ja[:], j16[:], 4, ALU.logical_shift_right), j16b, [ja])
            kb.s.op("dve", lambda g_: g_.tensor_single_scalar(jb[:], j16[:], 15, ALU.bitwise_and), j16b, [jb])
            kb.copy(jaf[:], ja[:], [ja], [jaf])
            kb.copy(jbf[:], jb[:], [jb], [jbf])
            i16v = i16f[:].rearrange("p (h two) k -> p h two k", two=2)
            iota16 = iota[:, 0:16].unsqueeze(1).unsqueeze(1).to_broadcast([128, 8, 16, 16])
            for which, jf in ((0, jaf), (1, jbf)):
                kb.tt(eq[:], iota16, jf[:].unsqueeze(3).to_broadcast([128, 8, 16, 16]), ALU.is_equal, [iota, jf], [eq])
                kb.tt(eq[:], eq[:], i16v[:, :, which, :].unsqueeze(2).to_broadcast([128, 8, 16, 16]), ALU.mult, [eq, i16f], [eq])
                kb.reduce(trio[:, which, :], eq[:].rearrange("p h k a -> p (h k) a"), ALU.add, [eq], [trio])
            for w3 in range(3):
                kb.tr(ps[:, 6 * 512 + w3 * 128:6 * 512 + (w3 + 1) * 128], trio[:, w3, :], ident[:], [trio, ident], [pb[6]])
            tT = trioT[p][j]
            kb.copy(tT[:].rearrange("p a t -> p (a t)"), ps[:, 6 * 512:6 * 512 + 384], [pb[6]], [tT], e="act")

        wstate = [0]

        def p1b(p, j):
            tT = trioT[p][j]
            for t0 in range(0, 128, 16):
                half = wstate[0] % 2
                wstate[0] += 1
                for tt_ in range(16):
                    tl = (t0 + tt_) % OHT
                    tg = t0 + tt_
                    kb.ts(oh1[:, tl, :], iotab[:], tT[:, 0, tg:tg + 1], tT[:, 2, tg:tg + 1], ALU.is_equal, ALU.mult, [iotab, tT], [ohb[tl]])
                    kb.ts(oh2[:, tl, :], iotab[:], tT[:, 1, tg:tg + 1], None, ALU.is_equal, None, [iotab, tT], [ohb2[tl]])
                    col = half * 2048 + tt_ * 128
                    kb.mm(ps[:, col:col + 128], oh2[:, tl, :], oh1[:, tl, :], True, True, [ohb[tl], ohb2[tl]], [pb[half * 4 + tt_ // 4]])
                tcol = j * 128 + t0
                kb.copy(WT[:, :, tcol:tcol + 16], ps[:, half * 2048:(half + 1) * 2048].rearrange("p (t e) -> p e t", e=128),
                        [pb[half * 4 + q_] for q_ in range(4)], [WTb[tcol // 16]], e="act")

        def p2(p, x_out, g, bg):
            per = (len(bg) + 127) // 128 if bg else 0

            def u_stage(e1):
                cg, el = e1 // CH, e1 % CH
                sl = cg % NSL
                if el == 0:
                    kb.dma(ub[sl][:], us_v[:, :, cg * CH * 128:(cg + 1) * CH * 128], [us], [ub[sl]])
                    kb.dma(vb[sl][:], vs_v[:, cg * CH:(cg + 1) * CH, :], [vs], [vb[sl]], e="pool")
                a = e1 % 2
                pu = ps[:, 2048 + a * 512:2048 + a * 512 + G]
                for kc in range(8):
                    kb.mm(pu, ub[sl][:, kc, el * 128:(el + 1) * 128], hT[p][:, kc, :], kc == 0, kc == 7, [ub[sl], hT[p]], [pb[4 + a]])
                kb.act(actT[a][:], pu, AF.Gelu, [pb[4 + a]], [actT[a]])
                kb.tt(ct[a][:], actT[a][:], WT[:, e1, :], ALU.mult, [actT[a]] + WTb, [ct[a]])

            def v_stage(e1):
                cg, el = e1 // CH, e1 % CH
                sl = cg % NSL
                a = e1 % 2
                for j in range(TPG):
                    for hh in range(2):
                        kb.mm(bank(2 * j + hh), ct[a][:, j * 128:(j + 1) * 128], vb[sl][:, el, hh * 512:(hh + 1) * 512],
                              e1 == 0, e1 == 127, [ct[a], vb[sl]], [pb[2 * j + hh]])

            u_stage(0)
            for e1 in range(128):
                if e1 + 1 < 128:
                    u_stage(e1 + 1)
                v_stage(e1)
                if bg:
                    kb.s.replay(bg, per)
            xo = tmp4[:].rearrange("p a b -> p (a b)")
            for j in range(TPG):
                tok0 = g * G + j * 128
                kb.tt(xo, ps[:, j * 1024:(j + 1) * 1024], xs[p][j][:], ALU.add, [pb[2 * j], pb[2 * j + 1], xs[p][j]], tmpb)
                kb.dma(x_out[tok0:tok0 + 128, :], xo, tmpb, [], e="pool")
            if bg:
                kb.s.replay(bg, len(bg))

        groups = [(a_, b_, g_) for (a_, b_, t_) in jobs for g_ in range(t_ // G)]
        for j in range(TPG):
            p1a(0, groups[0][0], groups[0][2], j)
        for gi, (xin_, xout_, g) in enumerate(groups):
            p = gi % 2
            for j in range(TPG):
                p1b(p, j)
            bg = []
            if gi + 1 < len(groups):
                kb.s.defer = bg
                for j in range(TPG):
                    p1a(1 - p, groups[gi + 1][0], groups[gi + 1][2], j)
                kb.s.defer = None
            p2(p, xout_, g, bg)
        kb.s.barrier()


def rms_to_hT(kb, xs_t, gB, ident, hf, sq, ss, ps, pbA, pbB, hT, col0, eps=1e-6):
    kb.act(sq[:], xs_t[:], AF.Square, [xs_t], [sq])
    kb.reduce(ss[:, 0:1], sq[:], ALU.add, [sq], [ss])
    kb.ts(ss[:, 1:2], ss[:, 0:1], 1.0 / 1024, eps, ALU.mult, ALU.add, [ss], [ss])
    kb.act(ss[:, 3:4], ss[:, 1:2], AF.Sqrt, [ss], [ss])
    kb.s.op("dve", lambda g_: g_.reciprocal(ss[:, 2:3], ss[:, 3:4]), [ss], [ss])
    kb.stt(hf[:], xs_t[:], ss[:, 2:3], gB[:], ALU.mult, ALU.mult, [xs_t, ss, gB], [hf])
    for kc in range(8):
        kb.tr(ps[:, kc * 128:(kc + 1) * 128], hf[:, kc * 128:(kc + 1) * 128], ident[:], [hf, ident], [pbA if kc < 4 else pbB])
    kb.copy(hT[:, 0:4, col0:col0 + 128], ps[:, 0:512].rearrange("p (k t) -> p k t", t=128), [pbA], [hT], e="act")
    kb.copy(hT[:, 4:8, col0:col0 + 128], ps[:, 512:1024].rearrange("p (k t) -> p k t", t=128), [pbB], [hT], e="dve")


def mixer0_block(kb0, x_own, x_pre, x_out, gvec, w_in, w_out, convwT, convb, lng, lnb, w2, gateb, glag,
                 ident_d, tri_d, T, TP, tag="m0", x_out_pre=None):
    nc = kb0.nc
    NTO = T // 128
    NTP = TP // 128
    with ExitStack() as st:
        kb = kb0.scope(st)
        winb = kb.sb("winb", [128, 8, 2576], BF16)
        load_bf16_resident(kb, winb, lambda kc, c0, w: winb[:, kc, c0:c0 + w], w_in, 8, 2576, "win")
        woutb = kb.sb("woutb", [128, 8, 1024], BF16)
        load_bf16_resident(kb, woutb, lambda kc, c0, w: woutb[:, kc, c0:c0 + w], w_out, 8, 1024, "wout")
        gB = kb.sb("gBm", [128, 1024], F32)
        kb.dma(gB[:], gvec.t.partition_broadcast(128)[:, 0, :], [gvec], [gB])
        gbB = kb.sb("gbB", [128, 256], F32)
        kb.dma(gbB[:], gateb.t.partition_broadcast(128)[:, 0, :], [gateb], [gbB])
        ident = kb.sb("identm", [128, 128], F32)
        kb.dma(ident[:], ident_d[:], [ident_d], [ident])
        tri = kb.sb("trim", [128, 128], F32)
        kb.dma(tri[:], tri_d[:], [tri_d], [tri])
        ones = kb.sb("onesm", [128, 128], F32)
        kb.memset(ones[:], 1.0, [ones])
        cw = kb.sb("cw", [128, 4, 31], F32)
        kb.dma(cw[:], convwT[:], [convwT], [cw])
        cols = kb.sb("colsm", [128, 16], F32)
        kb.dma(cols[:, 0:4], convb[:], [convb], [cols])
        kb.dma(cols[:, 4:8], lng[:], [lng], [cols])
        kb.dma(cols[:, 8:12], lnb[:], [lnb], [cols])
        kb.dma(cols[:, 12:13], glag[:], [glag], [cols])
        w2f = kb.sb("w2f", [16, 256], F32)
        kb.dma(w2f[:], w2[:], [w2], [w2f])
        w2b = kb.sb("w2b", [16, 256], BF16)
        kb.copy(w2b[:], w2f[:], [w2f], [w2b])
        diag = kb.sb("diag", [128, 4 * 31, 128], BF16)
        for c in range(4):
            for j in range(31):
                kb.ts(diag[:, c * 31 + j, :], ident[:], cw[:, c, j:j + 1], None, ALU.mult, None, [ident, cw], [diag],
                      e=("dve" if (c * 31 + j) % 2 == 0 else "pool"))
        full = x_out_pre is not None
        UW = 158 if full else 30 + T
        uT = kb.sb("uT", [128, 4, UW], BF16)
        kb.memset(uT[:, :, 0:30], 0.0, [uT])
        Sf = [kb.sb(f"Sf{h}", [64, 128], F32) for h in range(4)]
        Sb = [kb.sb(f"Sb{h}", [64, 128], BF16) for h in range(4)]
        for h in range(4):
            kb.memset(Sf[h][:], 0.0, [Sf[h]])
            kb.memset(Sb[h][:], 0.0, [Sb[h]])
        xs = kb.sb("xsm", [128, 1024], F32)
        hf = kb.sb("hfm", [128, 1024], F32)
        sq = kb.sb("sqm", [128, 1024], BF16)
        ss = kb.sb("ssm", [128, 4], F32)
        hT = kb.sb("hTm", [128, 8, 128], BF16)
        sg = kb.sb("sg", [128, 512], F32)
        glrT = kb.sb("glrT", [16, 128], BF16)
        zb = kb.sb("zb", [128, 256], F32)
        la = kb.sb("la", [128, 256], F32)
        enb_tm = kb.sb("enb_tm", [128, 256], F32)
        ebl_tm = kb.sb("ebl_tm", [128, 256], F32)
        kdec = kb.sb("kdec", [128, 256], BF16)
        vbf = kb.sb("vbf", [128, 512], BF16)
        eblc = kb.sb("eblc", [64, 8], F32)
        eb = kb.sb("eb", [64, 512], F32)
        enb = kb.sb("enb", [64, 512], F32)
        qt = kb.sb("qt", [64, 4, 128], BF16)
        kt = kb.sb("kt", [64, 4, 128], BF16)
        Am = kb.sb("Am", [128, 4, 128], BF16)
        osq = kb.sb("osq", [128, 512], F32)
        rs = kb.sb("rs", [128, 512], F32)
        sr = kb.sb("sr", [128, 512], F32)
        t1 = kb.sb("t1", [128, 512], F32)
        ybT = kb.sb("ybT", [128, 4, 128], BF16)
        ycs = kb.sb("ycs", [128, 4, 128], F32)
        ysq = kb.sb("ysq", [128, 4, 128], F32)
        mean = kb.sb("mean", [128, 128], F32)
        var = kb.sb("var", [128, 128], F32)
        yaT = kb.sb("yaT", [128, 4, 128], BF16)
        xo = kb.sb("xom", [128, 1024], F32)
        ps = st.enter_context(nc.psum_tensor("psm", [128, 4096], F32))
        pb = [Buf(f"pbm{i}") for i in range(8)]

        def B(i, lo=0, hi=512):
            return ps[:, i * 512 + lo:i * 512 + hi]

        def fm_proj(colbase, ncols, bank, slot):
            for kc in range(8):
                kb.mm(ps[0:ncols, bank * 512 + slot * 128:bank * 512 + (slot + 1) * 128], winb[:, kc, colbase:colbase + ncols],
                      hT[:, kc, :], kc == 0, kc == 7, [winb, hT], [pb[bank]])

        def state_part(u_needed, own):
            for kc in range(8):
                kb.mm(B(5), hT[:, kc, :], winb[:, kc, 1536:2048], kc == 0, kc == 7, [hT, winb], [pb[5]])
            for kc in range(8):
                kb.mm(B(6, 0, 256), hT[:, kc, :], winb[:, kc, 1280:1536], kc == 0, kc == 7, [hT, winb], [pb[6]])
            fm_proj(2560, 16, 4, 0)
            kb.copy(glrT[:], ps[0:16, 4 * 512:4 * 512 + 128], [pb[4]], [glrT], e="act")
            kb.mm(B(6, 256, 512), glrT[:], w2b[:], True, True, [glrT, w2b], [pb[6]])
            kb.tt(zb[:], B(6, 256, 512), gbB[:], ALU.add, [pb[6], gbB], [zb])
            kb.act(zb[:], zb[:], AF.Exp, [zb], [zb], scale=-1.0)
            kb.act(zb[:], zb[:], AF.Ln, [zb], [zb], bias=1.0)
            kb.ts(la[:], zb[:], -1.0 / 16.0, None, ALU.mult, None, [zb], [la])
            kb.mm(B(7, 0, 256), tri[:], la[:], True, True, [tri, la], [pb[7]])
            kb.mm(B(7, 256, 512), ones[:], la[:], True, True, [ones, la], [pb[7]])
            for h in range(4):
                kb.mm(ps[0:64, 4 * 512 + 384 + 2 * h:4 * 512 + 386 + 2 * h], la[:, h * 64:(h + 1) * 64], ones[:, 0:2], True, True,
                      [la, ones], [pb[4]])
            kb.act(eblc[:], ps[0:64, 4 * 512 + 384:4 * 512 + 392], AF.Exp, [pb[4]], [eblc])
            kb.act(enb_tm[:], B(7, 0, 256), AF.Exp, [pb[7]], [enb_tm], scale=-1.0)
            kb.act(ebl_tm[:], B(7, 256, 512), AF.Exp, [pb[7]], [ebl_tm])
            kb.tt(enb_tm[:], enb_tm[:], ebl_tm[:], ALU.mult, [enb_tm, ebl_tm], [enb_tm])
            kb.tt(kdec[:], B(6, 0, 256), enb_tm[:], ALU.mult, [pb[6], enb_tm], [kdec])
            kb.copy(vbf[:], B(5), [pb[5]], [vbf], e="act")

        def state_update():
            for h in range(4):
                kb.mm(ps[0:64, 2 * 512 + h * 128:2 * 512 + (h + 1) * 128], kdec[:, h * 64:(h + 1) * 64], vbf[:, h * 128:(h + 1) * 128],
                      True, True, [kdec, vbf], [pb[2]])
            for h in range(4):
                kb.stt(Sf[h][:], Sf[h][:], eblc[:, 2 * h:2 * h + 1], ps[0:64, 2 * 512 + h * 128:2 * 512 + (h + 1) * 128],
                       ALU.mult, ALU.add, [Sf[h], eblc, pb[2]], [Sf[h]])
                kb.copy(Sb[h][:], Sf[h][:], [Sf[h]], [Sb[h]], e="act")

        def conv_u(tokcol):
            for c in range(4):
                fm_proj(c * 128, 128, 0, c)
                fm_proj(512 + c * 128, 128, 1, c)
            kb.act(sg[:], B(1), AF.Sigmoid, [pb[1]], [sg])
            kb.tt(uT[:, :, 30 + tokcol:30 + tokcol + 128], B(0).rearrange("p (c t) -> p c t", t=128),
                  sg[:].rearrange("p (c t) -> p c t", t=128), ALU.mult, [pb[0], sg], [uT])

        for i in range(0 if full else NTP):
            kb.dma(xs[:], x_pre[i * 128:(i + 1) * 128, :], [x_pre], [xs])
            rms_to_hT(kb, xs, gB, ident, hf, sq, ss, ps, pb[0], pb[1], hT, 0)
            state_part(False, False)
            state_update()
            if i == NTP - 1:
                for c in range(4):
                    fm_proj(c * 128, 128, 0, c)
                    fm_proj(512 + c * 128, 128, 1, c)
                kb.act(sg[:], B(1), AF.Sigmoid, [pb[1]], [sg])
                kb.tt(uT[:, :, 0:30], B(0).rearrange("p (c t) -> p c t", t=128)[:, :, 98:128],
                      sg[:].rearrange("p (c t) -> p c t", t=128)[:, :, 98:128], ALU.mult, [pb[0], sg], [uT])
        for ii in range((NTP + NTO) if full else NTO):
            if full:
                isown = ii >= NTP
                i = 0
                srcx = x_own[(ii - NTP) * 128:(ii - NTP + 1) * 128, :] if isown else x_pre[ii * 128:(ii + 1) * 128, :]
                dsty = x_out[(ii - NTP) * 128:(ii - NTP + 1) * 128, :] if isown else x_out_pre[ii * 128:(ii + 1) * 128, :]
            else:
                i = ii
                srcx = x_own[i * 128:(i + 1) * 128, :]
                dsty = x_out[i * 128:(i + 1) * 128, :]
            kb.dma(xs[:], srcx, [x_own, x_pre], [xs])
            rms_to_hT(kb, xs, gB, ident, hf, sq, ss, ps, pb[0], pb[1], hT, 0)
            state_part(True, True)
            conv_u(i * 128)
            for h in range(4):
                fm_proj(1024 + h * 64, 64, 2, h)
                fm_proj(1280 + h * 64, 64, 3, h)
            for sl in range(4):
                fm_proj(2048 + sl * 128, 128, 4, sl)
            for h in range(4):
                kb.mm(ps[0:64, 5 * 512 + h * 128:5 * 512 + (h + 1) * 128], la[:, h * 64:(h + 1) * 64], tri[:], True, True,
                      [la, tri], [pb[5]])
            kb.act(eb[:], ps[0:64, 5 * 512:6 * 512], AF.Exp, [pb[5]], [eb])
            kb.act(enb[:], ps[0:64, 5 * 512:6 * 512], AF.Exp, [pb[5]], [enb], scale=-1.0)
            kb.stt(qt[:].rearrange("p h t -> p (h t)"), ps[0:64, 2 * 512:3 * 512], 0.125, eb[:], ALU.mult, ALU.mult, [pb[2], eb], [qt])
            kb.tt(kt[:].rearrange("p h t -> p (h t)"), ps[0:64, 3 * 512:4 * 512], enb[:], ALU.mult, [pb[3], enb], [kt])
            kb.act(sr[:], B(4), AF.Silu, [pb[4]], [sr])
            for h in range(4):
                kb.mm(B(0, h * 128, (h + 1) * 128), kt[:, h, :], qt[:, h, :], True, True, [kt, qt], [pb[0]])
            kb.tt(Am[:], B(0).rearrange("p (h t) -> p h t", t=128), tri[:].unsqueeze(1).to_broadcast([128, 4, 128]), ALU.mult,
                  [pb[0], tri], [Am])
            for h in range(4):
                kb.mm(B(1, h * 128, (h + 1) * 128), vbf[:, h * 128:(h + 1) * 128], Am[:, h, :], True, False, [vbf, Am], [pb[1]])
                kb.mm(B(1, h * 128, (h + 1) * 128), Sb[h][:], qt[:, h, :], False, True, [Sb[h], qt], [pb[1]])
            state_update()
            kb.act(osq[:], B(1), AF.Square, [pb[1]], [osq])
            kb.mm(B(3), ones[:], osq[:], True, True, [ones, osq], [pb[3]])
            kb.ts(rs[:], B(3), 1.0 / 128, 1e-6, ALU.mult, ALU.add, [pb[3]], [rs])
            kb.act(rs[:], rs[:], AF.Sqrt, [rs], [rs])
            kb.s.op("dve", lambda g_: g_.reciprocal(rs[:], rs[:]), [rs], [rs])
            kb.tt(t1[:], B(1), rs[:], ALU.mult, [pb[1], rs], [t1])
            kb.stt(ybT[:].rearrange("p h t -> p (h t)"), t1[:], cols[:, 12:13], sr[:], ALU.mult, ALU.mult, [t1, cols, sr], [ybT])
            for c in range(4):
                for j in range(31):
                    kb.mm(B(4, c * 128, (c + 1) * 128), diag[:, c * 31 + j, :], uT[:, c, i * 128 + j:i * 128 + j + 128],
                          j == 0, j == 30, [diag, uT], [pb[4]])
            for c in range(4):
                kb.ts(ycs[:, c, :], B(4, c * 128, (c + 1) * 128), cols[:, c:c + 1], None, ALU.add, None, [pb[4], cols], [ycs])
            kb.act(ysq[:], ycs[:], AF.Square, [ycs], [ysq])
            for c in range(4):
                kb.mm(B(5, 0, 128), ones[:], ycs[:, c, :], c == 0, c == 3, [ones, ycs], [pb[5]])
            for c in range(4):
                kb.mm(B(5, 128, 256), ones[:], ysq[:, c, :], c == 0, c == 3, [ones, ysq], [pb[5]])
            kb.ts(mean[:], B(5, 0, 128), 1.0 / 512, None, ALU.mult, None, [pb[5]], [mean])
            kb.tt(var[:], mean[:], mean[:], ALU.mult, [mean], [var])
            kb.stt(var[:], B(5, 128, 256), 1.0 / 512, var[:], ALU.mult, ALU.subtract, [pb[5], var], [var])
            kb.ts(var[:], var[:], 1e-6, None, ALU.add, None, [var], [var])
            kb.act(var[:], var[:], AF.Sqrt, [var], [var])
            kb.s.op("dve", lambda g_: g_.reciprocal(var[:], var[:]), [var], [var])
            kb.tt(ycs[:], ycs[:], mean[:].unsqueeze(1).to_broadcast([128, 4, 128]), ALU.subtract, [ycs, mean], [ycs])
            kb.tt(ycs[:], ycs[:], var[:].unsqueeze(1).to_broadcast([128, 4, 128]), ALU.mult, [ycs, var], [ycs])
            for c in range(4):
                kb.ts(ycs[:, c, :], ycs[:, c, :], cols[:, 4 + c:5 + c], cols[:, 8 + c:9 + c], ALU.mult, ALU.add, [ycs, cols], [ycs])
            kb.act(yaT[:], ycs[:], AF.Silu, [ycs], [yaT])
            for hh in range(2):
                for kc in range(8):
                    lhsT = yaT[:, kc, :] if kc < 4 else ybT[:, kc - 4, :]
                    kb.mm(B(6 + hh), lhsT, woutb[:, kc, hh * 512:(hh + 1) * 512], kc == 0, kc == 7, [yaT, ybT, woutb], [pb[6 + hh]])
            kb.tt(xo[:], ps[:, 6 * 512:8 * 512], xs[:], ALU.add, [pb[6], pb[7], xs], [xo])
            kb.dma(dsty, xo[:], [xo], [], e="pool")
            if full:
                kb.copy(uT[:, :, 0:30], uT[:, :, 128:158], [uT], [uT], e="pool")
        kb.s.barrier()


def fox_block(kb0, x_own, x_pre, x_out, gvec, w_in, w_out, fb, qg, kg, pflag, ident_d, tri_d, masks_d, T, TP, tag="fx"):
    nc = kb0.nc
    NTO = T // 128
    NTP = TP // 128
    NTA = NTO + NTP
    NSB = T // 512
    QT = kb0.dram("fxQT", [16, 65, T], BF16, "Internal")
    KT = kb0.dram("fxKT", [16, 65, TP + T], BF16, "Internal")
    VA = kb0.dram("fxVA", [NTA, 128, 16 * 65], BF16, "Internal")
    OT = kb0.dram("fxOT", [16, 64, T], BF16, "Internal")
    with ExitStack() as st0:
        kbp = kb0.scope(st0)
        negc = kbp.sb("negc", [128, NTA, 16], F32)
        ident = kbp.sb("identf", [128, 128], F32)
        kbp.dma(ident[:], ident_d[:], [ident_d], [ident])
        ones = kbp.sb("onesf", [128, 128], F32)
        kbp.memset(ones[:], 1.0, [ones])
        woutb = kbp.sb("woutbf", [128, 8, 1024], BF16)
        load_bf16_resident(kbp, woutb, lambda kc, c0, w: woutb[:, kc, c0:c0 + w], w_out, 8, 1024, "fwout")
        with ExitStack() as st:
            kb = kbp.scope(st)
            winb = kb.sb("winbf", [128, 8, 3088], BF16)
            load_bf16_resident(kb, winb, lambda kc, c0, w: winb[:, kc, c0:c0 + w], w_in, 8, 3088, "fwin")
            gB = kb.sb("gBf", [128, 1024], F32)
            kb.dma(gB[:], gvec.t.partition_broadcast(128)[:, 0, :], [gvec], [gB])
            fbB = kb.sb("fbB", [128, 16], F32)
            kb.dma(fbB[:], fb.t.partition_broadcast(128)[:, 0, :], [fb], [fbB])
            qgB = kb.sb("qgB", [128, 64], F32)
            kb.dma(qgB[:], qg.t.partition_broadcast(128)[:, 0, :], [qg], [qgB])
            kgB = kb.sb("kgB", [128, 64], F32)
            kb.dma(kgB[:], kg.t.partition_broadcast(128)[:, 0, :], [kg], [kgB])
            pfl = kb.sb("pfl", [128, 1], F32)
            kb.dma(pfl[:], pflag[:], [pflag], [pfl])
            tri = kb.sb("trif", [128, 128], F32)
            kb.dma(tri[:], tri_d[:], [tri_d], [tri])
            xs = kb.sb("xsf", [128, 1024], F32)
            hf = kb.sb("hff", [128, 1024], F32)
            sq = kb.sb("sqf", [128, 1024], BF16)
            ss = kb.sb("ssf", [128, 4], F32)
            hT = kb.sb("hTf", [128, 8, 128], BF16)
            nsq = kb.sb("nsq", [128, 1024], F32)
            nss = kb.sb("nss", [128, 16], F32)
            qa = kb.sb("qa", [128, 16, 65], F32)
            ka = kb.sb("ka", [128, 16, 65], F32)
            kb.memset(ka[:, :, 64:65], 1.0, [ka])
            va = kb.sb("va", [128, 16, 65], BF16)
            kb.memset(va[:, :, 0:1], 1.0, [va])
            qTs = kb.sb("qTs", [65, 16, 128], BF16)
            kTs = kb.sb("kTs", [65, 16, 128], BF16)
            lf = kb.sb("lf", [128, 16], F32)
            Lsum = kb.sb("Lsum", [128, 16], F32)
            kb.memset(Lsum[:], 0.0, [Lsum])
            ctile = kb.sb("ctile", [128, 16], F32)
            ps = st.enter_context(nc.psum_tensor("psf1", [128, 4096], F32))
            pb = [Buf(f"pbf{i}") for i in range(8)]

            def normed(dst, bank0, gt):
                src = ps[:, bank0 * 512:(bank0 + 2) * 512]
                kb.act(nsq[:], src, AF.Square, [pb[bank0], pb[bank0 + 1]], [nsq])
                kb.reduce(nss[:], nsq[:].rearrange("p (h d) -> p h d", d=64), ALU.add, [nsq], [nss])
                kb.ts(nss[:], nss[:], 1.0 / 64, 1e-6, ALU.mult, ALU.add, [nss], [nss])
                kb.act(nss[:], nss[:], AF.Sqrt, [nss], [nss])
                kb.s.op("dve", lambda g_: g_.reciprocal(nss[:], nss[:]), [nss], [nss])
                kb.tt(dst[:, :, 0:64], src.rearrange("p (h d) -> p h d", d=64), nss[:].unsqueeze(2).to_broadcast([128, 16, 64]),
                      ALU.mult, [pb[bank0], pb[bank0 + 1], nss], [dst])
                kb.tt(dst[:, :, 0:64], dst[:, :, 0:64], gt[:].unsqueeze(1).to_broadcast([128, 16, 64]), ALU.mult, [dst, gt], [dst])

            def transposed_store(src, dstT, dram, tokcol, b0):
                for h in range(16):
                    kb.tr(ps[0:65, b0 * 512 + h * 128:b0 * 512 + (h + 1) * 128], src[:, h, :], ident[:], [src, ident], [pb[b0 + h // 4]])
                for q4 in range(4):
                    kb.copy(dstT[:, q4 * 4:(q4 + 1) * 4, :], ps[0:65, (b0 + q4) * 512:(b0 + q4 + 1) * 512].rearrange("p (h t) -> p h t", t=128),
                            [pb[b0 + q4]], [dstT], e=("act" if q4 % 2 == 0 else "dve"))
                kb.dma(dram.t.rearrange("h r t -> r h t")[:, :, tokcol:tokcol + 128], dstT[:], [dstT], [], e="pool")

            for i in range(NTA):
                own = i >= NTP
                src = x_own[(i - NTP) * 128:(i - NTP + 1) * 128, :] if own else x_pre[i * 128:(i + 1) * 128, :]
                kb.dma(xs[:], src, [x_own, x_pre], [xs])
                rms_to_hT(kb, xs, gB, ident, hf, sq, ss, ps, pb[0], pb[1], hT, 0)
                for kc in range(8):
                    kb.mm(ps[:, 6 * 512:6 * 512 + 16], hT[:, kc, :], winb[:, kc, 3072:3088], kc == 0, kc == 7, [hT, winb], [pb[6]])
                kb.tt(lf[:], ps[:, 6 * 512:6 * 512 + 16], fbB[:], ALU.add, [pb[6], fbB], [lf])
                kb.act(lf[:], lf[:], AF.Exp, [lf], [lf], scale=-1.0)
                kb.act(lf[:], lf[:], AF.Ln, [lf], [lf], bias=1.0)
                kb.ts(lf[:], lf[:], -1.0, None, ALU.mult, None, [lf], [lf])
                kb.mm(ps[:, 6 * 512 + 16:6 * 512 + 32], tri[:], lf[:], True, False, [tri, lf], [pb[6]])
                kb.mm(ps[:, 6 * 512 + 16:6 * 512 + 32], ones[:], Lsum[:], False, True, [ones, Lsum], [pb[6]])
                kb.tt(Lsum[:], Lsum[:], lf[:], ALU.add, [Lsum, lf], [Lsum])
                kb.copy(ctile[:], ps[:, 6 * 512 + 16:6 * 512 + 32], [pb[6]], [ctile], e="act")
                if own:
                    kb.ts(negc[:, i, :], ctile[:], -1.0, None, ALU.mult, None, [ctile], [negc])
                else:
                    kb.ts(negc[:, i, :], ctile[:], -1.0, pfl[:, 0:1], ALU.mult, ALU.add, [ctile, pfl], [negc])
                for hh in range(2):
                    for kc in range(8):
                        kb.mm(ps[:, (2 + hh) * 512:(3 + hh) * 512], hT[:, kc, :], winb[:, kc, 1024 + hh * 512:1536 + hh * 512],
                              kc == 0, kc == 7, [hT, winb], [pb[2 + hh]])
                normed(ka, 2, kgB)
                for hh in range(2):
                    for kc in range(8):
                        kb.mm(ps[:, (4 + hh) * 512:(5 + hh) * 512], hT[:, kc, :], winb[:, kc, 2048 + hh * 512:2560 + hh * 512],
                              kc == 0, kc == 7, [hT, winb], [pb[4 + hh]])
                kb.copy(va[:, :, 1:65], ps[:, 4 * 512:6 * 512].rearrange("p (h d) -> p h d", d=64), [pb[4], pb[5]], [va], e="act")
                kb.dma(VA[i], va[:].rearrange("p h d -> p (h d)"), [va], [], e="pool")
                if own:
                    for hh in range(2):
                        for kc in range(8):
                            kb.mm(ps[:, hh * 512:(hh + 1) * 512], hT[:, kc, :], winb[:, kc, hh * 512:(hh + 1) * 512],
                                  kc == 0, kc == 7, [hT, winb], [pb[hh]])
                    normed(qa, 0, qgB)
                    kb.ts(qa[:, :, 64:65], ctile[:].unsqueeze(2), 8.0, None, ALU.mult, None, [ctile], [qa])
                transposed_store(ka, kTs, KT, i * 128, 2)
                if own:
                    transposed_store(qa, qTs, QT, (i - NTP) * 128, 2)
            kb.s.barrier()
        with ExitStack() as st:
            kb = kbp.scope(st)
            kts = kb.sb("kts", [65, TP + T], BF16)
            vas = kb.sb("vas", [128, NTA, 65], BF16)
            qts = kb.sb("qts", [65, T], BF16)
            masks = kb.sb("masksb", [128, 4, 512], F32)
            kb.dma(masks[:], masks_d[:], [masks_d], [masks])
            stmp = kb.sb("stmp", [128, 512], F32)
            pT = [kb.sb(f"pT{i}", [128, 512], BF16) for i in range(3)]
            osb = kb.sb("osb", [65, 512], F32)
            rden = kb.sb("rden", [1, 512], F32)
            oTs = kb.sb("oTs", [65, 512], BF16)
            ps = st.enter_context(nc.psum_tensor("psf2", [128, 4096], F32))
            pb = [Buf(f"pbg{i}") for i in range(8)]
            VAv = VA.t.rearrange("n p (h d) -> p n h d", d=65)
            cnt = 0
            for h in range(16):
                kb.dma(kts[:], KT[h], [], [kts])
                kb.dma(qts[:], QT[h], [], [qts])
                kb.dma(vas[:], VAv[:, :, h, :], [], [vas])
                for j in range(NSB):
                    nkb = NTP + 4 * j + 4
                    ob = 4 + (j % 2)
                    def s_stage(kb_, a):
                        kb.mm(ps[:, a * 512:(a + 1) * 512], kts[:, kb_ * 128:(kb_ + 1) * 128], qts[:, j * 512:(j + 1) * 512], True, True,
                              [kts, qts], [pb[a]])
                        m = kb_ - NTP - 4 * j
                        if m >= 0:
                            kb.tt(stmp[:], ps[:, a * 512:(a + 1) * 512], masks[:, m, :], ALU.add, [pb[a], masks], [stmp])
                            kb.act(pT[a][:], stmp[:], AF.Exp, [stmp, negc], [pT[a]], bias=negc[:, kb_, h:h + 1], scale=0.125)
                        else:
                            kb.act(pT[a][:], ps[:, a * 512:(a + 1) * 512], AF.Exp, [pb[a], negc], [pT[a]], bias=negc[:, kb_, h:h + 1], scale=0.125)

                    def pv_stage(kb_, a):
                        kb.mm(ps[0:65, ob * 512:(ob + 1) * 512], vas[:, kb_, :], pT[a][:], kb_ == 0, kb_ == nkb - 1, [vas, pT[a]], [pb[ob]])

                    NBUF = 3
                    for kb_ in range(min(NBUF - 1, nkb)):
                        s_stage(kb_, kb_ % NBUF)
                    for kb_ in range(nkb):
                        if kb_ + NBUF - 1 < nkb:
                            s_stage(kb_ + NBUF - 1, (kb_ + NBUF - 1) % NBUF)
                        pv_stage(kb_, kb_ % NBUF)
                    kb.copy(osb[:], ps[0:65, ob * 512:(ob + 1) * 512], [pb[ob]], [osb], e="act")
                    kb.s.op("dve", lambda g_: g_.reciprocal(rden[:], osb[0:1, :]), [osb], [rden])
                    kb.mm(ps[0:65, 6 * 512:7 * 512], ones[0:1, 0:65], rden[:], True, True, [ones, rden], [pb[6]])
                    kb.tt(oTs[:], osb[:], ps[0:65, 6 * 512:7 * 512], ALU.mult, [osb, pb[6]], [oTs])
                    kb.dma(OT[h, :, j * 512:(j + 1) * 512], oTs[1:65, :], [oTs], [], e="pool")
            kb.s.barrier()
        with ExitStack() as st:
            kb = kbp.scope(st)
            oTt = [kb.sb(f"oTt{i}", [128, 8, 128], BF16) for i in range(2)]
            xs2 = [kb.sb(f"xs2{i}", [128, 1024], F32) for i in range(2)]
            xo = [kb.sb(f"xof{i}", [128, 1024], F32) for i in range(2)]
            ps = st.enter_context(nc.psum_tensor("psf3", [128, 4096], F32))
            pb = [Buf(f"pbh{i}") for i in range(8)]
            OTv = OT.t.rearrange("(p two) d t -> (two d) p t", two=2)
            for i in range(NTO):
                a = i % 2
                kb.dma(oTt[a][:], OTv[:, :, i * 128:(i + 1) * 128], [], [oTt[a]])
                kb.dma(xs2[a][:], x_own[i * 128:(i + 1) * 128, :], [x_own], [xs2[a]])
                for hh in range(2):
                    for p in range(8):
                        kb.mm(ps[:, (2 * a + hh) * 512:(2 * a + hh + 1) * 512], oTt[a][:, p, :], woutb[:, p, hh * 512:(hh + 1) * 512],
                              p == 0, p == 7, [oTt[a], woutb], [pb[2 * a + hh]])
                kb.tt(xo[a][:], ps[:, 2 * a * 512:(2 * a + 2) * 512], xs2[a][:], ALU.add, [pb[2 * a], pb[2 * a + 1], xs2[a]], [xo[a]])
                kb.dma(x_out[i * 128:(i + 1) * 128, :], xo[a][:], [xo[a]], [], e="pool")
            kb.s.barrier()


TOK = 4096


def _consts():
    k = np.arange(128)[:, None]
    q = np.arange(512)[None, :]
    fm = np.zeros((128, 4, 512), np.float32)
    for mm in range(4):
        fm[:, mm, :] = np.where((mm * 128 + k) <= q, 0.0, -240000.0)
    return {
        "ident": np.eye(128, dtype=np.float32),
        "iota": np.tile(np.arange(128, dtype=np.float32), (128, 1)),
        "tri": np.triu(np.ones((128, 128), np.float32)),
        "masks": fm,
    }


def _col4(v):
    return np.ascontiguousarray(v.reshape(4, 128).T)


_NC_CACHE = {}


def _build_fused(T):
    key = ("fused", T)
    if key in _NC_CACHE:
        return _NC_CACHE[key]
    nc = bass.Bass("TRN2", target_bir_lowering=False)
    with ExitStack() as st:
        kb = KB(nc, st)
        D = lambda n, s: kb.dram(n, s, F32, "ExternalInput")
        I = lambda n: kb.dram(n, [T, 1024], F32, "Internal")
        x, xp = D("x", [T, 1024]), D("xp", [T, 1024])
        y = kb.dram("y", [T, 1024], F32, "ExternalOutput")
        ident, iota, tri, masks = D("ident", [128, 128]), D("iota", [128, 128]), D("tri", [128, 128]), D("masks", [128, 4, 512])
        x1o, x1p, x2o, x2p, x3o = I("x1o"), I("x1p"), I("x2o"), I("x2p"), I("x3o")
        mixer0_block(kb, x, xp, x1o, D("m_g", [1, 1024]), D("m_w_in", [1024, 2576]), D("m_w_out", [1024, 1024]),
                     D("m_convwT", [128, 4, 31]), D("m_convb", [128, 4]), D("m_lng", [128, 4]), D("m_lnb", [128, 4]),
                     D("m_w2", [16, 256]), D("m_gateb", [1, 256]), D("m_glag", [128, 1]), ident, tri, T, T, x_out_pre=x1p)
        tabs0 = peer_tables(kb, D("p0_uT", [1024, 16384]), D("p0_v", [16384, 1024]), "L0")
        peer_block(kb, None, None, D("p0_g", [1, 1024]), D("p0_wq", [1024, 2048]), D("p0_keysT", [128, 2048]), None, None,
                   ident, iota, T, "L0", tables=tabs0, jobs=[(x1p, x2p, T), (x1o, x2o, T)])
        fox_block(kb, x2o, x2p, x3o, D("f_g", [1, 1024]), D("f_w_in", [1024, 3088]), D("f_w_out", [1024, 1024]), D("f_fb", [1, 16]),
                  D("f_qg", [1, 64]), D("f_kg", [1, 64]), D("pflag", [128, 1]), ident, tri, masks, T, T)
        peer_block(kb, x3o, y, D("p1_g", [1, 1024]), D("p1_wq", [1024, 2048]), D("p1_keysT", [128, 2048]),
                   D("p1_uT", [1024, 16384]), D("p1_v", [16384, 1024]), ident, iota, T, "L1")
        kb.s.emit()
        print("fused program: ninst", kb.s.ninst, "nsem", kb.s.nsem, flush=True)
    _NC_CACHE[key] = nc
    return nc


def _fused_common(inp, cst):
    c = {"m_g": np.ascontiguousarray(inp["ev_norm_mix"][0][None]), "m_w_in": np.ascontiguousarray(inp["ev_w_in"][0]),
         "m_w_out": np.ascontiguousarray(inp["ev_w_out"][0]),
         "m_convwT": np.ascontiguousarray(inp["ev_conv_w"][0].T.reshape(4, 128, 31).transpose(1, 0, 2)),
         "m_convb": _col4(inp["ev_conv_b"][0]), "m_lng": _col4(inp["ev_conv_ln_g"][0]), "m_lnb": _col4(inp["ev_conv_ln_b"][0]),
         "m_w2": np.ascontiguousarray(inp["ev_gate_w2"][0]), "m_gateb": np.ascontiguousarray(inp["ev_gate_b"][0][None]),
         "m_glag": np.ascontiguousarray(inp["ev_gla_norm_g"][0][:, None]),
         "f_g": np.ascontiguousarray(inp["od_norm_mix"][0][None]), "f_w_in": np.ascontiguousarray(inp["od_w_in"][0]),
         "f_w_out": np.ascontiguousarray(inp["od_w_out"][0]), "f_fb": np.ascontiguousarray(inp["od_fgate_b"][0][None]),
         "f_qg": np.ascontiguousarray(inp["od_q_norm_g"][0][None]), "f_kg": np.ascontiguousarray(inp["od_k_norm_g"][0][None]),
         "ident": cst["ident"], "iota": cst["iota"], "tri": cst["tri"], "masks": cst["masks"]}
    for L in range(2):
        p = _peer_inputs(inp, L, cst)
        for k in ("g", "wq", "keysT", "uT", "v"):
            c[f"p{L}_{k}"] = p[k]
    return c


def _build(kind):
    if kind in _NC_CACHE:
        return _NC_CACHE[kind]
    nc = bass.Bass("TRN2", target_bir_lowering=False)
    with ExitStack() as st:
        kb = KB(nc, st)
        D = lambda n, s: kb.dram(n, s, F32, "ExternalInput")
        T = TOK
        if kind == "m0":
            x, xp = D("x", [T, 1024]), D("xp", [T, 1024])
            y = kb.dram("y", [T, 1024], F32, "ExternalOutput")
            mixer0_block(kb, x, xp, y, D("g", [1, 1024]), D("w_in", [1024, 2576]), D("w_out", [1024, 1024]), D("convwT", [128, 4, 31]),
                         D("convb", [128, 4]), D("lng", [128, 4]), D("lnb", [128, 4]), D("w2", [16, 256]), D("gateb", [1, 256]),
                         D("glag", [128, 1]), D("ident", [128, 128]), D("tri", [128, 128]), T, T)
        elif kind == "peer":
            x = D("x", [T, 1024])
            y = kb.dram("y", [T, 1024], F32, "ExternalOutput")
            peer_block(kb, x, y, D("g", [1, 1024]), D("wq", [1024, 2048]), D("keysT", [128, 2048]), D("uT", [1024, 16384]),
                       D("v", [16384, 1024]), D("ident", [128, 128]), D("iota", [128, 128]), T, "p0")
        elif kind == "fox":
            x, xp = D("x", [T, 1024]), D("xp", [T, 1024])
            y = kb.dram("y", [T, 1024], F32, "ExternalOutput")
            fox_block(kb, x, xp, y, D("g", [1, 1024]), D("w_in", [1024, 3088]), D("w_out", [1024, 1024]), D("fb", [1, 16]),
                      D("qg", [1, 64]), D("kg", [1, 64]), D("pflag", [128, 1]), D("ident", [128, 128]), D("tri", [128, 128]),
                      D("masks", [128, 4, 512]), T, T)
        kb.s.emit()
    _NC_CACHE[kind] = nc
    return nc


def _shards(xfull):
    own, pre = [], []
    for c in range(NCORES):
        b, half = c // 2, c % 2
        own.append(np.ascontiguousarray(xfull[b, half * TOK:(half + 1) * TOK]))
        pre.append(np.ascontiguousarray(xfull[b, 0:TOK]) if half == 1 else np.zeros((TOK, 1024), np.float32))
    return own, pre


def _gather(res):
    out = np.empty((4, 8192, 1024), np.float32)
    for c in range(NCORES):
        b, half = c // 2, c % 2
        out[b, half * TOK:(half + 1) * TOK] = res.results[c]["y"]
    return out


def _run(kind, per_core):
    nc = _build(kind)
    return run_bass_kernel_spmd(nc, per_core, core_ids=list(range(NCORES)))


def _peer_inputs(inp, layer, cst):
    keys = inp["peer_keys"][layer]
    return {"g": np.ascontiguousarray(inp["ffn_norm"][layer][None]), "wq": np.ascontiguousarray(inp["peer_wq"][layer]),
            "keysT": np.ascontiguousarray(keys.transpose(3, 0, 1, 2).reshape(128, 2048)),
            "uT": np.ascontiguousarray(inp["peer_u"][layer].T), "v": np.ascontiguousarray(inp["peer_v"][layer]),
            "ident": cst["ident"], "iota": cst["iota"]}


def kernel(**inp):
    inp = {k: np.asarray(v, dtype=np.float32) for k, v in inp.items()}
    cst = _consts()
    own, pre = _shards(inp["x"])
    common = _fused_common(inp, cst)
    pfl = [np.full((128, 1), 0.0 if c % 2 == 1 else -30000.0, np.float32) for c in range(NCORES)]
    nc = _build_fused(TOK)
    res = run_bass_kernel_spmd(nc, [dict(common, x=own[c], xp=pre[c], pflag=pfl[c]) for c in range(NCORES)],
                               core_ids=list(range(NCORES)))
    return _gather(res)


def kernel_unfused(**inp):
    inp = {k: np.asarray(v, dtype=np.float32) for k, v in inp.items()}
    cst = _consts()
    x = inp["x"]
    own, pre = _shards(x)
    common = {"g": np.ascontiguousarray(inp["ev_norm_mix"][0][None]), "w_in": np.ascontiguousarray(inp["ev_w_in"][0]),
              "w_out": np.ascontiguousarray(inp["ev_w_out"][0]),
              "convwT": np.ascontiguousarray(inp["ev_conv_w"][0].T.reshape(4, 128, 31).transpose(1, 0, 2)),
              "convb": _col4(inp["ev_conv_b"][0]), "lng": _col4(inp["ev_conv_ln_g"][0]), "lnb": _col4(inp["ev_conv_ln_b"][0]),
              "w2": np.ascontiguousarray(inp["ev_gate_w2"][0]), "gateb": np.ascontiguousarray(inp["ev_gate_b"][0][None]),
              "glag": np.ascontiguousarray(inp["ev_gla_norm_g"][0][:, None]), "ident": cst["ident"], "tri": cst["tri"]}
    x1 = _gather(_run("m0", [dict(common, x=own[c], xp=pre[c]) for c in range(NCORES)]))
    own, _ = _shards(x1)
    common = _peer_inputs(inp, 0, cst)
    x2 = _gather(_run("peer", [dict(common, x=own[c]) for c in range(NCORES)]))
    own, pre = _shards(x2)
    common = {"g": np.ascontiguousarray(inp["od_norm_mix"][0][None]), "w_in": np.ascontiguousarray(inp["od_w_in"][0]),
              "w_out": np.ascontiguousarray(inp["od_w_out"][0]), "fb": np.ascontiguousarray(inp["od_fgate_b"][0][None]),
              "qg": np.ascontiguousarray(inp["od_q_norm_g"][0][None]), "kg": np.ascontiguousarray(inp["od_k_norm_g"][0][None]),
              "ident": cst["ident"], "tri": cst["tri"], "masks": cst["masks"]}
    pfl = [np.full((128, 1), 0.0 if c % 2 == 1 else -30000.0, np.float32) for c in range(NCORES)]
    x3 = _gather(_run("fox", [dict(common, x=own[c], xp=pre[c], pflag=pfl[c]) for c in range(NCORES)]))
    own, _ = _shards(x3)
    common = _peer_inputs(inp, 1, cst)
    x4 = _gather(_run("peer", [dict(common, x=own[c]) for c in range(NCORES)]))
    return x4
```

```python
from contextlib import ExitStack

import numpy as np
import concourse.bass as bass
import concourse.mybir as mybir
from concourse.bass_utils import run_bass_kernel_spmd

F32 = mybir.dt.float32
BF16 = mybir.dt.bfloat16
U32 = mybir.dt.uint32
I32 = mybir.dt.int32
ALU = mybir.AluOpType
AF = mybir.ActivationFunctionType
AX = mybir.AxisListType

NCORES = 8
EPOCH = 30000
ENGS = ("pe", "dve", "act", "pool", "sp")


class Buf:
    __slots__ = ("w", "r", "name")

    def __init__(self, name=""):
        self.w = None
        self.r = []
        self.name = name


class TT:
    def __init__(self, t, name):
        self.t = t
        self.b = Buf(name)

    def __getitem__(self, k):
        return self.t[k]


class Sched:
    def __init__(self, nc, stack):
        self.nc = nc
        self.stack = stack
        self.q = {e: [] for e in ENGS}
        self.csem = {e: None for e in ENGS}
        self.ccnt = {e: 0 for e in ENGS}
        self.dsem = {e: [] for e in ENGS}
        self.dcnt = {e: [] for e in ENGS}
        self.drr = {e: 0 for e in ENGS}
        self.seen = {e: {} for e in ENGS}
        self.nsem = 0
        self.ninst = 0
        self.defer = None

    def _newsem(self, nm):
        self.nsem += 1
        return self.stack.enter_context(self.nc.semaphore(f"{nm}{self.nsem}"))

    def _ticket(self, e, dma):
        if not dma:
            if self.csem[e] is None or self.ccnt[e] >= EPOCH:
                self.csem[e] = self._newsem("c" + e)
                self.ccnt[e] = 0
            self.ccnt[e] += 1
            return (self.csem[e], self.ccnt[e], e, 1)
        if not self.dsem[e]:
            self.dsem[e] = [self._newsem("d" + e) for _ in range(8)]
            self.dcnt[e] = [0] * 8
        i = self.drr[e] % 8
        self.drr[e] += 1
        if self.dcnt[e][i] + 16 >= EPOCH:
            self.dsem[e][i] = self._newsem("d" + e)
            self.dcnt[e][i] = 0
        self.dcnt[e][i] += 16
        return (self.dsem[e][i], self.dcnt[e][i], e + "_dma", 16)

    def op(self, e, fn, reads=(), writes=(), dma=False):
        if self.defer is not None:
            self.defer.append((e, fn, list(reads), list(writes), dma))
            return None
        deps = {}

        def add(t):
            if t is None:
                return
            sem, val, src, _ = t
            if src == "pe" and e == "pe" and not dma:
                return
            k = id(sem)
            if self.seen[e].get(k, 0) >= val:
                return
            if k not in deps or deps[k][1] < val:
                deps[k] = (sem, val)

        for b in reads:
            b = b.b if isinstance(b, TT) else b
            add(b.w)
        for b in writes:
            b = b.b if isinstance(b, TT) else b
            add(b.w)
            for t in b.r:
                add(t)
        if dma and self.dsem[e]:
            i = self.drr[e] % 8
            if self.dcnt[e][i] > 0 and self.dcnt[e][i] + 16 < EPOCH:
                sem_, val_ = self.dsem[e][i], self.dcnt[e][i]
                k = id(sem_)
                if self.seen[e].get(k, 0) < val_ and (k not in deps or deps[k][1] < val_):
                    deps[k] = (sem_, val_)
        waits = list(deps.values())
        for sem, val in waits:
            self.seen[e][id(sem)] = val
        t = self._ticket(e, dma)
        self.q[e].append((waits, fn, t[0], t[3]))
        self.ninst += 1 + len(waits)
        for b in reads:
            b = b.b if isinstance(b, TT) else b
            b.r = [x for x in b.r if x[0] is not t[0]] + [t]
        for b in writes:
            b = b.b if isinstance(b, TT) else b
            b.w = t
            b.r = []
        return t

    def replay(self, lst, n):
        k = min(n, len(lst))
        for e, fn, reads, writes, dma in lst[:k]:
            self.op(e, fn, reads, writes, dma)
        del lst[:k]

    def barrier(self):
        waits = []
        for e in ENGS:
            if self.csem[e] is not None and self.ccnt[e] > 0:
                waits.append((self.csem[e], self.ccnt[e]))
            for sem, c in zip(self.dsem[e], self.dcnt[e]):
                if c > 0:
                    waits.append((sem, c))
        for e in ENGS:
            self.q[e].append((list(waits), None, None, 0))
            for sem, val in waits:
                self.seen[e][id(sem)] = max(self.seen[e].get(id(sem), 0), val)

    def final_wait(self, e, bufs):
        waits = []
        for b in bufs:
            b = b.b if isinstance(b, TT) else b
            for t in [b.w] + list(b.r):
                if t is not None:
                    waits.append((t[0], t[1]))
        self.q[e].append((waits, None, None, 0))

    def emit(self):
        nc = self.nc
        q = self.q

        def run(e, eng):
            for waits, fn, sem, inc in q[e]:
                for s, v in waits:
                    eng.wait_ge(s, v)
                if fn is not None:
                    ins = fn(eng)
                    ins.then_inc(sem, inc)

        with nc.Block() as block:

            @block.tensor
            def _(eng):
                run("pe", eng)

            @block.vector
            def _(eng):
                run("dve", eng)

            @block.scalar
            def _(eng):
                run("act", eng)

            @block.gpsimd
            def _(eng):
                run("pool", eng)

            @block.sync
            def _(eng):
                run("sp", eng)


class KB:
    def __init__(self, nc, stack, sched=None):
        self.nc = nc
        self.stack = stack
        self.s = sched if sched is not None else Sched(nc, stack)

    def scope(self, stack):
        return KB(self.nc, stack, self.s)

    def sb(self, name, shape, dt):
        t = self.stack.enter_context(self.nc.sbuf_tensor(name, list(shape), dt))
        return TT(t, name)

    def dram(self, name, shape, dt, kind):
        t = self.nc.dram_tensor(name, list(shape), dt, kind=kind)
        return TT(t.ap(), name)

    def dma(self, out, in_, reads, writes, e="sp", **kw):
        return self.s.op(e, lambda g: g.dma_start(out=out, in_=in_, **kw), reads, writes, dma=True)

    def mm(self, out, lhsT, rhs, start, stop, reads, writes):
        return self.s.op("pe", lambda g: g.matmul(out, lhsT, rhs, start=start, stop=stop), reads, writes)

    def tr(self, out, in_, ident, reads, writes):
        return self.s.op("pe", lambda g: g.transpose(out, in_, ident), reads, writes)

    def act(self, out, in_, func, reads, writes, bias=None, scale=None, accum_out=None):
        kw = {}
        if bias is not None:
            kw["bias"] = bias
        if scale is not None:
            kw["scale"] = scale
        if accum_out is not None:
            kw["accum_out"] = accum_out
        return self.s.op("act", lambda g: g.activation(out, in_, func, **kw), reads, writes)

    def tt(self, out, in0, in1, op, reads, writes, e="dve"):
        return self.s.op(e, lambda g: g.tensor_tensor(out, in0, in1, op), reads, writes)

    def ts(self, out, in0, s1, s2, op0, op1, reads, writes, e="dve", accum_out=None):
        if op1 is None:
            return self.s.op(e, lambda g: g.tensor_scalar(out, in0, s1, None, op0), reads, writes)
        if accum_out is not None:
            return self.s.op(e, lambda g: g.tensor_scalar(out, in0, s1, s2, op0, op1, accum_out), reads, writes)
        return self.s.op(e, lambda g: g.tensor_scalar(out, in0, s1, s2, op0, op1), reads, writes)

    def stt(self, out, in0, scalar, in1, op0, op1, reads, writes, e="dve"):
        return self.s.op(e, lambda g: g.scalar_tensor_tensor(out, in0, scalar, in1, op0, op1), reads, writes)

    def copy(self, out, in_, reads, writes, e="dve"):
        if e == "act":
            return self.s.op(e, lambda g: g.copy(out, in_), reads, writes)
        return self.s.op(e, lambda g: g.tensor_copy(out, in_), reads, writes)

    def memset(self, ap, val, writes, e="dve"):
        return self.s.op(e, lambda g: g.memset(ap, val), (), writes)

    def reduce(self, out, in_, op, reads, writes, axis=AX.X, e="dve"):
        return self.s.op(e, lambda g: g.tensor_reduce(out, in_, axis, op), reads, writes)


def to_bf16_dram(kb, src, dst, R, C, tag):
    with ExitStack() as st:
        k = kb.scope(st)
        W = 2048
        stg = [k.sb(f"cv_in{tag}{i}", [128, W], F32) for i in range(2)]
        outb = [k.sb(f"cv_out{tag}{i}", [128, W], BF16) for i in range(2)]
        n = 0
        for r in range(R // 128):
            for c0 in range(0, C, W):
                w = min(W, C - c0)
                i = n % 2
                k.dma(stg[i][:, 0:w], src[r * 128:(r + 1) * 128, c0:c0 + w], [src], [stg[i]])
                k.copy(outb[i][:, 0:w], stg[i][:, 0:w], [stg[i]], [outb[i]], e=("dve" if n % 2 == 0 else "act"))
                k.dma(dst[r * 128:(r + 1) * 128, c0:c0 + w], outb[i][:, 0:w], [outb[i]], [], e="pool")
                n += 1
        k.s.barrier()


def load_bf16_resident(kb, dst_tt, dst_ap_fn, src, nrows_blocks, C, tag):
    with ExitStack() as st:
        k = kb.scope(st)
        W = 2048
        stg = [k.sb(f"ld_in{tag}{i}", [128, W], F32) for i in range(2)]
        n = 0
        for kc in range(nrows_blocks):
            for c0 in range(0, C, W):
                w = min(W, C - c0)
                i = n % 2
                k.dma(stg[i][:, 0:w], src[kc * 128:(kc + 1) * 128, c0:c0 + w], [src], [stg[i]])
                k.copy(dst_ap_fn(kc, c0, w), stg[i][:, 0:w], [stg[i]], [dst_tt], e=("dve" if n % 2 == 0 else "act"))
                n += 1
        k.s.barrier()


def peer_tables(kb0, uT, vtab, tag):
    us = kb0.dram(f"us{tag}", [1024, 16384], BF16, "Internal")
    vs = kb0.dram(f"vs{tag}", [16384, 1024], BF16, "Internal")
    to_bf16_dram(kb0, uT, us, 1024, 16384, tag + "u")
    to_bf16_dram(kb0, vtab, vs, 16384, 1024, tag + "v")
    return us, vs


def peer_block(kb0, x_in, x_out, gvec, wq, keysT, uT, vtab, ident_d, iota_d, T, tag, G=256, CH=2, OHT=16, tables=None, jobs=None, NSL=3):
    nc = kb0.nc
    if jobs is None:
        jobs = [(x_in, x_out, T)]
    with ExitStack() as st:
        kb = kb0.scope(st)
        TPG = G // 128
        if tables is None:
            tables = peer_tables(kb, uT, vtab, tag)
        us, vs = tables
        wqb = kb.sb("wqb" + tag, [128, 8, 2048], BF16)
        load_bf16_resident(kb, wqb, lambda kc, c0, w: wqb[:, kc, c0:c0 + w], wq, 8, 2048, tag + "wq")
        kTb = kb.sb("kTb" + tag, [128, 16 * 128], BF16)
        load_bf16_resident(kb, kTb, lambda kc, c0, w: kTb[:, c0:c0 + w], keysT, 1, 2048, tag + "kt")
        gB = kb.sb("gB" + tag, [128, 1024], F32)
        kb.dma(gB[:], gvec.t.partition_broadcast(128)[:, 0, :], [gvec], [gB])
        ident = kb.sb("ident" + tag, [128, 128], F32)
        kb.dma(ident[:], ident_d[:], [ident_d], [ident])
        iota = kb.sb("iota" + tag, [128, 128], F32)
        kb.dma(iota[:], iota_d[:], [iota_d], [iota])
        iotab = kb.sb("iotab" + tag, [128, 128], BF16)
        kb.copy(iotab[:], iota[:], [iota], [iotab])

        xs = [[kb.sb(f"xs{tag}{p}{j}", [128, 1024], F32) for j in range(TPG)] for p in range(2)]
        hT = [kb.sb(f"hT{tag}{p}", [128, 8, G], BF16) for p in range(2)]
        trioT = [[kb.sb(f"trioT{tag}{p}{j}", [128, 3, 128], F32) for j in range(TPG)] for p in range(2)]
        hf = kb.sb("hf" + tag, [128, 1024], F32)
        ss = kb.sb("ss" + tag, [128, 4], F32)
        qT = kb.sb("qT" + tag, [128, 16, 128], BF16)
        sc = kb.sb("sc" + tag, [128, 16, 128], F32)
        tmp4 = kb.sb("tmp4" + tag, [128, 4, 256], F32)
        tmpb = [Buf() for _ in range(4)]
        scb = [Buf() for _ in range(4)]
        v16 = kb.sb("v16" + tag, [128, 16, 16], F32)
        i16 = kb.sb("i16" + tag, [128, 16, 16], U32)
        i16f = kb.sb("i16f" + tag, [128, 16, 16], F32)
        cand = kb.sb("cand" + tag, [128, 8, 16, 16], F32)
        s16 = kb.sb("s16" + tag, [128, 8, 16], F32)
        j16 = kb.sb("j16" + tag, [128, 8, 16], U32)
        jaf = kb.sb("jaf" + tag, [128, 8, 16], F32)
        ja = kb.sb("ja" + tag, [128, 8, 16], U32)
        jb = kb.sb("jb" + tag, [128, 8, 16], U32)
        jbf = kb.sb("jbf" + tag, [128, 8, 16], F32)
        eq = cand
        trio = kb.sb("trio" + tag, [128, 3, 128], F32)
        zz = kb.sb("zz" + tag, [128, 8], F32)
        oh1 = kb.sb("oh1" + tag, [128, OHT, 128], BF16)
        oh2 = kb.sb("oh2" + tag, [128, OHT, 128], BF16)
        ohb = [Buf() for _ in range(OHT)]
        ohb2 = [Buf() for _ in range(OHT)]
        WT = kb.sb("WT" + tag, [128, 128, G], BF16)
        WTb = [Buf() for _ in range(G // 16)]
        ub = [kb.sb(f"ub{tag}{i}", [128, 8, CH * 128], BF16) for i in range(NSL)]
        vb = [kb.sb(f"vb{tag}{i}", [128, CH, 1024], BF16) for i in range(NSL)]
        actT = [kb.sb(f"actT{tag}{i}", [128, G], BF16) for i in range(2)]
        ct = [kb.sb(f"ct{tag}{i}", [128, G], BF16) for i in range(2)]
        ps = st.enter_context(nc.psum_tensor("ps" + tag, [128, 4096], F32))
        pb = [Buf(f"pb{i}") for i in range(8)]

        def bank(i, w=512):
            return ps[:, i * 512:i * 512 + w]

        us_v = us.t.rearrange("(kc p) e -> p kc e", p=128)
        vs_v = vs.t.rearrange("(e1 p) d -> p e1 d", p=128)

        def topk_chain(vals, vdst, idst, tm, vb_, ib_, tb_, srcb):
            yield kb.s.op("dve", lambda g_: g_.max(out=vdst[:, 0:8], in_=vals), [srcb], [vb_])
            yield kb.s.op("dve", lambda g_: g_.match_replace(out=tm, in_to_replace=vdst[:, 0:8], in_values=vals,
                                                              imm_value=-1e30), [srcb, vb_], [tb_])
            yield kb.s.op("dve", lambda g_: g_.max(out=vdst[:, 8:16], in_=tm), [tb_], [vb_])
            yield kb.s.op("dve", lambda g_: g_.max_index(out=idst[:, 0:8], in_max=vdst[:, 0:8], in_values=vals), [srcb, vb_], [ib_])
            yield kb.s.op("dve", lambda g_: g_.max_index(out=idst[:, 8:16], in_max=vdst[:, 8:16], in_values=vals), [srcb, vb_], [ib_])

        def run_interleaved(chains, width=4):
            live = []
            chains = list(chains)
            while chains or live:
                while chains and len(live) < width:
                    live.append(chains.pop(0))
                nxt = []
                for ch_ in live:
                    try:
                        next(ch_)
                        nxt.append(ch_)
                    except StopIteration:
                        pass
                live = nxt

        def p1a(p, x_in, g, j):
            tok0 = g * G + j * 128
            xt = xs[p][j]
            kb.dma(xt[:], x_in[tok0:tok0 + 128, :], [x_in], [xt])
            kb.act(hf[:], xt[:], AF.Square, [xt], [hf])
            kb.reduce(ss[:, 0:1], hf[:], ALU.add, [hf], [ss])
            kb.ts(ss[:, 1:2], ss[:, 0:1], 1.0 / 1024, 1e-6, ALU.mult, ALU.add, [ss], [ss])
            kb.act(ss[:, 3:4], ss[:, 1:2], AF.Sqrt, [ss], [ss])
            kb.s.op("dve", lambda g_: g_.reciprocal(ss[:, 2:3], ss[:, 3:4]), [ss], [ss])
            kb.stt(hf[:], xt[:], ss[:, 2:3], gB[:], ALU.mult, ALU.mult, [xt, ss, gB], [hf])
            for hb in range(2):
                for k4 in range(4):
                    kc = hb * 4 + k4
                    kb.tr(ps[:, (6 + hb) * 512 + k4 * 128:(6 + hb) * 512 + (k4 + 1) * 128], hf[:, kc * 128:(kc + 1) * 128], ident[:],
                          [hf, ident], [pb[6 + hb]])
                kb.copy(hT[p][:, hb * 4:(hb + 1) * 4, j * 128:(j + 1) * 128],
                        bank(6 + hb).rearrange("p (k t) -> p k t", t=128), [pb[6 + hb]], [hT[p]], e="act")
            for q4 in range(4):
                bk = 6 + q4 % 2
                for c4 in range(4):
                    c = q4 * 4 + c4
                    for kc in range(8):
                        kb.mm(ps[:, bk * 512 + c4 * 128:bk * 512 + (c4 + 1) * 128], wqb[:, kc, c * 128:(c + 1) * 128],
                              hT[p][:, kc, j * 128:(j + 1) * 128], kc == 0, kc == 7, [wqb, hT[p]], [pb[bk]])
                kb.copy(qT[:, q4 * 4:(q4 + 1) * 4, :], bank(bk).rearrange("p (k t) -> p k t", t=128), [pb[bk]], [qT], e="act")
            for q4 in range(4):
                bk = 6 + q4 % 2
                for c4 in range(4):
                    c = q4 * 4 + c4
                    kb.mm(ps[:, bk * 512 + c4 * 128:bk * 512 + (c4 + 1) * 128], qT[:, c, :], kTb[:, c * 128:(c + 1) * 128], True, True,
                          [qT, kTb], [pb[bk]])
                kb.copy(sc[:, q4 * 4:(q4 + 1) * 4, :], bank(bk).rearrange("p (k t) -> p k t", t=128), [pb[bk]], [scb[q4]], e="act")
            v16b = [Buf() for _ in range(16)]
            i16b = [Buf() for _ in range(16)]
            run_interleaved([topk_chain(sc[:, c, :], v16[:, c, :], i16[:, c, :], tmp4[:, c % 4, 0:128], v16b[c], i16b[c], tmpb[c % 4], scb[c // 4])
                             for c in range(16)])
            kb.copy(i16f[:], i16[:], i16b, [i16f], e="pool")
            v16v = v16[:].rearrange("p (h two) k -> p h two k", two=2)
            kb.tt(cand[:], v16v[:, :, 0, :].unsqueeze(3).to_broadcast([128, 8, 16, 16]),
                  v16v[:, :, 1, :].unsqueeze(2).to_broadcast([128, 8, 16, 16]), ALU.add, v16b, [cand])
            s16b = [Buf() for _ in range(8)]
            j16b = [Buf() for _ in range(8)]
            run_interleaved([topk_chain(cand[:, h, :, :].rearrange("p a b -> p (a b)"), s16[:, h, :], j16[:, h, :], tmp4[:, h % 4, :],
                                        s16b[h], j16b[h], tmpb[h % 4], cand.b) for h in range(8)])
            gt = trio[:, 2, :].rearrange("p (h k) -> p h k", k=16)
            kb.tt(gt, s16[:], s16[:, :, 0:1].to_broadcast([128, 8, 16]), ALU.subtract, s16b, [trio], e="pool")
            kb.act(gt, gt, AF.Exp, [trio], [trio])
            kb.reduce(zz[:], gt, ALU.add, [trio], [zz])
            kb.s.op("dve", lambda g_: g_.reciprocal(zz[:], zz[:]), [zz], [zz])
            kb.tt(gt, gt, zz[:].unsqueeze(2).to_broadcast([128, 8, 16]), ALU.mult, [trio, zz], [trio])
            kb.s.op("dve", lambda g_: g_.tensor_single_scalar(ja[:], j16[:], 4, ALU.logical_shift_right), j16b, [ja])
            kb.s.op("dve", lambda g_: g_.tensor_single_scalar(jb[:], j16[:], 15, ALU.bitwise_and), j16b, [jb])
            kb.copy(jaf[:], ja[:], [ja], [jaf])
            kb.copy(jbf[:], jb[:], [jb], [jbf])
            i16v = i16f[:].rearrange("p (h two) k -> p h two k", two=2)
            iota16 = iota[:, 0:16].unsqueeze(1).unsqueeze(1).to_broadcast([128, 8, 16, 16])
            for which, jf in ((0, jaf), (1, jbf)):
                kb.tt(eq[:], iota16, jf[:].unsqueeze(3).to_broadcast([128, 8, 16, 16]), ALU.is_equal, [iota, jf], [eq])
                kb.tt(eq[:], eq[:], i16v[:, :, which, :].unsqueeze(2).to_broadcast([128, 8, 16, 16]), ALU.mult, [eq, i16f], [eq])
                kb.reduce(trio[:, which, :], eq[:].rearrange("p h k a -> p (h k) a"), ALU.add, [eq], [trio])
            for w3 in range(3):
                kb.tr(ps[:, 6 * 512 + w3 * 128:6 * 512 + (w3 + 1) * 128], trio[:, w3, :], ident[:], [trio, ident], [pb[6]])
            tT = trioT[p][j]
            kb.copy(tT[:].rearrange("p a t -> p (a t)"), ps[:, 6 * 512:6 * 512 + 384], [pb[6]], [tT], e="act")

        wstate = [0]

        def p1b(p, j):
            tT = trioT[p][j]
            for t0 in range(0, 128, 16):
                half = wstate[0] % 2
                wstate[0] += 1
                for tt_ in range(16):
                    tl = (t0 + tt_) % OHT
                    tg = t0 + tt_
                    kb.ts(oh1[:, tl, :], iotab[:], tT[:, 0, tg:tg + 1], tT[:, 2, tg:tg + 1], ALU.is_equal, ALU.mult, [iotab, tT], [ohb[tl]])
                    kb.ts(oh2[:, tl, :], iotab[:], tT[:, 1, tg:tg + 1], None, ALU.is_equal, None, [iotab, tT], [ohb2[tl]])
                    col = half * 2048 + tt_ * 128
                    kb.mm(ps[:, col:col + 128], oh2[:, tl, :], oh1[:, tl, :], True, True, [ohb[tl], ohb2[tl]], [pb[half * 4 + tt_ // 4]])
                tcol = j * 128 + t0
                kb.copy(WT[:, :, tcol:tcol + 16], ps[:, half * 2048:(half + 1) * 2048].rearrange("p (t e) -> p e t", e=128),
                        [pb[half * 4 + q_] for q_ in range(4)], [WTb[tcol // 16]], e="act")

        def p2(p, x_out, g, bg):
            per = (len(bg) + 127) // 128 if bg else 0

            def u_stage(e1):
                cg, el = e1 // CH, e1 % CH
                sl = cg % NSL
                if el == 0:
                    kb.dma(ub[sl][:], us_v[:, :, cg * CH * 128:(cg + 1) * CH * 128], [us], [ub[sl]])
                    kb.dma(vb[sl][:], vs_v[:, cg * CH:(cg + 1) * CH, :], [vs], [vb[sl]], e="pool")
                a = e1 % 2
                pu = ps[:, 2048 + a * 512:2048 + a * 512 + G]
                for kc in range(8):
                    kb.mm(pu, ub[sl][:, kc, el * 128:(el + 1) * 128], hT[p][:, kc, :], kc == 0, kc == 7, [ub[sl], hT[p]], [pb[4 + a]])
                kb.act(actT[a][:], pu, AF.Gelu, [pb[4 + a]], [actT[a]])
                kb.tt(ct[a][:], actT[a][:], WT[:, e1, :], ALU.mult, [actT[a]] + WTb, [ct[a]])

            def v_stage(e1):
                cg, el = e1 // CH, e1 % CH
                sl = cg % NSL
                a = e1 % 2
                for j in range(TPG):
                    for hh in range(2):
                        kb.mm(bank(2 * j + hh), ct[a][:, j * 128:(j + 1) * 128], vb[sl][:, el, hh * 512:(hh + 1) * 512],
                              e1 == 0, e1 == 127, [ct[a], vb[sl]], [pb[2 * j + hh]])

            u_stage(0)
            for e1 in range(128):
                if e1 + 1 < 128:
                    u_stage(e1 + 1)
                v_stage(e1)
                if bg:
                    kb.s.replay(bg, per)
            xo = tmp4[:].rearrange("p a b -> p (a b)")
            for j in range(TPG):
                tok0 = g * G + j * 128
                kb.tt(xo, ps[:, j * 1024:(j + 1) * 1024], xs[p][j][:], ALU.add, [pb[2 * j], pb[2 * j + 1], xs[p][j]], tmpb)
                kb.dma(x_out[tok0:tok0 + 128, :], xo, tmpb, [], e="pool")
            if bg:
                kb.s.replay(bg, len(bg))

        groups = [(a_, b_, g_) for (a_, b_, t_) in jobs for g_ in range(t_ // G)]
        for j in range(TPG):
            p1a(0, groups[0][0], groups[0][2], j)
        for gi, (xin_, xout_, g) in enumerate(groups):
            p = gi % 2
            for j in range(TPG):
                p1b(p, j)
            bg = []
            if gi + 1 < len(groups):
                kb.s.defer = bg
                for j in range(TPG):
                    p1a(1 - p, groups[gi + 1][0], groups[gi + 1][2], j)
                kb.s.defer = None
            p2(p, xout_, g, bg)
        kb.s.barrier()


def rms_to_hT(kb, xs_t, gB, ident, hf, sq, ss, ps, pbA, pbB, hT, col0, eps=1e-6):
    kb.act(sq[:], xs_t[:], AF.Square, [xs_t], [sq])
    kb.reduce(ss[:, 0:1], sq[:], ALU.add, [sq], [ss])
    kb.ts(ss[:, 1:2], ss[:, 0:1], 1.0 / 1024, eps, ALU.mult, ALU.add, [ss], [ss])
    kb.act(ss[:, 3:4], ss[:, 1:2], AF.Sqrt, [ss], [ss])
    kb.s.op("dve", lambda g_: g_.reciprocal(ss[:, 2:3], ss[:, 3:4]), [ss], [ss])
    kb.stt(hf[:], xs_t[:], ss[:, 2:3], gB[:], ALU.mult, ALU.mult, [xs_t, ss, gB], [hf])
    for kc in range(8):
        kb.tr(ps[:, kc * 128:(kc + 1) * 128], hf[:, kc * 128:(kc + 1) * 128], ident[:], [hf, ident], [pbA if kc < 4 else pbB])
    kb.copy(hT[:, 0:4, col0:col0 + 128], ps[:, 0:512].rearrange("p (k t) -> p k t", t=128), [pbA], [hT], e="act")
    kb.copy(hT[:, 4:8, col0:col0 + 128], ps[:, 512:1024].rearrange("p (k t) -> p k t", t=128), [pbB], [hT], e="dve")


def mixer0_block(kb0, x_own, x_pre, x_out, gvec, w_in, w_out, convwT, convb, lng, lnb, w2, gateb, glag,
                 ident_d, tri_d, T, TP, tag="m0", x_out_pre=None):
    nc = kb0.nc
    NTO = T // 128
    NTP = TP // 128
    with ExitStack() as st:
        kb = kb0.scope(st)
        winb = kb.sb("winb", [128, 8, 2576], BF16)
        load_bf16_resident(kb, winb, lambda kc, c0, w: winb[:, kc, c0:c0 + w], w_in, 8, 2576, "win")
        woutb = kb.sb("woutb", [128, 8, 1024], BF16)
        load_bf16_resident(kb, woutb, lambda kc, c0, w: woutb[:, kc, c0:c0 + w], w_out, 8, 1024, "wout")
        gB = kb.sb("gBm", [128, 1024], F32)
        kb.dma(gB[:], gvec.t.partition_broadcast(128)[:, 0, :], [gvec], [gB])
        gbB = kb.sb("gbB", [128, 256], F32)
        kb.dma(gbB[:], gateb.t.partition_broadcast(128)[:, 0, :], [gateb], [gbB])
        ident = kb.sb("identm", [128, 128], F32)
        kb.dma(ident[:], ident_d[:], [ident_d], [ident])
        tri = kb.sb("trim", [128, 128], F32)
        kb.dma(tri[:], tri_d[:], [tri_d], [tri])
        ones = kb.sb("onesm", [128, 128], F32)
        kb.memset(ones[:], 1.0, [ones])
        cw = kb.sb("cw", [128, 4, 31], F32)
        kb.dma(cw[:], convwT[:], [convwT], [cw])
        cols = kb.sb("colsm", [128, 16], F32)
        kb.dma(cols[:, 0:4], convb[:], [convb], [cols])
        kb.dma(cols[:, 4:8], lng[:], [lng], [cols])
        kb.dma(cols[:, 8:12], lnb[:], [lnb], [cols])
        kb.dma(cols[:, 12:13], glag[:], [glag], [cols])
        w2f = kb.sb("w2f", [16, 256], F32)
        kb.dma(w2f[:], w2[:], [w2], [w2f])
        w2b = kb.sb("w2b", [16, 256], BF16)
        kb.copy(w2b[:], w2f[:], [w2f], [w2b])
        diag = kb.sb("diag", [128, 4 * 31, 128], BF16)
        for c in range(4):
            for j in range(31):
                kb.ts(diag[:, c * 31 + j, :], ident[:], cw[:, c, j:j + 1], None, ALU.mult, None, [ident, cw], [diag],
                      e=("dve" if (c * 31 + j) % 2 == 0 else "pool"))
        full = x_out_pre is not None
        UW = 158 if full else 30 + T
        uT = kb.sb("uT", [128, 4, UW], BF16)
        kb.memset(uT[:, :, 0:30], 0.0, [uT])
        Sf = [kb.sb(f"Sf{h}", [64, 128], F32) for h in range(4)]
        Sb = [kb.sb(f"Sb{h}", [64, 128], BF16) for h in range(4)]
        for h in range(4):
            kb.memset(Sf[h][:], 0.0, [Sf[h]])
            kb.memset(Sb[h][:], 0.0, [Sb[h]])
        xs = kb.sb("xsm", [128, 1024], F32)
        hf = kb.sb("hfm", [128, 1024], F32)
        sq = kb.sb("sqm", [128, 1024], BF16)
        ss = kb.sb("ssm", [128, 4], F32)
        hT = kb.sb("hTm", [128, 8, 128], BF16)
        sg = kb.sb("sg", [128, 512], F32)
        glrT = kb.sb("glrT", [16, 128], BF16)
        zb = kb.sb("zb", [128, 256], F32)
        la = kb.sb("la", [128, 256], F32)
        enb_tm = kb.sb("enb_tm", [128, 256], F32)
        ebl_tm = kb.sb("ebl_tm", [128, 256], F32)
        kdec = kb.sb("kdec", [128, 256], BF16)
        vbf = kb.sb("vbf", [128, 512], BF16)
        eblc = kb.sb("eblc", [64, 8], F32)
        eb = kb.sb("eb", [64, 512], F32)
        enb = kb.sb("enb", [64, 512], F32)
        qt = kb.sb("qt", [64, 4, 128], BF16)
        kt = kb.sb("kt", [64, 4, 128], BF16)
        Am = kb.sb("Am", [128, 4, 128], BF16)
        osq = kb.sb("osq", [128, 512], F32)
        rs = kb.sb("rs", [128, 512], F32)
        sr = kb.sb("sr", [128, 512], F32)
        t1 = kb.sb("t1", [128, 512], F32)
        ybT = kb.sb("ybT", [128, 4, 128], BF16)
        ycs = kb.sb("ycs", [128, 4, 128], F32)
        ysq = kb.sb("ysq", [128, 4, 128], F32)
        mean = kb.sb("mean", [128, 128], F32)
        var = kb.sb("var", [128, 128], F32)
        yaT = kb.sb("yaT", [128, 4, 128], BF16)
        xo = kb.sb("xom", [128, 1024], F32)
        ps = st.enter_context(nc.psum_tensor("psm", [128, 4096], F32))
        pb = [Buf(f"pbm{i}") for i in range(8)]

        def B(i, lo=0, hi=512):
            return ps[:, i * 512 + lo:i * 512 + hi]

        def fm_proj(colbase, ncols, bank, slot):
            for kc in range(8):
                kb.mm(ps[0:ncols, bank * 512 + slot * 128:bank * 512 + (slot + 1) * 128], winb[:, kc, colbase:colbase + ncols],
                      hT[:, kc, :], kc == 0, kc == 7, [winb, hT], [pb[bank]])

        def state_part(u_needed, own):
            for kc in range(8):
                kb.mm(B(5), hT[:, kc, :], winb[:, kc, 1536:2048], kc == 0, kc == 7, [hT, winb], [pb[5]])
            for kc in range(8):
                kb.mm(B(6, 0, 256), hT[:, kc, :], winb[:, kc, 1280:1536], kc == 0, kc == 7, [hT, winb], [pb[6]])
            fm_proj(2560, 16, 4, 0)
            kb.copy(glrT[:], ps[0:16, 4 * 512:4 * 512 + 128], [pb[4]], [glrT], e="act")
            kb.mm(B(6, 256, 512), glrT[:], w2b[:], True, True, [glrT, w2b], [pb[6]])
            kb.tt(zb[:], B(6, 256, 512), gbB[:], ALU.add, [pb[6], gbB], [zb])
            kb.act(zb[:], zb[:], AF.Exp, [zb], [zb], scale=-1.0)
            kb.act(zb[:], zb[:], AF.Ln, [zb], [zb], bias=1.0)
            kb.ts(la[:], zb[:], -1.0 / 16.0, None, ALU.mult, None, [zb], [la])
            kb.mm(B(7, 0, 256), tri[:], la[:], True, True, [tri, la], [pb[7]])
            kb.mm(B(7, 256, 512), ones[:], la[:], True, True, [ones, la], [pb[7]])
            for h in range(4):
                kb.mm(ps[0:64, 4 * 512 + 384 + 2 * h:4 * 512 + 386 + 2 * h], la[:, h * 64:(h + 1) * 64], ones[:, 0:2], True, True,
                      [la, ones], [pb[4]])
            kb.act(eblc[:], ps[0:64, 4 * 512 + 384:4 * 512 + 392], AF.Exp, [pb[4]], [eblc])
            kb.act(enb_tm[:], B(7, 0, 256), AF.Exp, [pb[7]], [enb_tm], scale=-1.0)
            kb.act(ebl_tm[:], B(7, 256, 512), AF.Exp, [pb[7]], [ebl_tm])
            kb.tt(enb_tm[:], enb_tm[:], ebl_tm[:], ALU.mult, [enb_tm, ebl_tm], [enb_tm])
            kb.tt(kdec[:], B(6, 0, 256), enb_tm[:], ALU.mult, [pb[6], enb_tm], [kdec])
            kb.copy(vbf[:], B(5), [pb[5]], [vbf], e="act")

        def state_update():
            for h in range(4):
                kb.mm(ps[0:64, 2 * 512 + h * 128:2 * 512 + (h + 1) * 128], kdec[:, h * 64:(h + 1) * 64], vbf[:, h * 128:(h + 1) * 128],
                      True, True, [kdec, vbf], [pb[2]])
            for h in range(4):
                kb.stt(Sf[h][:], Sf[h][:], eblc[:, 2 * h:2 * h + 1], ps[0:64, 2 * 512 + h * 128:2 * 512 + (h + 1) * 128],
                       ALU.mult, ALU.add, [Sf[h], eblc, pb[2]], [Sf[h]])
                kb.copy(Sb[h][:], Sf[h][:], [Sf[h]], [Sb[h]], e="act")

        def conv_u(tokcol):
            for c in range(4):
                fm_proj(c * 128, 128, 0, c)
                fm_proj(512 + c * 128, 128, 1, c)
            kb.act(sg[:], B(1), AF.Sigmoid, [pb[1]], [sg])
            kb.tt(uT[:, :, 30 + tokcol:30 + tokcol + 128], B(0).rearrange("p (c t) -> p c t", t=128),
                  sg[:].rearrange("p (c t) -> p c t", t=128), ALU.mult, [pb[0], sg], [uT])

        for i in range(0 if full else NTP):
            kb.dma(xs[:], x_pre[i * 128:(i + 1) * 128, :], [x_pre], [xs])
            rms_to_hT(kb, xs, gB, ident, hf, sq, ss, ps, pb[0], pb[1], hT, 0)
            state_part(False, False)
            state_update()
            if i == NTP - 1:
                for c in range(4):
                    fm_proj(c * 128, 128, 0, c)
                    fm_proj(512 + c * 128, 128, 1, c)
                kb.act(sg[:], B(1), AF.Sigmoid, [pb[1]], [sg])
                kb.tt(uT[:, :, 0:30], B(0).rearrange("p (c t) -> p c t", t=128)[:, :, 98:128],
                      sg[:].rearrange("p (c t) -> p c t", t=128)[:, :, 98:128], ALU.mult, [pb[0], sg], [uT])
        for ii in range((NTP + NTO) if full else NTO):
            if full:
                isown = ii >= NTP
                i = 0
                srcx = x_own[(ii - NTP) * 128:(ii - NTP + 1) * 128, :] if isown else x_pre[ii * 128:(ii + 1) * 128, :]
                dsty = x_out[(ii - NTP) * 128:(ii - NTP + 1) * 128, :] if isown else x_out_pre[ii * 128:(ii + 1) * 128, :]
            else:
                i = ii
                srcx = x_own[i * 128:(i + 1) * 128, :]
                dsty = x_out[i * 128:(i + 1) * 128, :]
            kb.dma(xs[:], srcx, [x_own, x_pre], [xs])
            rms_to_hT(kb, xs, gB, ident, hf, sq, ss, ps, pb[0], pb[1], hT, 0)
            state_part(True, True)
            conv_u(i * 128)
            for h in range(4):
                fm_proj(1024 + h * 64, 64, 2, h)
                fm_proj(1280 + h * 64, 64, 3, h)
            for sl in range(4):
                fm_proj(2048 + sl * 128, 128, 4, sl)
            for h in range(4):
                kb.mm(ps[0:64, 5 * 512 + h * 128:5 * 512 + (h + 1) * 128], la[:, h * 64:(h + 1) * 64], tri[:], True, True,
                      [la, tri], [pb[5]])
            kb.act(eb[:], ps[0:64, 5 * 512:6 * 512], AF.Exp, [pb[5]], [eb])
            kb.act(enb[:], ps[0:64, 5 * 512:6 * 512], AF.Exp, [pb[5]], [enb], scale=-1.0)
            kb.stt(qt[:].rearrange("p h t -> p (h t)"), ps[0:64, 2 * 512:3 * 512], 0.125, eb[:], ALU.mult, ALU.mult, [pb[2], eb], [qt])
            kb.tt(kt[:].rearrange("p h t -> p (h t)"), ps[0:64, 3 * 512:4 * 512], enb[:], ALU.mult, [pb[3], enb], [kt])
            kb.act(sr[:], B(4), AF.Silu, [pb[4]], [sr])
            for h in range(4):
                kb.mm(B(0, h * 128, (h + 1) * 128), kt[:, h, :], qt[:, h, :], True, True, [kt, qt], [pb[0]])
            kb.tt(Am[:], B(0).rearrange("p (h t) -> p h t", t=128), tri[:].unsqueeze(1).to_broadcast([128, 4, 128]), ALU.mult,
                  [pb[0], tri], [Am])
            for h in range(4):
                kb.mm(B(1, h * 128, (h + 1) * 128), vbf[:, h * 128:(h + 1) * 128], Am[:, h, :], True, False, [vbf, Am], [pb[1]])
                kb.mm(B(1, h * 128, (h + 1) * 128), Sb[h][:], qt[:, h, :], False, True, [Sb[h], qt], [pb[1]])
            state_update()
            kb.act(osq[:], B(1), AF.Square, [pb[1]], [osq])
            kb.mm(B(3), ones[:], osq[:], True, True, [ones, osq], [pb[3]])
            kb.ts(rs[:], B(3), 1.0 / 128, 1e-6, ALU.mult, ALU.add, [pb[3]], [rs])
            kb.act(rs[:], rs[:], AF.Sqrt, [rs], [rs])
            kb.s.op("dve", lambda g_: g_.reciprocal(rs[:], rs[:]), [rs], [rs])
            kb.tt(t1[:], B(1), rs[:], ALU.mult, [pb[1], rs], [t1])
            kb.stt(ybT[:].rearrange("p h t -> p (h t)"), t1[:], cols[:, 12:13], sr[:], ALU.mult, ALU.mult, [t1, cols, sr], [ybT])
            for c in range(4):
                for j in range(31):
                    kb.mm(B(4, c * 128, (c + 1) * 128), diag[:, c * 31 + j, :], uT[:, c, i * 128 + j:i * 128 + j + 128],
                          j == 0, j == 30, [diag, uT], [pb[4]])
            for c in range(4):
                kb.ts(ycs[:, c, :], B(4, c * 128, (c + 1) * 128), cols[:, c:c + 1], None, ALU.add, None, [pb[4], cols], [ycs])
            kb.act(ysq[:], ycs[:], AF.Square, [ycs], [ysq])
            for c in range(4):
                kb.mm(B(5, 0, 128), ones[:], ycs[:, c, :], c == 0, c == 3, [ones, ycs], [pb[5]])
            for c in range(4):
                kb.mm(B(5, 128, 256), ones[:], ysq[:, c, :], c == 0, c == 3, [ones, ysq], [pb[5]])
            kb.ts(mean[:], B(5, 0, 128), 1.0 / 512, None, ALU.mult, None, [pb[5]], [mean])
            kb.tt(var[:], mean[:], mean[:], ALU.mult, [mean], [var])
            kb.stt(var[:], B(5, 128, 256), 1.0 / 512, var[:], ALU.mult, ALU.subtract, [pb[5], var], [var])
            kb.ts(var[:], var[:], 1e-6, None, ALU.add, None, [var], [var])
            kb.act(var[:], var[:], AF.Sqrt, [var], [var])
            kb.s.op("dve", lambda g_: g_.reciprocal(var[:], var[:]), [var], [var])
            kb.tt(ycs[:], ycs[:], mean[:].unsqueeze(1).to_broadcast([128, 4, 128]), ALU.subtract, [ycs, mean], [ycs])
            kb.tt(ycs[:], ycs[:], var[:].unsqueeze(1).to_broadcast([128, 4, 128]), ALU.mult, [ycs, var], [ycs])
            for c in range(4):
                kb.ts(ycs[:, c, :], ycs[:, c, :], cols[:, 4 + c:5 + c], cols[:, 8 + c:9 + c], ALU.mult, ALU.add, [ycs, cols], [ycs])
            kb.act(yaT[:], ycs[:], AF.Silu, [ycs], [yaT])
            for hh in range(2):
                for kc in range(8):
                    lhsT = yaT[:, kc, :] if kc < 4 else ybT[:, kc - 4, :]
                    kb.mm(B(6 + hh), lhsT, woutb[:, kc, hh * 512:(hh + 1) * 512], kc == 0, kc == 7, [yaT, ybT, woutb], [pb[6 + hh]])
            kb.tt(xo[:], ps[:, 6 * 512:8 * 512], xs[:], ALU.add, [pb[6], pb[7], xs], [xo])
            kb.dma(dsty, xo[:], [xo], [], e="pool")
            if full:
                kb.copy(uT[:, :, 0:30], uT[:, :, 128:158], [uT], [uT], e="pool")
        kb.s.barrier()


def fox_block(kb0, x_own, x_pre, x_out, gvec, w_in, w_out, fb, qg, kg, pflag, ident_d, tri_d, masks_d, T, TP, tag="fx"):
    nc = kb0.nc
    NTO = T // 128
    NTP = TP // 128
    NTA = NTO + NTP
    NSB = T // 512
    QT = kb0.dram("fxQT", [16, 65, T], BF16, "Internal")
    KT = kb0.dram("fxKT", [16, 65, TP + T], BF16, "Internal")
    VA = kb0.dram("fxVA", [NTA, 128, 16 * 65], BF16, "Internal")
    OT = kb0.dram("fxOT", [16, 64, T], BF16, "Internal")
    with ExitStack() as st0:
        kbp = kb0.scope(st0)
        negc = kbp.sb("negc", [128, NTA, 16], F32)
        ident = kbp.sb("identf", [128, 128], F32)
        kbp.dma(ident[:], ident_d[:], [ident_d], [ident])
        ones = kbp.sb("onesf", [128, 128], F32)
        kbp.memset(ones[:], 1.0, [ones])
        woutb = kbp.sb("woutbf", [128, 8, 1024], BF16)
        load_bf16_resident(kbp, woutb, lambda kc, c0, w: woutb[:, kc, c0:c0 + w], w_out, 8, 1024, "fwout")
        with ExitStack() as st:
            kb = kbp.scope(st)
            winb = kb.sb("winbf", [128, 8, 3088], BF16)
            load_bf16_resident(kb, winb, lambda kc, c0, w: winb[:, kc, c0:c0 + w], w_in, 8, 3088, "fwin")
            gB = kb.sb("gBf", [128, 1024], F32)
            kb.dma(gB[:], gvec.t.partition_broadcast(128)[:, 0, :], [gvec], [gB])
            fbB = kb.sb("fbB", [128, 16], F32)
            kb.dma(fbB[:], fb.t.partition_broadcast(128)[:, 0, :], [fb], [fbB])
            qgB = kb.sb("qgB", [128, 64], F32)
            kb.dma(qgB[:], qg.t.partition_broadcast(128)[:, 0, :], [qg], [qgB])
            kgB = kb.sb("kgB", [128, 64], F32)
            kb.dma(kgB[:], kg.t.partition_broadcast(128)[:, 0, :], [kg], [kgB])
            pfl = kb.sb("pfl", [128, 1], F32)
            kb.dma(pfl[:], pflag[:], [pflag], [pfl])
            tri = kb.sb("trif", [128, 128], F32)
            kb.dma(tri[:], tri_d[:], [tri_d], [tri])
            xs = kb.sb("xsf", [128, 1024], F32)
            hf = kb.sb("hff", [128, 1024], F32)
            sq = kb.sb("sqf", [128, 1024], BF16)
            ss = kb.sb("ssf", [128, 4], F32)
            hT = kb.sb("hTf", [128, 8, 128], BF16)
            nsq = kb.sb("nsq", [128, 1024], F32)
            nss = kb.sb("nss", [128, 16], F32)
            qa = kb.sb("qa", [128, 16, 65], F32)
            ka = kb.sb("ka", [128, 16, 65], F32)
            kb.memset(ka[:, :, 64:65], 1.0, [ka])
            va = kb.sb("va", [128, 16, 65], BF16)
            kb.memset(va[:, :, 0:1], 1.0, [va])
            qTs = kb.sb("qTs", [65, 16, 128], BF16)
            kTs = kb.sb("kTs", [65, 16, 128], BF16)
            lf = kb.sb("lf", [128, 16], F32)
            Lsum = kb.sb("Lsum", [128, 16], F32)
            kb.memset(Lsum[:], 0.0, [Lsum])
            ctile = kb.sb("ctile", [128, 16], F32)
            ps = st.enter_context(nc.psum_tensor("psf1", [128, 4096], F32))
            pb = [Buf(f"pbf{i}") for i in range(8)]

            def normed(dst, bank0, gt):
                src = ps[:, bank0 * 512:(bank0 + 2) * 512]
                kb.act(nsq[:], src, AF.Square, [pb[bank0], pb[bank0 + 1]], [nsq])
                kb.reduce(nss[:], nsq[:].rearrange("p (h d) -> p h d", d=64), ALU.add, [nsq], [nss])
                kb.ts(nss[:], nss[:], 1.0 / 64, 1e-6, ALU.mult, ALU.add, [nss], [nss])
                kb.act(nss[:], nss[:], AF.Sqrt, [nss], [nss])
                kb.s.op("dve", lambda g_: g_.reciprocal(nss[:], nss[:]), [nss], [nss])
                kb.tt(dst[:, :, 0:64], src.rearrange("p (h d) -> p h d", d=64), nss[:].unsqueeze(2).to_broadcast([128, 16, 64]),
                      ALU.mult, [pb[bank0], pb[bank0 + 1], nss], [dst])
                kb.tt(dst[:, :, 0:64], dst[:, :, 0:64], gt[:].unsqueeze(1).to_broadcast([128, 16, 64]), ALU.mult, [dst, gt], [dst])

            def transposed_store(src, dstT, dram, tokcol, b0):
                for h in range(16):
                    kb.tr(ps[0:65, b0 * 512 + h * 128:b0 * 512 + (h + 1) * 128], src[:, h, :], ident[:], [src, ident], [pb[b0 + h // 4]])
                for q4 in range(4):
                    kb.copy(dstT[:, q4 * 4:(q4 + 1) * 4, :], ps[0:65, (b0 + q4) * 512:(b0 + q4 + 1) * 512].rearrange("p (h t) -> p h t", t=128),
                            [pb[b0 + q4]], [dstT], e=("act" if q4 % 2 == 0 else "dve"))
                kb.dma(dram.t.rearrange("h r t -> r h t")[:, :, tokcol:tokcol + 128], dstT[:], [dstT], [], e="pool")

            for i in range(NTA):
                own = i >= NTP
                src = x_own[(i - NTP) * 128:(i - NTP + 1) * 128, :] if own else x_pre[i * 128:(i + 1) * 128, :]
                kb.dma(xs[:], src, [x_own, x_pre], [xs])
                rms_to_hT(kb, xs, gB, ident, hf, sq, ss, ps, pb[0], pb[1], hT, 0)
                for kc in range(8):
                    kb.mm(ps[:, 6 * 512:6 * 512 + 16], hT[:, kc, :], winb[:, kc, 3072:3088], kc == 0, kc == 7, [hT, winb], [pb[6]])
                kb.tt(lf[:], ps[:, 6 * 512:6 * 512 + 16], fbB[:], ALU.add, [pb[6], fbB], [lf])
                kb.act(lf[:], lf[:], AF.Exp, [lf], [lf], scale=-1.0)
                kb.act(lf[:], lf[:], AF.Ln, [lf], [lf], bias=1.0)
                kb.ts(lf[:], lf[:], -1.0, None, ALU.mult, None, [lf], [lf])
                kb.mm(ps[:, 6 * 512 + 16:6 * 512 + 32], tri[:], lf[:], True, False, [tri, lf], [pb[6]])
                kb.mm(ps[:, 6 * 512 + 16:6 * 512 + 32], ones[:], Lsum[:], False, True, [ones, Lsum], [pb[6]])
                kb.tt(Lsum[:], Lsum[:], lf[:], ALU.add, [Lsum, lf], [Lsum])
                kb.copy(ctile[:], ps[:, 6 * 512 + 16:6 * 512 + 32], [pb[6]], [ctile], e="act")
                if own:
                    kb.ts(negc[:, i, :], ctile[:], -1.0, None, ALU.mult, None, [ctile], [negc])
                else:
                    kb.ts(negc[:, i, :], ctile[:], -1.0, pfl[:, 0:1], ALU.mult, ALU.add, [ctile, pfl], [negc])
                for hh in range(2):
                    for kc in range(8):
                        kb.mm(ps[:, (2 + hh) * 512:(3 + hh) * 512], hT[:, kc, :], winb[:, kc, 1024 + hh * 512:1536 + hh * 512],
                              kc == 0, kc == 7, [hT, winb], [pb[2 + hh]])
                normed(ka, 2, kgB)
                for hh in range(2):
                    for kc in range(8):
                        kb.mm(ps[:, (4 + hh) * 512:(5 + hh) * 512], hT[:, kc, :], winb[:, kc, 2048 + hh * 512:2560 + hh * 512],
                              kc == 0, kc == 7, [hT, winb], [pb[4 + hh]])
                kb.copy(va[:, :, 1:65], ps[:, 4 * 512:6 * 512].rearrange("p (h d) -> p h d", d=64), [pb[4], pb[5]], [va], e="act")
                kb.dma(VA[i], va[:].rearrange("p h d -> p (h d)"), [va], [], e="pool")
                if own:
                    for hh in range(2):
                        for kc in range(8):
                            kb.mm(ps[:, hh * 512:(hh + 1) * 512], hT[:, kc, :], winb[:, kc, hh * 512:(hh + 1) * 512],
                                  kc == 0, kc == 7, [hT, winb], [pb[hh]])
                    normed(qa, 0, qgB)
                    kb.ts(qa[:, :, 64:65], ctile[:].unsqueeze(2), 8.0, None, ALU.mult, None, [ctile], [qa])
                transposed_store(ka, kTs, KT, i * 128, 2)
                if own:
                    transposed_store(qa, qTs, QT, (i - NTP) * 128, 2)
            kb.s.barrier()
        with ExitStack() as st:
            kb = kbp.scope(st)
            kts = kb.sb("kts", [65, TP + T], BF16)
            vas = kb.sb("vas", [128, NTA, 65], BF16)
            qts = kb.sb("qts", [65, T], BF16)
            masks = kb.sb("masksb", [128, 4, 512], F32)
            kb.dma(masks[:], masks_d[:], [masks_d], [masks])
            stmp = kb.sb("stmp", [128, 512], F32)
            pT = [kb.sb(f"pT{i}", [128, 512], BF16) for i in range(3)]
            osb = kb.sb("osb", [65, 512], F32)
            rden = kb.sb("rden", [1, 512], F32)
            oTs = kb.sb("oTs", [65, 512], BF16)
            ps = st.enter_context(nc.psum_tensor("psf2", [128, 4096], F32))
            pb = [Buf(f"pbg{i}") for i in range(8)]
            VAv = VA.t.rearrange("n p (h d) -> p n h d", d=65)
            cnt = 0
            for h in range(16):
                kb.dma(kts[:], KT[h], [], [kts])
                kb.dma(qts[:], QT[h], [], [qts])
                kb.dma(vas[:], VAv[:, :, h, :], [], [vas])
                for j in range(NSB):
                    nkb = NTP + 4 * j + 4
                    ob = 4 + (j % 2)
                    def s_stage(kb_, a):
                        kb.mm(ps[:, a * 512:(a + 1) * 512], kts[:, kb_ * 128:(kb_ + 1) * 128], qts[:, j * 512:(j + 1) * 512], True, True,
                              [kts, qts], [pb[a]])
                        m = kb_ - NTP - 4 * j
                        if m >= 0:
                            kb.tt(stmp[:], ps[:, a * 512:(a + 1) * 512], masks[:, m, :], ALU.add, [pb[a], masks], [stmp])
                            kb.act(pT[a][:], stmp[:], AF.Exp, [stmp, negc], [pT[a]], bias=negc[:, kb_, h:h + 1], scale=0.125)
                        else:
                            kb.act(pT[a][:], ps[:, a * 512:(a + 1) * 512], AF.Exp, [pb[a], negc], [pT[a]], bias=negc[:, kb_, h:h + 1], scale=0.125)

                    def pv_stage(kb_, a):
                        kb.mm(ps[0:65, ob * 512:(ob + 1) * 512], vas[:, kb_, :], pT[a][:], kb_ == 0, kb_ == nkb - 1, [vas, pT[a]], [pb[ob]])

                    NBUF = 3
                    for kb_ in range(min(NBUF - 1, nkb)):
                        s_stage(kb_, kb_ % NBUF)
                    for kb_ in range(nkb):
                        if kb_ + NBUF - 1 < nkb:
                            s_stage(kb_ + NBUF - 1, (kb_ + NBUF - 1) % NBUF)
                        pv_stage(kb_, kb_ % NBUF)
                    kb.copy(osb[:], ps[0:65, ob * 512:(ob + 1) * 512], [pb[ob]], [osb], e="act")
                    kb.s.op("dve", lambda g_: g_.reciprocal(rden[:], osb[0:1, :]), [osb], [rden])
                    kb.mm(ps[0:65, 6 * 512:7 * 512], ones[0:1, 0:65], rden[:], True, True, [ones, rden], [pb[6]])
                    kb.tt(oTs[:], osb[:], ps[0:65, 6 * 512:7 * 512], ALU.mult, [osb, pb[6]], [oTs])
                    kb.dma(OT[h, :, j * 512:(j + 1) * 512], oTs[1:65, :], [oTs], [], e="pool")
            kb.s.barrier()
        with ExitStack() as st:
            kb = kbp.scope(st)
            oTt = [kb.sb(f"oTt{i}", [128, 8, 128], BF16) for i in range(2)]
            xs2 = [kb.sb(f"xs2{i}", [128, 1024], F32) for i in range(2)]
            xo = [kb.sb(f"xof{i}", [128, 1024], F32) for i in range(2)]
            ps = st.enter_context(nc.psum_tensor("psf3", [128, 4096], F32))
            pb = [Buf(f"pbh{i}") for i in range(8)]
            OTv = OT.t.rearrange("(p two) d t -> (two d) p t", two=2)
            for i in range(NTO):
                a = i % 2
                kb.dma(oTt[a][:], OTv[:, :, i * 128:(i + 1) * 128], [], [oTt[a]])
                kb.dma(xs2[a][:], x_own[i * 128:(i + 1) * 128, :], [x_own], [xs2[a]])
                for hh in range(2):
                    for p in range(8):
                        kb.mm(ps[:, (2 * a + hh) * 512:(2 * a + hh + 1) * 512], oTt[a][:, p, :], woutb[:, p, hh * 512:(hh + 1) * 512],
                              p == 0, p == 7, [oTt[a], woutb], [pb[2 * a + hh]])
                kb.tt(xo[a][:], ps[:, 2 * a * 512:(2 * a + 2) * 512], xs2[a][:], ALU.add, [pb[2 * a], pb[2 * a + 1], xs2[a]], [xo[a]])
                kb.dma(x_out[i * 128:(i + 1) * 128, :], xo[a][:], [xo[a]], [], e="pool")
            kb.s.barrier()


TOK = 4096


def _consts():
    k = np.arange(128)[:, None]
    q = np.arange(512)[None, :]
    fm = np.zeros((128, 4, 512), np.float32)
    for mm in range(4):
        fm[:, mm, :] = np.where((mm * 128 + k) <= q, 0.0, -240000.0)
    return {
        "ident": np.eye(128, dtype=np.float32),
        "iota": np.tile(np.arange(128, dtype=np.float32), (128, 1)),
        "tri": np.triu(np.ones((128, 128), np.float32)),
        "masks": fm,
    }


def _col4(v):
    return np.ascontiguousarray(v.reshape(4, 128).T)


_NC_CACHE = {}


def _build_fused(T):
    key = ("fused", T)
    if key in _NC_CACHE:
        return _NC_CACHE[key]
    nc = bass.Bass("TRN2", target_bir_lowering=False)
    with ExitStack() as st:
        kb = KB(nc, st)
        D = lambda n, s: kb.dram(n, s, F32, "ExternalInput")
        I = lambda n: kb.dram(n, [T, 1024], F32, "Internal")
        x, xp = D("x", [T, 1024]), D("xp", [T, 1024])
        y = kb.dram("y", [T, 1024], F32, "ExternalOutput")
        ident, iota, tri, masks = D("ident", [128, 128]), D("iota", [128, 128]), D("tri", [128, 128]), D("masks", [128, 4, 512])
        x1o, x1p, x2o, x2p, x3o = I("x1o"), I("x1p"), I("x2o"), I("x2p"), I("x3o")
        mixer0_block(kb, x, xp, x1o, D("m_g", [1, 1024]), D("m_w_in", [1024, 2576]), D("m_w_out", [1024, 1024]),
                     D("m_convwT", [128, 4, 31]), D("m_convb", [128, 4]), D("m_lng", [128, 4]), D("m_lnb", [128, 4]),
                     D("m_w2", [16, 256]), D("m_gateb", [1, 256]), D("m_glag", [128, 1]), ident, tri, T, T, x_out_pre=x1p)
        tabs0 = peer_tables(kb, D("p0_uT", [1024, 16384]), D("p0_v", [16384, 1024]), "L0")
        peer_block(kb, None, None, D("p0_g", [1, 1024]), D("p0_wq", [1024, 2048]), D("p0_keysT", [128, 2048]), None, None,
                   ident, iota, T, "L0", tables=tabs0, jobs=[(x1p, x2p, T), (x1o, x2o, T)])
        fox_block(kb, x2o, x2p, x3o, D("f_g", [1, 1024]), D("f_w_in", [1024, 3088]), D("f_w_out", [1024, 1024]), D("f_fb", [1, 16]),
                  D("f_qg", [1, 64]), D("f_kg", [1, 64]), D("pflag", [128, 1]), ident, tri, masks, T, T)
        peer_block(kb, x3o, y, D("p1_g", [1, 1024]), D("p1_wq", [1024, 2048]), D("p1_keysT", [128, 2048]),
                   D("p1_uT", [1024, 16384]), D("p1_v", [16384, 1024]), ident, iota, T, "L1")
        kb.s.emit()
        print("fused program: ninst", kb.s.ninst, "nsem", kb.s.nsem, flush=True)
    _NC_CACHE[key] = nc
    return nc


def _fused_common(inp, cst):
    c = {"m_g": np.ascontiguousarray(inp["ev_norm_mix"][0][None]), "m_w_in": np.ascontiguousarray(inp["ev_w_in"][0]),
         "m_w_out": np.ascontiguousarray(inp["ev_w_out"][0]),
         "m_convwT": np.ascontiguousarray(inp["ev_conv_w"][0].T.reshape(4, 128, 31).transpose(1, 0, 2)),
         "m_convb": _col4(inp["ev_conv_b"][0]), "m_lng": _col4(inp["ev_conv_ln_g"][0]), "m_lnb": _col4(inp["ev_conv_ln_b"][0]),
         "m_w2": np.ascontiguousarray(inp["ev_gate_w2"][0]), "m_gateb": np.ascontiguousarray(inp["ev_gate_b"][0][None]),
         "m_glag": np.ascontiguousarray(inp["ev_gla_norm_g"][0][:, None]),
         "f_g": np.ascontiguousarray(inp["od_norm_mix"][0][None]), "f_w_in": np.ascontiguousarray(inp["od_w_in"][0]),
         "f_w_out": np.ascontiguousarray(inp["od_w_out"][0]), "f_fb": np.ascontiguousarray(inp["od_fgate_b"][0][None]),
         "f_qg": np.ascontiguousarray(inp["od_q_norm_g"][0][None]), "f_kg": np.ascontiguousarray(inp["od_k_norm_g"][0][None]),
         "ident": cst["ident"], "iota": cst["iota"], "tri": cst["tri"], "masks": cst["masks"]}
    for L in range(2):
        p = _peer_inputs(inp, L, cst)
        for k in ("g", "wq", "keysT", "uT", "v"):
            c[f"p{L}_{k}"] = p[k]
    return c


def _build(kind):
    if kind in _NC_CACHE:
        return _NC_CACHE[kind]
    nc = bass.Bass("TRN2", target_bir_lowering=False)
    with ExitStack() as st:
        kb = KB(nc, st)
        D = lambda n, s: kb.dram(n, s, F32, "ExternalInput")
        T = TOK
        if kind == "m0":
            x, xp = D("x", [T, 1024]), D("xp", [T, 1024])
            y = kb.dram("y", [T, 1024], F32, "ExternalOutput")
            mixer0_block(kb, x, xp, y, D("g", [1, 1024]), D("w_in", [1024, 2576]), D("w_out", [1024, 1024]), D("convwT", [128, 4, 31]),
                         D("convb", [128, 4]), D("lng", [128, 4]), D("lnb", [128, 4]), D("w2", [16, 256]), D("gateb", [1, 256]),
                         D("glag", [128, 1]), D("ident", [128, 128]), D("tri", [128, 128]), T, T)
        elif kind == "peer":
            x = D("x", [T, 1024])
            y = kb.dram("y", [T, 1024], F32, "ExternalOutput")
            peer_block(kb, x, y, D("g", [1, 1024]), D("wq", [1024, 2048]), D("keysT", [128, 2048]), D("uT", [1024, 16384]),
                       D("v", [16384, 1024]), D("ident", [128, 128]), D("iota", [128, 128]), T, "p0")
        elif kind == "fox":
            x, xp = D("x", [T, 1024]), D("xp", [T, 1024])
            y = kb.dram("y", [T, 1024], F32, "ExternalOutput")
            fox_block(kb, x, xp, y, D("g", [1, 1024]), D("w_in", [1024, 3088]), D("w_out", [1024, 1024]), D("fb", [1, 16]),
                      D("qg", [1, 64]), D("kg", [1, 64]), D("pflag", [128, 1]), D("ident", [128, 128]), D("tri", [128, 128]),
                      D("masks", [128, 4, 512]), T, T)
        kb.s.emit()
    _NC_CACHE[kind] = nc
    return nc


def _shards(xfull):
    own, pre = [], []
    for c in range(NCORES):
        b, half = c // 2, c % 2
        own.append(np.ascontiguousarray(xfull[b, half * TOK:(half + 1) * TOK]))
        pre.append(np.ascontiguousarray(xfull[b, 0:TOK]) if half == 1 else np.zeros((TOK, 1024), np.float32))
    return own, pre


def _gather(res):
    out = np.empty((4, 8192, 1024), np.float32)
    for c in range(NCORES):
        b, half = c // 2, c % 2
        out[b, half * TOK:(half + 1) * TOK] = res.results[c]["y"]
    return out


def _run(kind, per_core):
    nc = _build(kind)
    return run_bass_kernel_spmd(nc, per_core, core_ids=list(range(NCORES)))


def _peer_inputs(inp, layer, cst):
    keys = inp["peer_keys"][layer]
    return {"g": np.ascontiguousarray(inp["ffn_norm"][layer][None]), "wq": np.ascontiguousarray(inp["peer_wq"][layer]),
            "keysT": np.ascontiguousarray(keys.transpose(3, 0, 1, 2).reshape(128, 2048)),
            "uT": np.ascontiguousarray(inp["peer_u"][layer].T), "v": np.ascontiguousarray(inp["peer_v"][layer]),
            "ident": cst["ident"], "iota": cst["iota"]}


def kernel(**inp):
    inp = {k: np.asarray(v, dtype=np.float32) for k, v in inp.items()}
    cst = _consts()
    own, pre = _shards(inp["x"])
    common = _fused_common(inp, cst)
    pfl = [np.full((128, 1), 0.0 if c % 2 == 1 else -30000.0, np.float32) for c in range(NCORES)]
    nc = _build_fused(TOK)
    res = run_bass_kernel_spmd(nc, [dict(common, x=own[c], xp=pre[c], pflag=pfl[c]) for c in range(NCORES)],
                               core_ids=list(range(NCORES)))
    return _gather(res)


def kernel_unfused(**inp):
    inp = {k: np.asarray(v, dtype=np.float32) for k, v in inp.items()}
    cst = _consts()
    x = inp["x"]
    own, pre = _shards(x)
    common = {"g": np.ascontiguousarray(inp["ev_norm_mix"][0][None]), "w_in": np.ascontiguousarray(inp["ev_w_in"][0]),
              "w_out": np.ascontiguousarray(inp["ev_w_out"][0]),
              "convwT": np.ascontiguousarray(inp["ev_conv_w"][0].T.reshape(4, 128, 31).transpose(1, 0, 2)),
              "convb": _col4(inp["ev_conv_b"][0]), "lng": _col4(inp["ev_conv_ln_g"][0]), "lnb": _col4(inp["ev_conv_ln_b"][0]),
              "w2": np.ascontiguousarray(inp["ev_gate_w2"][0]), "gateb": np.ascontiguousarray(inp["ev_gate_b"][0][None]),
              "glag": np.ascontiguousarray(inp["ev_gla_norm_g"][0][:, None]), "ident": cst["ident"], "tri": cst["tri"]}
    x1 = _gather(_run("m0", [dict(common, x=own[c], xp=pre[c]) for c in range(NCORES)]))
    own, _ = _shards(x1)
    common = _peer_inputs(inp, 0, cst)
    x2 = _gather(_run("peer", [dict(common, x=own[c]) for c in range(NCORES)]))
    own, pre = _shards(x2)
    common = {"g": np.ascontiguousarray(inp["od_norm_mix"][0][None]), "w_in": np.ascontiguousarray(inp["od_w_in"][0]),
              "w_out": np.ascontiguousarray(inp["od_w_out"][0]), "fb": np.ascontiguousarray(inp["od_fgate_b"][0][None]),
              "qg": np.ascontiguousarray(inp["od_q_norm_g"][0][None]), "kg": np.ascontiguousarray(inp["od_k_norm_g"][0][None]),
              "ident": cst["ident"], "tri": cst["tri"], "masks": cst["masks"]}
    pfl = [np.full((128, 1), 0.0 if c % 2 == 1 else -30000.0, np.float32) for c in range(NCORES)]
    x3 = _gather(_run("fox", [dict(common, x=own[c], xp=pre[c], pflag=pfl[c]) for c in range(NCORES)]))
    own, _ = _shards(x3)
    common = _peer_inputs(inp, 1, cst)
    x4 = _gather(_run("peer", [dict(common, x=own[c]) for c in range(NCORES)]))
    return x4
```

```python
from contextlib import ExitStack

import numpy as np
import concourse.bass as bass
import concourse.mybir as mybir
from concourse.bass_utils import run_bass_kernel_spmd

F32 = mybir.dt.float32
BF16 = mybir.dt.bfloat16
U32 = mybir.dt.uint32
I32 = mybir.dt.int32
ALU = mybir.AluOpType
AF = mybir.ActivationFunctionType
AX = mybir.AxisListType

NCORES = 8
EPOCH = 30000
ENGS = ("pe", "dve", "act", "pool", "sp")


class Buf:
    __slots__ = ("w", "r", "name")

    def __init__(self, name=""):
        self.w = None
        self.r = []
        self.name = name


class TT:
    def __init__(self, t, name):
        self.t = t
        self.b = Buf(name)

    def __getitem__(self, k):
        return self.t[k]


class Sched:
    def __init__(self, nc, stack):
        self.nc = nc
        self.stack = stack
        self.q = {e: [] for e in ENGS}
        self.csem = {e: None for e in ENGS}
        self.ccnt = {e: 0 for e in ENGS}
        self.dsem = {e: [] for e in ENGS}
        self.dcnt = {e: [] for e in ENGS}
        self.drr = {e: 0 for e in ENGS}
        self.seen = {e: {} for e in ENGS}
        self.nsem = 0
        self.ninst = 0
        self.defer = None

    def _newsem(self, nm):
        self.nsem += 1
        return self.stack.enter_context(self.nc.semaphore(f"{nm}{self.nsem}"))

    def _ticket(self, e, dma):
        if not dma:
            if self.csem[e] is None or self.ccnt[e] >= EPOCH:
                self.csem[e] = self._newsem("c" + e)
                self.ccnt[e] = 0
            self.ccnt[e] += 1
            return (self.csem[e], self.ccnt[e], e, 1)
        if not self.dsem[e]:
            self.dsem[e] = [self._newsem("d" + e) for _ in range(8)]
            self.dcnt[e] = [0] * 8
        i = self.drr[e] % 8
        self.drr[e] += 1
        if self.dcnt[e][i] + 16 >= EPOCH:
            self.dsem[e][i] = self._newsem("d" + e)
            self.dcnt[e][i] = 0
        self.dcnt[e][i] += 16
        return (self.dsem[e][i], self.dcnt[e][i], e + "_dma", 16)

    def op(self, e, fn, reads=(), writes=(), dma=False):
        if self.defer is not None:
            self.defer.append((e, fn, list(reads), list(writes), dma))
            return None
        deps = {}

        def add(t):
            if t is None:
                return
            sem, val, src, _ = t
            if src == "pe" and e == "pe" and not dma:
                return
            k = id(sem)
            if self.seen[e].get(k, 0) >= val:
                return
            if k not in deps or deps[k][1] < val:
                deps[k] = (sem, val)

        for b in reads:
            b = b.b if isinstance(b, TT) else b
            add(b.w)
        for b in writes:
            b = b.b if isinstance(b, TT) else b
            add(b.w)
            for t in b.r:
                add(t)
        if dma and self.dsem[e]:
            i = self.drr[e] % 8
            if self.dcnt[e][i] > 0 and self.dcnt[e][i] + 16 < EPOCH:
                sem_, val_ = self.dsem[e][i], self.dcnt[e][i]
                k = id(sem_)
                if self.seen[e].get(k, 0) < val_ and (k not in deps or deps[k][1] < val_):
                    deps[k] = (sem_, val_)
        waits = list(deps.values())
        for sem, val in waits:
            self.seen[e][id(sem)] = val
        t = self._ticket(e, dma)
        self.q[e].append((waits, fn, t[0], t[3]))
        self.ninst += 1 + len(waits)
        for b in reads:
            b = b.b if isinstance(b, TT) else b
            b.r = [x for x in b.r if x[0] is not t[0]] + [t]
        for b in writes:
            b = b.b if isinstance(b, TT) else b
            b.w = t
            b.r = []
        return t

    def replay(self, lst, n):
        k = min(n, len(lst))
        for e, fn, reads, writes, dma in lst[:k]:
            self.op(e, fn, reads, writes, dma)
        del lst[:k]

    def barrier(self):
        waits = []
        for e in ENGS:
            if self.csem[e] is not None and self.ccnt[e] > 0:
                waits.append((self.csem[e], self.ccnt[e]))
            for sem, c in zip(self.dsem[e], self.dcnt[e]):
                if c > 0:
                    waits.append((sem, c))
        for e in ENGS:
            self.q[e].append((list(waits), None, None, 0))
            for sem, val in waits:
                self.seen[e][id(sem)] = max(self.seen[e].get(id(sem), 0), val)

    def final_wait(self, e, bufs):
        waits = []
        for b in bufs:
            b = b.b if isinstance(b, TT) else b
            for t in [b.w] + list(b.r):
                if t is not None:
                    waits.append((t[0], t[1]))
        self.q[e].append((waits, None, None, 0))

    def emit(self):
        nc = self.nc
        q = self.q

        def run(e, eng):
            for waits, fn, sem, inc in q[e]:
                for s, v in waits:
                    eng.wait_ge(s, v)
                if fn is not None:
                    ins = fn(eng)
                    ins.then_inc(sem, inc)

        with nc.Block() as block:

            @block.tensor
            def _(eng):
                run("pe", eng)

            @block.vector
            def _(eng):
                run("dve", eng)

            @block.scalar
            def _(eng):
                run("act", eng)

            @block.gpsimd
            def _(eng):
                run("pool", eng)

            @block.sync
            def _(eng):
                run("sp", eng)


class KB:
    def __init__(self, nc, stack, sched=None):
        self.nc = nc
        self.stack = stack
        self.s = sched if sched is not None else Sched(nc, stack)

    def scope(self, stack):
        return KB(self.nc, stack, self.s)

    def sb(self, name, shape, dt):
        t = self.stack.enter_context(self.nc.sbuf_tensor(name, list(shape), dt))
        return TT(t, name)

    def dram(self, name, shape, dt, kind):
        t = self.nc.dram_tensor(name, list(shape), dt, kind=kind)
        return TT(t.ap(), name)

    def dma(self, out, in_, reads, writes, e="sp", **kw):
        return self.s.op(e, lambda g: g.dma_start(out=out, in_=in_, **kw), reads, writes, dma=True)

    def mm(self, out, lhsT, rhs, start, stop, reads, writes):
        return self.s.op("pe", lambda g: g.matmul(out, lhsT, rhs, start=start, stop=stop), reads, writes)

    def tr(self, out, in_, ident, reads, writes):
        return self.s.op("pe", lambda g: g.transpose(out, in_, ident), reads, writes)

    def act(self, out, in_, func, reads, writes, bias=None, scale=None, accum_out=None):
        kw = {}
        if bias is not None:
            kw["bias"] = bias
        if scale is not None:
            kw["scale"] = scale
        if accum_out is not None:
            kw["accum_out"] = accum_out
        return self.s.op("act", lambda g: g.activation(out, in_, func, **kw), reads, writes)

    def tt(self, out, in0, in1, op, reads, writes, e="dve"):
        return self.s.op(e, lambda g: g.tensor_tensor(out, in0, in1, op), reads, writes)

    def ts(self, out, in0, s1, s2, op0, op1, reads, writes, e="dve", accum_out=None):
        if op1 is None:
            return self.s.op(e, lambda g: g.tensor_scalar(out, in0, s1, None, op0), reads, writes)
        if accum_out is not None:
            return self.s.op(e, lambda g: g.tensor_scalar(out, in0, s1, s2, op0, op1, accum_out), reads, writes)
        return self.s.op(e, lambda g: g.tensor_scalar(out, in0, s1, s2, op0, op1), reads, writes)

    def stt(self, out, in0, scalar, in1, op0, op1, reads, writes, e="dve"):
        return self.s.op(e, lambda g: g.scalar_tensor_tensor(out, in0, scalar, in1, op0, op1), reads, writes)

    def copy(self, out, in_, reads, writes, e="dve"):
        if e == "act":
            return self.s.op(e, lambda g: g.copy(out, in_), reads, writes)
        return self.s.op(e, lambda g: g.tensor_copy(out, in_), reads, writes)

    def memset(self, ap, val, writes, e="dve"):
        return self.s.op(e, lambda g: g.memset(ap, val), (), writes)

    def reduce(self, out, in_, op, reads, writes, axis=AX.X, e="dve"):
        return self.s.op(e, lambda g: g.tensor_reduce(out, in_, axis, op), reads, writes)


def to_bf16_dram(kb, src, dst, R, C, tag, bufs=None):
    with ExitStack() as st:
        k = kb.scope(st)
        W = 2048
        if bufs is None:
            stg = [k.sb(f"cv_in{tag}{i}", [128, W], F32) for i in range(2)]
            outb = [k.sb(f"cv_out{tag}{i}", [128, W], BF16) for i in range(2)]
        else:
            stg, outb = bufs
        n = 0
        for r in range(R // 128):
            for c0 in range(0, C, W):
                w = min(W, C - c0)
                i = n % 2
                k.dma(stg[i][:, 0:w], src[r * 128:(r + 1) * 128, c0:c0 + w], [src], [stg[i]])
                k.copy(outb[i][:, 0:w], stg[i][:, 0:w], [stg[i]], [outb[i]], e=("dve" if n % 2 == 0 else "act"))
                k.dma(dst[r * 128:(r + 1) * 128, c0:c0 + w], outb[i][:, 0:w], [outb[i]], [], e="pool")
                n += 1
        if bufs is None:
            k.s.barrier()


def load_bf16_resident(kb, dst_tt, dst_ap_fn, src, nrows_blocks, C, tag):
    with ExitStack() as st:
        k = kb.scope(st)
        W = 2048
        stg = [k.sb(f"ld_in{tag}{i}", [128, W], F32) for i in range(2)]
        n = 0
        for kc in range(nrows_blocks):
            for c0 in range(0, C, W):
                w = min(W, C - c0)
                i = n % 2
                k.dma(stg[i][:, 0:w], src[kc * 128:(kc + 1) * 128, c0:c0 + w], [src], [stg[i]])
                k.copy(dst_ap_fn(kc, c0, w), stg[i][:, 0:w], [stg[i]], [dst_tt], e=("dve" if n % 2 == 0 else "act"))
                n += 1
        k.s.barrier()


def peer_tables(kb0, uT, vtab, tag, bufs=None):
    us = kb0.dram(f"us{tag}", [1024, 16384], BF16, "Internal")
    vs = kb0.dram(f"vs{tag}", [16384, 1024], BF16, "Internal")
    to_bf16_dram(kb0, uT, us, 1024, 16384, tag + "u", bufs)
    to_bf16_dram(kb0, vtab, vs, 16384, 1024, tag + "v", bufs)
    return us, vs


def peer_block(kb0, x_in, x_out, gvec, wq, keysT, uT, vtab, ident_d, iota_d, T, tag, G=256, CH=2, OHT=16, tables=None, jobs=None, NSL=3):
    nc = kb0.nc
    if jobs is None:
        jobs = [(x_in, x_out, T)]
    with ExitStack() as st:
        kb = kb0.scope(st)
        TPG = G // 128
        if tables is None:
            tables = peer_tables(kb, uT, vtab, tag)
        us, vs = tables
        wqb = kb.sb("wqb" + tag, [128, 8, 2048], BF16)
        load_bf16_resident(kb, wqb, lambda kc, c0, w: wqb[:, kc, c0:c0 + w], wq, 8, 2048, tag + "wq")
        kTb = kb.sb("kTb" + tag, [128, 16 * 128], BF16)
        load_bf16_resident(kb, kTb, lambda kc, c0, w: kTb[:, c0:c0 + w], keysT, 1, 2048, tag + "kt")
        gB = kb.sb("gB" + tag, [128, 1024], F32)
        kb.dma(gB[:], gvec.t.partition_broadcast(128)[:, 0, :], [gvec], [gB])
        ident = kb.sb("ident" + tag, [128, 128], F32)
        kb.dma(ident[:], ident_d[:], [ident_d], [ident])
        iota = kb.sb("iota" + tag, [128, 128], F32)
        kb.dma(iota[:], iota_d[:], [iota_d], [iota])
        iotab = kb.sb("iotab" + tag, [128, 128], BF16)
        kb.copy(iotab[:], iota[:], [iota], [iotab])

        xs = [[kb.sb(f"xs{tag}{p}{j}", [128, 1024], F32) for j in range(TPG)] for p in range(2)]
        hT = [kb.sb(f"hT{tag}{p}", [128, 8, G], BF16) for p in range(2)]
        trioT = [[kb.sb(f"trioT{tag}{p}{j}", [128, 3, 128], F32) for j in range(TPG)] for p in range(2)]
        hf = kb.sb("hf" + tag, [128, 1024], F32)
        ss = kb.sb("ss" + tag, [128, 4], F32)
        qT = kb.sb("qT" + tag, [128, 16, 128], BF16)
        sc = kb.sb("sc" + tag, [128, 16, 128], F32)
        tmp4 = kb.sb("tmp4" + tag, [128, 4, 256], F32)
        tmpb = [Buf() for _ in range(4)]
        scb = [Buf() for _ in range(4)]
        v16 = kb.sb("v16" + tag, [128, 16, 16], F32)
        i16 = kb.sb("i16" + tag, [128, 16, 16], U32)
        i16f = kb.sb("i16f" + tag, [128, 16, 16], F32)
        cand = kb.sb("cand" + tag, [128, 8, 16, 16], F32)
        s16 = kb.sb("s16" + tag, [128, 8, 16], F32)
        j16 = kb.sb("j16" + tag, [128, 8, 16], U32)
        jaf = kb.sb("jaf" + tag, [128, 8, 16], F32)
        ja = kb.sb("ja" + tag, [128, 8, 16], U32)
        jb = kb.sb("jb" + tag, [128, 8, 16], U32)
        jbf = kb.sb("jbf" + tag, [128, 8, 16], F32)
        eq = cand
        trio = kb.sb("trio" + tag, [128, 3, 128], F32)
        zz = kb.sb("zz" + tag, [128, 8], F32)
        oh1 = kb.sb("oh1" + tag, [128, OHT, 128], BF16)
        oh2 = kb.sb("oh2" + tag, [128, OHT, 128], BF16)
        ohb = [Buf() for _ in range(OHT)]
        ohb2 = [Buf() for _ in range(OHT)]
        WT = kb.sb("WT" + tag, [128, 128, G], BF16)
        WTb = [Buf() for _ in range(G // 16)]
        ub = [kb.sb(f"ub{tag}{i}", [128, 8, CH * 128], BF16) for i in range(NSL)]
        vb = [kb.sb(f"vb{tag}{i}", [128, CH, 1024], BF16) for i in range(NSL)]
        actT = [kb.sb(f"actT{tag}{i}", [128, G], BF16) for i in range(2)]
        ct = [kb.sb(f"ct{tag}{i}", [128, G], BF16) for i in range(2)]
        ps = st.enter_context(nc.psum_tensor("ps" + tag, [128, 4096], F32))
        pb = [Buf(f"pb{i}") for i in range(8)]

        def bank(i, w=512):
            return ps[:, i * 512:i * 512 + w]

        us_v = us.t.rearrange("(kc p) e -> p kc e", p=128)
        vs_v = vs.t.rearrange("(e1 p) d -> p e1 d", p=128)

        def topk_chain(vals, vdst, idst, tm, vb_, ib_, tb_, srcb):
            yield kb.s.op("dve", lambda g_: g_.max(out=vdst[:, 0:8], in_=vals), [srcb], [vb_])
            yield kb.s.op("dve", lambda g_: g_.match_replace(out=tm, in_to_replace=vdst[:, 0:8], in_values=vals,
                                                              imm_value=-1e30), [srcb, vb_], [tb_])
            yield kb.s.op("dve", lambda g_: g_.max(out=vdst[:, 8:16], in_=tm), [tb_], [vb_])
            yield kb.s.op("dve", lambda g_: g_.max_index(out=idst[:, 0:8], in_max=vdst[:, 0:8], in_values=vals), [srcb, vb_], [ib_])
            yield kb.s.op("dve", lambda g_: g_.max_index(out=idst[:, 8:16], in_max=vdst[:, 8:16], in_values=vals), [srcb, vb_], [ib_])

        def run_interleaved(chains, width=4):
            live = []
            chains = list(chains)
            while chains or live:
                while chains and len(live) < width:
                    live.append(chains.pop(0))
                nxt = []
                for ch_ in live:
                    try:
                        next(ch_)
                        nxt.append(ch_)
                    except StopIteration:
                        pass
                live = nxt

        def p1a(p, x_in, g, j):
            tok0 = g * G + j * 128
            xt = xs[p][j]
            kb.dma(xt[:], x_in[tok0:tok0 + 128, :], [x_in], [xt])
            kb.act(hf[:], xt[:], AF.Square, [xt], [hf])
            kb.reduce(ss[:, 0:1], hf[:], ALU.add, [hf], [ss])
            kb.ts(ss[:, 1:2], ss[:, 0:1], 1.0 / 1024, 1e-6, ALU.mult, ALU.add, [ss], [ss])
            kb.act(ss[:, 3:4], ss[:, 1:2], AF.Sqrt, [ss], [ss])
            kb.s.op("dve", lambda g_: g_.reciprocal(ss[:, 2:3], ss[:, 3:4]), [ss], [ss])
            kb.stt(hf[:], xt[:], ss[:, 2:3], gB[:], ALU.mult, ALU.mult, [xt, ss, gB], [hf])
            for hb in range(2):
                for k4 in range(4):
                    kc = hb * 4 + k4
                    kb.tr(ps[:, (6 + hb) * 512 + k4 * 128:(6 + hb) * 512 + (k4 + 1) * 128], hf[:, kc * 128:(kc + 1) * 128], ident[:],
                          [hf, ident], [pb[6 + hb]])
                kb.copy(hT[p][:, hb * 4:(hb + 1) * 4, j * 128:(j + 1) * 128],
                        bank(6 + hb).rearrange("p (k t) -> p k t", t=128), [pb[6 + hb]], [hT[p]], e="act")
            for q4 in range(4):
                bk = 6 + q4 % 2
                for c4 in range(4):
                    c = q4 * 4 + c4
                    for kc in range(8):
                        kb.mm(ps[:, bk * 512 + c4 * 128:bk * 512 + (c4 + 1) * 128], wqb[:, kc, c * 128:(c + 1) * 128],
                              hT[p][:, kc, j * 128:(j + 1) * 128], kc == 0, kc == 7, [wqb, hT[p]], [pb[bk]])
                kb.copy(qT[:, q4 * 4:(q4 + 1) * 4, :], bank(bk).rearrange("p (k t) -> p k t", t=128), [pb[bk]], [qT], e="act")
            for q4 in range(4):
                bk = 6 + q4 % 2
                for c4 in range(4):
                    c = q4 * 4 + c4
                    kb.mm(ps[:, bk * 512 + c4 * 128:bk * 512 + (c4 + 1) * 128], qT[:, c, :], kTb[:, c * 128:(c + 1) * 128], True, True,
                          [qT, kTb], [pb[bk]])
                kb.copy(sc[:, q4 * 4:(q4 + 1) * 4, :], bank(bk).rearrange("p (k t) -> p k t", t=128), [pb[bk]], [scb[q4]], e="act")
            v16b = [Buf() for _ in range(16)]
            i16b = [Buf() for _ in range(16)]
            run_interleaved([topk_chain(sc[:, c, :], v16[:, c, :], i16[:, c, :], tmp4[:, c % 4, 0:128], v16b[c], i16b[c], tmpb[c % 4], scb[c // 4])
                             for c in range(16)])
            kb.copy(i16f[:], i16[:], i16b, [i16f], e="pool")
            v16v = v16[:].rearrange("p (h two) k -> p h two k", two=2)
            kb.tt(cand[:], v16v[:, :, 0, :].unsqueeze(3).to_broadcast([128, 8, 16, 16]),
                  v16v[:, :, 1, :].unsqueeze(2).to_broadcast([128, 8, 16, 16]), ALU.add, v16b, [cand])
            s16b = [Buf() for _ in range(8)]
            j16b = [Buf() for _ in range(8)]
            run_interleaved([topk_chain(cand[:, h, :, :].rearrange("p a b -> p (a b)"), s16[:, h, :], j16[:, h, :], tmp4[:, h % 4, :],
                                        s16b[h], j16b[h], tmpb[h % 4], cand.b) for h in range(8)])
            gt = trio[:, 2, :].rearrange("p (h k) -> p h k", k=16)
            kb.tt(gt, s16[:], s16[:, :, 0:1].to_broadcast([128, 8, 16]), ALU.subtract, s16b, [trio], e="pool")
            kb.act(gt, gt, AF.Exp, [trio], [trio])
            kb.reduce(zz[:], gt, ALU.add, [trio], [zz])
            kb.s.op("dve", lambda g_: g_.reciprocal(zz[:], zz[:]), [zz], [zz])
            kb.tt(gt, gt, zz[:].unsqueeze(2).to_broadcast([128, 8, 16]), ALU.mult, [trio, zz], [trio])
            kb.s.op("dve", lambda g_: g_.tensor_single_scalar(ja[:], j16[:], 4, ALU.logical_shift_right), j16b, [ja])
            kb.s.op("dve", lambda g_: g_.tensor_single_scalar(jb[:], j16[:], 15, ALU.bitwise_and), j16b, [jb])
            kb.copy(jaf[:], ja[:], [ja], [jaf])
            kb.copy(jbf[:], jb[:], [jb], [jbf])
            i16v = i16f[:].rearrange("p (h two) k -> p h two k", two=2)
            iota16 = iota[:, 0:16].unsqueeze(1).unsqueeze(1).to_broadcast([128, 8, 16, 16])
            for which, jf in ((0, jaf), (1, jbf)):
                kb.tt(eq[:], iota16, jf[:].unsqueeze(3).to_broadcast([128, 8, 16, 16]), ALU.is_equal, [iota, jf], [eq])
                kb.tt(eq[:], eq[:], i16v[:, :, which, :].unsqueeze(2).to_broadcast([128, 8, 16, 16]), ALU.mult, [eq, i16f], [eq])
                kb.reduce(trio[:, which, :], eq[:].rearrange("p h k a -> p (h k) a"), ALU.add, [eq], [trio])
            for w3 in range(3):
                kb.tr(ps[:, 6 * 512 + w3 * 128:6 * 512 + (w3 + 1) * 128], trio[:, w3, :], ident[:], [trio, ident], [pb[6]])
            tT = trioT[p][j]
            kb.copy(tT[:].rearrange("p a t -> p (a t)"), ps[:, 6 * 512:6 * 512 + 384], [pb[6]], [tT], e="act")

        wstate = [0]

        def p1b(p, j):
            tT = trioT[p][j]
            for t0 in range(0, 128, 16):
                half = wstate[0] % 2
                wstate[0] += 1
                for tt_ in range(16):
                    tl = (t0 + tt_) % OHT
                    tg = t0 + tt_
                    kb.ts(oh1[:, tl, :], iotab[:], tT[:, 0, tg:tg + 1], tT[:, 2, tg:tg + 1], ALU.is_equal, ALU.mult, [iotab, tT], [ohb[tl]])
                    kb.ts(oh2[:, tl, :], iotab[:], tT[:, 1, tg:tg + 1], None, ALU.is_equal, None, [iotab, tT], [ohb2[tl]])
                    col = half * 2048 + tt_ * 128
                    kb.mm(ps[:, col:col + 128], oh2[:, tl, :], oh1[:, tl, :], True, True, [ohb[tl], ohb2[tl]], [pb[half * 4 + tt_ // 4]])
                tcol = j * 128 + t0
                kb.copy(WT[:, :, tcol:tcol + 16], ps[:, half * 2048:(half + 1) * 2048].rearrange("p (t e) -> p e t", e=128),
                        [pb[half * 4 + q_] for q_ in range(4)], [WTb[tcol // 16]], e="act")

        def p2(p, x_out, g, bg):
            per = (len(bg) + 127) // 128 if bg else 0

            def u_stage(e1):
                cg, el = e1 // CH, e1 % CH
                sl = cg % NSL
                if el == 0:
                    kb.dma(ub[sl][:], us_v[:, :, cg * CH * 128:(cg + 1) * CH * 128], [us], [ub[sl]])
                    kb.dma(vb[sl][:], vs_v[:, cg * CH:(cg + 1) * CH, :], [vs], [vb[sl]], e="pool")
                a = e1 % 2
                pu = ps[:, 2048 + a * 512:2048 + a * 512 + G]
                for kc in range(8):
                    kb.mm(pu, ub[sl][:, kc, el * 128:(el + 1) * 128], hT[p][:, kc, :], kc == 0, kc == 7, [ub[sl], hT[p]], [pb[4 + a]])
                kb.act(actT[a][:], pu, AF.Gelu, [pb[4 + a]], [actT[a]])
                kb.tt(ct[a][:], actT[a][:], WT[:, e1, :], ALU.mult, [actT[a]] + WTb, [ct[a]])

            def v_stage(e1):
                cg, el = e1 // CH, e1 % CH
                sl = cg % NSL
                a = e1 % 2
                for j in range(TPG):
                    for hh in range(2):
                        kb.mm(bank(2 * j + hh), ct[a][:, j * 128:(j + 1) * 128], vb[sl][:, el, hh * 512:(hh + 1) * 512],
                              e1 == 0, e1 == 127, [ct[a], vb[sl]], [pb[2 * j + hh]])

            u_stage(0)
            for e1 in range(128):
                if e1 + 1 < 128:
                    u_stage(e1 + 1)
                v_stage(e1)
                if bg:
                    kb.s.replay(bg, per)
            xo = tmp4[:].rearrange("p a b -> p (a b)")
            for j in range(TPG):
                tok0 = g * G + j * 128
                kb.tt(xo, ps[:, j * 1024:(j + 1) * 1024], xs[p][j][:], ALU.add, [pb[2 * j], pb[2 * j + 1], xs[p][j]], tmpb)
                kb.dma(x_out[tok0:tok0 + 128, :], xo, tmpb, [], e="pool")
            if bg:
                kb.s.replay(bg, len(bg))

        groups = [(a_, b_, g_) for (a_, b_, t_) in jobs for g_ in range(t_ // G)]
        for j in range(TPG):
            p1a(0, groups[0][0], groups[0][2], j)
        for gi, (xin_, xout_, g) in enumerate(groups):
            p = gi % 2
            for j in range(TPG):
                p1b(p, j)
            bg = []
            if gi + 1 < len(groups):
                kb.s.defer = bg
                for j in range(TPG):
                    p1a(1 - p, groups[gi + 1][0], groups[gi + 1][2], j)
                kb.s.defer = None
            p2(p, xout_, g, bg)
        kb.s.barrier()


def rms_to_hT(kb, xs_t, gB, ident, hf, sq, ss, ps, pbA, pbB, hT, col0, eps=1e-6):
    kb.act(sq[:], xs_t[:], AF.Square, [xs_t], [sq])
    kb.reduce(ss[:, 0:1], sq[:], ALU.add, [sq], [ss])
    kb.ts(ss[:, 1:2], ss[:, 0:1], 1.0 / 1024, eps, ALU.mult, ALU.add, [ss], [ss])
    kb.act(ss[:, 3:4], ss[:, 1:2], AF.Sqrt, [ss], [ss])
    kb.s.op("dve", lambda g_: g_.reciprocal(ss[:, 2:3], ss[:, 3:4]), [ss], [ss])
    kb.stt(hf[:], xs_t[:], ss[:, 2:3], gB[:], ALU.mult, ALU.mult, [xs_t, ss, gB], [hf])
    for kc in range(8):
        kb.tr(ps[:, kc * 128:(kc + 1) * 128], hf[:, kc * 128:(kc + 1) * 128], ident[:], [hf, ident], [pbA if kc < 4 else pbB])
    kb.copy(hT[:, 0:4, col0:col0 + 128], ps[:, 0:512].rearrange("p (k t) -> p k t", t=128), [pbA], [hT], e="act")
    kb.copy(hT[:, 4:8, col0:col0 + 128], ps[:, 512:1024].rearrange("p (k t) -> p k t", t=128), [pbB], [hT], e="dve")


def mixer0_block(kb0, x_own, x_pre, x_out, gvec, w_in, w_out, convwT, convb, lng, lnb, w2, gateb, glag,
                 ident_d, tri_d, T, TP, tag="m0", x_out_pre=None, bg=None):
    nc = kb0.nc
    NTO = T // 128
    NTP = TP // 128
    with ExitStack() as st:
        kb = kb0.scope(st)
        winb = kb.sb("winb", [128, 8, 2576], BF16)
        load_bf16_resident(kb, winb, lambda kc, c0, w: winb[:, kc, c0:c0 + w], w_in, 8, 2576, "win")
        woutb = kb.sb("woutb", [128, 8, 1024], BF16)
        load_bf16_resident(kb, woutb, lambda kc, c0, w: woutb[:, kc, c0:c0 + w], w_out, 8, 1024, "wout")
        gB = kb.sb("gBm", [128, 1024], F32)
        kb.dma(gB[:], gvec.t.partition_broadcast(128)[:, 0, :], [gvec], [gB])
        gbB = kb.sb("gbB", [128, 256], F32)
        kb.dma(gbB[:], gateb.t.partition_broadcast(128)[:, 0, :], [gateb], [gbB])
        ident = kb.sb("identm", [128, 128], F32)
        kb.dma(ident[:], ident_d[:], [ident_d], [ident])
        tri = kb.sb("trim", [128, 128], F32)
        kb.dma(tri[:], tri_d[:], [tri_d], [tri])
        ones = kb.sb("onesm", [128, 128], F32)
        kb.memset(ones[:], 1.0, [ones])
        cw = kb.sb("cw", [128, 4, 31], F32)
        kb.dma(cw[:], convwT[:], [convwT], [cw])
        cols = kb.sb("colsm", [128, 16], F32)
        kb.dma(cols[:, 0:4], convb[:], [convb], [cols])
        kb.dma(cols[:, 4:8], lng[:], [lng], [cols])
        kb.dma(cols[:, 8:12], lnb[:], [lnb], [cols])
        kb.dma(cols[:, 12:13], glag[:], [glag], [cols])
        w2f = kb.sb("w2f", [16, 256], F32)
        kb.dma(w2f[:], w2[:], [w2], [w2f])
        w2b = kb.sb("w2b", [16, 256], BF16)
        kb.copy(w2b[:], w2f[:], [w2f], [w2b])
        diag = kb.sb("diag", [128, 4 * 31, 128], BF16)
        for c in range(4):
            for j in range(31):
                kb.ts(diag[:, c * 31 + j, :], ident[:], cw[:, c, j:j + 1], None, ALU.mult, None, [ident, cw], [diag],
                      e=("dve" if (c * 31 + j) % 2 == 0 else "pool"))
        full = x_out_pre is not None
        UW = 158 if full else 30 + T
        uT = kb.sb("uT", [128, 4, UW], BF16)
        kb.memset(uT[:, :, 0:30], 0.0, [uT])
        Sf = [kb.sb(f"Sf{h}", [64, 128], F32) for h in range(4)]
        Sb = [kb.sb(f"Sb{h}", [64, 128], BF16) for h in range(4)]
        for h in range(4):
            kb.memset(Sf[h][:], 0.0, [Sf[h]])
            kb.memset(Sb[h][:], 0.0, [Sb[h]])
        xs = kb.sb("xsm", [128, 1024], F32)
        hf = kb.sb("hfm", [128, 1024], F32)
        sq = kb.sb("sqm", [128, 1024], BF16)
        ss = kb.sb("ssm", [128, 4], F32)
        hT = kb.sb("hTm", [128, 8, 128], BF16)
        sg = kb.sb("sg", [128, 512], F32)
        glrT = kb.sb("glrT", [16, 128], BF16)
        zb = kb.sb("zb", [128, 256], F32)
        la = kb.sb("la", [128, 256], F32)
        enb_tm = kb.sb("enb_tm", [128, 256], F32)
        ebl_tm = kb.sb("ebl_tm", [128, 256], F32)
        kdec = kb.sb("kdec", [128, 256], BF16)
        vbf = kb.sb("vbf", [128, 512], BF16)
        eblc = kb.sb("eblc", [64, 8], F32)
        eb = kb.sb("eb", [64, 512], F32)
        enb = kb.sb("enb", [64, 512], F32)
        qt = kb.sb("qt", [64, 4, 128], BF16)
        kt = kb.sb("kt", [64, 4, 128], BF16)
        Am = kb.sb("Am", [128, 4, 128], BF16)
        osq = kb.sb("osq", [128, 512], F32)
        rs = kb.sb("rs", [128, 512], F32)
        sr = kb.sb("sr", [128, 512], F32)
        t1 = kb.sb("t1", [128, 512], F32)
        ybT = kb.sb("ybT", [128, 4, 128], BF16)
        ycs = kb.sb("ycs", [128, 4, 128], F32)
        ysq = kb.sb("ysq", [128, 4, 128], F32)
        mean = kb.sb("mean", [128, 128], F32)
        var = kb.sb("var", [128, 128], F32)
        yaT = kb.sb("yaT", [128, 4, 128], BF16)
        xo = kb.sb("xom", [128, 1024], F32)
        ps = st.enter_context(nc.psum_tensor("psm", [128, 4096], F32))
        pb = [Buf(f"pbm{i}") for i in range(8)]

        def B(i, lo=0, hi=512):
            return ps[:, i * 512 + lo:i * 512 + hi]

        def fm_proj(colbase, ncols, bank, slot):
            for kc in range(8):
                kb.mm(ps[0:ncols, bank * 512 + slot * 128:bank * 512 + (slot + 1) * 128], winb[:, kc, colbase:colbase + ncols],
                      hT[:, kc, :], kc == 0, kc == 7, [winb, hT], [pb[bank]])

        def state_part(u_needed, own):
            for kc in range(8):
                kb.mm(B(5), hT[:, kc, :], winb[:, kc, 1536:2048], kc == 0, kc == 7, [hT, winb], [pb[5]])
            for kc in range(8):
                kb.mm(B(6, 0, 256), hT[:, kc, :], winb[:, kc, 1280:1536], kc == 0, kc == 7, [hT, winb], [pb[6]])
            fm_proj(2560, 16, 4, 0)
            kb.copy(glrT[:], ps[0:16, 4 * 512:4 * 512 + 128], [pb[4]], [glrT], e="act")
            kb.mm(B(6, 256, 512), glrT[:], w2b[:], True, True, [glrT, w2b], [pb[6]])
            kb.tt(zb[:], B(6, 256, 512), gbB[:], ALU.add, [pb[6], gbB], [zb])
            kb.act(zb[:], zb[:], AF.Exp, [zb], [zb], scale=-1.0)
            kb.act(zb[:], zb[:], AF.Ln, [zb], [zb], bias=1.0)
            kb.ts(la[:], zb[:], -1.0 / 16.0, None, ALU.mult, None, [zb], [la])
            kb.mm(B(7, 0, 256), tri[:], la[:], True, True, [tri, la], [pb[7]])
            kb.mm(B(7, 256, 512), ones[:], la[:], True, True, [ones, la], [pb[7]])
            for h in range(4):
                kb.mm(ps[0:64, 4 * 512 + 384 + 2 * h:4 * 512 + 386 + 2 * h], la[:, h * 64:(h + 1) * 64], ones[:, 0:2], True, True,
                      [la, ones], [pb[4]])
            kb.act(eblc[:], ps[0:64, 4 * 512 + 384:4 * 512 + 392], AF.Exp, [pb[4]], [eblc])
            kb.act(enb_tm[:], B(7, 0, 256), AF.Exp, [pb[7]], [enb_tm], scale=-1.0)
            kb.act(ebl_tm[:], B(7, 256, 512), AF.Exp, [pb[7]], [ebl_tm])
            kb.tt(enb_tm[:], enb_tm[:], ebl_tm[:], ALU.mult, [enb_tm, ebl_tm], [enb_tm])
            kb.tt(kdec[:], B(6, 0, 256), enb_tm[:], ALU.mult, [pb[6], enb_tm], [kdec])
            kb.copy(vbf[:], B(5), [pb[5]], [vbf], e="act")

        def state_update():
            for h in range(4):
                kb.mm(ps[0:64, 2 * 512 + h * 128:2 * 512 + (h + 1) * 128], kdec[:, h * 64:(h + 1) * 64], vbf[:, h * 128:(h + 1) * 128],
                      True, True, [kdec, vbf], [pb[2]])
            for h in range(4):
                kb.stt(Sf[h][:], Sf[h][:], eblc[:, 2 * h:2 * h + 1], ps[0:64, 2 * 512 + h * 128:2 * 512 + (h + 1) * 128],
                       ALU.mult, ALU.add, [Sf[h], eblc, pb[2]], [Sf[h]])
                kb.copy(Sb[h][:], Sf[h][:], [Sf[h]], [Sb[h]], e="act")

        def conv_u(tokcol):
            for c in range(4):
                fm_proj(c * 128, 128, 0, c)
                fm_proj(512 + c * 128, 128, 1, c)
            kb.act(sg[:], B(1), AF.Sigmoid, [pb[1]], [sg])
            kb.tt(uT[:, :, 30 + tokcol:30 + tokcol + 128], B(0).rearrange("p (c t) -> p c t", t=128),
                  sg[:].rearrange("p (c t) -> p c t", t=128), ALU.mult, [pb[0], sg], [uT])

        for i in range(0 if full else NTP):
            kb.dma(xs[:], x_pre[i * 128:(i + 1) * 128, :], [x_pre], [xs])
            rms_to_hT(kb, xs, gB, ident, hf, sq, ss, ps, pb[0], pb[1], hT, 0)
            state_part(False, False)
            state_update()
            if i == NTP - 1:
                for c in range(4):
                    fm_proj(c * 128, 128, 0, c)
                    fm_proj(512 + c * 128, 128, 1, c)
                kb.act(sg[:], B(1), AF.Sigmoid, [pb[1]], [sg])
                kb.tt(uT[:, :, 0:30], B(0).rearrange("p (c t) -> p c t", t=128)[:, :, 98:128],
                      sg[:].rearrange("p (c t) -> p c t", t=128)[:, :, 98:128], ALU.mult, [pb[0], sg], [uT])
        bg_per = (len(bg) + (NTP + NTO) - 1) // (NTP + NTO) if bg else 0
        for ii in range((NTP + NTO) if full else NTO):
            if full:
                isown = ii >= NTP
                i = 0
                srcx = x_own[(ii - NTP) * 128:(ii - NTP + 1) * 128, :] if isown else x_pre[ii * 128:(ii + 1) * 128, :]
                dsty = x_out[(ii - NTP) * 128:(ii - NTP + 1) * 128, :] if isown else x_out_pre[ii * 128:(ii + 1) * 128, :]
            else:
                i = ii
                srcx = x_own[i * 128:(i + 1) * 128, :]
                dsty = x_out[i * 128:(i + 1) * 128, :]
            if bg:
                kb.s.replay(bg, bg_per)
            kb.dma(xs[:], srcx, [x_own, x_pre], [xs])
            rms_to_hT(kb, xs, gB, ident, hf, sq, ss, ps, pb[0], pb[1], hT, 0)
            state_part(True, True)
            conv_u(i * 128)
            for h in range(4):
                fm_proj(1024 + h * 64, 64, 2, h)
                fm_proj(1280 + h * 64, 64, 3, h)
            for sl in range(4):
                fm_proj(2048 + sl * 128, 128, 4, sl)
            for h in range(4):
                kb.mm(ps[0:64, 5 * 512 + h * 128:5 * 512 + (h + 1) * 128], la[:, h * 64:(h + 1) * 64], tri[:], True, True,
                      [la, tri], [pb[5]])
            kb.act(eb[:], ps[0:64, 5 * 512:6 * 512], AF.Exp, [pb[5]], [eb])
            kb.act(enb[:], ps[0:64, 5 * 512:6 * 512], AF.Exp, [pb[5]], [enb], scale=-1.0)
            kb.stt(qt[:].rearrange("p h t -> p (h t)"), ps[0:64, 2 * 512:3 * 512], 0.125, eb[:], ALU.mult, ALU.mult, [pb[2], eb], [qt])
            kb.tt(kt[:].rearrange("p h t -> p (h t)"), ps[0:64, 3 * 512:4 * 512], enb[:], ALU.mult, [pb[3], enb], [kt])
            kb.act(sr[:], B(4), AF.Silu, [pb[4]], [sr])
            for h in range(4):
                kb.mm(B(0, h * 128, (h + 1) * 128), kt[:, h, :], qt[:, h, :], True, True, [kt, qt], [pb[0]])
            kb.tt(Am[:], B(0).rearrange("p (h t) -> p h t", t=128), tri[:].unsqueeze(1).to_broadcast([128, 4, 128]), ALU.mult,
                  [pb[0], tri], [Am])
            for h in range(4):
                kb.mm(B(1, h * 128, (h + 1) * 128), vbf[:, h * 128:(h + 1) * 128], Am[:, h, :], True, False, [vbf, Am], [pb[1]])
                kb.mm(B(1, h * 128, (h + 1) * 128), Sb[h][:], qt[:, h, :], False, True, [Sb[h], qt], [pb[1]])
            state_update()
            kb.act(osq[:], B(1), AF.Square, [pb[1]], [osq])
            kb.mm(B(3), ones[:], osq[:], True, True, [ones, osq], [pb[3]])
            kb.ts(rs[:], B(3), 1.0 / 128, 1e-6, ALU.mult, ALU.add, [pb[3]], [rs])
            kb.act(rs[:], rs[:], AF.Sqrt, [rs], [rs])
            kb.s.op("dve", lambda g_: g_.reciprocal(rs[:], rs[:]), [rs], [rs])
            kb.tt(t1[:], B(1), rs[:], ALU.mult, [pb[1], rs], [t1])
            kb.stt(ybT[:].rearrange("p h t -> p (h t)"), t1[:], cols[:, 12:13], sr[:], ALU.mult, ALU.mult, [t1, cols, sr], [ybT])
            for c in range(4):
                for j in range(31):
                    kb.mm(B(4, c * 128, (c + 1) * 128), diag[:, c * 31 + j, :], uT[:, c, i * 128 + j:i * 128 + j + 128],
                          j == 0, j == 30, [diag, uT], [pb[4]])
            for c in range(4):
                kb.ts(ycs[:, c, :], B(4, c * 128, (c + 1) * 128), cols[:, c:c + 1], None, ALU.add, None, [pb[4], cols], [ycs])
            kb.act(ysq[:], ycs[:], AF.Square, [ycs], [ysq])
            for c in range(4):
                kb.mm(B(5, 0, 128), ones[:], ycs[:, c, :], c == 0, c == 3, [ones, ycs], [pb[5]])
            for c in range(4):
                kb.mm(B(5, 128, 256), ones[:], ysq[:, c, :], c == 0, c == 3, [ones, ysq], [pb[5]])
            kb.ts(mean[:], B(5, 0, 128), 1.0 / 512, None, ALU.mult, None, [pb[5]], [mean])
            kb.tt(var[:], mean[:], mean[:], ALU.mult, [mean], [var])
            kb.stt(var[:], B(5, 128, 256), 1.0 / 512, var[:], ALU.mult, ALU.subtract, [pb[5], var], [var])
            kb.ts(var[:], var[:], 1e-6, None, ALU.add, None, [var], [var])
            kb.act(var[:], var[:], AF.Sqrt, [var], [var])
            kb.s.op("dve", lambda g_: g_.reciprocal(var[:], var[:]), [var], [var])
            kb.tt(ycs[:], ycs[:], mean[:].unsqueeze(1).to_broadcast([128, 4, 128]), ALU.subtract, [ycs, mean], [ycs])
            kb.tt(ycs[:], ycs[:], var[:].unsqueeze(1).to_broadcast([128, 4, 128]), ALU.mult, [ycs, var], [ycs])
            for c in range(4):
                kb.ts(ycs[:, c, :], ycs[:, c, :], cols[:, 4 + c:5 + c], cols[:, 8 + c:9 + c], ALU.mult, ALU.add, [ycs, cols], [ycs])
            kb.act(yaT[:], ycs[:], AF.Silu, [ycs], [yaT])
            for hh in range(2):
                for kc in range(8):
                    lhsT = yaT[:, kc, :] if kc < 4 else ybT[:, kc - 4, :]
                    kb.mm(B(6 + hh), lhsT, woutb[:, kc, hh * 512:(hh + 1) * 512], kc == 0, kc == 7, [yaT, ybT, woutb], [pb[6 + hh]])
            kb.tt(xo[:], ps[:, 6 * 512:8 * 512], xs[:], ALU.add, [pb[6], pb[7], xs], [xo])
            kb.dma(dsty, xo[:], [xo], [], e="pool")
            if full:
                kb.copy(uT[:, :, 0:30], uT[:, :, 128:158], [uT], [uT], e="pool")
        if bg:
            kb.s.replay(bg, len(bg))
        kb.s.barrier()


def fox_block(kb0, x_own, x_pre, x_out, gvec, w_in, w_out, fb, qg, kg, pflag, ident_d, tri_d, masks_d, T, TP, tag="fx", bg=None):
    nc = kb0.nc
    NTO = T // 128
    NTP = TP // 128
    NTA = NTO + NTP
    NSB = T // 512
    QT = kb0.dram("fxQT", [16, 65, T], BF16, "Internal")
    KT = kb0.dram("fxKT", [16, 65, TP + T], BF16, "Internal")
    VA = kb0.dram("fxVA", [NTA, 128, 16 * 65], BF16, "Internal")
    OT = kb0.dram("fxOT", [16, 64, T], BF16, "Internal")
    with ExitStack() as st0:
        kbp = kb0.scope(st0)
        negc = kbp.sb("negc", [128, NTA, 16], F32)
        ident = kbp.sb("identf", [128, 128], F32)
        kbp.dma(ident[:], ident_d[:], [ident_d], [ident])
        ones = kbp.sb("onesf", [128, 128], F32)
        kbp.memset(ones[:], 1.0, [ones])
        woutb = kbp.sb("woutbf", [128, 8, 1024], BF16)
        load_bf16_resident(kbp, woutb, lambda kc, c0, w: woutb[:, kc, c0:c0 + w], w_out, 8, 1024, "fwout")
        with ExitStack() as st:
            kb = kbp.scope(st)
            winb = kb.sb("winbf", [128, 8, 3088], BF16)
            load_bf16_resident(kb, winb, lambda kc, c0, w: winb[:, kc, c0:c0 + w], w_in, 8, 3088, "fwin")
            gB = kb.sb("gBf", [128, 1024], F32)
            kb.dma(gB[:], gvec.t.partition_broadcast(128)[:, 0, :], [gvec], [gB])
            fbB = kb.sb("fbB", [128, 16], F32)
            kb.dma(fbB[:], fb.t.partition_broadcast(128)[:, 0, :], [fb], [fbB])
            qgB = kb.sb("qgB", [128, 64], F32)
            kb.dma(qgB[:], qg.t.partition_broadcast(128)[:, 0, :], [qg], [qgB])
            kgB = kb.sb("kgB", [128, 64], F32)
            kb.dma(kgB[:], kg.t.partition_broadcast(128)[:, 0, :], [kg], [kgB])
            pfl = kb.sb("pfl", [128, 1], F32)
            kb.dma(pfl[:], pflag[:], [pflag], [pfl])
            tri = kb.sb("trif", [128, 128], F32)
            kb.dma(tri[:], tri_d[:], [tri_d], [tri])
            xs = kb.sb("xsf", [128, 1024], F32)
            hf = kb.sb("hff", [128, 1024], F32)
            sq = kb.sb("sqf", [128, 1024], BF16)
            ss = kb.sb("ssf", [128, 4], F32)
            hT = kb.sb("hTf", [128, 8, 128], BF16)
            nsq = kb.sb("nsq", [128, 1024], F32)
            nss = kb.sb("nss", [128, 16], F32)
            qa = kb.sb("qa", [128, 16, 65], F32)
            ka = kb.sb("ka", [128, 16, 65], F32)
            kb.memset(ka[:, :, 64:65], 1.0, [ka])
            va = kb.sb("va", [128, 16, 65], BF16)
            kb.memset(va[:, :, 0:1], 1.0, [va])
            qTs = kb.sb("qTs", [65, 16, 128], BF16)
            kTs = kb.sb("kTs", [65, 16, 128], BF16)
            lf = kb.sb("lf", [128, 16], F32)
            Lsum = kb.sb("Lsum", [128, 16], F32)
            kb.memset(Lsum[:], 0.0, [Lsum])
            ctile = kb.sb("ctile", [128, 16], F32)
            ps = st.enter_context(nc.psum_tensor("psf1", [128, 4096], F32))
            pb = [Buf(f"pbf{i}") for i in range(8)]

            def normed(dst, bank0, gt):
                src = ps[:, bank0 * 512:(bank0 + 2) * 512]
                kb.act(nsq[:], src, AF.Square, [pb[bank0], pb[bank0 + 1]], [nsq])
                kb.reduce(nss[:], nsq[:].rearrange("p (h d) -> p h d", d=64), ALU.add, [nsq], [nss])
                kb.ts(nss[:], nss[:], 1.0 / 64, 1e-6, ALU.mult, ALU.add, [nss], [nss])
                kb.act(nss[:], nss[:], AF.Sqrt, [nss], [nss])
                kb.s.op("dve", lambda g_: g_.reciprocal(nss[:], nss[:]), [nss], [nss])
                kb.tt(dst[:, :, 0:64], src.rearrange("p (h d) -> p h d", d=64), nss[:].unsqueeze(2).to_broadcast([128, 16, 64]),
                      ALU.mult, [pb[bank0], pb[bank0 + 1], nss], [dst])
                kb.tt(dst[:, :, 0:64], dst[:, :, 0:64], gt[:].unsqueeze(1).to_broadcast([128, 16, 64]), ALU.mult, [dst, gt], [dst])

            def transposed_store(src, dstT, dram, tokcol, b0):
                for h in range(16):
                    kb.tr(ps[0:65, b0 * 512 + h * 128:b0 * 512 + (h + 1) * 128], src[:, h, :], ident[:], [src, ident], [pb[b0 + h // 4]])
                for q4 in range(4):
                    kb.copy(dstT[:, q4 * 4:(q4 + 1) * 4, :], ps[0:65, (b0 + q4) * 512:(b0 + q4 + 1) * 512].rearrange("p (h t) -> p h t", t=128),
                            [pb[b0 + q4]], [dstT], e=("act" if q4 % 2 == 0 else "dve"))
                kb.dma(dram.t.rearrange("h r t -> r h t")[:, :, tokcol:tokcol + 128], dstT[:], [dstT], [], e="pool")

            for i in range(NTA):
                own = i >= NTP
                src = x_own[(i - NTP) * 128:(i - NTP + 1) * 128, :] if own else x_pre[i * 128:(i + 1) * 128, :]
                kb.dma(xs[:], src, [x_own, x_pre], [xs])
                rms_to_hT(kb, xs, gB, ident, hf, sq, ss, ps, pb[0], pb[1], hT, 0)
                for kc in range(8):
                    kb.mm(ps[:, 6 * 512:6 * 512 + 16], hT[:, kc, :], winb[:, kc, 3072:3088], kc == 0, kc == 7, [hT, winb], [pb[6]])
                kb.tt(lf[:], ps[:, 6 * 512:6 * 512 + 16], fbB[:], ALU.add, [pb[6], fbB], [lf])
                kb.act(lf[:], lf[:], AF.Exp, [lf], [lf], scale=-1.0)
                kb.act(lf[:], lf[:], AF.Ln, [lf], [lf], bias=1.0)
                kb.ts(lf[:], lf[:], -1.0, None, ALU.mult, None, [lf], [lf])
                kb.mm(ps[:, 6 * 512 + 16:6 * 512 + 32], tri[:], lf[:], True, False, [tri, lf], [pb[6]])
                kb.mm(ps[:, 6 * 512 + 16:6 * 512 + 32], ones[:], Lsum[:], False, True, [ones, Lsum], [pb[6]])
                kb.tt(Lsum[:], Lsum[:], lf[:], ALU.add, [Lsum, lf], [Lsum])
                kb.copy(ctile[:], ps[:, 6 * 512 + 16:6 * 512 + 32], [pb[6]], [ctile], e="act")
                if own:
                    kb.ts(negc[:, i, :], ctile[:], -1.0, None, ALU.mult, None, [ctile], [negc])
                else:
                    kb.ts(negc[:, i, :], ctile[:], -1.0, pfl[:, 0:1], ALU.mult, ALU.add, [ctile, pfl], [negc])
                for hh in range(2):
                    for kc in range(8):
                        kb.mm(ps[:, (2 + hh) * 512:(3 + hh) * 512], hT[:, kc, :], winb[:, kc, 1024 + hh * 512:1536 + hh * 512],
                              kc == 0, kc == 7, [hT, winb], [pb[2 + hh]])
                normed(ka, 2, kgB)
                for hh in range(2):
                    for kc in range(8):
                        kb.mm(ps[:, (4 + hh) * 512:(5 + hh) * 512], hT[:, kc, :], winb[:, kc, 2048 + hh * 512:2560 + hh * 512],
                              kc == 0, kc == 7, [hT, winb], [pb[4 + hh]])
                kb.copy(va[:, :, 1:65], ps[:, 4 * 512:6 * 512].rearrange("p (h d) -> p h d", d=64), [pb[4], pb[5]], [va], e="act")
                kb.dma(VA[i], va[:].rearrange("p h d -> p (h d)"), [va], [], e="pool")
                if own:
                    for hh in range(2):
                        for kc in range(8):
                            kb.mm(ps[:, hh * 512:(hh + 1) * 512], hT[:, kc, :], winb[:, kc, hh * 512:(hh + 1) * 512],
                                  kc == 0, kc == 7, [hT, winb], [pb[hh]])
                    normed(qa, 0, qgB)
                    kb.ts(qa[:, :, 64:65], ctile[:].unsqueeze(2), 8.0, None, ALU.mult, None, [ctile], [qa])
                transposed_store(ka, kTs, KT, i * 128, 2)
                if own:
                    transposed_store(qa, qTs, QT, (i - NTP) * 128, 2)
            kb.s.barrier()
        with ExitStack() as st:
            kb = kbp.scope(st)
            kts = kb.sb("kts", [65, TP + T], BF16)
            vas = kb.sb("vas", [128, NTA, 65], BF16)
            qts = kb.sb("qts", [65, T], BF16)
            masks = kb.sb("masksb", [128, 4, 512], F32)
            kb.dma(masks[:], masks_d[:], [masks_d], [masks])
            stmp = kb.sb("stmp", [128, 512], F32)
            pT = [kb.sb(f"pT{i}", [128, 512], BF16) for i in range(3)]
            osb = kb.sb("osb", [65, 512], F32)
            rden = kb.sb("rden", [1, 512], F32)
            oTs = kb.sb("oTs", [65, 512], BF16)
            ps = st.enter_context(nc.psum_tensor("psf2", [128, 4096], F32))
            pb = [Buf(f"pbg{i}") for i in range(8)]
            VAv = VA.t.rearrange("n p (h d) -> p n h d", d=65)
            cnt = 0
            bg_total = len(bg) if bg else 0
            for h in range(16):
                kb.dma(kts[:], KT[h], [], [kts])
                kb.dma(qts[:], QT[h], [], [qts])
                kb.dma(vas[:], VAv[:, :, h, :], [], [vas])
                for j in range(NSB):
                    if bg:
                        kb.s.replay(bg, (bg_total + 16 * NSB - 1) // (16 * NSB))
                    nkb = NTP + 4 * j + 4
                    ob = 4 + (j % 2)
                    def s_stage(kb_, a):
                        kb.mm(ps[:, a * 512:(a + 1) * 512], kts[:, kb_ * 128:(kb_ + 1) * 128], qts[:, j * 512:(j + 1) * 512], True, True,
                              [kts, qts], [pb[a]])
                        m = kb_ - NTP - 4 * j
                        if m >= 0:
                            kb.tt(stmp[:], ps[:, a * 512:(a + 1) * 512], masks[:, m, :], ALU.add, [pb[a], masks], [stmp])
                            kb.act(pT[a][:], stmp[:], AF.Exp, [stmp, negc], [pT[a]], bias=negc[:, kb_, h:h + 1], scale=0.125)
                        else:
                            kb.act(pT[a][:], ps[:, a * 512:(a + 1) * 512], AF.Exp, [pb[a], negc], [pT[a]], bias=negc[:, kb_, h:h + 1], scale=0.125)

                    def pv_stage(kb_, a):
                        kb.mm(ps[0:65, ob * 512:(ob + 1) * 512], vas[:, kb_, :], pT[a][:], kb_ == 0, kb_ == nkb - 1, [vas, pT[a]], [pb[ob]])

                    NBUF = 3
                    for kb_ in range(min(NBUF - 1, nkb)):
                        s_stage(kb_, kb_ % NBUF)
                    for kb_ in range(nkb):
                        if kb_ + NBUF - 1 < nkb:
                            s_stage(kb_ + NBUF - 1, (kb_ + NBUF - 1) % NBUF)
                        pv_stage(kb_, kb_ % NBUF)
                    kb.copy(osb[:], ps[0:65, ob * 512:(ob + 1) * 512], [pb[ob]], [osb], e="act")
                    kb.s.op("dve", lambda g_: g_.reciprocal(rden[:], osb[0:1, :]), [osb], [rden])
                    kb.mm(ps[0:65, 6 * 512:7 * 512], ones[0:1, 0:65], rden[:], True, True, [ones, rden], [pb[6]])
                    kb.tt(oTs[:], osb[:], ps[0:65, 6 * 512:7 * 512], ALU.mult, [osb, pb[6]], [oTs])
                    kb.dma(OT[h, :, j * 512:(j + 1) * 512], oTs[1:65, :], [oTs], [], e="pool")
            if bg:
                kb.s.replay(bg, len(bg))
            kb.s.barrier()
        with ExitStack() as st:
            kb = kbp.scope(st)
            oTt = [kb.sb(f"oTt{i}", [128, 8, 128], BF16) for i in range(2)]
            xs2 = [kb.sb(f"xs2{i}", [128, 1024], F32) for i in range(2)]
            xo = [kb.sb(f"xof{i}", [128, 1024], F32) for i in range(2)]
            ps = st.enter_context(nc.psum_tensor("psf3", [128, 4096], F32))
            pb = [Buf(f"pbh{i}") for i in range(8)]
            OTv = OT.t.rearrange("(p two) d t -> (two d) p t", two=2)
            for i in range(NTO):
                a = i % 2
                kb.dma(oTt[a][:], OTv[:, :, i * 128:(i + 1) * 128], [], [oTt[a]])
                kb.dma(xs2[a][:], x_own[i * 128:(i + 1) * 128, :], [x_own], [xs2[a]])
                for hh in range(2):
                    for p in range(8):
                        kb.mm(ps[:, (2 * a + hh) * 512:(2 * a + hh + 1) * 512], oTt[a][:, p, :], woutb[:, p, hh * 512:(hh + 1) * 512],
                              p == 0, p == 7, [oTt[a], woutb], [pb[2 * a + hh]])
                kb.tt(xo[a][:], ps[:, 2 * a * 512:(2 * a + 2) * 512], xs2[a][:], ALU.add, [pb[2 * a], pb[2 * a + 1], xs2[a]], [xo[a]])
                kb.dma(x_out[i * 128:(i + 1) * 128, :], xo[a][:], [xo[a]], [], e="pool")
            kb.s.barrier()


TOK = 4096


def _consts():
    k = np.arange(128)[:, None]
    q = np.arange(512)[None, :]
    fm = np.zeros((128, 4, 512), np.float32)
    for mm in range(4):
        fm[:, mm, :] = np.where((mm * 128 + k) <= q, 0.0, -240000.0)
    return {
        "ident": np.eye(128, dtype=np.float32),
        "iota": np.tile(np.arange(128, dtype=np.float32), (128, 1)),
        "tri": np.triu(np.ones((128, 128), np.float32)),
        "masks": fm,
    }


def _col4(v):
    return np.ascontiguousarray(v.reshape(4, 128).T)


_NC_CACHE = {}


def _build_fused(T):
    key = ("fused", T)
    if key in _NC_CACHE:
        return _NC_CACHE[key]
    nc = bass.Bass("TRN2", target_bir_lowering=False)
    with ExitStack() as st:
        kb = KB(nc, st)
        D = lambda n, s: kb.dram(n, s, F32, "ExternalInput")
        I = lambda n: kb.dram(n, [T, 1024], F32, "Internal")
        x, xp = D("x", [T, 1024]), D("xp", [T, 1024])
        y = kb.dram("y", [T, 1024], F32, "ExternalOutput")
        ident, iota, tri, masks = D("ident", [128, 128]), D("iota", [128, 128]), D("tri", [128, 128]), D("masks", [128, 4, 512])
        x1o, x1p, x2o, x2p, x3o = I("x1o"), I("x1p"), I("x2o"), I("x2p"), I("x3o")
        with ExitStack() as stA:
            kA = kb.scope(stA)
            cvbufs = ([kA.sb(f"cvAin{i}", [128, 2048], F32) for i in range(2)], [kA.sb(f"cvAout{i}", [128, 2048], BF16) for i in range(2)])
            bg0 = []
            kb.s.defer = bg0
            tabs0 = peer_tables(kb, D("p0_uT", [1024, 16384]), D("p0_v", [16384, 1024]), "L0", cvbufs)
            kb.s.defer = None
            mixer0_block(kb, x, xp, x1o, D("m_g", [1, 1024]), D("m_w_in", [1024, 2576]), D("m_w_out", [1024, 1024]),
                         D("m_convwT", [128, 4, 31]), D("m_convb", [128, 4]), D("m_lng", [128, 4]), D("m_lnb", [128, 4]),
                         D("m_w2", [16, 256]), D("m_gateb", [1, 256]), D("m_glag", [128, 1]), ident, tri, T, T, x_out_pre=x1p, bg=bg0)
        peer_block(kb, None, None, D("p0_g", [1, 1024]), D("p0_wq", [1024, 2048]), D("p0_keysT", [128, 2048]), None, None,
                   ident, iota, T, "L0", tables=tabs0, jobs=[(x1p, x2p, T), (x1o, x2o, T)])
        with ExitStack() as stB:
            kB = kb.scope(stB)
            cvbufs = ([kB.sb(f"cvBin{i}", [128, 2048], F32) for i in range(2)], [kB.sb(f"cvBout{i}", [128, 2048], BF16) for i in range(2)])
            bg1 = []
            kb.s.defer = bg1
            tabs1 = peer_tables(kb, D("p1_uT", [1024, 16384]), D("p1_v", [16384, 1024]), "L1", cvbufs)
            kb.s.defer = None
            fox_block(kb, x2o, x2p, x3o, D("f_g", [1, 1024]), D("f_w_in", [1024, 3088]), D("f_w_out", [1024, 1024]), D("f_fb", [1, 16]),
                      D("f_qg", [1, 64]), D("f_kg", [1, 64]), D("pflag", [128, 1]), ident, tri, masks, T, T, bg=bg1)
        peer_block(kb, x3o, y, D("p1_g", [1, 1024]), D("p1_wq", [1024, 2048]), D("p1_keysT", [128, 2048]),
                   None, None, ident, iota, T, "L1", tables=tabs1)
        kb.s.emit()
        print("fused program: ninst", kb.s.ninst, "nsem", kb.s.nsem, flush=True)
    _NC_CACHE[key] = nc
    return nc


def _fused_common(inp, cst):
    c = {"m_g": np.ascontiguousarray(inp["ev_norm_mix"][0][None]), "m_w_in": np.ascontiguousarray(inp["ev_w_in"][0]),
         "m_w_out": np.ascontiguousarray(inp["ev_w_out"][0]),
         "m_convwT": np.ascontiguousarray(inp["ev_conv_w"][0].T.reshape(4, 128, 31).transpose(1, 0, 2)),
         "m_convb": _col4(inp["ev_conv_b"][0]), "m_lng": _col4(inp["ev_conv_ln_g"][0]), "m_lnb": _col4(inp["ev_conv_ln_b"][0]),
         "m_w2": np.ascontiguousarray(inp["ev_gate_w2"][0]), "m_gateb": np.ascontiguousarray(inp["ev_gate_b"][0][None]),
         "m_glag": np.ascontiguousarray(inp["ev_gla_norm_g"][0][:, None]),
         "f_g": np.ascontiguousarray(inp["od_norm_mix"][0][None]), "f_w_in": np.ascontiguousarray(inp["od_w_in"][0]),
         "f_w_out": np.ascontiguousarray(inp["od_w_out"][0]), "f_fb": np.ascontiguousarray(inp["od_fgate_b"][0][None]),
         "f_qg": np.ascontiguousarray(inp["od_q_norm_g"][0][None]), "f_kg": np.ascontiguousarray(inp["od_k_norm_g"][0][None]),
         "ident": cst["ident"], "iota": cst["iota"], "tri": cst["tri"], "masks": cst["masks"]}
    for L in range(2):
        p = _peer_inputs(inp, L, cst)
        for k in ("g", "wq", "keysT", "uT", "v"):
            c[f"p{L}_{k}"] = p[k]
    return c


def _build(kind):
    if kind in _NC_CACHE:
        return _NC_CACHE[kind]
    nc = bass.Bass("TRN2", target_bir_lowering=False)
    with ExitStack() as st:
        kb = KB(nc, st)
        D = lambda n, s: kb.dram(n, s, F32, "ExternalInput")
        T = TOK
        if kind == "m0":
            x, xp = D("x", [T, 1024]), D("xp", [T, 1024])
            y = kb.dram("y", [T, 1024], F32, "ExternalOutput")
            mixer0_block(kb, x, xp, y, D("g", [1, 1024]), D("w_in", [1024, 2576]), D("w_out", [1024, 1024]), D("convwT", [128, 4, 31]),
                         D("convb", [128, 4]), D("lng", [128, 4]), D("lnb", [128, 4]), D("w2", [16, 256]), D("gateb", [1, 256]),
                         D("glag", [128, 1]), D("ident", [128, 128]), D("tri", [128, 128]), T, T)
        elif kind == "peer":
            x = D("x", [T, 1024])
            y = kb.dram("y", [T, 1024], F32, "ExternalOutput")
            peer_block(kb, x, y, D("g", [1, 1024]), D("wq", [1024, 2048]), D("keysT", [128, 2048]), D("uT", [1024, 16384]),
                       D("v", [16384, 1024]), D("ident", [128, 128]), D("iota", [128, 128]), T, "p0")
        elif kind == "fox":
            x, xp = D("x", [T, 1024]), D("xp", [T, 1024])
            y = kb.dram("y", [T, 1024], F32, "ExternalOutput")
            fox_block(kb, x, xp, y, D("g", [1, 1024]), D("w_in", [1024, 3088]), D("w_out", [1024, 1024]), D("fb", [1, 16]),
                      D("qg", [1, 64]), D("kg", [1, 64]), D("pflag", [128, 1]), D("ident", [128, 128]), D("tri", [128, 128]),
                      D("masks", [128, 4, 512]), T, T)
        kb.s.emit()
    _NC_CACHE[kind] = nc
    return nc


def _shards(xfull):
    own, pre = [], []
    for c in range(NCORES):
        b, half = c // 2, c % 2
        own.append(np.ascontiguousarray(xfull[b, half * TOK:(half + 1) * TOK]))
        pre.append(np.ascontiguousarray(xfull[b, 0:TOK]) if half == 1 else np.zeros((TOK, 1024), np.float32))
    return own, pre


def _gather(res):
    out = np.empty((4, 8192, 1024), np.float32)
    for c in range(NCORES):
        b, half = c // 2, c % 2
        out[b, half * TOK:(half + 1) * TOK] = res.results[c]["y"]
    return out


def _run(kind, per_core):
    nc = _build(kind)
    return run_bass_kernel_spmd(nc, per_core, core_ids=list(range(NCORES)))


def _peer_inputs(inp, layer, cst):
    keys = inp["peer_keys"][layer]
    return {"g": np.ascontiguousarray(inp["ffn_norm"][layer][None]), "wq": np.ascontiguousarray(inp["peer_wq"][layer]),
            "keysT": np.ascontiguousarray(keys.transpose(3, 0, 1, 2).reshape(128, 2048)),
            "uT": np.ascontiguousarray(inp["peer_u"][layer].T), "v": np.ascontiguousarray(inp["peer_v"][layer]),
            "ident": cst["ident"], "iota": cst["iota"]}


def kernel(**inp):
    inp = {k: np.asarray(v, dtype=np.float32) for k, v in inp.items()}
    cst = _consts()
    own, pre = _shards(inp["x"])
    common = _fused_common(inp, cst)
    pfl = [np.full((128, 1), 0.0 if c % 2 == 1 else -30000.0, np.float32) for c in range(NCORES)]
    nc = _build_fused(TOK)
    res = run_bass_kernel_spmd(nc, [dict(common, x=own[c], xp=pre[c], pflag=pfl[c]) for c in range(NCORES)],
                               core_ids=list(range(NCORES)))
    return _gather(res)


def kernel_unfused(**inp):
    inp = {k: np.asarray(v, dtype=np.float32) for k, v in inp.items()}
    cst = _consts()
    x = inp["x"]
    own, pre = _shards(x)
    common = {"g": np.ascontiguousarray(inp["ev_norm_mix"][0][None]), "w_in": np.ascontiguousarray(inp["ev_w_in"][0]),
              "w_out": np.ascontiguousarray(inp["ev_w_out"][0]),
              "convwT": np.ascontiguousarray(inp["ev_conv_w"][0].T.reshape(4, 128, 31).transpose(1, 0, 2)),
              "convb": _col4(inp["ev_conv_b"][0]), "lng": _col4(inp["ev_conv_ln_g"][0]), "lnb": _col4(inp["ev_conv_ln_b"][0]),
              "w2": np.ascontiguousarray(inp["ev_gate_w2"][0]), "gateb": np.ascontiguousarray(inp["ev_gate_b"][0][None]),
              "glag": np.ascontiguousarray(inp["ev_gla_norm_g"][0][:, None]), "ident": cst["ident"], "tri": cst["tri"]}
    x1 = _gather(_run("m0", [dict(common, x=own[c], xp=pre[c]) for c in range(NCORES)]))
    own, _ = _shards(x1)
    common = _peer_inputs(inp, 0, cst)
    x2 = _gather(_run("peer", [dict(common, x=own[c]) for c in range(NCORES)]))
    own, pre = _shards(x2)
    common = {"g": np.ascontiguousarray(inp["od_norm_mix"][0][None]), "w_in": np.ascontiguousarray(inp["od_w_in"][0]),
              "w_out": np.ascontiguousarray(inp["od_w_out"][0]), "fb": np.ascontiguousarray(inp["od_fgate_b"][0][None]),
              "qg": np.ascontiguousarray(inp["od_q_norm_g"][0][None]), "kg": np.ascontiguousarray(inp["od_k_norm_g"][0][None]),
              "ident": cst["ident"], "tri": cst["tri"], "masks": cst["masks"]}
    pfl = [np.full((128, 1), 0.0 if c % 2 == 1 else -30000.0, np.float32) for c in range(NCORES)]
    x3 = _gather(_run("fox", [dict(common, x=own[c], xp=pre[c], pflag=pfl[c]) for c in range(NCORES)]))
    own, _ = _shards(x3)
    common = _peer_inputs(inp, 1, cst)
    x4 = _gather(_run("peer", [dict(common, x=own[c]) for c in range(NCORES)]))
    return x4
```

```python
from contextlib import ExitStack

import numpy as np
import concourse.bass as bass
import concourse.mybir as mybir
from concourse.bass_utils import run_bass_kernel_spmd

F32 = mybir.dt.float32
BF16 = mybir.dt.bfloat16
U32 = mybir.dt.uint32
I32 = mybir.dt.int32
ALU = mybir.AluOpType
AF = mybir.ActivationFunctionType
AX = mybir.AxisListType

NCORES = 8
EPOCH = 30000
ENGS = ("pe", "dve", "act", "pool", "sp")


class Buf:
    __slots__ = ("w", "r", "name")

    def __init__(self, name=""):
        self.w = None
        self.r = []
        self.name = name


class TT:
    def __init__(self, t, name):
        self.t = t
        self.b = Buf(name)

    def __getitem__(self, k):
        return self.t[k]


class Sched:
    def __init__(self, nc, stack):
        self.nc = nc
        self.stack = stack
        self.q = {e: [] for e in ENGS}
        self.csem = {e: None for e in ENGS}
        self.ccnt = {e: 0 for e in ENGS}
        self.dsem = {e: [] for e in ENGS}
        self.dcnt = {e: [] for e in ENGS}
        self.drr = {e: 0 for e in ENGS}
        self.seen = {e: {} for e in ENGS}
        self.nsem = 0
        self.ninst = 0
        self.defer = None

    def _newsem(self, nm):
        self.nsem += 1
        return self.stack.enter_context(self.nc.semaphore(f"{nm}{self.nsem}"))

    def _ticket(self, e, dma):
        if not dma:
            if self.csem[e] is None or self.ccnt[e] >= EPOCH:
                self.csem[e] = self._newsem("c" + e)
                self.ccnt[e] = 0
            self.ccnt[e] += 1
            return (self.csem[e], self.ccnt[e], e, 1)
        if not self.dsem[e]:
            self.dsem[e] = [self._newsem("d" + e) for _ in range(8)]
            self.dcnt[e] = [0] * 8
        i = self.drr[e] % 8
        self.drr[e] += 1
        if self.dcnt[e][i] + 16 >= EPOCH:
            self.dsem[e][i] = self._newsem("d" + e)
            self.dcnt[e][i] = 0
        self.dcnt[e][i] += 16
        return (self.dsem[e][i], self.dcnt[e][i], e + "_dma", 16)

    def op(self, e, fn, reads=(), writes=(), dma=False):
        if self.defer is not None:
            self.defer.append((e, fn, list(reads), list(writes), dma))
            return None
        deps = {}

        def add(t):
            if t is None:
                return
            sem, val, src, _ = t
            if src == "pe" and e == "pe" and not dma:
                return
            k = id(sem)
            if self.seen[e].get(k, 0) >= val:
                return
            if k not in deps or deps[k][1] < val:
                deps[k] = (sem, val)

        for b in reads:
            b = b.b if isinstance(b, TT) else b
            add(b.w)
        for b in writes:
            b = b.b if isinstance(b, TT) else b
            add(b.w)
            for t in b.r:
                add(t)
        if dma and self.dsem[e]:
            i = self.drr[e] % 8
            if self.dcnt[e][i] > 0 and self.dcnt[e][i] + 16 < EPOCH:
                sem_, val_ = self.dsem[e][i], self.dcnt[e][i]
                k = id(sem_)
                if self.seen[e].get(k, 0) < val_ and (k not in deps or deps[k][1] < val_):
                    deps[k] = (sem_, val_)
        waits = list(deps.values())
        for sem, val in waits:
            self.seen[e][id(sem)] = val
        t = self._ticket(e, dma)
        self.q[e].append((waits, fn, t[0], t[3]))
        self.ninst += 1 + len(waits)
        for b in reads:
            b = b.b if isinstance(b, TT) else b
            b.r = [x for x in b.r if x[0] is not t[0]] + [t]
        for b in writes:
            b = b.b if isinstance(b, TT) else b
            b.w = t
            b.r = []
        return t

    def replay(self, lst, n):
        k = min(n, len(lst))
        for e, fn, reads, writes, dma in lst[:k]:
            self.op(e, fn, reads, writes, dma)
        del lst[:k]

    def barrier(self):
        waits = []
        for e in ENGS:
            if self.csem[e] is not None and self.ccnt[e] > 0:
                waits.append((self.csem[e], self.ccnt[e]))
            for sem, c in zip(self.dsem[e], self.dcnt[e]):
                if c > 0:
                    waits.append((sem, c))
        for e in ENGS:
            self.q[e].append((list(waits), None, None, 0))
            for sem, val in waits:
                self.seen[e][id(sem)] = max(self.seen[e].get(id(sem), 0), val)

    def final_wait(self, e, bufs):
        waits = []
        for b in bufs:
            b = b.b if isinstance(b, TT) else b
            for t in [b.w] + list(b.r):
                if t is not None:
                    waits.append((t[0], t[1]))
        self.q[e].append((waits, None, None, 0))

    def emit(self):
        nc = self.nc
        q = self.q

        def run(e, eng):
            for waits, fn, sem, inc in q[e]:
                for s, v in waits:
                    eng.wait_ge(s, v)
                if fn is not None:
                    ins = fn(eng)
                    ins.then_inc(sem, inc)

        with nc.Block() as block:

            @block.tensor
            def _(eng):
                run("pe", eng)

            @block.vector
            def _(eng):
                run("dve", eng)

            @block.scalar
            def _(eng):
                run("act", eng)

            @block.gpsimd
            def _(eng):
                run("pool", eng)

            @block.sync
            def _(eng):
                run("sp", eng)


class KB:
    def __init__(self, nc, stack, sched=None):
        self.nc = nc
        self.stack = stack
        self.s = sched if sched is not None else Sched(nc, stack)

    def scope(self, stack):
        return KB(self.nc, stack, self.s)

    def sb(self, name, shape, dt):
        t = self.stack.enter_context(self.nc.sbuf_tensor(name, list(shape), dt))
        return TT(t, name)

    def dram(self, name, shape, dt, kind):
        t = self.nc.dram_tensor(name, list(shape), dt, kind=kind)
        return TT(t.ap(), name)

    def dma(self, out, in_, reads, writes, e="sp", **kw):
        return self.s.op(e, lambda g: g.dma_start(out=out, in_=in_, **kw), reads, writes, dma=True)

    def mm(self, out, lhsT, rhs, start, stop, reads, writes):
        return self.s.op("pe", lambda g: g.matmul(out, lhsT, rhs, start=start, stop=stop), reads, writes)

    def tr(self, out, in_, ident, reads, writes):
        return self.s.op("pe", lambda g: g.transpose(out, in_, ident), reads, writes)

    def act(self, out, in_, func, reads, writes, bias=None, scale=None, accum_out=None):
        kw = {}
        if bias is not None:
            kw["bias"] = bias
        if scale is not None:
            kw["scale"] = scale
        if accum_out is not None:
            kw["accum_out"] = accum_out
        return self.s.op("act", lambda g: g.activation(out, in_, func, **kw), reads, writes)

    def tt(self, out, in0, in1, op, reads, writes, e="dve"):
        return self.s.op(e, lambda g: g.tensor_tensor(out, in0, in1, op), reads, writes)

    def ts(self, out, in0, s1, s2, op0, op1, reads, writes, e="dve", accum_out=None):
        if op1 is None:
            return self.s.op(e, lambda g: g.tensor_scalar(out, in0, s1, None, op0), reads, writes)
        if accum_out is not None:
            return self.s.op(e, lambda g: g.tensor_scalar(out, in0, s1, s2, op0, op1, accum_out), reads, writes)
        return self.s.op(e, lambda g: g.tensor_scalar(out, in0, s1, s2, op0, op1), reads, writes)

    def stt(self, out, in0, scalar, in1, op0, op1, reads, writes, e="dve"):
        return self.s.op(e, lambda g: g.scalar_tensor_tensor(out, in0, scalar, in1, op0, op1), reads, writes)

    def copy(self, out, in_, reads, writes, e="dve"):
        if e == "act":
            return self.s.op(e, lambda g: g.copy(out, in_), reads, writes)
        return self.s.op(e, lambda g: g.tensor_copy(out, in_), reads, writes)

    def memset(self, ap, val, writes, e="dve"):
        return self.s.op(e, lambda g: g.memset(ap, val), (), writes)

    def reduce(self, out, in_, op, reads, writes, axis=AX.X, e="dve"):
        return self.s.op(e, lambda g: g.tensor_reduce(out, in_, axis, op), reads, writes)


def to_bf16_dram(kb, src, dst, R, C, tag, bufs=None):
    with ExitStack() as st:
        k = kb.scope(st)
        W = 2048
        if bufs is None:
            stg = [k.sb(f"cv_in{tag}{i}", [128, W], F32) for i in range(2)]
            outb = [k.sb(f"cv_out{tag}{i}", [128, W], BF16) for i in range(2)]
        else:
            stg, outb = bufs
        n = 0
        for r in range(R // 128):
            for c0 in range(0, C, W):
                w = min(W, C - c0)
                i = n % 2
                k.dma(stg[i][:, 0:w], src[r * 128:(r + 1) * 128, c0:c0 + w], [src], [stg[i]])
                k.copy(outb[i][:, 0:w], stg[i][:, 0:w], [stg[i]], [outb[i]], e=("dve" if n % 2 == 0 else "act"))
                k.dma(dst[r * 128:(r + 1) * 128, c0:c0 + w], outb[i][:, 0:w], [outb[i]], [], e="pool")
                n += 1
        if bufs is None:
            k.s.barrier()


def load_bf16_resident(kb, dst_tt, dst_ap_fn, src, nrows_blocks, C, tag):
    with ExitStack() as st:
        k = kb.scope(st)
        W = 2048
        stg = [k.sb(f"ld_in{tag}{i}", [128, W], F32) for i in range(2)]
        n = 0
        for kc in range(nrows_blocks):
            for c0 in range(0, C, W):
                w = min(W, C - c0)
                i = n % 2
                k.dma(stg[i][:, 0:w], src[kc * 128:(kc + 1) * 128, c0:c0 + w], [src], [stg[i]])
                k.copy(dst_ap_fn(kc, c0, w), stg[i][:, 0:w], [stg[i]], [dst_tt], e=("dve" if n % 2 == 0 else "act"))
                n += 1
        k.s.barrier()


def peer_tables(kb0, uT, vtab, tag, bufs=None):
    us = kb0.dram(f"us{tag}", [1024, 16384], BF16, "Internal")
    vs = kb0.dram(f"vs{tag}", [16384, 1024], BF16, "Internal")
    to_bf16_dram(kb0, uT, us, 1024, 16384, tag + "u", bufs)
    to_bf16_dram(kb0, vtab, vs, 16384, 1024, tag + "v", bufs)
    return us, vs


def peer_block(kb0, x_in, x_out, gvec, wq, keysT, uT, vtab, ident_d, iota_d, T, tag, G=256, CH=2, OHT=8, tables=None, jobs=None, NSL=3):
    nc = kb0.nc
    if jobs is None:
        jobs = [(x_in, x_out, T)]
    with ExitStack() as st:
        kb = kb0.scope(st)
        TPG = G // 128
        if tables is None:
            tables = peer_tables(kb, uT, vtab, tag)
        us, vs = tables
        wqb = kb.sb("wqb" + tag, [128, 8, 2048], BF16)
        load_bf16_resident(kb, wqb, lambda kc, c0, w: wqb[:, kc, c0:c0 + w], wq, 8, 2048, tag + "wq")
        kTb = kb.sb("kTb" + tag, [128, 16 * 128], BF16)
        load_bf16_resident(kb, kTb, lambda kc, c0, w: kTb[:, c0:c0 + w], keysT, 1, 2048, tag + "kt")
        gB = kb.sb("gB" + tag, [128, 1024], F32)
        kb.dma(gB[:], gvec.t.partition_broadcast(128)[:, 0, :], [gvec], [gB])
        ident = kb.sb("ident" + tag, [128, 128], F32)
        kb.dma(ident[:], ident_d[:], [ident_d], [ident])
        iota = kb.sb("iota" + tag, [128, 128], F32)
        kb.dma(iota[:], iota_d[:], [iota_d], [iota])
        iotab = kb.sb("iotab" + tag, [128, 128], BF16)
        kb.copy(iotab[:], iota[:], [iota], [iotab])

        xs = [[kb.sb(f"xs{tag}{p}{j}", [128, 1024], F32) for j in range(TPG)] for p in range(2)]
        hT = [kb.sb(f"hT{tag}{p}", [128, 8, G], BF16) for p in range(2)]
        trioT = [[kb.sb(f"trioT{tag}{p}{j}", [128, 3, 128], F32) for j in range(TPG)] for p in range(2)]
        hf = kb.sb("hf" + tag, [128, 1024], F32)
        ss = kb.sb("ss" + tag, [128, 4], F32)
        qT = kb.sb("qT" + tag, [128, 16, G], BF16)
        sc = kb.sb("sc" + tag, [128, 16, 128], F32)
        tmp4 = kb.sb("tmp4" + tag, [128, 4, 256], F32)
        tmpb = [Buf() for _ in range(4)]
        scb = [Buf() for _ in range(4)]
        v16 = kb.sb("v16" + tag, [128, 16, 16], F32)
        i16 = kb.sb("i16" + tag, [128, 16, 16], U32)
        i16f = kb.sb("i16f" + tag, [128, 16, 16], F32)
        cand = kb.sb("cand" + tag, [128, 8, 16, 16], F32)
        s16 = kb.sb("s16" + tag, [128, 8, 16], F32)
        j16 = kb.sb("j16" + tag, [128, 8, 16], U32)
        jaf = kb.sb("jaf" + tag, [128, 8, 16], F32)
        ja = kb.sb("ja" + tag, [128, 8, 16], U32)
        jb = kb.sb("jb" + tag, [128, 8, 16], U32)
        jbf = kb.sb("jbf" + tag, [128, 8, 16], F32)
        eq = cand
        trio = kb.sb("trio" + tag, [128, 3, 128], F32)
        zz = kb.sb("zz" + tag, [128, 8], F32)
        oh1 = kb.sb("oh1" + tag, [128, OHT, 128], BF16)
        oh2 = kb.sb("oh2" + tag, [128, OHT, 128], BF16)
        ohb = [Buf() for _ in range(OHT)]
        ohb2 = [Buf() for _ in range(OHT)]
        WT = kb.sb("WT" + tag, [128, 128, G], BF16)
        WTb = [Buf() for _ in range(G // 16)]
        ub = [kb.sb(f"ub{tag}{i}", [128, 8, CH * 128], BF16) for i in range(NSL)]
        vb = [kb.sb(f"vb{tag}{i}", [128, CH, 1024], BF16) for i in range(NSL)]
        actT = [kb.sb(f"actT{tag}{i}", [128, G], BF16) for i in range(2)]
        ct = [kb.sb(f"ct{tag}{i}", [128, G], BF16) for i in range(2)]
        ps = st.enter_context(nc.psum_tensor("ps" + tag, [128, 4096], F32))
        pb = [Buf(f"pb{i}") for i in range(8)]

        def bank(i, w=512):
            return ps[:, i * 512:i * 512 + w]

        us_v = us.t.rearrange("(kc p) e -> p kc e", p=128)
        vs_v = vs.t.rearrange("(e1 p) d -> p e1 d", p=128)

        def topk_chain(vals, vdst, idst, tm, vb_, ib_, tb_, srcb):
            yield kb.s.op("dve", lambda g_: g_.max(out=vdst[:, 0:8], in_=vals), [srcb], [vb_])
            yield kb.s.op("dve", lambda g_: g_.match_replace(out=tm, in_to_replace=vdst[:, 0:8], in_values=vals,
                                                              imm_value=-1e30), [srcb, vb_], [tb_])
            yield kb.s.op("dve", lambda g_: g_.max(out=vdst[:, 8:16], in_=tm), [tb_], [vb_])
            yield kb.s.op("dve", lambda g_: g_.max_index(out=idst[:, 0:8], in_max=vdst[:, 0:8], in_values=vals), [srcb, vb_], [ib_])
            yield kb.s.op("dve", lambda g_: g_.max_index(out=idst[:, 8:16], in_max=vdst[:, 8:16], in_values=vals), [srcb, vb_], [ib_])

        def run_interleaved(chains, width=4):
            live = []
            chains = list(chains)
            while chains or live:
                while chains and len(live) < width:
                    live.append(chains.pop(0))
                nxt = []
                for ch_ in live:
                    try:
                        next(ch_)
                        nxt.append(ch_)
                    except StopIteration:
                        pass
                live = nxt

        def p1a_front(p, x_in, g, j):
            tok0 = g * G + j * 128
            xt = xs[p][j]
            kb.dma(xt[:], x_in[tok0:tok0 + 128, :], [x_in], [xt])
            kb.act(hf[:], xt[:], AF.Square, [xt], [hf])
            kb.reduce(ss[:, 0:1], hf[:], ALU.add, [hf], [ss])
            kb.ts(ss[:, 1:2], ss[:, 0:1], 1.0 / 1024, 1e-6, ALU.mult, ALU.add, [ss], [ss])
            kb.act(ss[:, 3:4], ss[:, 1:2], AF.Sqrt, [ss], [ss])
            kb.s.op("dve", lambda g_: g_.reciprocal(ss[:, 2:3], ss[:, 3:4]), [ss], [ss])
            kb.stt(hf[:], xt[:], ss[:, 2:3], gB[:], ALU.mult, ALU.mult, [xt, ss, gB], [hf])
            for hb in range(2):
                for k4 in range(4):
                    kc = hb * 4 + k4
                    kb.tr(ps[:, (6 + hb) * 512 + k4 * 128:(6 + hb) * 512 + (k4 + 1) * 128], hf[:, kc * 128:(kc + 1) * 128], ident[:],
                          [hf, ident], [pb[6 + hb]])
                kb.copy(hT[p][:, hb * 4:(hb + 1) * 4, j * 128:(j + 1) * 128],
                        bank(6 + hb).rearrange("p (k t) -> p k t", t=128), [pb[6 + hb]], [hT[p]], e="act")

        def p1a_q(p):
            for q4 in range(4):
                for c4 in range(4):
                    c = q4 * 4 + c4
                    bk = 6 + c4 // 2
                    col = bk * 512 + (c4 % 2) * G
                    for kc in range(8):
                        kb.mm(ps[:, col:col + G], wqb[:, kc, c * 128:(c + 1) * 128], hT[p][:, kc, :], kc == 0, kc == 7,
                              [wqb, hT[p]], [pb[bk]])
                for hb in range(2):
                    kb.copy(qT[:, q4 * 4 + hb * 2:q4 * 4 + hb * 2 + 2, :], bank(6 + hb).rearrange("p (k t) -> p k t", t=G),
                            [pb[6 + hb]], [qT], e="act")

        def p1a_back(p, j):
            for q4 in range(4):
                bk = 6 + q4 % 2
                for c4 in range(4):
                    c = q4 * 4 + c4
                    kb.mm(ps[:, bk * 512 + c4 * 128:bk * 512 + (c4 + 1) * 128], qT[:, c, j * 128:(j + 1) * 128], kTb[:, c * 128:(c + 1) * 128],
                          True, True, [qT, kTb], [pb[bk]])
                kb.copy(sc[:, q4 * 4:(q4 + 1) * 4, :], bank(bk).rearrange("p (k t) -> p k t", t=128), [pb[bk]], [scb[q4]], e="act")
            v16b = [Buf() for _ in range(16)]
            i16b = [Buf() for _ in range(16)]
            run_interleaved([topk_chain(sc[:, c, :], v16[:, c, :], i16[:, c, :], tmp4[:, c % 4, 0:128], v16b[c], i16b[c], tmpb[c % 4], scb[c // 4])
                             for c in range(16)])
            kb.copy(i16f[:], i16[:], i16b, [i16f], e="pool")
            v16v = v16[:].rearrange("p (h two) k -> p h two k", two=2)
            kb.tt(cand[:], v16v[:, :, 0, :].unsqueeze(3).to_broadcast([128, 8, 16, 16]),
                  v16v[:, :, 1, :].unsqueeze(2).to_broadcast([128, 8, 16, 16]), ALU.add, v16b, [cand])
            s16b = [Buf() for _ in range(8)]
            j16b = [Buf() for _ in range(8)]
            run_interleaved([topk_chain(cand[:, h, :, :].rearrange("p a b -> p (a b)"), s16[:, h, :], j16[:, h, :], tmp4[:, h % 4, :],
                                        s16b[h], j16b[h], tmpb[h % 4], cand.b) for h in range(8)])
            gt = trio[:, 2, :].rearrange("p (h k) -> p h k", k=16)
            kb.tt(gt, s16[:], s16[:, :, 0:1].to_broadcast([128, 8, 16]), ALU.subtract, s16b, [trio], e="pool")
            kb.act(gt, gt, AF.Exp, [trio], [trio])
            kb.reduce(zz[:], gt, ALU.add, [trio], [zz])
            kb.s.op("dve", lambda g_: g_.reciprocal(zz[:], zz[:]), [zz], [zz])
            kb.tt(gt, gt, zz[:].unsqueeze(2).to_broadcast([128, 8, 16]), ALU.mult, [trio, zz], [trio])
            kb.s.op("dve", lambda g_: g_.tensor_single_scalar(ja[:], j16[:], 4, ALU.logical_shift_right), j16b, [ja])
            kb.s.op("dve", lambda g_: g_.tensor_single_scalar(jb[:], j16[:], 15, ALU.bitwise_and), j16b, [jb])
            kb.copy(jaf[:], ja[:], [ja], [jaf])
            kb.copy(jbf[:], jb[:], [jb], [jbf])
            i16v = i16f[:].rearrange("p (h two) k -> p h two k", two=2)
            iota16 = iota[:, 0:16].unsqueeze(1).unsqueeze(1).to_broadcast([128, 8, 16, 16])
            for which, jf in ((0, jaf), (1, jbf)):
                kb.tt(eq[:], iota16, jf[:].unsqueeze(3).to_broadcast([128, 8, 16, 16]), ALU.is_equal, [iota, jf], [eq])
                kb.tt(eq[:], eq[:], i16v[:, :, which, :].unsqueeze(2).to_broadcast([128, 8, 16, 16]), ALU.mult, [eq, i16f], [eq])
                kb.reduce(trio[:, which, :], eq[:].rearrange("p h k a -> p (h k) a"), ALU.add, [eq], [trio])
            for w3 in range(3):
                kb.tr(ps[:, 6 * 512 + w3 * 128:6 * 512 + (w3 + 1) * 128], trio[:, w3, :], ident[:], [trio, ident], [pb[6]])
            tT = trioT[p][j]
            kb.copy(tT[:].rearrange("p a t -> p (a t)"), ps[:, 6 * 512:6 * 512 + 384], [pb[6]], [tT], e="act")

        wstate = [0]

        def p1b(p, j):
            tT = trioT[p][j]
            for t0 in range(0, 128, 16):
                half = wstate[0] % 2
                wstate[0] += 1
                for tt_ in range(16):
                    tl = (t0 + tt_) % OHT
                    tg = t0 + tt_
                    kb.ts(oh1[:, tl, :], iotab[:], tT[:, 0, tg:tg + 1], tT[:, 2, tg:tg + 1], ALU.is_equal, ALU.mult, [iotab, tT], [ohb[tl]])
                    kb.ts(oh2[:, tl, :], iotab[:], tT[:, 1, tg:tg + 1], None, ALU.is_equal, None, [iotab, tT], [ohb2[tl]])
                    col = half * 2048 + tt_ * 128
                    kb.mm(ps[:, col:col + 128], oh2[:, tl, :], oh1[:, tl, :], True, True, [ohb[tl], ohb2[tl]], [pb[half * 4 + tt_ // 4]])
                tcol = j * 128 + t0
                kb.copy(WT[:, :, tcol:tcol + 16], ps[:, half * 2048:(half + 1) * 2048].rearrange("p (t e) -> p e t", e=128),
                        [pb[half * 4 + q_] for q_ in range(4)], [WTb[tcol // 16]], e="act")

        def p2(p, x_out, g, bg):
            per = (len(bg) + 127) // 128 if bg else 0

            def u_stage(e1):
                cg, el = e1 // CH, e1 % CH
                sl = cg % NSL
                if el == 0:
                    kb.dma(ub[sl][:], us_v[:, :, cg * CH * 128:(cg + 1) * CH * 128], [us], [ub[sl]])
                    kb.dma(vb[sl][:], vs_v[:, cg * CH:(cg + 1) * CH, :], [vs], [vb[sl]], e="pool")
                a = e1 % 2
                pu = ps[:, 2048 + a * 512:2048 + a * 512 + G]
                for kc in range(8):
                    kb.mm(pu, ub[sl][:, kc, el * 128:(el + 1) * 128], hT[p][:, kc, :], kc == 0, kc == 7, [ub[sl], hT[p]], [pb[4 + a]])
                kb.act(actT[a][:], pu, AF.Gelu, [pb[4 + a]], [actT[a]])
                kb.tt(ct[a][:], actT[a][:], WT[:, e1, :], ALU.mult, [actT[a]] + WTb, [ct[a]])

            def v_stage(e1):
                cg, el = e1 // CH, e1 % CH
                sl = cg % NSL
                a = e1 % 2
                for j in range(TPG):
                    for hh in range(2):
                        kb.mm(bank(2 * j + hh), ct[a][:, j * 128:(j + 1) * 128], vb[sl][:, el, hh * 512:(hh + 1) * 512],
                              e1 == 0, e1 == 127, [ct[a], vb[sl]], [pb[2 * j + hh]])

            u_stage(0)
            for e1 in range(128):
                if e1 + 1 < 128:
                    u_stage(e1 + 1)
                v_stage(e1)
                if bg:
                    kb.s.replay(bg, per)
            xo = tmp4[:].rearrange("p a b -> p (a b)")
            for j in range(TPG):
                tok0 = g * G + j * 128
                kb.tt(xo, ps[:, j * 1024:(j + 1) * 1024], xs[p][j][:], ALU.add, [pb[2 * j], pb[2 * j + 1], xs[p][j]], tmpb)
                kb.dma(x_out[tok0:tok0 + 128, :], xo, tmpb, [], e="pool")
            if bg:
                kb.s.replay(bg, len(bg))

        groups = [(a_, b_, g_) for (a_, b_, t_) in jobs for g_ in range(t_ // G)]
        def p1a_group(p, x_in, g):
            for j in range(TPG):
                p1a_front(p, x_in, g, j)
            p1a_q(p)
            for j in range(TPG):
                p1a_back(p, j)

        p1a_group(0, groups[0][0], groups[0][2])
        for gi, (xin_, xout_, g) in enumerate(groups):
            p = gi % 2
            for j in range(TPG):
                p1b(p, j)
            bg = []
            if gi + 1 < len(groups):
                kb.s.defer = bg
                p1a_group(1 - p, groups[gi + 1][0], groups[gi + 1][2])
                kb.s.defer = None
            p2(p, xout_, g, bg)
        kb.s.barrier()


def rms_to_hT(kb, xs_t, gB, ident, hf, sq, ss, ps, pbA, pbB, hT, col0, eps=1e-6):
    kb.act(sq[:], xs_t[:], AF.Square, [xs_t], [sq])
    kb.reduce(ss[:, 0:1], sq[:], ALU.add, [sq], [ss])
    kb.ts(ss[:, 1:2], ss[:, 0:1], 1.0 / 1024, eps, ALU.mult, ALU.add, [ss], [ss])
    kb.act(ss[:, 3:4], ss[:, 1:2], AF.Sqrt, [ss], [ss])
    kb.s.op("dve", lambda g_: g_.reciprocal(ss[:, 2:3], ss[:, 3:4]), [ss], [ss])
    kb.stt(hf[:], xs_t[:], ss[:, 2:3], gB[:], ALU.mult, ALU.mult, [xs_t, ss, gB], [hf])
    for kc in range(8):
        kb.tr(ps[:, kc * 128:(kc + 1) * 128], hf[:, kc * 128:(kc + 1) * 128], ident[:], [hf, ident], [pbA if kc < 4 else pbB])
    kb.copy(hT[:, 0:4, col0:col0 + 128], ps[:, 0:512].rearrange("p (k t) -> p k t", t=128), [pbA], [hT], e="act")
    kb.copy(hT[:, 4:8, col0:col0 + 128], ps[:, 512:1024].rearrange("p (k t) -> p k t", t=128), [pbB], [hT], e="dve")


def mixer0_block(kb0, x_own, x_pre, x_out, gvec, w_in, w_out, convwT, convb, lng, lnb, w2, gateb, glag,
                 ident_d, tri_d, T, TP, tag="m0", x_out_pre=None, bg=None):
    nc = kb0.nc
    NTO = T // 128
    NTP = TP // 128
    with ExitStack() as st:
        kb = kb0.scope(st)
        winb = kb.sb("winb", [128, 8, 2576], BF16)
        load_bf16_resident(kb, winb, lambda kc, c0, w: winb[:, kc, c0:c0 + w], w_in, 8, 2576, "win")
        woutb = kb.sb("woutb", [128, 8, 1024], BF16)
        load_bf16_resident(kb, woutb, lambda kc, c0, w: woutb[:, kc, c0:c0 + w], w_out, 8, 1024, "wout")
        gB = kb.sb("gBm", [128, 1024], F32)
        kb.dma(gB[:], gvec.t.partition_broadcast(128)[:, 0, :], [gvec], [gB])
        gbB = kb.sb("gbB", [128, 256], F32)
        kb.dma(gbB[:], gateb.t.partition_broadcast(128)[:, 0, :], [gateb], [gbB])
        ident = kb.sb("identm", [128, 128], F32)
        kb.dma(ident[:], ident_d[:], [ident_d], [ident])
        tri = kb.sb("trim", [128, 128], F32)
        kb.dma(tri[:], tri_d[:], [tri_d], [tri])
        ones = kb.sb("onesm", [128, 128], F32)
        kb.memset(ones[:], 1.0, [ones])
        cw = kb.sb("cw", [128, 4, 31], F32)
        kb.dma(cw[:], convwT[:], [convwT], [cw])
        cols = kb.sb("colsm", [128, 16], F32)
        kb.dma(cols[:, 0:4], convb[:], [convb], [cols])
        kb.dma(cols[:, 4:8], lng[:], [lng], [cols])
        kb.dma(cols[:, 8:12], lnb[:], [lnb], [cols])
        kb.dma(cols[:, 12:13], glag[:], [glag], [cols])
        w2f = kb.sb("w2f", [16, 256], F32)
        kb.dma(w2f[:], w2[:], [w2], [w2f])
        w2b = kb.sb("w2b", [16, 256], BF16)
        kb.copy(w2b[:], w2f[:], [w2f], [w2b])
        diag = kb.sb("diag", [128, 4 * 31, 128], BF16)
        for c in range(4):
            for j in range(31):
                kb.ts(diag[:, c * 31 + j, :], ident[:], cw[:, c, j:j + 1], None, ALU.mult, None, [ident, cw], [diag],
                      e=("dve" if (c * 31 + j) % 2 == 0 else "pool"))
        full = x_out_pre is not None
        UW = 158 if full else 30 + T
        uT = kb.sb("uT", [128, 4, UW], BF16)
        kb.memset(uT[:, :, 0:30], 0.0, [uT])
        Sf = [kb.sb(f"Sf{h}", [64, 128], F32) for h in range(4)]
        Sb = [kb.sb(f"Sb{h}", [64, 128], BF16) for h in range(4)]
        for h in range(4):
            kb.memset(Sf[h][:], 0.0, [Sf[h]])
            kb.memset(Sb[h][:], 0.0, [Sb[h]])
        xs = kb.sb("xsm", [128, 1024], F32)
        hf = kb.sb("hfm", [128, 1024], F32)
        sq = kb.sb("sqm", [128, 1024], BF16)
        ss = kb.sb("ssm", [128, 4], F32)
        hT = kb.sb("hTm", [128, 8, 128], BF16)
        sg = kb.sb("sg", [128, 512], F32)
        glrT = kb.sb("glrT", [16, 128], BF16)
        zb = kb.sb("zb", [128, 256], F32)
        la = kb.sb("la", [128, 256], F32)
        enb_tm = kb.sb("enb_tm", [128, 256], F32)
        ebl_tm = kb.sb("ebl_tm", [128, 256], F32)
        kdec = kb.sb("kdec", [128, 256], BF16)
        vbf = kb.sb("vbf", [128, 512], BF16)
        eblc = kb.sb("eblc", [64, 8], F32)
        eb = kb.sb("eb", [64, 512], F32)
        enb = kb.sb("enb", [64, 512], F32)
        qt = kb.sb("qt", [64, 4, 128], BF16)
        kt = kb.sb("kt", [64, 4, 128], BF16)
        Am = kb.sb("Am", [128, 4, 128], BF16)
        osq = kb.sb("osq", [128, 512], F32)
        rs = kb.sb("rs", [128, 512], F32)
        sr = kb.sb("sr", [128, 512], F32)
        t1 = kb.sb("t1", [128, 512], F32)
        ybT = kb.sb("ybT", [128, 4, 128], BF16)
        ycs = kb.sb("ycs", [128, 4, 128], F32)
        ysq = kb.sb("ysq", [128, 4, 128], F32)
        mean = kb.sb("mean", [128, 128], F32)
        var = kb.sb("var", [128, 128], F32)
        yaT = kb.sb("yaT", [128, 4, 128], BF16)
        xo = kb.sb("xom", [128, 1024], F32)
        ps = st.enter_context(nc.psum_tensor("psm", [128, 4096], F32))
        pb = [Buf(f"pbm{i}") for i in range(8)]

        def B(i, lo=0, hi=512):
            return ps[:, i * 512 + lo:i * 512 + hi]

        def fm_proj(colbase, ncols, bank, slot):
            for kc in range(8):
                kb.mm(ps[0:ncols, bank * 512 + slot * 128:bank * 512 + (slot + 1) * 128], winb[:, kc, colbase:colbase + ncols],
                      hT[:, kc, :], kc == 0, kc == 7, [winb, hT], [pb[bank]])

        def state_part(u_needed, own):
            for kc in range(8):
                kb.mm(B(5), hT[:, kc, :], winb[:, kc, 1536:2048], kc == 0, kc == 7, [hT, winb], [pb[5]])
            for kc in range(8):
                kb.mm(B(6, 0, 256), hT[:, kc, :], winb[:, kc, 1280:1536], kc == 0, kc == 7, [hT, winb], [pb[6]])
            fm_proj(2560, 16, 4, 0)
            kb.copy(glrT[:], ps[0:16, 4 * 512:4 * 512 + 128], [pb[4]], [glrT], e="act")
            kb.mm(B(6, 256, 512), glrT[:], w2b[:], True, True, [glrT, w2b], [pb[6]])
            kb.tt(zb[:], B(6, 256, 512), gbB[:], ALU.add, [pb[6], gbB], [zb])
            kb.act(zb[:], zb[:], AF.Exp, [zb], [zb], scale=-1.0)
            kb.act(zb[:], zb[:], AF.Ln, [zb], [zb], bias=1.0)
            kb.ts(la[:], zb[:], -1.0 / 16.0, None, ALU.mult, None, [zb], [la])
            kb.mm(B(7, 0, 256), tri[:], la[:], True, True, [tri, la], [pb[7]])
            kb.mm(B(7, 256, 512), ones[:], la[:], True, True, [ones, la], [pb[7]])
            for h in range(4):
                kb.mm(ps[0:64, 4 * 512 + 384 + 2 * h:4 * 512 + 386 + 2 * h], la[:, h * 64:(h + 1) * 64], ones[:, 0:2], True, True,
                      [la, ones], [pb[4]])
            kb.act(eblc[:], ps[0:64, 4 * 512 + 384:4 * 512 + 392], AF.Exp, [pb[4]], [eblc])
            kb.act(enb_tm[:], B(7, 0, 256), AF.Exp, [pb[7]], [enb_tm], scale=-1.0)
            kb.act(ebl_tm[:], B(7, 256, 512), AF.Exp, [pb[7]], [ebl_tm])
            kb.tt(enb_tm[:], enb_tm[:], ebl_tm[:], ALU.mult, [enb_tm, ebl_tm], [enb_tm])
            kb.tt(kdec[:], B(6, 0, 256), enb_tm[:], ALU.mult, [pb[6], enb_tm], [kdec])
            kb.copy(vbf[:], B(5), [pb[5]], [vbf], e="act")

        def state_update():
            for h in range(4):
                kb.mm(ps[0:64, 2 * 512 + h * 128:2 * 512 + (h + 1) * 128], kdec[:, h * 64:(h + 1) * 64], vbf[:, h * 128:(h + 1) * 128],
                      True, True, [kdec, vbf], [pb[2]])
            for h in range(4):
                kb.stt(Sf[h][:], Sf[h][:], eblc[:, 2 * h:2 * h + 1], ps[0:64, 2 * 512 + h * 128:2 * 512 + (h + 1) * 128],
                       ALU.mult, ALU.add, [Sf[h], eblc, pb[2]], [Sf[h]])
                kb.copy(Sb[h][:], Sf[h][:], [Sf[h]], [Sb[h]], e="act")

        def conv_u(tokcol):
            for c in range(4):
                fm_proj(c * 128, 128, 0, c)
                fm_proj(512 + c * 128, 128, 1, c)
            kb.act(sg[:], B(1), AF.Sigmoid, [pb[1]], [sg])
            kb.tt(uT[:, :, 30 + tokcol:30 + tokcol + 128], B(0).rearrange("p (c t) -> p c t", t=128),
                  sg[:].rearrange("p (c t) -> p c t", t=128), ALU.mult, [pb[0], sg], [uT])

        def conv_chunk(c, i):
            for j in range(31):
                kb.mm(B(4, c * 128, (c + 1) * 128), diag[:, c * 31 + j, :], uT[:, c, i * 128 + j:i * 128 + j + 128],
                      j == 0, j == 30, [diag, uT], [pb[4]])

        for i in range(0 if full else NTP):
            kb.dma(xs[:], x_pre[i * 128:(i + 1) * 128, :], [x_pre], [xs])
            rms_to_hT(kb, xs, gB, ident, hf, sq, ss, ps, pb[0], pb[1], hT, 0)
            state_part(False, False)
            state_update()
            if i == NTP - 1:
                for c in range(4):
                    fm_proj(c * 128, 128, 0, c)
                    fm_proj(512 + c * 128, 128, 1, c)
                kb.act(sg[:], B(1), AF.Sigmoid, [pb[1]], [sg])
                kb.tt(uT[:, :, 0:30], B(0).rearrange("p (c t) -> p c t", t=128)[:, :, 98:128],
                      sg[:].rearrange("p (c t) -> p c t", t=128)[:, :, 98:128], ALU.mult, [pb[0], sg], [uT])
        bg_per = (len(bg) + (NTP + NTO) - 1) // (NTP + NTO) if bg else 0
        for ii in range((NTP + NTO) if full else NTO):
            if full:
                isown = ii >= NTP
                i = 0
                srcx = x_own[(ii - NTP) * 128:(ii - NTP + 1) * 128, :] if isown else x_pre[ii * 128:(ii + 1) * 128, :]
                dsty = x_out[(ii - NTP) * 128:(ii - NTP + 1) * 128, :] if isown else x_out_pre[ii * 128:(ii + 1) * 128, :]
            else:
                i = ii
                srcx = x_own[i * 128:(i + 1) * 128, :]
                dsty = x_out[i * 128:(i + 1) * 128, :]
            if bg:
                kb.s.replay(bg, bg_per)
            kb.dma(xs[:], srcx, [x_own, x_pre], [xs])
            rms_to_hT(kb, xs, gB, ident, hf, sq, ss, ps, pb[0], pb[1], hT, 0)
            state_part(True, True)
            conv_u(i * 128)
            for h in range(4):
                fm_proj(1024 + h * 64, 64, 2, h)
                fm_proj(1280 + h * 64, 64, 3, h)
            for sl in range(4):
                fm_proj(2048 + sl * 128, 128, 4, sl)
            kb.act(sr[:], B(4), AF.Silu, [pb[4]], [sr])
            for h in range(4):
                kb.mm(ps[0:64, 5 * 512 + h * 128:5 * 512 + (h + 1) * 128], la[:, h * 64:(h + 1) * 64], tri[:], True, True,
                      [la, tri], [pb[5]])
            kb.act(eb[:], ps[0:64, 5 * 512:6 * 512], AF.Exp, [pb[5]], [eb])
            kb.act(enb[:], ps[0:64, 5 * 512:6 * 512], AF.Exp, [pb[5]], [enb], scale=-1.0)
            kb.stt(qt[:].rearrange("p h t -> p (h t)"), ps[0:64, 2 * 512:3 * 512], 0.125, eb[:], ALU.mult, ALU.mult, [pb[2], eb], [qt])
            kb.tt(kt[:].rearrange("p h t -> p (h t)"), ps[0:64, 3 * 512:4 * 512], enb[:], ALU.mult, [pb[3], enb], [kt])
            conv_chunk(0, i)
            for h in range(4):
                kb.mm(B(0, h * 128, (h + 1) * 128), kt[:, h, :], qt[:, h, :], True, True, [kt, qt], [pb[0]])
            conv_chunk(1, i)
            kb.tt(Am[:], B(0).rearrange("p (h t) -> p h t", t=128), tri[:].unsqueeze(1).to_broadcast([128, 4, 128]), ALU.mult,
                  [pb[0], tri], [Am])
            for h in range(4):
                kb.mm(B(1, h * 128, (h + 1) * 128), vbf[:, h * 128:(h + 1) * 128], Am[:, h, :], True, False, [vbf, Am], [pb[1]])
                kb.mm(B(1, h * 128, (h + 1) * 128), Sb[h][:], qt[:, h, :], False, True, [Sb[h], qt], [pb[1]])
            state_update()
            conv_chunk(2, i)
            kb.act(osq[:], B(1), AF.Square, [pb[1]], [osq])
            kb.mm(B(3), ones[:], osq[:], True, True, [ones, osq], [pb[3]])
            kb.ts(rs[:], B(3), 1.0 / 128, 1e-6, ALU.mult, ALU.add, [pb[3]], [rs])
            kb.act(rs[:], rs[:], AF.Sqrt, [rs], [rs])
            kb.s.op("dve", lambda g_: g_.reciprocal(rs[:], rs[:]), [rs], [rs])
            kb.tt(t1[:], B(1), rs[:], ALU.mult, [pb[1], rs], [t1])
            kb.stt(ybT[:].rearrange("p h t -> p (h t)"), t1[:], cols[:, 12:13], sr[:], ALU.mult, ALU.mult, [t1, cols, sr], [ybT])
            conv_chunk(3, i)
            for c in range(4):
                kb.ts(ycs[:, c, :], B(4, c * 128, (c + 1) * 128), cols[:, c:c + 1], None, ALU.add, None, [pb[4], cols], [ycs])
            kb.act(ysq[:], ycs[:], AF.Square, [ycs], [ysq])
            for c in range(4):
                kb.mm(B(5, 0, 128), ones[:], ycs[:, c, :], c == 0, c == 3, [ones, ycs], [pb[5]])
            for c in range(4):
                kb.mm(B(5, 128, 256), ones[:], ysq[:, c, :], c == 0, c == 3, [ones, ysq], [pb[5]])
            kb.ts(mean[:], B(5, 0, 128), 1.0 / 512, None, ALU.mult, None, [pb[5]], [mean])
            kb.tt(var[:], mean[:], mean[:], ALU.mult, [mean], [var])
            kb.stt(var[:], B(5, 128, 256), 1.0 / 512, var[:], ALU.mult, ALU.subtract, [pb[5], var], [var])
            kb.ts(var[:], var[:], 1e-6, None, ALU.add, None, [var], [var])
            kb.act(var[:], var[:], AF.Sqrt, [var], [var])
            kb.s.op("dve", lambda g_: g_.reciprocal(var[:], var[:]), [var], [var])
            kb.tt(ycs[:], ycs[:], mean[:].unsqueeze(1).to_broadcast([128, 4, 128]), ALU.subtract, [ycs, mean], [ycs])
            kb.tt(ycs[:], ycs[:], var[:].unsqueeze(1).to_broadcast([128, 4, 128]), ALU.mult, [ycs, var], [ycs])
            for c in range(4):
                kb.ts(ycs[:, c, :], ycs[:, c, :], cols[:, 4 + c:5 + c], cols[:, 8 + c:9 + c], ALU.mult, ALU.add, [ycs, cols], [ycs])
            kb.act(yaT[:], ycs[:], AF.Silu, [ycs], [yaT])
            for hh in range(2):
                for kc in range(8):
                    lhsT = yaT[:, kc, :] if kc < 4 else ybT[:, kc - 4, :]
                    kb.mm(B(6 + hh), lhsT, woutb[:, kc, hh * 512:(hh + 1) * 512], kc == 0, kc == 7, [yaT, ybT, woutb], [pb[6 + hh]])
            kb.tt(xo[:], ps[:, 6 * 512:8 * 512], xs[:], ALU.add, [pb[6], pb[7], xs], [xo])
            kb.dma(dsty, xo[:], [xo], [], e="pool")
            if full:
                kb.copy(uT[:, :, 0:30], uT[:, :, 128:158], [uT], [uT], e="pool")
        if bg:
            kb.s.replay(bg, len(bg))
        kb.s.barrier()


def fox_block(kb0, x_own, x_pre, x_out, gvec, w_in, w_out, fb, qg, kg, pflag, ident_d, tri_d, masks_d, T, TP, tag="fx", bg=None):
    nc = kb0.nc
    NTO = T // 128
    NTP = TP // 128
    NTA = NTO + NTP
    NSB = T // 512
    QT = kb0.dram("fxQT", [16, 65, T], BF16, "Internal")
    KT = kb0.dram("fxKT", [16, 65, TP + T], BF16, "Internal")
    VA = kb0.dram("fxVA", [NTA, 128, 16 * 65], BF16, "Internal")
    OT = kb0.dram("fxOT", [16, 64, T], BF16, "Internal")
    with ExitStack() as st0:
        kbp = kb0.scope(st0)
        negc = kbp.sb("negc", [128, NTA, 16], F32)
        ident = kbp.sb("identf", [128, 128], F32)
        kbp.dma(ident[:], ident_d[:], [ident_d], [ident])
        ones = kbp.sb("onesf", [128, 128], F32)
        kbp.memset(ones[:], 1.0, [ones])
        woutb = kbp.sb("woutbf", [128, 8, 1024], BF16)
        load_bf16_resident(kbp, woutb, lambda kc, c0, w: woutb[:, kc, c0:c0 + w], w_out, 8, 1024, "fwout")
        with ExitStack() as st:
            kb = kbp.scope(st)
            winb = kb.sb("winbf", [128, 8, 3088], BF16)
            load_bf16_resident(kb, winb, lambda kc, c0, w: winb[:, kc, c0:c0 + w], w_in, 8, 3088, "fwin")
            gB = kb.sb("gBf", [128, 1024], F32)
            kb.dma(gB[:], gvec.t.partition_broadcast(128)[:, 0, :], [gvec], [gB])
            fbB = kb.sb("fbB", [128, 16], F32)
            kb.dma(fbB[:], fb.t.partition_broadcast(128)[:, 0, :], [fb], [fbB])
            qgB = kb.sb("qgB", [128, 64], F32)
            kb.dma(qgB[:], qg.t.partition_broadcast(128)[:, 0, :], [qg], [qgB])
            kgB = kb.sb("kgB", [128, 64], F32)
            kb.dma(kgB[:], kg.t.partition_broadcast(128)[:, 0, :], [kg], [kgB])
            pfl = kb.sb("pfl", [128, 1], F32)
            kb.dma(pfl[:], pflag[:], [pflag], [pfl])
            tri = kb.sb("trif", [128, 128], F32)
            kb.dma(tri[:], tri_d[:], [tri_d], [tri])
            xs = kb.sb("xsf", [128, 1024], F32)
            hf = kb.sb("hff", [128, 1024], F32)
            sq = kb.sb("sqf", [128, 1024], BF16)
            ss = kb.sb("ssf", [128, 4], F32)
            hT = kb.sb("hTf", [128, 8, 128], BF16)
            nsq = kb.sb("nsq", [128, 1024], F32)
            nss = kb.sb("nss", [128, 16], F32)
            qa = kb.sb("qa", [128, 16, 65], F32)
            ka = kb.sb("ka", [128, 16, 65], F32)
            kb.memset(ka[:, :, 64:65], 1.0, [ka])
            va = kb.sb("va", [128, 16, 65], BF16)
            kb.memset(va[:, :, 0:1], 1.0, [va])
            qTs = kb.sb("qTs", [65, 16, 128], BF16)
            kTs = kb.sb("kTs", [65, 16, 128], BF16)
            lf = kb.sb("lf", [128, 16], F32)
            Lsum = kb.sb("Lsum", [128, 16], F32)
            kb.memset(Lsum[:], 0.0, [Lsum])
            ctile = kb.sb("ctile", [128, 16], F32)
            ps = st.enter_context(nc.psum_tensor("psf1", [128, 4096], F32))
            pb = [Buf(f"pbf{i}") for i in range(8)]

            def normed(dst, bank0, gt):
                src = ps[:, bank0 * 512:(bank0 + 2) * 512]
                kb.act(nsq[:], src, AF.Square, [pb[bank0], pb[bank0 + 1]], [nsq])
                kb.reduce(nss[:], nsq[:].rearrange("p (h d) -> p h d", d=64), ALU.add, [nsq], [nss])
                kb.ts(nss[:], nss[:], 1.0 / 64, 1e-6, ALU.mult, ALU.add, [nss], [nss])
                kb.act(nss[:], nss[:], AF.Sqrt, [nss], [nss])
                kb.s.op("dve", lambda g_: g_.reciprocal(nss[:], nss[:]), [nss], [nss])
                kb.tt(dst[:, :, 0:64], src.rearrange("p (h d) -> p h d", d=64), nss[:].unsqueeze(2).to_broadcast([128, 16, 64]),
                      ALU.mult, [pb[bank0], pb[bank0 + 1], nss], [dst])
                kb.tt(dst[:, :, 0:64], dst[:, :, 0:64], gt[:].unsqueeze(1).to_broadcast([128, 16, 64]), ALU.mult, [dst, gt], [dst])

            def transposed_store(src, dstT, dram, tokcol, b0):
                for h in range(16):
                    kb.tr(ps[0:65, b0 * 512 + h * 128:b0 * 512 + (h + 1) * 128], src[:, h, :], ident[:], [src, ident], [pb[b0 + h // 4]])
                for q4 in range(4):
                    kb.copy(dstT[:, q4 * 4:(q4 + 1) * 4, :], ps[0:65, (b0 + q4) * 512:(b0 + q4 + 1) * 512].rearrange("p (h t) -> p h t", t=128),
                            [pb[b0 + q4]], [dstT], e=("act" if q4 % 2 == 0 else "dve"))
                kb.dma(dram.t.rearrange("h r t -> r h t")[:, :, tokcol:tokcol + 128], dstT[:], [dstT], [], e="pool")

            for i in range(NTA):
                own = i >= NTP
                src = x_own[(i - NTP) * 128:(i - NTP + 1) * 128, :] if own else x_pre[i * 128:(i + 1) * 128, :]
                kb.dma(xs[:], src, [x_own, x_pre], [xs])
                rms_to_hT(kb, xs, gB, ident, hf, sq, ss, ps, pb[0], pb[1], hT, 0)
                for kc in range(8):
                    kb.mm(ps[:, 6 * 512:6 * 512 + 16], hT[:, kc, :], winb[:, kc, 3072:3088], kc == 0, kc == 7, [hT, winb], [pb[6]])
                kb.tt(lf[:], ps[:, 6 * 512:6 * 512 + 16], fbB[:], ALU.add, [pb[6], fbB], [lf])
                kb.act(lf[:], lf[:], AF.Exp, [lf], [lf], scale=-1.0)
                kb.act(lf[:], lf[:], AF.Ln, [lf], [lf], bias=1.0)
                kb.ts(lf[:], lf[:], -1.0, None, ALU.mult, None, [lf], [lf])
                kb.mm(ps[:, 6 * 512 + 16:6 * 512 + 32], tri[:], lf[:], True, False, [tri, lf], [pb[6]])
                kb.mm(ps[:, 6 * 512 + 16:6 * 512 + 32], ones[:], Lsum[:], False, True, [ones, Lsum], [pb[6]])
                kb.tt(Lsum[:], Lsum[:], lf[:], ALU.add, [Lsum, lf], [Lsum])
                kb.copy(ctile[:], ps[:, 6 * 512 + 16:6 * 512 + 32], [pb[6]], [ctile], e="act")
                if own:
                    kb.ts(negc[:, i, :], ctile[:], -1.0, None, ALU.mult, None, [ctile], [negc])
                else:
                    kb.ts(negc[:, i, :], ctile[:], -1.0, pfl[:, 0:1], ALU.mult, ALU.add, [ctile, pfl], [negc])
                for hh in range(2):
                    for kc in range(8):
                        kb.mm(ps[:, (2 + hh) * 512:(3 + hh) * 512], hT[:, kc, :], winb[:, kc, 1024 + hh * 512:1536 + hh * 512],
                              kc == 0, kc == 7, [hT, winb], [pb[2 + hh]])
                normed(ka, 2, kgB)
                for hh in range(2):
                    for kc in range(8):
                        kb.mm(ps[:, (4 + hh) * 512:(5 + hh) * 512], hT[:, kc, :], winb[:, kc, 2048 + hh * 512:2560 + hh * 512],
                              kc == 0, kc == 7, [hT, winb], [pb[4 + hh]])
                kb.copy(va[:, :, 1:65], ps[:, 4 * 512:6 * 512].rearrange("p (h d) -> p h d", d=64), [pb[4], pb[5]], [va], e="act")
                kb.dma(VA[i], va[:].rearrange("p h d -> p (h d)"), [va], [], e="pool")
                if own:
                    for hh in range(2):
                        for kc in range(8):
                            kb.mm(ps[:, hh * 512:(hh + 1) * 512], hT[:, kc, :], winb[:, kc, hh * 512:(hh + 1) * 512],
                                  kc == 0, kc == 7, [hT, winb], [pb[hh]])
                    normed(qa, 0, qgB)
                    kb.ts(qa[:, :, 64:65], ctile[:].unsqueeze(2), 8.0, None, ALU.mult, None, [ctile], [qa])
                transposed_store(ka, kTs, KT, i * 128, 2)
                if own:
                    transposed_store(qa, qTs, QT, (i - NTP) * 128, 2)
            kb.s.barrier()
        with ExitStack() as st:
            kb = kbp.scope(st)
            kts = kb.sb("kts", [65, TP + T], BF16)
            vas = kb.sb("vas", [128, NTA, 65], BF16)
            qts = kb.sb("qts", [65, T], BF16)
            masks = kb.sb("masksb", [128, 4, 512], F32)
            kb.dma(masks[:], masks_d[:], [masks_d], [masks])
            stmp = kb.sb("stmp", [128, 512], F32)
            pT = [kb.sb(f"pT{i}", [128, 512], BF16) for i in range(4)]
            osb = kb.sb("osb", [65, 512], F32)
            rden = kb.sb("rden", [1, 512], F32)
            oTs = kb.sb("oTs", [65, 512], BF16)
            ps = st.enter_context(nc.psum_tensor("psf2", [128, 4096], F32))
            pb = [Buf(f"pbg{i}") for i in range(8)]
            VAv = VA.t.rearrange("n p (h d) -> p n h d", d=65)
            cnt = 0
            bg_total = len(bg) if bg else 0
            for h in range(16):
                kb.dma(kts[:], KT[h], [], [kts])
                kb.dma(qts[:], QT[h], [], [qts])
                kb.dma(vas[:], VAv[:, :, h, :], [], [vas])
                for j in range(NSB):
                    if bg:
                        kb.s.replay(bg, (bg_total + 16 * NSB - 1) // (16 * NSB))
                    nkb = NTP + 4 * j + 4
                    ob = 4 + (j % 2)
                    def s_stage(kb_, a):
                        kb.mm(ps[:, a * 512:(a + 1) * 512], kts[:, kb_ * 128:(kb_ + 1) * 128], qts[:, j * 512:(j + 1) * 512], True, True,
                              [kts, qts], [pb[a]])
                        m = kb_ - NTP - 4 * j
                        if m >= 0:
                            kb.tt(stmp[:], ps[:, a * 512:(a + 1) * 512], masks[:, m, :], ALU.add, [pb[a], masks], [stmp])
                            kb.act(pT[a][:], stmp[:], AF.Exp, [stmp, negc], [pT[a]], bias=negc[:, kb_, h:h + 1], scale=0.125)
                        else:
                            kb.act(pT[a][:], ps[:, a * 512:(a + 1) * 512], AF.Exp, [pb[a], negc], [pT[a]], bias=negc[:, kb_, h:h + 1], scale=0.125)

                    def pv_stage(kb_, a):
                        kb.mm(ps[0:65, ob * 512:(ob + 1) * 512], vas[:, kb_, :], pT[a][:], kb_ == 0, kb_ == nkb - 1, [vas, pT[a]], [pb[ob]])

                    NBUF = 4
                    for kb_ in range(min(NBUF - 1, nkb)):
                        s_stage(kb_, kb_ % NBUF)
                    for kb_ in range(nkb):
                        if kb_ + NBUF - 1 < nkb:
                            s_stage(kb_ + NBUF - 1, (kb_ + NBUF - 1) % NBUF)
                        pv_stage(kb_, kb_ % NBUF)
                    kb.copy(osb[:], ps[0:65, ob * 512:(ob + 1) * 512], [pb[ob]], [osb], e="act")
                    kb.s.op("dve", lambda g_: g_.reciprocal(rden[:], osb[0:1, :]), [osb], [rden])
                    kb.mm(ps[0:65, 6 * 512:7 * 512], ones[0:1, 0:65], rden[:], True, True, [ones, rden], [pb[6]])
                    kb.tt(oTs[:], osb[:], ps[0:65, 6 * 512:7 * 512], ALU.mult, [osb, pb[6]], [oTs])
                    kb.dma(OT[h, :, j * 512:(j + 1) * 512], oTs[1:65, :], [oTs], [], e="pool")
            if bg:
                kb.s.replay(bg, len(bg))
            kb.s.barrier()
        with ExitStack() as st:
            kb = kbp.scope(st)
            oTt = [kb.sb(f"oTt{i}", [128, 8, 128], BF16) for i in range(2)]
            xs2 = [kb.sb(f"xs2{i}", [128, 1024], F32) for i in range(2)]
            xo = [kb.sb(f"xof{i}", [128, 1024], F32) for i in range(2)]
            ps = st.enter_context(nc.psum_tensor("psf3", [128, 4096], F32))
            pb = [Buf(f"pbh{i}") for i in range(8)]
            OTv = OT.t.rearrange("(p two) d t -> (two d) p t", two=2)
            for i in range(NTO):
                a = i % 2
                kb.dma(oTt[a][:], OTv[:, :, i * 128:(i + 1) * 128], [], [oTt[a]])
                kb.dma(xs2[a][:], x_own[i * 128:(i + 1) * 128, :], [x_own], [xs2[a]])
                for hh in range(2):
                    for p in range(8):
                        kb.mm(ps[:, (2 * a + hh) * 512:(2 * a + hh + 1) * 512], oTt[a][:, p, :], woutb[:, p, hh * 512:(hh + 1) * 512],
                              p == 0, p == 7, [oTt[a], woutb], [pb[2 * a + hh]])
                kb.tt(xo[a][:], ps[:, 2 * a * 512:(2 * a + 2) * 512], xs2[a][:], ALU.add, [pb[2 * a], pb[2 * a + 1], xs2[a]], [xo[a]])
                kb.dma(x_out[i * 128:(i + 1) * 128, :], xo[a][:], [xo[a]], [], e="pool")
            kb.s.barrier()


TOK = 4096


def _consts():
    k = np.arange(128)[:, None]
    q = np.arange(512)[None, :]
    fm = np.zeros((128, 4, 512), np.float32)
    for mm in range(4):
        fm[:, mm, :] = np.where((mm * 128 + k) <= q, 0.0, -240000.0)
    return {
        "ident": np.eye(128, dtype=np.float32),
        "iota": np.tile(np.arange(128, dtype=np.float32), (128, 1)),
        "tri": np.triu(np.ones((128, 128), np.float32)),
        "masks": fm,
    }


def _col4(v):
    return np.ascontiguousarray(v.reshape(4, 128).T)


_NC_CACHE = {}


def _build_fused(T):
    key = ("fused", T)
    if key in _NC_CACHE:
        return _NC_CACHE[key]
    nc = bass.Bass("TRN2", target_bir_lowering=False)
    with ExitStack() as st:
        kb = KB(nc, st)
        D = lambda n, s: kb.dram(n, s, F32, "ExternalInput")
        I = lambda n: kb.dram(n, [T, 1024], F32, "Internal")
        x, xp = D("x", [T, 1024]), D("xp", [T, 1024])
        y = kb.dram("y", [T, 1024], F32, "ExternalOutput")
        ident, iota, tri, masks = D("ident", [128, 128]), D("iota", [128, 128]), D("tri", [128, 128]), D("masks", [128, 4, 512])
        x1o, x1p, x2o, x2p, x3o = I("x1o"), I("x1p"), I("x2o"), I("x2p"), I("x3o")
        with ExitStack() as stA:
            kA = kb.scope(stA)
            cvbufs = ([kA.sb(f"cvAin{i}", [128, 2048], F32) for i in range(2)], [kA.sb(f"cvAout{i}", [128, 2048], BF16) for i in range(2)])
            bg0 = []
            kb.s.defer = bg0
            tabs0 = peer_tables(kb, D("p0_uT", [1024, 16384]), D("p0_v", [16384, 1024]), "L0", cvbufs)
            kb.s.defer = None
            mixer0_block(kb, x, xp, x1o, D("m_g", [1, 1024]), D("m_w_in", [1024, 2576]), D("m_w_out", [1024, 1024]),
                         D("m_convwT", [128, 4, 31]), D("m_convb", [128, 4]), D("m_lng", [128, 4]), D("m_lnb", [128, 4]),
                         D("m_w2", [16, 256]), D("m_gateb", [1, 256]), D("m_glag", [128, 1]), ident, tri, T, T, x_out_pre=x1p, bg=bg0)
        peer_block(kb, None, None, D("p0_g", [1, 1024]), D("p0_wq", [1024, 2048]), D("p0_keysT", [128, 2048]), None, None,
                   ident, iota, T, "L0", tables=tabs0, jobs=[(x1p, x2p, T), (x1o, x2o, T)])
        with ExitStack() as stB:
            kB = kb.scope(stB)
            cvbufs = ([kB.sb(f"cvBin{i}", [128, 2048], F32) for i in range(2)], [kB.sb(f"cvBout{i}", [128, 2048], BF16) for i in range(2)])
            bg1 = []
            kb.s.defer = bg1
            tabs1 = peer_tables(kb, D("p1_uT", [1024, 16384]), D("p1_v", [16384, 1024]), "L1", cvbufs)
            kb.s.defer = None
            fox_block(kb, x2o, x2p, x3o, D("f_g", [1, 1024]), D("f_w_in", [1024, 3088]), D("f_w_out", [1024, 1024]), D("f_fb", [1, 16]),
                      D("f_qg", [1, 64]), D("f_kg", [1, 64]), D("pflag", [128, 1]), ident, tri, masks, T, T, bg=bg1)
        peer_block(kb, x3o, y, D("p1_g", [1, 1024]), D("p1_wq", [1024, 2048]), D("p1_keysT", [128, 2048]),
                   None, None, ident, iota, T, "L1", tables=tabs1)
        kb.s.emit()
        print("fused program: ninst", kb.s.ninst, "nsem", kb.s.nsem, flush=True)
    _NC_CACHE[key] = nc
    return nc


def _fused_common(inp, cst):
    c = {"m_g": np.ascontiguousarray(inp["ev_norm_mix"][0][None]), "m_w_in": np.ascontiguousarray(inp["ev_w_in"][0]),
         "m_w_out": np.ascontiguousarray(inp["ev_w_out"][0]),
         "m_convwT": np.ascontiguousarray(inp["ev_conv_w"][0].T.reshape(4, 128, 31).transpose(1, 0, 2)),
         "m_convb": _col4(inp["ev_conv_b"][0]), "m_lng": _col4(inp["ev_conv_ln_g"][0]), "m_lnb": _col4(inp["ev_conv_ln_b"][0]),
         "m_w2": np.ascontiguousarray(inp["ev_gate_w2"][0]), "m_gateb": np.ascontiguousarray(inp["ev_gate_b"][0][None]),
         "m_glag": np.ascontiguousarray(inp["ev_gla_norm_g"][0][:, None]),
         "f_g": np.ascontiguousarray(inp["od_norm_mix"][0][None]), "f_w_in": np.ascontiguousarray(inp["od_w_in"][0]),
         "f_w_out": np.ascontiguousarray(inp["od_w_out"][0]), "f_fb": np.ascontiguousarray(inp["od_fgate_b"][0][None]),
         "f_qg": np.ascontiguousarray(inp["od_q_norm_g"][0][None]), "f_kg": np.ascontiguousarray(inp["od_k_norm_g"][0][None]),
         "ident": cst["ident"], "iota": cst["iota"], "tri": cst["tri"], "masks": cst["masks"]}
    for L in range(2):
        p = _peer_inputs(inp, L, cst)
        for k in ("g", "wq", "keysT", "uT", "v"):
            c[f"p{L}_{k}"] = p[k]
    return c


def _build(kind):
    if kind in _NC_CACHE:
        return _NC_CACHE[kind]
    nc = bass.Bass("TRN2", target_bir_lowering=False)
    with ExitStack() as st:
        kb = KB(nc, st)
        D = lambda n, s: kb.dram(n, s, F32, "ExternalInput")
        T = TOK
        if kind == "m0":
            x, xp = D("x", [T, 1024]), D("xp", [T, 1024])
            y = kb.dram("y", [T, 1024], F32, "ExternalOutput")
            mixer0_block(kb, x, xp, y, D("g", [1, 1024]), D("w_in", [1024, 2576]), D("w_out", [1024, 1024]), D("convwT", [128, 4, 31]),
                         D("convb", [128, 4]), D("lng", [128, 4]), D("lnb", [128, 4]), D("w2", [16, 256]), D("gateb", [1, 256]),
                         D("glag", [128, 1]), D("ident", [128, 128]), D("tri", [128, 128]), T, T)
        elif kind == "peer":
            x = D("x", [T, 1024])
            y = kb.dram("y", [T, 1024], F32, "ExternalOutput")
            peer_block(kb, x, y, D("g", [1, 1024]), D("wq", [1024, 2048]), D("keysT", [128, 2048]), D("uT", [1024, 16384]),
                       D("v", [16384, 1024]), D("ident", [128, 128]), D("iota", [128, 128]), T, "p0")
        elif kind == "fox":
            x, xp = D("x", [T, 1024]), D("xp", [T, 1024])
            y = kb.dram("y", [T, 1024], F32, "ExternalOutput")
            fox_block(kb, x, xp, y, D("g", [1, 1024]), D("w_in", [1024, 3088]), D("w_out", [1024, 1024]), D("fb", [1, 16]),
                      D("qg", [1, 64]), D("kg", [1, 64]), D("pflag", [128, 1]), D("ident", [128, 128]), D("tri", [128, 128]),
                      D("masks", [128, 4, 512]), T, T)
        kb.s.emit()
    _NC_CACHE[kind] = nc
    return nc


def _shards(xfull):
    own, pre = [], []
    for c in range(NCORES):
        b, half = c // 2, c % 2
        own.append(np.ascontiguousarray(xfull[b, half * TOK:(half + 1) * TOK]))
        pre.append(np.ascontiguousarray(xfull[b, 0:TOK]) if half == 1 else np.zeros((TOK, 1024), np.float32))
    return own, pre


def _gather(res):
    out = np.empty((4, 8192, 1024), np.float32)
    for c in range(NCORES):
        b, half = c // 2, c % 2
        out[b, half * TOK:(half + 1) * TOK] = res.results[c]["y"]
    return out


def _run(kind, per_core):
    nc = _build(kind)
    return run_bass_kernel_spmd(nc, per_core, core_ids=list(range(NCORES)))


def _peer_inputs(inp, layer, cst):
    keys = inp["peer_keys"][layer]
    return {"g": np.ascontiguousarray(inp["ffn_norm"][layer][None]), "wq": np.ascontiguousarray(inp["peer_wq"][layer]),
            "keysT": np.ascontiguousarray(keys.transpose(3, 0, 1, 2).reshape(128, 2048)),
            "uT": np.ascontiguousarray(inp["peer_u"][layer].T), "v": np.ascontiguousarray(inp["peer_v"][layer]),
            "ident": cst["ident"], "iota": cst["iota"]}


def kernel(**inp):
    inp = {k: np.asarray(v, dtype=np.float32) for k, v in inp.items()}
    cst = _consts()
    own, pre = _shards(inp["x"])
    common = _fused_common(inp, cst)
    pfl = [np.full((128, 1), 0.0 if c % 2 == 1 else -30000.0, np.float32) for c in range(NCORES)]
    nc = _build_fused(TOK)
    res = run_bass_kernel_spmd(nc, [dict(common, x=own[c], xp=pre[c], pflag=pfl[c]) for c in range(NCORES)],
                               core_ids=list(range(NCORES)))
    return _gather(res)


def kernel_unfused(**inp):
    inp = {k: np.asarray(v, dtype=np.float32) for k, v in inp.items()}
    cst = _consts()
    x = inp["x"]
    own, pre = _shards(x)
    common = {"g": np.ascontiguousarray(inp["ev_norm_mix"][0][None]), "w_in": np.ascontiguousarray(inp["ev_w_in"][0]),
              "w_out": np.ascontiguousarray(inp["ev_w_out"][0]),
              "convwT": np.ascontiguousarray(inp["ev_conv_w"][0].T.reshape(4, 128, 31).transpose(1, 0, 2)),
              "convb": _col4(inp["ev_conv_b"][0]), "lng": _col4(inp["ev_conv_ln_g"][0]), "lnb": _col4(inp["ev_conv_ln_b"][0]),
              "w2": np.ascontiguousarray(inp["ev_gate_w2"][0]), "gateb": np.ascontiguousarray(inp["ev_gate_b"][0][None]),
              "glag": np.ascontiguousarray(inp["ev_gla_norm_g"][0][:, None]), "ident": cst["ident"], "tri": cst["tri"]}
    x1 = _gather(_run("m0", [dict(common, x=own[c], xp=pre[c]) for c in range(NCORES)]))
    own, _ = _shards(x1)
    common = _peer_inputs(inp, 0, cst)
    x2 = _gather(_run("peer", [dict(common, x=own[c]) for c in range(NCORES)]))
    own, pre = _shards(x2)
    common = {"g": np.ascontiguousarray(inp["od_norm_mix"][0][None]), "w_in": np.ascontiguousarray(inp["od_w_in"][0]),
              "w_out": np.ascontiguousarray(inp["od_w_out"][0]), "fb": np.ascontiguousarray(inp["od_fgate_b"][0][None]),
              "qg": np.ascontiguousarray(inp["od_q_norm_g"][0][None]), "kg": np.ascontiguousarray(inp["od_k_norm_g"][0][None]),
              "ident": cst["ident"], "tri": cst["tri"], "masks": cst["masks"]}
    pfl = [np.full((128, 1), 0.0 if c % 2 == 1 else -30000.0, np.float32) for c in range(NCORES)]
    x3 = _gather(_run("fox", [dict(common, x=own[c], xp=pre[c], pflag=pfl[c]) for c in range(NCORES)]))
    own, _ = _shards(x3)
    common = _peer_inputs(inp, 1, cst)
    x4 = _gather(_run("peer", [dict(common, x=own[c]) for c in range(NCORES)]))
    return x4
```

```python
from contextlib import ExitStack

import numpy as np
import concourse.bass as bass
import concourse.mybir as mybir
from concourse.bass_utils import run_bass_kernel_spmd

F32 = mybir.dt.float32
BF16 = mybir.dt.bfloat16
U32 = mybir.dt.uint32
I32 = mybir.dt.int32
ALU = mybir.AluOpType
AF = mybir.ActivationFunctionType
AX = mybir.AxisListType

NCORES = 8
EPOCH = 30000
ENGS = ("pe", "dve", "act", "pool", "sp")


class Buf:
    __slots__ = ("w", "r", "name")

    def __init__(self, name=""):
        self.w = None
        self.r = []
        self.name = name


class TT:
    def __init__(self, t, name):
        self.t = t
        self.b = Buf(name)

    def __getitem__(self, k):
        return self.t[k]


class Sched:
    def __init__(self, nc, stack):
        self.nc = nc
        self.stack = stack
        self.q = {e: [] for e in ENGS}
        self.csem = {e: None for e in ENGS}
        self.ccnt = {e: 0 for e in ENGS}
        self.dsem = {e: [] for e in ENGS}
        self.dcnt = {e: [] for e in ENGS}
        self.drr = {e: 0 for e in ENGS}
        self.seen = {e: {} for e in ENGS}
        self.nsem = 0
        self.ninst = 0
        self.defer = None

    def _newsem(self, nm):
        self.nsem += 1
        return self.stack.enter_context(self.nc.semaphore(f"{nm}{self.nsem}"))

    def _ticket(self, e, dma):
        if not dma:
            if self.csem[e] is None or self.ccnt[e] >= EPOCH:
                self.csem[e] = self._newsem("c" + e)
                self.ccnt[e] = 0
            self.ccnt[e] += 1
            return (self.csem[e], self.ccnt[e], e, 1)
        if not self.dsem[e]:
            self.dsem[e] = [self._newsem("d" + e) for _ in range(8)]
            self.dcnt[e] = [0] * 8
        i = self.drr[e] % 8
        self.drr[e] += 1
        if self.dcnt[e][i] + 16 >= EPOCH:
            self.dsem[e][i] = self._newsem("d" + e)
            self.dcnt[e][i] = 0
        self.dcnt[e][i] += 16
        return (self.dsem[e][i], self.dcnt[e][i], e + "_dma", 16)

    def op(self, e, fn, reads=(), writes=(), dma=False):
        if self.defer is not None:
            self.defer.append((e, fn, list(reads), list(writes), dma))
            return None
        deps = {}

        def add(t):
            if t is None:
                return
            sem, val, src, _ = t
            if src == "pe" and e == "pe" and not dma:
                return
            k = id(sem)
            if self.seen[e].get(k, 0) >= val:
                return
            if k not in deps or deps[k][1] < val:
                deps[k] = (sem, val)

        for b in reads:
            b = b.b if isinstance(b, TT) else b
            add(b.w)
        for b in writes:
            b = b.b if isinstance(b, TT) else b
            add(b.w)
            for t in b.r:
                add(t)
        if dma and self.dsem[e]:
            i = self.drr[e] % 8
            if self.dcnt[e][i] > 0 and self.dcnt[e][i] + 16 < EPOCH:
                sem_, val_ = self.dsem[e][i], self.dcnt[e][i]
                k = id(sem_)
                if self.seen[e].get(k, 0) < val_ and (k not in deps or deps[k][1] < val_):
                    deps[k] = (sem_, val_)
        waits = list(deps.values())
        for sem, val in waits:
            self.seen[e][id(sem)] = val
        t = self._ticket(e, dma)
        self.q[e].append((waits, fn, t[0], t[3]))
        self.ninst += 1 + len(waits)
        for b in reads:
            b = b.b if isinstance(b, TT) else b
            b.r = [x for x in b.r if x[0] is not t[0]] + [t]
        for b in writes:
            b = b.b if isinstance(b, TT) else b
            b.w = t
            b.r = []
        return t

    def replay(self, lst, n):
        k = min(n, len(lst))
        for e, fn, reads, writes, dma in lst[:k]:
            self.op(e, fn, reads, writes, dma)
        del lst[:k]

    def barrier(self):
        waits = []
        for e in ENGS:
            if self.csem[e] is not None and self.ccnt[e] > 0:
                waits.append((self.csem[e], self.ccnt[e]))
            for sem, c in zip(self.dsem[e], self.dcnt[e]):
                if c > 0:
                    waits.append((sem, c))
        for e in ENGS:
            self.q[e].append((list(waits), None, None, 0))
            for sem, val in waits:
                self.seen[e][id(sem)] = max(self.seen[e].get(id(sem), 0), val)

    def final_wait(self, e, bufs):
        waits = []
        for b in bufs:
            b = b.b if isinstance(b, TT) else b
            for t in [b.w] + list(b.r):
                if t is not None:
                    waits.append((t[0], t[1]))
        self.q[e].append((waits, None, None, 0))

    def emit(self):
        nc = self.nc
        q = self.q

        def run(e, eng):
            for waits, fn, sem, inc in q[e]:
                for s, v in waits:
                    eng.wait_ge(s, v)
                if fn is not None:
                    ins = fn(eng)
                    ins.then_inc(sem, inc)

        with nc.Block() as block:

            @block.tensor
            def _(eng):
                run("pe", eng)

            @block.vector
            def _(eng):
                run("dve", eng)

            @block.scalar
            def _(eng):
                run("act", eng)

            @block.gpsimd
            def _(eng):
                run("pool", eng)

            @block.sync
            def _(eng):
                run("sp", eng)


class KB:
    def __init__(self, nc, stack, sched=None):
        self.nc = nc
        self.stack = stack
        self.s = sched if sched is not None else Sched(nc, stack)

    def scope(self, stack):
        return KB(self.nc, stack, self.s)

    def sb(self, name, shape, dt):
        t = self.stack.enter_context(self.nc.sbuf_tensor(name, list(shape), dt))
        return TT(t, name)

    def dram(self, name, shape, dt, kind):
        t = self.nc.dram_tensor(name, list(shape), dt, kind=kind)
        return TT(t.ap(), name)

    def dma(self, out, in_, reads, writes, e="sp", **kw):
        return self.s.op(e, lambda g: g.dma_start(out=out, in_=in_, **kw), reads, writes, dma=True)

    def mm(self, out, lhsT, rhs, start, stop, reads, writes):
        return self.s.op("pe", lambda g: g.matmul(out, lhsT, rhs, start=start, stop=stop), reads, writes)

    def tr(self, out, in_, ident, reads, writes):
        return self.s.op("pe", lambda g: g.transpose(out, in_, ident), reads, writes)

    def act(self, out, in_, func, reads, writes, bias=None, scale=None, accum_out=None):
        kw = {}
        if bias is not None:
            kw["bias"] = bias
        if scale is not None:
            kw["scale"] = scale
        if accum_out is not None:
            kw["accum_out"] = accum_out
        return self.s.op("act", lambda g: g.activation(out, in_, func, **kw), reads, writes)

    def tt(self, out, in0, in1, op, reads, writes, e="dve"):
        return self.s.op(e, lambda g: g.tensor_tensor(out, in0, in1, op), reads, writes)

    def ts(self, out, in0, s1, s2, op0, op1, reads, writes, e="dve", accum_out=None):
        if op1 is None:
            return self.s.op(e, lambda g: g.tensor_scalar(out, in0, s1, None, op0), reads, writes)
        if accum_out is not None:
            return self.s.op(e, lambda g: g.tensor_scalar(out, in0, s1, s2, op0, op1, accum_out), reads, writes)
        return self.s.op(e, lambda g: g.tensor_scalar(out, in0, s1, s2, op0, op1), reads, writes)

    def stt(self, out, in0, scalar, in1, op0, op1, reads, writes, e="dve"):
        return self.s.op(e, lambda g: g.scalar_tensor_tensor(out, in0, scalar, in1, op0, op1), reads, writes)

    def copy(self, out, in_, reads, writes, e="dve"):
        if e == "act":
            return self.s.op(e, lambda g: g.copy(out, in_), reads, writes)
        return self.s.op(e, lambda g: g.tensor_copy(out, in_), reads, writes)

    def memset(self, ap, val, writes, e="dve"):
        return self.s.op(e, lambda g: g.memset(ap, val), (), writes)

    def reduce(self, out, in_, op, reads, writes, axis=AX.X, e="dve"):
        return self.s.op(e, lambda g: g.tensor_reduce(out, in_, axis, op), reads, writes)


def to_bf16_dram(kb, src, dst, R, C, tag, bufs=None):
    with ExitStack() as st:
        k = kb.scope(st)
        W = 2048
        if bufs is None:
            stg = [k.sb(f"cv_in{tag}{i}", [128, W], F32) for i in range(2)]
            outb = [k.sb(f"cv_out{tag}{i}", [128, W], BF16) for i in range(2)]
        else:
            stg, outb = bufs
        n = 0
        for r in range(R // 128):
            for c0 in range(0, C, W):
                w = min(W, C - c0)
                i = n % 2
                k.dma(stg[i][:, 0:w], src[r * 128:(r + 1) * 128, c0:c0 + w], [src], [stg[i]])
                k.copy(outb[i][:, 0:w], stg[i][:, 0:w], [stg[i]], [outb[i]], e=("dve" if n % 2 == 0 else "act"))
                k.dma(dst[r * 128:(r + 1) * 128, c0:c0 + w], outb[i][:, 0:w], [outb[i]], [], e="pool")
                n += 1
        if bufs is None:
            k.s.barrier()


def load_bf16_resident(kb, dst_tt, dst_ap_fn, src, nrows_blocks, C, tag):
    with ExitStack() as st:
        k = kb.scope(st)
        W = 2048
        stg = [k.sb(f"ld_in{tag}{i}", [128, W], F32) for i in range(2)]
        n = 0
        for kc in range(nrows_blocks):
            for c0 in range(0, C, W):
                w = min(W, C - c0)
                i = n % 2
                k.dma(stg[i][:, 0:w], src[kc * 128:(kc + 1) * 128, c0:c0 + w], [src], [stg[i]])
                k.copy(dst_ap_fn(kc, c0, w), stg[i][:, 0:w], [stg[i]], [dst_tt], e=("dve" if n % 2 == 0 else "act"))
                n += 1
        k.s.barrier()


def peer_tables(kb0, uT, vtab, tag, bufs=None):
    us = kb0.dram(f"us{tag}", [1024, 16384], BF16, "Internal")
    vs = kb0.dram(f"vs{tag}", [16384, 1024], BF16, "Internal")
    to_bf16_dram(kb0, uT, us, 1024, 16384, tag + "u", bufs)
    to_bf16_dram(kb0, vtab, vs, 16384, 1024, tag + "v", bufs)
    return us, vs


def peer_block(kb0, x_in, x_out, gvec, wq, keysT, uT, vtab, ident_d, iota_d, T, tag, G=256, CH=2, OHT=8, tables=None, jobs=None, NSL=3):
    nc = kb0.nc
    if jobs is None:
        jobs = [(x_in, x_out, T)]
    with ExitStack() as st:
        kb = kb0.scope(st)
        TPG = G // 128
        if tables is None:
            tables = peer_tables(kb, uT, vtab, tag)
        us, vs = tables
        wqb = kb.sb("wqb" + tag, [128, 8, 2048], BF16)
        load_bf16_resident(kb, wqb, lambda kc, c0, w: wqb[:, kc, c0:c0 + w], wq, 8, 2048, tag + "wq")
        kTb = kb.sb("kTb" + tag, [128, 16 * 128], BF16)
        load_bf16_resident(kb, kTb, lambda kc, c0, w: kTb[:, c0:c0 + w], keysT, 1, 2048, tag + "kt")
        gB = kb.sb("gB" + tag, [128, 1024], F32)
        kb.dma(gB[:], gvec.t.partition_broadcast(128)[:, 0, :], [gvec], [gB])
        ident = kb.sb("ident" + tag, [128, 128], F32)
        kb.dma(ident[:], ident_d[:], [ident_d], [ident])
        iota = kb.sb("iota" + tag, [128, 128], F32)
        kb.dma(iota[:], iota_d[:], [iota_d], [iota])
        iotab = kb.sb("iotab" + tag, [128, 128], BF16)
        kb.copy(iotab[:], iota[:], [iota], [iotab])

        xs = [[kb.sb(f"xs{tag}{p}{j}", [128, 1024], F32) for j in range(TPG)] for p in range(2)]
        hT = [kb.sb(f"hT{tag}{p}", [128, 8, G], BF16) for p in range(2)]
        trioT = [[kb.sb(f"trioT{tag}{p}{j}", [128, 3, 128], F32) for j in range(TPG)] for p in range(2)]
        hf = kb.sb("hf" + tag, [128, 1024], F32)
        ss = kb.sb("ss" + tag, [128, 4], F32)
        qT = kb.sb("qT" + tag, [128, 16, G], BF16)
        sc = kb.sb("sc" + tag, [128, 16, 128], F32)
        tmp4 = kb.sb("tmp4" + tag, [128, 4, 256], F32)
        tmpb = [Buf() for _ in range(4)]
        scb = [Buf() for _ in range(4)]
        v16 = kb.sb("v16" + tag, [128, 16, 16], F32)
        i16 = kb.sb("i16" + tag, [128, 16, 16], U32)
        i16f = kb.sb("i16f" + tag, [128, 16, 16], F32)
        cand = kb.sb("cand" + tag, [128, 8, 16, 16], F32)
        s16 = kb.sb("s16" + tag, [128, 8, 16], F32)
        j16 = kb.sb("j16" + tag, [128, 8, 16], U32)
        jaf = kb.sb("jaf" + tag, [128, 8, 16], F32)
        ja = kb.sb("ja" + tag, [128, 8, 16], U32)
        jb = kb.sb("jb" + tag, [128, 8, 16], U32)
        jbf = kb.sb("jbf" + tag, [128, 8, 16], F32)
        eq = cand
        trio = kb.sb("trio" + tag, [128, 3, 128], F32)
        zz = kb.sb("zz" + tag, [128, 8], F32)
        oh1 = kb.sb("oh1" + tag, [128, OHT, 128], BF16)
        oh2 = kb.sb("oh2" + tag, [128, OHT, 128], BF16)
        ohb = [Buf() for _ in range(OHT)]
        ohb2 = [Buf() for _ in range(OHT)]
        WT = kb.sb("WT" + tag, [128, 128, G], BF16)
        WTb = [Buf() for _ in range(G // 16)]
        ub = [kb.sb(f"ub{tag}{i}", [128, 8, CH * 128], BF16) for i in range(NSL)]
        vb = [kb.sb(f"vb{tag}{i}", [128, CH, 1024], BF16) for i in range(NSL)]
        actT = [kb.sb(f"actT{tag}{i}", [128, G], BF16) for i in range(2)]
        ct = [kb.sb(f"ct{tag}{i}", [128, G], BF16) for i in range(2)]
        ps = st.enter_context(nc.psum_tensor("ps" + tag, [128, 4096], F32))
        pb = [Buf(f"pb{i}") for i in range(8)]

        def bank(i, w=512):
            return ps[:, i * 512:i * 512 + w]

        us_v = us.t.rearrange("(kc p) e -> p kc e", p=128)
        vs_v = vs.t.rearrange("(e1 p) d -> p e1 d", p=128)

        def topk_chain(vals, vdst, idst, tm, vb_, ib_, tb_, srcb):
            yield kb.s.op("dve", lambda g_: g_.max(out=vdst[:, 0:8], in_=vals), [srcb], [vb_])
            yield kb.s.op("dve", lambda g_: g_.match_replace(out=tm, in_to_replace=vdst[:, 0:8], in_values=vals,
                                                              imm_value=-1e30), [srcb, vb_], [tb_])
            yield kb.s.op("dve", lambda g_: g_.max(out=vdst[:, 8:16], in_=tm), [tb_], [vb_])
            yield kb.s.op("dve", lambda g_: g_.max_index(out=idst[:, 0:8], in_max=vdst[:, 0:8], in_values=vals), [srcb, vb_], [ib_])
            yield kb.s.op("dve", lambda g_: g_.max_index(out=idst[:, 8:16], in_max=vdst[:, 8:16], in_values=vals), [srcb, vb_], [ib_])

        def run_interleaved(chains, width=4):
            live = []
            chains = list(chains)
            while chains or live:
                while chains and len(live) < width:
                    live.append(chains.pop(0))
                nxt = []
                for ch_ in live:
                    try:
                        next(ch_)
                        nxt.append(ch_)
                    except StopIteration:
                        pass
                live = nxt

        def p1a_front(p, x_in, g, j):
            tok0 = g * G + j * 128
            xt = xs[p][j]
            kb.dma(xt[:], x_in[tok0:tok0 + 128, :], [x_in], [xt])
            kb.act(hf[:], xt[:], AF.Square, [xt], [hf])
            kb.reduce(ss[:, 0:1], hf[:], ALU.add, [hf], [ss])
            kb.ts(ss[:, 1:2], ss[:, 0:1], 1.0 / 1024, 1e-6, ALU.mult, ALU.add, [ss], [ss])
            kb.act(ss[:, 3:4], ss[:, 1:2], AF.Sqrt, [ss], [ss])
            kb.s.op("dve", lambda g_: g_.reciprocal(ss[:, 2:3], ss[:, 3:4]), [ss], [ss])
            kb.stt(hf[:], xt[:], ss[:, 2:3], gB[:], ALU.mult, ALU.mult, [xt, ss, gB], [hf])
            for hb in range(2):
                for k4 in range(4):
                    kc = hb * 4 + k4
                    kb.tr(ps[:, (6 + hb) * 512 + k4 * 128:(6 + hb) * 512 + (k4 + 1) * 128], hf[:, kc * 128:(kc + 1) * 128], ident[:],
                          [hf, ident], [pb[6 + hb]])
                kb.copy(hT[p][:, hb * 4:(hb + 1) * 4, j * 128:(j + 1) * 128],
                        bank(6 + hb).rearrange("p (k t) -> p k t", t=128), [pb[6 + hb]], [hT[p]], e="act")

        def p1a_q(p):
            for q4 in range(4):
                for c4 in range(4):
                    c = q4 * 4 + c4
                    bk = 6 + c4 // 2
                    col = bk * 512 + (c4 % 2) * G
                    for kc in range(8):
                        kb.mm(ps[:, col:col + G], wqb[:, kc, c * 128:(c + 1) * 128], hT[p][:, kc, :], kc == 0, kc == 7,
                              [wqb, hT[p]], [pb[bk]])
                for hb in range(2):
                    kb.copy(qT[:, q4 * 4 + hb * 2:q4 * 4 + hb * 2 + 2, :], bank(6 + hb).rearrange("p (k t) -> p k t", t=G),
                            [pb[6 + hb]], [qT], e="act")

        def p1a_back(p, j):
            for q4 in range(4):
                bk = 6 + q4 % 2
                for c4 in range(4):
                    c = q4 * 4 + c4
                    kb.mm(ps[:, bk * 512 + c4 * 128:bk * 512 + (c4 + 1) * 128], qT[:, c, j * 128:(j + 1) * 128], kTb[:, c * 128:(c + 1) * 128],
                          True, True, [qT, kTb], [pb[bk]])
                kb.copy(sc[:, q4 * 4:(q4 + 1) * 4, :], bank(bk).rearrange("p (k t) -> p k t", t=128), [pb[bk]], [scb[q4]], e="act")
            v16b = [Buf() for _ in range(16)]
            i16b = [Buf() for _ in range(16)]
            run_interleaved([topk_chain(sc[:, c, :], v16[:, c, :], i16[:, c, :], tmp4[:, c % 4, 0:128], v16b[c], i16b[c], tmpb[c % 4], scb[c // 4])
                             for c in range(16)])
            kb.copy(i16f[:], i16[:], i16b, [i16f], e="pool")
            v16v = v16[:].rearrange("p (h two) k -> p h two k", two=2)
            kb.tt(cand[:], v16v[:, :, 0, :].unsqueeze(3).to_broadcast([128, 8, 16, 16]),
                  v16v[:, :, 1, :].unsqueeze(2).to_broadcast([128, 8, 16, 16]), ALU.add, v16b, [cand])
            s16b = [Buf() for _ in range(8)]
            j16b = [Buf() for _ in range(8)]
            run_interleaved([topk_chain(cand[:, h, :, :].rearrange("p a b -> p (a b)"), s16[:, h, :], j16[:, h, :], tmp4[:, h % 4, :],
                                        s16b[h], j16b[h], tmpb[h % 4], cand.b) for h in range(8)])
            gt = trio[:, 2, :].rearrange("p (h k) -> p h k", k=16)
            kb.tt(gt, s16[:], s16[:, :, 0:1].to_broadcast([128, 8, 16]), ALU.subtract, s16b, [trio], e="pool")
            kb.act(gt, gt, AF.Exp, [trio], [trio])
            kb.reduce(zz[:], gt, ALU.add, [trio], [zz])
            kb.s.op("dve", lambda g_: g_.reciprocal(zz[:], zz[:]), [zz], [zz])
            kb.tt(gt, gt, zz[:].unsqueeze(2).to_broadcast([128, 8, 16]), ALU.mult, [trio, zz], [trio])
            kb.s.op("dve", lambda g_: g_.tensor_single_scalar(ja[:], j16[:], 4, ALU.logical_shift_right), j16b, [ja])
            kb.s.op("dve", lambda g_: g_.tensor_single_scalar(jb[:], j16[:], 15, ALU.bitwise_and), j16b, [jb])
            kb.copy(jaf[:], ja[:], [ja], [jaf])
            kb.copy(jbf[:], jb[:], [jb], [jbf])
            i16v = i16f[:].rearrange("p (h two) k -> p h two k", two=2)
            iota16 = iota[:, 0:16].unsqueeze(1).unsqueeze(1).to_broadcast([128, 8, 16, 16])
            for which, jf in ((0, jaf), (1, jbf)):
                kb.tt(eq[:], iota16, jf[:].unsqueeze(3).to_broadcast([128, 8, 16, 16]), ALU.is_equal, [iota, jf], [eq])
                kb.tt(eq[:], eq[:], i16v[:, :, which, :].unsqueeze(2).to_broadcast([128, 8, 16, 16]), ALU.mult, [eq, i16f], [eq])
                kb.reduce(trio[:, which, :], eq[:].rearrange("p h k a -> p (h k) a"), ALU.add, [eq], [trio])
            for w3 in range(3):
                kb.tr(ps[:, 6 * 512 + w3 * 128:6 * 512 + (w3 + 1) * 128], trio[:, w3, :], ident[:], [trio, ident], [pb[6]])
            tT = trioT[p][j]
            kb.copy(tT[:].rearrange("p a t -> p (a t)"), ps[:, 6 * 512:6 * 512 + 384], [pb[6]], [tT], e="act")

        wstate = [0]

        def p1b(p, j):
            tT = trioT[p][j]
            for t0 in range(0, 128, 16):
                half = wstate[0] % 2
                wstate[0] += 1
                for tt_ in range(16):
                    tl = (t0 + tt_) % OHT
                    tg = t0 + tt_
                    kb.ts(oh1[:, tl, :], iotab[:], tT[:, 0, tg:tg + 1], tT[:, 2, tg:tg + 1], ALU.is_equal, ALU.mult, [iotab, tT], [ohb[tl]])
                    kb.ts(oh2[:, tl, :], iotab[:], tT[:, 1, tg:tg + 1], None, ALU.is_equal, None, [iotab, tT], [ohb2[tl]])
                    col = half * 2048 + tt_ * 128
                    kb.mm(ps[:, col:col + 128], oh2[:, tl, :], oh1[:, tl, :], True, True, [ohb[tl], ohb2[tl]], [pb[half * 4 + tt_ // 4]])
                tcol = j * 128 + t0
                kb.copy(WT[:, :, tcol:tcol + 16], ps[:, half * 2048:(half + 1) * 2048].rearrange("p (t e) -> p e t", e=128),
                        [pb[half * 4 + q_] for q_ in range(4)], [WTb[tcol // 16]], e="act")

        def p2(p, x_out, g, bg):
            per = (len(bg) + 127) // 128 if bg else 0

            def u_stage(e1):
                cg, el = e1 // CH, e1 % CH
                sl = cg % NSL
                if el == 0:
                    kb.dma(ub[sl][:], us_v[:, :, cg * CH * 128:(cg + 1) * CH * 128], [us], [ub[sl]])
                    kb.dma(vb[sl][:], vs_v[:, cg * CH:(cg + 1) * CH, :], [vs], [vb[sl]], e="pool")
                a = e1 % 2
                pu = ps[:, 2048 + a * 512:2048 + a * 512 + G]
                for kc in range(8):
                    kb.mm(pu, ub[sl][:, kc, el * 128:(el + 1) * 128], hT[p][:, kc, :], kc == 0, kc == 7, [ub[sl], hT[p]], [pb[4 + a]])
                kb.act(actT[a][:], pu, AF.Gelu, [pb[4 + a]], [actT[a]])
                kb.tt(ct[a][:], actT[a][:], WT[:, e1, :], ALU.mult, [actT[a]] + WTb, [ct[a]])

            def v_stage(e1):
                cg, el = e1 // CH, e1 % CH
                sl = cg % NSL
                a = e1 % 2
                for j in range(TPG):
                    for hh in range(2):
                        kb.mm(bank(2 * j + hh), ct[a][:, j * 128:(j + 1) * 128], vb[sl][:, el, hh * 512:(hh + 1) * 512],
                              e1 == 0, e1 == 127, [ct[a], vb[sl]], [pb[2 * j + hh]])

            u_stage(0)
            for e1 in range(128):
                if e1 + 1 < 128:
                    u_stage(e1 + 1)
                v_stage(e1)
                if bg:
                    kb.s.replay(bg, per)
            xo = tmp4[:].rearrange("p a b -> p (a b)")
            for j in range(TPG):
                tok0 = g * G + j * 128
                kb.tt(xo, ps[:, j * 1024:(j + 1) * 1024], xs[p][j][:], ALU.add, [pb[2 * j], pb[2 * j + 1], xs[p][j]], tmpb)
                kb.dma(x_out[tok0:tok0 + 128, :], xo, tmpb, [], e="pool")
            if bg:
                kb.s.replay(bg, len(bg))

        groups = [(a_, b_, g_) for (a_, b_, t_) in jobs for g_ in range(t_ // G)]
        def p1a_group(p, x_in, g):
            for j in range(TPG):
                p1a_front(p, x_in, g, j)
            p1a_q(p)
            for j in range(TPG):
                p1a_back(p, j)

        p1a_group(0, groups[0][0], groups[0][2])
        for gi, (xin_, xout_, g) in enumerate(groups):
            p = gi % 2
            for j in range(TPG):
                p1b(p, j)
            bg = []
            if gi + 1 < len(groups):
                kb.s.defer = bg
                p1a_group(1 - p, groups[gi + 1][0], groups[gi + 1][2])
                kb.s.defer = None
            p2(p, xout_, g, bg)
        kb.s.barrier()


def rms_to_hT(kb, xs_t, gB, ident, hf, sq, ss, ps, pbA, pbB, hT, col0, eps=1e-6):
    kb.act(sq[:], xs_t[:], AF.Square, [xs_t], [sq])
    kb.reduce(ss[:, 0:1], sq[:], ALU.add, [sq], [ss])
    kb.ts(ss[:, 1:2], ss[:, 0:1], 1.0 / 1024, eps, ALU.mult, ALU.add, [ss], [ss])
    kb.act(ss[:, 3:4], ss[:, 1:2], AF.Sqrt, [ss], [ss])
    kb.s.op("dve", lambda g_: g_.reciprocal(ss[:, 2:3], ss[:, 3:4]), [ss], [ss])
    kb.stt(hf[:], xs_t[:], ss[:, 2:3], gB[:], ALU.mult, ALU.mult, [xs_t, ss, gB], [hf])
    for kc in range(8):
        kb.tr(ps[:, kc * 128:(kc + 1) * 128], hf[:, kc * 128:(kc + 1) * 128], ident[:], [hf, ident], [pbA if kc < 4 else pbB])
    kb.copy(hT[:, 0:4, col0:col0 + 128], ps[:, 0:512].rearrange("p (k t) -> p k t", t=128), [pbA], [hT], e="act")
    kb.copy(hT[:, 4:8, col0:col0 + 128], ps[:, 512:1024].rearrange("p (k t) -> p k t", t=128), [pbB], [hT], e="dve")


def mixer0_block(kb0, x_own, x_pre, x_out, gvec, w_in, w_out, convwT, convb, lng, lnb, w2, gateb, glag,
                 ident_d, tri_d, T, TP, tag="m0", x_out_pre=None, bg=None):
    nc = kb0.nc
    NTO = T // 128
    NTP = TP // 128
    with ExitStack() as st:
        kb = kb0.scope(st)
        winb = kb.sb("winb", [128, 8, 2576], BF16)
        load_bf16_resident(kb, winb, lambda kc, c0, w: winb[:, kc, c0:c0 + w], w_in, 8, 2576, "win")
        woutb = kb.sb("woutb", [128, 8, 1024], BF16)
        load_bf16_resident(kb, woutb, lambda kc, c0, w: woutb[:, kc, c0:c0 + w], w_out, 8, 1024, "wout")
        gB = kb.sb("gBm", [128, 1024], F32)
        kb.dma(gB[:], gvec.t.partition_broadcast(128)[:, 0, :], [gvec], [gB])
        gbB = kb.sb("gbB", [128, 256], F32)
        kb.dma(gbB[:], gateb.t.partition_broadcast(128)[:, 0, :], [gateb], [gbB])
        ident = kb.sb("identm", [128, 128], F32)
        kb.dma(ident[:], ident_d[:], [ident_d], [ident])
        tri = kb.sb("trim", [128, 128], F32)
        kb.dma(tri[:], tri_d[:], [tri_d], [tri])
        ones = kb.sb("onesm", [128, 128], F32)
        kb.memset(ones[:], 1.0, [ones])
        cw = kb.sb("cw", [128, 4, 31], F32)
        kb.dma(cw[:], convwT[:], [convwT], [cw])
        cols = kb.sb("colsm", [128, 16], F32)
        kb.dma(cols[:, 0:4], convb[:], [convb], [cols])
        kb.dma(cols[:, 4:8], lng[:], [lng], [cols])
        kb.dma(cols[:, 8:12], lnb[:], [lnb], [cols])
        kb.dma(cols[:, 12:13], glag[:], [glag], [cols])
        w2f = kb.sb("w2f", [16, 256], F32)
        kb.dma(w2f[:], w2[:], [w2], [w2f])
        w2b = kb.sb("w2b", [16, 256], BF16)
        kb.copy(w2b[:], w2f[:], [w2f], [w2b])
        diag = kb.sb("diag", [128, 4 * 31, 128], BF16)
        for c in range(4):
            for j in range(31):
                kb.ts(diag[:, c * 31 + j, :], ident[:], cw[:, c, j:j + 1], None, ALU.mult, None, [ident, cw], [diag],
                      e=("dve" if (c * 31 + j) % 2 == 0 else "pool"))
        full = x_out_pre is not None
        UW = 158 if full else 30 + T
        uT = kb.sb("uT", [128, 4, UW], BF16)
        kb.memset(uT[:, :, 0:30], 0.0, [uT])
        Sf = [kb.sb(f"Sf{h}", [64, 128], F32) for h in range(4)]
        Sb = [kb.sb(f"Sb{h}", [64, 128], BF16) for h in range(4)]
        for h in range(4):
            kb.memset(Sf[h][:], 0.0, [Sf[h]])
            kb.memset(Sb[h][:], 0.0, [Sb[h]])
        xs = kb.sb("xsm", [128, 1024], F32)
        hf = kb.sb("hfm", [128, 1024], F32)
        sq = kb.sb("sqm", [128, 1024], BF16)
        ss = kb.sb("ssm", [128, 4], F32)
        hT = kb.sb("hTm", [128, 8, 128], BF16)
        sg = kb.sb("sg", [128, 512], F32)
        glrT = kb.sb("glrT", [16, 128], BF16)
        zb = kb.sb("zb", [128, 256], F32)
        la = kb.sb("la", [128, 256], F32)
        enb_tm = kb.sb("enb_tm", [128, 256], F32)
        ebl_tm = kb.sb("ebl_tm", [128, 256], F32)
        kdec = kb.sb("kdec", [128, 256], BF16)
        vbf = kb.sb("vbf", [128, 512], BF16)
        eblc = kb.sb("eblc", [64, 8], F32)
        eb = kb.sb("eb", [64, 512], F32)
        enb = kb.sb("enb", [64, 512], F32)
        qt = kb.sb("qt", [64, 4, 128], BF16)
        kt = kb.sb("kt", [64, 4, 128], BF16)
        Am = kb.sb("Am", [128, 4, 128], BF16)
        osq = kb.sb("osq", [128, 512], F32)
        rs = kb.sb("rs", [128, 512], F32)
        sr = kb.sb("sr", [128, 512], F32)
        t1 = kb.sb("t1", [128, 512], F32)
        ybT = kb.sb("ybT", [128, 4, 128], BF16)
        ycs = kb.sb("ycs", [128, 4, 128], F32)
        ysq = kb.sb("ysq", [128, 4, 128], F32)
        mean = kb.sb("mean", [128, 128], F32)
        var = kb.sb("var", [128, 128], F32)
        yaT = kb.sb("yaT", [128, 4, 128], BF16)
        xo = kb.sb("xom", [128, 1024], F32)
        ps = st.enter_context(nc.psum_tensor("psm", [128, 4096], F32))
        pb = [Buf(f"pbm{i}") for i in range(8)]

        def B(i, lo=0, hi=512):
            return ps[:, i * 512 + lo:i * 512 + hi]

        def fm_proj(colbase, ncols, bank, slot):
            for kc in range(8):
                kb.mm(ps[0:ncols, bank * 512 + slot * 128:bank * 512 + (slot + 1) * 128], winb[:, kc, colbase:colbase + ncols],
                      hT[:, kc, :], kc == 0, kc == 7, [winb, hT], [pb[bank]])

        def state_part(u_needed, own):
            for kc in range(8):
                kb.mm(B(5), hT[:, kc, :], winb[:, kc, 1536:2048], kc == 0, kc == 7, [hT, winb], [pb[5]])
            for kc in range(8):
                kb.mm(B(6, 0, 256), hT[:, kc, :], winb[:, kc, 1280:1536], kc == 0, kc == 7, [hT, winb], [pb[6]])
            fm_proj(2560, 16, 4, 0)
            kb.copy(glrT[:], ps[0:16, 4 * 512:4 * 512 + 128], [pb[4]], [glrT], e="act")
            kb.mm(B(6, 256, 512), glrT[:], w2b[:], True, True, [glrT, w2b], [pb[6]])
            kb.tt(zb[:], B(6, 256, 512), gbB[:], ALU.add, [pb[6], gbB], [zb])
            kb.act(zb[:], zb[:], AF.Exp, [zb], [zb], scale=-1.0)
            kb.act(zb[:], zb[:], AF.Ln, [zb], [zb], bias=1.0)
            kb.ts(la[:], zb[:], -1.0 / 16.0, None, ALU.mult, None, [zb], [la])
            kb.mm(B(7, 0, 256), tri[:], la[:], True, True, [tri, la], [pb[7]])
            kb.mm(B(7, 256, 512), ones[:], la[:], True, True, [ones, la], [pb[7]])
            for h in range(4):
                kb.mm(ps[0:64, 4 * 512 + 384 + 2 * h:4 * 512 + 386 + 2 * h], la[:, h * 64:(h + 1) * 64], ones[:, 0:2], True, True,
                      [la, ones], [pb[4]])
            kb.act(eblc[:], ps[0:64, 4 * 512 + 384:4 * 512 + 392], AF.Exp, [pb[4]], [eblc])
            kb.act(enb_tm[:], B(7, 0, 256), AF.Exp, [pb[7]], [enb_tm], scale=-1.0)
            kb.act(ebl_tm[:], B(7, 256, 512), AF.Exp, [pb[7]], [ebl_tm])
            kb.tt(enb_tm[:], enb_tm[:], ebl_tm[:], ALU.mult, [enb_tm, ebl_tm], [enb_tm])
            kb.tt(kdec[:], B(6, 0, 256), enb_tm[:], ALU.mult, [pb[6], enb_tm], [kdec])
            kb.copy(vbf[:], B(5), [pb[5]], [vbf], e="act")

        def state_update():
            for h in range(4):
                kb.mm(ps[0:64, 2 * 512 + h * 128:2 * 512 + (h + 1) * 128], kdec[:, h * 64:(h + 1) * 64], vbf[:, h * 128:(h + 1) * 128],
                      True, True, [kdec, vbf], [pb[2]])
            for h in range(4):
                kb.stt(Sf[h][:], Sf[h][:], eblc[:, 2 * h:2 * h + 1], ps[0:64, 2 * 512 + h * 128:2 * 512 + (h + 1) * 128],
                       ALU.mult, ALU.add, [Sf[h], eblc, pb[2]], [Sf[h]])
                kb.copy(Sb[h][:], Sf[h][:], [Sf[h]], [Sb[h]], e="act")

        def conv_u(tokcol):
            for c in range(4):
                fm_proj(c * 128, 128, 0, c)
                fm_proj(512 + c * 128, 128, 1, c)
            kb.act(sg[:], B(1), AF.Sigmoid, [pb[1]], [sg])
            kb.tt(uT[:, :, 30 + tokcol:30 + tokcol + 128], B(0).rearrange("p (c t) -> p c t", t=128),
                  sg[:].rearrange("p (c t) -> p c t", t=128), ALU.mult, [pb[0], sg], [uT])

        def conv_chunk(c, i):
            for j in range(31):
                kb.mm(B(4, c * 128, (c + 1) * 128), diag[:, c * 31 + j, :], uT[:, c, i * 128 + j:i * 128 + j + 128],
                      j == 0, j == 30, [diag, uT], [pb[4]])

        for i in range(0 if full else NTP):
            kb.dma(xs[:], x_pre[i * 128:(i + 1) * 128, :], [x_pre], [xs])
            rms_to_hT(kb, xs, gB, ident, hf, sq, ss, ps, pb[0], pb[1], hT, 0)
            state_part(False, False)
            state_update()
            if i == NTP - 1:
                for c in range(4):
                    fm_proj(c * 128, 128, 0, c)
                    fm_proj(512 + c * 128, 128, 1, c)
                kb.act(sg[:], B(1), AF.Sigmoid, [pb[1]], [sg])
                kb.tt(uT[:, :, 0:30], B(0).rearrange("p (c t) -> p c t", t=128)[:, :, 98:128],
                      sg[:].rearrange("p (c t) -> p c t", t=128)[:, :, 98:128], ALU.mult, [pb[0], sg], [uT])
        bg_per = (len(bg) + (NTP + NTO) - 1) // (NTP + NTO) if bg else 0
        for ii in range((NTP + NTO) if full else NTO):
            if full:
                isown = ii >= NTP
                i = 0
                srcx = x_own[(ii - NTP) * 128:(ii - NTP + 1) * 128, :] if isown else x_pre[ii * 128:(ii + 1) * 128, :]
                dsty = x_out[(ii - NTP) * 128:(ii - NTP + 1) * 128, :] if isown else x_out_pre[ii * 128:(ii + 1) * 128, :]
            else:
                i = ii
                srcx = x_own[i * 128:(i + 1) * 128, :]
                dsty = x_out[i * 128:(i + 1) * 128, :]
            if bg:
                kb.s.replay(bg, bg_per)
            kb.dma(xs[:], srcx, [x_own, x_pre], [xs])
            rms_to_hT(kb, xs, gB, ident, hf, sq, ss, ps, pb[0], pb[1], hT, 0)
            state_part(True, True)
            conv_u(i * 128)
            for h in range(4):
                fm_proj(1024 + h * 64, 64, 2, h)
                fm_proj(1280 + h * 64, 64, 3, h)
            for sl in range(4):
                fm_proj(2048 + sl * 128, 128, 4, sl)
            kb.act(sr[:], B(4), AF.Silu, [pb[4]], [sr])
            for h in range(4):
                kb.mm(ps[0:64, 5 * 512 + h * 128:5 * 512 + (h + 1) * 128], la[:, h * 64:(h + 1) * 64], tri[:], True, True,
                      [la, tri], [pb[5]])
            kb.act(eb[:], ps[0:64, 5 * 512:6 * 512], AF.Exp, [pb[5]], [eb])
            kb.act(enb[:], ps[0:64, 5 * 512:6 * 512], AF.Exp, [pb[5]], [enb], scale=-1.0)
            kb.stt(qt[:].rearrange("p h t -> p (h t)"), ps[0:64, 2 * 512:3 * 512], 0.125, eb[:], ALU.mult, ALU.mult, [pb[2], eb], [qt])
            kb.tt(kt[:].rearrange("p h t -> p (h t)"), ps[0:64, 3 * 512:4 * 512], enb[:], ALU.mult, [pb[3], enb], [kt])
            conv_chunk(0, i)
            for h in range(4):
                kb.mm(B(0, h * 128, (h + 1) * 128), kt[:, h, :], qt[:, h, :], True, True, [kt, qt], [pb[0]])
            conv_chunk(1, i)
            kb.tt(Am[:], B(0).rearrange("p (h t) -> p h t", t=128), tri[:].unsqueeze(1).to_broadcast([128, 4, 128]), ALU.mult,
                  [pb[0], tri], [Am])
            for h in range(4):
                kb.mm(B(1, h * 128, (h + 1) * 128), vbf[:, h * 128:(h + 1) * 128], Am[:, h, :], True, False, [vbf, Am], [pb[1]])
                kb.mm(B(1, h * 128, (h + 1) * 128), Sb[h][:], qt[:, h, :], False, True, [Sb[h], qt], [pb[1]])
            state_update()
            conv_chunk(2, i)
            kb.act(osq[:], B(1), AF.Square, [pb[1]], [osq])
            kb.mm(B(3), ones[:], osq[:], True, True, [ones, osq], [pb[3]])
            kb.ts(rs[:], B(3), 1.0 / 128, 1e-6, ALU.mult, ALU.add, [pb[3]], [rs])
            kb.act(rs[:], rs[:], AF.Sqrt, [rs], [rs])
            kb.s.op("dve", lambda g_: g_.reciprocal(rs[:], rs[:]), [rs], [rs])
            kb.tt(t1[:], B(1), rs[:], ALU.mult, [pb[1], rs], [t1])
            kb.stt(ybT[:].rearrange("p h t -> p (h t)"), t1[:], cols[:, 12:13], sr[:], ALU.mult, ALU.mult, [t1, cols, sr], [ybT])
            conv_chunk(3, i)
            for c in range(4):
                kb.ts(ycs[:, c, :], B(4, c * 128, (c + 1) * 128), cols[:, c:c + 1], None, ALU.add, None, [pb[4], cols], [ycs])
            kb.act(ysq[:], ycs[:], AF.Square, [ycs], [ysq])
            for c in range(4):
                kb.mm(B(5, 0, 128), ones[:], ycs[:, c, :], c == 0, c == 3, [ones, ycs], [pb[5]])
            for c in range(4):
                kb.mm(B(5, 128, 256), ones[:], ysq[:, c, :], c == 0, c == 3, [ones, ysq], [pb[5]])
            kb.ts(mean[:], B(5, 0, 128), 1.0 / 512, None, ALU.mult, None, [pb[5]], [mean])
            kb.tt(var[:], mean[:], mean[:], ALU.mult, [mean], [var])
            kb.stt(var[:], B(5, 128, 256), 1.0 / 512, var[:], ALU.mult, ALU.subtract, [pb[5], var], [var])
            kb.ts(var[:], var[:], 1e-6, None, ALU.add, None, [var], [var])
            kb.act(var[:], var[:], AF.Sqrt, [var], [var])
            kb.s.op("dve", lambda g_: g_.reciprocal(var[:], var[:]), [var], [var])
            kb.tt(ycs[:], ycs[:], mean[:].unsqueeze(1).to_broadcast([128, 4, 128]), ALU.subtract, [ycs, mean], [ycs])
            kb.tt(ycs[:], ycs[:], var[:].unsqueeze(1).to_broadcast([128, 4, 128]), ALU.mult, [ycs, var], [ycs])
            for c in range(4):
                kb.ts(ycs[:, c, :], ycs[:, c, :], cols[:, 4 + c:5 + c], cols[:, 8 + c:9 + c], ALU.mult, ALU.add, [ycs, cols], [ycs])
            kb.act(yaT[:], ycs[:], AF.Silu, [ycs], [yaT])
            for hh in range(2):
                for kc in range(8):
                    lhsT = yaT[:, kc, :] if kc < 4 else ybT[:, kc - 4, :]
                    kb.mm(B(6 + hh), lhsT, woutb[:, kc, hh * 512:(hh + 1) * 512], kc == 0, kc == 7, [yaT, ybT, woutb], [pb[6 + hh]])
            kb.tt(xo[:], ps[:, 6 * 512:8 * 512], xs[:], ALU.add, [pb[6], pb[7], xs], [xo])
            kb.dma(dsty, xo[:], [xo], [], e="pool")
            if full:
                kb.copy(uT[:, :, 0:30], uT[:, :, 128:158], [uT], [uT], e="pool")
        if bg:
            kb.s.replay(bg, len(bg))
        kb.s.barrier()


def fox_block(kb0, x_own, x_pre, x_out, gvec, w_in, w_out, fb, qg, kg, pflag, ident_d, tri_d, masks_d, T, TP, tag="fx", bg=None):
    nc = kb0.nc
    NTO = T // 128
    NTP = TP // 128
    NTA = NTO + NTP
    NSB = T // 512
    QT = kb0.dram("fxQT", [16, 65, T], BF16, "Internal")
    KT = kb0.dram("fxKT", [16, 65, TP + T], BF16, "Internal")
    VA = kb0.dram("fxVA", [NTA, 128, 16 * 65], BF16, "Internal")
    OT = kb0.dram("fxOT", [16, 64, T], BF16, "Internal")
    with ExitStack() as st0:
        kbp = kb0.scope(st0)
        negc = kbp.sb("negc", [128, NTA, 16], F32)
        ident = kbp.sb("identf", [128, 128], F32)
        kbp.dma(ident[:], ident_d[:], [ident_d], [ident])
        ones = kbp.sb("onesf", [128, 128], F32)
        kbp.memset(ones[:], 1.0, [ones])
        woutb = kbp.sb("woutbf", [128, 8, 1024], BF16)
        load_bf16_resident(kbp, woutb, lambda kc, c0, w: woutb[:, kc, c0:c0 + w], w_out, 8, 1024, "fwout")
        with ExitStack() as st:
            kb = kbp.scope(st)
            winb = kb.sb("winbf", [128, 8, 3088], BF16)
            load_bf16_resident(kb, winb, lambda kc, c0, w: winb[:, kc, c0:c0 + w], w_in, 8, 3088, "fwin")
            gB = kb.sb("gBf", [128, 1024], F32)
            kb.dma(gB[:], gvec.t.partition_broadcast(128)[:, 0, :], [gvec], [gB])
            fbB = kb.sb("fbB", [128, 16], F32)
            kb.dma(fbB[:], fb.t.partition_broadcast(128)[:, 0, :], [fb], [fbB])
            qgB = kb.sb("qgB", [128, 64], F32)
            kb.dma(qgB[:], qg.t.partition_broadcast(128)[:, 0, :], [qg], [qgB])
            kgB = kb.sb("kgB", [128, 64], F32)
            kb.dma(kgB[:], kg.t.partition_broadcast(128)[:, 0, :], [kg], [kgB])
            pfl = kb.sb("pfl", [128, 1], F32)
            kb.dma(pfl[:], pflag[:], [pflag], [pfl])
            tri = kb.sb("trif", [128, 128], F32)
            kb.dma(tri[:], tri_d[:], [tri_d], [tri])
            xs = kb.sb("xsf", [128, 1024], F32)
            hf = kb.sb("hff", [128, 1024], F32)
            sq = kb.sb("sqf", [128, 1024], BF16)
            ss = kb.sb("ssf", [128, 4], F32)
            hT = kb.sb("hTf", [128, 8, 128], BF16)
            nsq = kb.sb("nsq", [128, 1024], F32)
            nss = kb.sb("nss", [128, 16], F32)
            qa = kb.sb("qa", [128, 16, 65], BF16)
            ka = kb.sb("ka", [128, 16, 65], BF16)
            identb = kb.sb("identbf", [128, 128], BF16)
            kb.copy(identb[:], ident[:], [ident], [identb])
            kb.memset(ka[:, :, 64:65], 1.0, [ka])
            va = kb.sb("va", [128, 16, 65], BF16)
            kb.memset(va[:, :, 0:1], 1.0, [va])
            qTs = kb.sb("qTs", [65, 16, 128], BF16)
            kTs = kb.sb("kTs", [65, 16, 128], BF16)
            lf = kb.sb("lf", [128, 16], F32)
            Lsum = kb.sb("Lsum", [128, 16], F32)
            kb.memset(Lsum[:], 0.0, [Lsum])
            ctile = kb.sb("ctile", [128, 16], F32)
            ps = st.enter_context(nc.psum_tensor("psf1", [128, 4096], F32))
            psb = ps[:].bitcast(BF16)
            pb = [Buf(f"pbf{i}") for i in range(8)]

            def normed(dst, bank0, gt):
                src = ps[:, bank0 * 512:(bank0 + 2) * 512]
                kb.act(nsq[:], src, AF.Square, [pb[bank0], pb[bank0 + 1]], [nsq])
                kb.reduce(nss[:], nsq[:].rearrange("p (h d) -> p h d", d=64), ALU.add, [nsq], [nss])
                kb.ts(nss[:], nss[:], 1.0 / 64, 1e-6, ALU.mult, ALU.add, [nss], [nss])
                kb.act(nss[:], nss[:], AF.Sqrt, [nss], [nss])
                kb.s.op("dve", lambda g_: g_.reciprocal(nss[:], nss[:]), [nss], [nss])
                kb.tt(nsq[:].rearrange("p (h d) -> p h d", d=64), src.rearrange("p (h d) -> p h d", d=64),
                      nss[:].unsqueeze(2).to_broadcast([128, 16, 64]), ALU.mult, [pb[bank0], pb[bank0 + 1], nss], [nsq])
                kb.tt(dst[:, :, 0:64], nsq[:].rearrange("p (h d) -> p h d", d=64), gt[:].unsqueeze(1).to_broadcast([128, 16, 64]),
                      ALU.mult, [nsq, gt], [dst])

            def transposed_store(src, dstT, dram, tokcol, b0):
                for h in range(16):
                    col = b0 * 1024 + h * 128
                    kb.tr(psb[0:65, col:col + 128], src[:, h, :], identb[:], [src, identb], [pb[b0 + h // 8]])
                for q2 in range(2):
                    kb.copy(dstT[:, q2 * 8:(q2 + 1) * 8, :], psb[0:65, (b0 + q2) * 1024:(b0 + q2 + 1) * 1024].rearrange("p (h t) -> p h t", t=128),
                            [pb[b0 + q2]], [dstT], e=("act" if q2 % 2 == 0 else "dve"))
                kb.dma(dram.t.rearrange("h r t -> r h t")[:, :, tokcol:tokcol + 128], dstT[:], [dstT], [], e="pool")

            for i in range(NTA):
                own = i >= NTP
                src = x_own[(i - NTP) * 128:(i - NTP + 1) * 128, :] if own else x_pre[i * 128:(i + 1) * 128, :]
                kb.dma(xs[:], src, [x_own, x_pre], [xs])
                rms_to_hT(kb, xs, gB, ident, hf, sq, ss, ps, pb[0], pb[1], hT, 0)
                for kc in range(8):
                    kb.mm(ps[:, 6 * 512:6 * 512 + 16], hT[:, kc, :], winb[:, kc, 3072:3088], kc == 0, kc == 7, [hT, winb], [pb[6]])
                kb.tt(lf[:], ps[:, 6 * 512:6 * 512 + 16], fbB[:], ALU.add, [pb[6], fbB], [lf])
                kb.act(lf[:], lf[:], AF.Exp, [lf], [lf], scale=-1.0)
                kb.act(lf[:], lf[:], AF.Ln, [lf], [lf], bias=1.0)
                kb.ts(lf[:], lf[:], -1.0, None, ALU.mult, None, [lf], [lf])
                kb.mm(ps[:, 6 * 512 + 16:6 * 512 + 32], tri[:], lf[:], True, False, [tri, lf], [pb[6]])
                kb.mm(ps[:, 6 * 512 + 16:6 * 512 + 32], ones[:], Lsum[:], False, True, [ones, Lsum], [pb[6]])
                kb.tt(Lsum[:], Lsum[:], lf[:], ALU.add, [Lsum, lf], [Lsum])
                kb.copy(ctile[:], ps[:, 6 * 512 + 16:6 * 512 + 32], [pb[6]], [ctile], e="act")
                if own:
                    kb.ts(negc[:, i, :], ctile[:], -1.0, None, ALU.mult, None, [ctile], [negc])
                else:
                    kb.ts(negc[:, i, :], ctile[:], -1.0, pfl[:, 0:1], ALU.mult, ALU.add, [ctile, pfl], [negc])
                for hh in range(2):
                    for kc in range(8):
                        kb.mm(ps[:, (2 + hh) * 512:(3 + hh) * 512], hT[:, kc, :], winb[:, kc, 1024 + hh * 512:1536 + hh * 512],
                              kc == 0, kc == 7, [hT, winb], [pb[2 + hh]])
                normed(ka, 2, kgB)
                for hh in range(2):
                    for kc in range(8):
                        kb.mm(ps[:, (4 + hh) * 512:(5 + hh) * 512], hT[:, kc, :], winb[:, kc, 2048 + hh * 512:2560 + hh * 512],
                              kc == 0, kc == 7, [hT, winb], [pb[4 + hh]])
                kb.copy(va[:, :, 1:65], ps[:, 4 * 512:6 * 512].rearrange("p (h d) -> p h d", d=64), [pb[4], pb[5]], [va], e="act")
                kb.dma(VA[i], va[:].rearrange("p h d -> p (h d)"), [va], [], e="pool")
                if own:
                    for hh in range(2):
                        for kc in range(8):
                            kb.mm(ps[:, hh * 512:(hh + 1) * 512], hT[:, kc, :], winb[:, kc, hh * 512:(hh + 1) * 512],
                                  kc == 0, kc == 7, [hT, winb], [pb[hh]])
                    normed(qa, 0, qgB)
                    kb.ts(qa[:, :, 64:65], ctile[:].unsqueeze(2), 8.0, None, ALU.mult, None, [ctile], [qa])
                transposed_store(ka, kTs, KT, i * 128, 2)
                if own:
                    transposed_store(qa, qTs, QT, (i - NTP) * 128, 2)
            kb.s.barrier()
        with ExitStack() as st:
            kb = kbp.scope(st)
            kts = kb.sb("kts", [65, TP + T], BF16)
            vas = kb.sb("vas", [128, NTA, 65], BF16)
            qts = kb.sb("qts", [65, T], BF16)
            masks = kb.sb("masksb", [128, 4, 512], F32)
            kb.dma(masks[:], masks_d[:], [masks_d], [masks])
            stmp = kb.sb("stmp", [128, 512], F32)
            pT = [kb.sb(f"pT{i}", [128, 512], BF16) for i in range(4)]
            osb = kb.sb("osb", [65, 512], F32)
            rden = kb.sb("rden", [1, 512], F32)
            oTs = kb.sb("oTs", [65, 512], BF16)
            ps = st.enter_context(nc.psum_tensor("psf2", [128, 4096], F32))
            pb = [Buf(f"pbg{i}") for i in range(8)]
            VAv = VA.t.rearrange("n p (h d) -> p n h d", d=65)
            cnt = 0
            bg_total = len(bg) if bg else 0
            for h in range(16):
                kb.dma(kts[:], KT[h], [], [kts])
                kb.dma(qts[:], QT[h], [], [qts])
                kb.dma(vas[:], VAv[:, :, h, :], [], [vas])
                for j in range(NSB):
                    if bg:
                        kb.s.replay(bg, (bg_total + 16 * NSB - 1) // (16 * NSB))
                    nkb = NTP + 4 * j + 4
                    ob = 4 + (j % 2)
                    def s_stage(kb_, a):
                        kb.mm(ps[:, a * 512:(a + 1) * 512], kts[:, kb_ * 128:(kb_ + 1) * 128], qts[:, j * 512:(j + 1) * 512], True, True,
                              [kts, qts], [pb[a]])
                        m = kb_ - NTP - 4 * j
                        if m >= 0:
                            kb.tt(stmp[:], ps[:, a * 512:(a + 1) * 512], masks[:, m, :], ALU.add, [pb[a], masks], [stmp])
                            kb.act(pT[a][:], stmp[:], AF.Exp, [stmp, negc], [pT[a]], bias=negc[:, kb_, h:h + 1], scale=0.125)
                        else:
                            kb.act(pT[a][:], ps[:, a * 512:(a + 1) * 512], AF.Exp, [pb[a], negc], [pT[a]], bias=negc[:, kb_, h:h + 1], scale=0.125)

                    def pv_stage(kb_, a):
                        kb.mm(ps[0:65, ob * 512:(ob + 1) * 512], vas[:, kb_, :], pT[a][:], kb_ == 0, kb_ == nkb - 1, [vas, pT[a]], [pb[ob]])

                    NBUF = 4
                    for kb_ in range(min(NBUF - 1, nkb)):
                        s_stage(kb_, kb_ % NBUF)
                    for kb_ in range(nkb):
                        if kb_ + NBUF - 1 < nkb:
                            s_stage(kb_ + NBUF - 1, (kb_ + NBUF - 1) % NBUF)
                        pv_stage(kb_, kb_ % NBUF)
                    kb.copy(osb[:], ps[0:65, ob * 512:(ob + 1) * 512], [pb[ob]], [osb], e="act")
                    kb.s.op("dve", lambda g_: g_.reciprocal(rden[:], osb[0:1, :]), [osb], [rden])
                    kb.mm(ps[0:65, 6 * 512:7 * 512], ones[0:1, 0:65], rden[:], True, True, [ones, rden], [pb[6]])
                    kb.tt(oTs[:], osb[:], ps[0:65, 6 * 512:7 * 512], ALU.mult, [osb, pb[6]], [oTs])
                    kb.dma(OT[h, :, j * 512:(j + 1) * 512], oTs[1:65, :], [oTs], [], e="pool")
            if bg:
                kb.s.replay(bg, len(bg))
            kb.s.barrier()
        with ExitStack() as st:
            kb = kbp.scope(st)
            oTt = [kb.sb(f"oTt{i}", [128, 8, 128], BF16) for i in range(2)]
            xs2 = [kb.sb(f"xs2{i}", [128, 1024], F32) for i in range(2)]
            xo = [kb.sb(f"xof{i}", [128, 1024], F32) for i in range(2)]
            ps = st.enter_context(nc.psum_tensor("psf3", [128, 4096], F32))
            pb = [Buf(f"pbh{i}") for i in range(8)]
            OTv = OT.t.rearrange("(p two) d t -> (two d) p t", two=2)
            for i in range(NTO):
                a = i % 2
                kb.dma(oTt[a][:], OTv[:, :, i * 128:(i + 1) * 128], [], [oTt[a]])
                kb.dma(xs2[a][:], x_own[i * 128:(i + 1) * 128, :], [x_own], [xs2[a]])
                for hh in range(2):
                    for p in range(8):
                        kb.mm(ps[:, (2 * a + hh) * 512:(2 * a + hh + 1) * 512], oTt[a][:, p, :], woutb[:, p, hh * 512:(hh + 1) * 512],
                              p == 0, p == 7, [oTt[a], woutb], [pb[2 * a + hh]])
                kb.tt(xo[a][:], ps[:, 2 * a * 512:(2 * a + 2) * 512], xs2[a][:], ALU.add, [pb[2 * a], pb[2 * a + 1], xs2[a]], [xo[a]])
                kb.dma(x_out[i * 128:(i + 1) * 128, :], xo[a][:], [xo[a]], [], e="pool")
            kb.s.barrier()


TOK = 4096


def _consts():
    k = np.arange(128)[:, None]
    q = np.arange(512)[None, :]
    fm = np.zeros((128, 4, 512), np.float32)
    for mm in range(4):
        fm[:, mm, :] = np.where((mm * 128 + k) <= q, 0.0, -240000.0)
    return {
        "ident": np.eye(128, dtype=np.float32),
        "iota": np.tile(np.arange(128, dtype=np.float32), (128, 1)),
        "tri": np.triu(np.ones((128, 128), np.float32)),
        "masks": fm,
    }


def _col4(v):
    return np.ascontiguousarray(v.reshape(4, 128).T)


_NC_CACHE = {}


def _build_fused(T):
    key = ("fused", T)
    if key in _NC_CACHE:
        return _NC_CACHE[key]
    nc = bass.Bass("TRN2", target_bir_lowering=False)
    with ExitStack() as st:
        kb = KB(nc, st)
        D = lambda n, s: kb.dram(n, s, F32, "ExternalInput")
        I = lambda n: kb.dram(n, [T, 1024], F32, "Internal")
        x, xp = D("x", [T, 1024]), D("xp", [T, 1024])
        y = kb.dram("y", [T, 1024], F32, "ExternalOutput")
        ident, iota, tri, masks = D("ident", [128, 128]), D("iota", [128, 128]), D("tri", [128, 128]), D("masks", [128, 4, 512])
        x1o, x1p, x2o, x2p, x3o = I("x1o"), I("x1p"), I("x2o"), I("x2p"), I("x3o")
        with ExitStack() as stA:
            kA = kb.scope(stA)
            cvbufs = ([kA.sb(f"cvAin{i}", [128, 2048], F32) for i in range(2)], [kA.sb(f"cvAout{i}", [128, 2048], BF16) for i in range(2)])
            bg0 = []
            kb.s.defer = bg0
            tabs0 = peer_tables(kb, D("p0_uT", [1024, 16384]), D("p0_v", [16384, 1024]), "L0", cvbufs)
            kb.s.defer = None
            mixer0_block(kb, x, xp, x1o, D("m_g", [1, 1024]), D("m_w_in", [1024, 2576]), D("m_w_out", [1024, 1024]),
                         D("m_convwT", [128, 4, 31]), D("m_convb", [128, 4]), D("m_lng", [128, 4]), D("m_lnb", [128, 4]),
                         D("m_w2", [16, 256]), D("m_gateb", [1, 256]), D("m_glag", [128, 1]), ident, tri, T, T, x_out_pre=x1p, bg=bg0)
        peer_block(kb, None, None, D("p0_g", [1, 1024]), D("p0_wq", [1024, 2048]), D("p0_keysT", [128, 2048]), None, None,
                   ident, iota, T, "L0", tables=tabs0, jobs=[(x1p, x2p, T), (x1o, x2o, T)])
        with ExitStack() as stB:
            kB = kb.scope(stB)
            cvbufs = ([kB.sb(f"cvBin{i}", [128, 2048], F32) for i in range(2)], [kB.sb(f"cvBout{i}", [128, 2048], BF16) for i in range(2)])
            bg1 = []
            kb.s.defer = bg1
            tabs1 = peer_tables(kb, D("p1_uT", [1024, 16384]), D("p1_v", [16384, 1024]), "L1", cvbufs)
            kb.s.defer = None
            fox_block(kb, x2o, x2p, x3o, D("f_g", [1, 1024]), D("f_w_in", [1024, 3088]), D("f_w_out", [1024, 1024]), D("f_fb", [1, 16]),
                      D("f_qg", [1, 64]), D("f_kg", [1, 64]), D("pflag", [128, 1]), ident, tri, masks, T, T, bg=bg1)
        peer_block(kb, x3o, y, D("p1_g", [1, 1024]), D("p1_wq", [1024, 2048]), D("p1_keysT", [128, 2048]),
                   None, None, ident, iota, T, "L1", tables=tabs1)
        kb.s.emit()
        print("fused program: ninst", kb.s.ninst, "nsem", kb.s.nsem, flush=True)
    _NC_CACHE[key] = nc
    return nc


def _fused_common(inp, cst):
    c = {"m_g": np.ascontiguousarray(inp["ev_norm_mix"][0][None]), "m_w_in": np.ascontiguousarray(inp["ev_w_in"][0]),
         "m_w_out": np.ascontiguousarray(inp["ev_w_out"][0]),
         "m_convwT": np.ascontiguousarray(inp["ev_conv_w"][0].T.reshape(4, 128, 31).transpose(1, 0, 2)),
         "m_convb": _col4(inp["ev_conv_b"][0]), "m_lng": _col4(inp["ev_conv_ln_g"][0]), "m_lnb": _col4(inp["ev_conv_ln_b"][0]),
         "m_w2": np.ascontiguousarray(inp["ev_gate_w2"][0]), "m_gateb": np.ascontiguousarray(inp["ev_gate_b"][0][None]),
         "m_glag": np.ascontiguousarray(inp["ev_gla_norm_g"][0][:, None]),
         "f_g": np.ascontiguousarray(inp["od_norm_mix"][0][None]), "f_w_in": np.ascontiguousarray(inp["od_w_in"][0]),
         "f_w_out": np.ascontiguousarray(inp["od_w_out"][0]), "f_fb": np.ascontiguousarray(inp["od_fgate_b"][0][None]),
         "f_qg": np.ascontiguousarray(inp["od_q_norm_g"][0][None]), "f_kg": np.ascontiguousarray(inp["od_k_norm_g"][0][None]),
         "ident": cst["ident"], "iota": cst["iota"], "tri": cst["tri"], "masks": cst["masks"]}
    for L in range(2):
        p = _peer_inputs(inp, L, cst)
        for k in ("g", "wq", "keysT", "uT", "v"):
            c[f"p{L}_{k}"] = p[k]
    return c


def _build(kind):
    if kind in _NC_CACHE:
        return _NC_CACHE[kind]
    nc = bass.Bass("TRN2", target_bir_lowering=False)
    with ExitStack() as st:
        kb = KB(nc, st)
        D = lambda n, s: kb.dram(n, s, F32, "ExternalInput")
        T = TOK
        if kind == "m0":
            x, xp = D("x", [T, 1024]), D("xp", [T, 1024])
            y = kb.dram("y", [T, 1024], F32, "ExternalOutput")
            mixer0_block(kb, x, xp, y, D("g", [1, 1024]), D("w_in", [1024, 2576]), D("w_out", [1024, 1024]), D("convwT", [128, 4, 31]),
                         D("convb", [128, 4]), D("lng", [128, 4]), D("lnb", [128, 4]), D("w2", [16, 256]), D("gateb", [1, 256]),
                         D("glag", [128, 1]), D("ident", [128, 128]), D("tri", [128, 128]), T, T)
        elif kind == "peer":
            x = D("x", [T, 1024])
            y = kb.dram("y", [T, 1024], F32, "ExternalOutput")
            peer_block(kb, x, y, D("g", [1, 1024]), D("wq", [1024, 2048]), D("keysT", [128, 2048]), D("uT", [1024, 16384]),
                       D("v", [16384, 1024]), D("ident", [128, 128]), D("iota", [128, 128]), T, "p0")
        elif kind == "fox":
            x, xp = D("x", [T, 1024]), D("xp", [T, 1024])
            y = kb.dram("y", [T, 1024], F32, "ExternalOutput")
            fox_block(kb, x, xp, y, D("g", [1, 1024]), D("w_in", [1024, 3088]), D("w_out", [1024, 1024]), D("fb", [1, 16]),
                      D("qg", [1, 64]), D("kg", [1, 64]), D("pflag", [128, 1]), D("ident", [128, 128]), D("tri", [128, 128]),
                      D("masks", [128, 4, 512]), T, T)
        kb.s.emit()
    _NC_CACHE[kind] = nc
    return nc


def _shards(xfull):
    own, pre = [], []
    for c in range(NCORES):
        b, half = c // 2, c % 2
        own.append(np.ascontiguousarray(xfull[b, half * TOK:(half + 1) * TOK]))
        pre.append(np.ascontiguousarray(xfull[b, 0:TOK]) if half == 1 else np.zeros((TOK, 1024), np.float32))
    return own, pre


def _gather(res):
    out = np.empty((4, 8192, 1024), np.float32)
    for c in range(NCORES):
        b, half = c // 2, c % 2
        out[b, half * TOK:(half + 1) * TOK] = res.results[c]["y"]
    return out


def _run(kind, per_core):
    nc = _build(kind)
    return run_bass_kernel_spmd(nc, per_core, core_ids=list(range(NCORES)))


def _peer_inputs(inp, layer, cst):
    keys = inp["peer_keys"][layer]
    return {"g": np.ascontiguousarray(inp["ffn_norm"][layer][None]), "wq": np.ascontiguousarray(inp["peer_wq"][layer]),
            "keysT": np.ascontiguousarray(keys.transpose(3, 0, 1, 2).reshape(128, 2048)),
            "uT": np.ascontiguousarray(inp["peer_u"][layer].T), "v": np.ascontiguousarray(inp["peer_v"][layer]),
            "ident": cst["ident"], "iota": cst["iota"]}


def kernel(**inp):
    inp = {k: np.asarray(v, dtype=np.float32) for k, v in inp.items()}
    cst = _consts()
    own, pre = _shards(inp["x"])
    common = _fused_common(inp, cst)
    pfl = [np.full((128, 1), 0.0 if c % 2 == 1 else -30000.0, np.float32) for c in range(NCORES)]
    nc = _build_fused(TOK)
    res = run_bass_kernel_spmd(nc, [dict(common, x=own[c], xp=pre[c], pflag=pfl[c]) for c in range(NCORES)],
                               core_ids=list(range(NCORES)))
    return _gather(res)


def kernel_unfused(**inp):
    inp = {k: np.asarray(v, dtype=np.float32) for k, v in inp.items()}
    cst = _consts()
    x = inp["x"]
    own, pre = _shards(x)
    common = {"g": np.ascontiguousarray(inp["ev_norm_mix"][0][None]), "w_in": np.ascontiguousarray(inp["ev_w_in"][0]),
              "w_out": np.ascontiguousarray(inp["ev_w_out"][0]),
              "convwT": np.ascontiguousarray(inp["ev_conv_w"][0].T.reshape(4, 128, 31).transpose(1, 0, 2)),
              "convb": _col4(inp["ev_conv_b"][0]), "lng": _col4(inp["ev_conv_ln_g"][0]), "lnb": _col4(inp["ev_conv_ln_b"][0]),
              "w2": np.ascontiguousarray(inp["ev_gate_w2"][0]), "gateb": np.ascontiguousarray(inp["ev_gate_b"][0][None]),
              "glag": np.ascontiguousarray(inp["ev_gla_norm_g"][0][:, None]), "ident": cst["ident"], "tri": cst["tri"]}
    x1 = _gather(_run("m0", [dict(common, x=own[c], xp=pre[c]) for c in range(NCORES)]))
    own, _ = _shards(x1)
    common = _peer_inputs(inp, 0, cst)
    x2 = _gather(_run("peer", [dict(common, x=own[c]) for c in range(NCORES)]))
    own, pre = _shards(x2)
    common = {"g": np.ascontiguousarray(inp["od_norm_mix"][0][None]), "w_in": np.ascontiguousarray(inp["od_w_in"][0]),
              "w_out": np.ascontiguousarray(inp["od_w_out"][0]), "fb": np.ascontiguousarray(inp["od_fgate_b"][0][None]),
              "qg": np.ascontiguousarray(inp["od_q_norm_g"][0][None]), "kg": np.ascontiguousarray(inp["od_k_norm_g"][0][None]),
              "ident": cst["ident"], "tri": cst["tri"], "masks": cst["masks"]}
    pfl = [np.full((128, 1), 0.0 if c % 2 == 1 else -30000.0, np.float32) for c in range(NCORES)]
    x3 = _gather(_run("fox", [dict(common, x=own[c], xp=pre[c], pflag=pfl[c]) for c in range(NCORES)]))
    own, _ = _shards(x3)
    common = _peer_inputs(inp, 1, cst)
    x4 = _gather(_run("peer", [dict(common, x=own[c]) for c in range(NCORES)]))
    return x4
```
